# Optimizing a Trainium2 kernel written in Bass

```python
import math
import jax, jax.numpy as jnp
from jax import lax
import numpy as np

D_MODEL = 1024
BATCH = 2
SEQ = 16384
DEPTH = 1
DEC_BATCH = 16
DEC_SEQ = 32
PAST_LEN = 1024

CHUNK = 64
HEAD_DIM = 64
N_HEADS_A = 4
N_HEADS_B = 8
WIDTH_A = N_HEADS_A * 2 * HEAD_DIM
WIDTH_B = N_HEADS_B * HEAD_DIM
MIX_WIDTH = WIDTH_A + WIDTH_B
IN_COLS = 3 * WIDTH_A + 3 * WIDTH_B
D_FF = 4 * D_MODEL
D_PLE = 256
N_BUCKETS = 32
MAX_DISTANCE = 128
BAND_CHUNKS = 8
WINDOW_B = BAND_CHUNKS * CHUNK
BAND_LEN = WINDOW_B + CHUNK
REL_CLIP = 128
Q_BLOCK = 128
EPS = 1e-6
SUBLN_EPS = 1e-5
NEG = -1e30

kernel_name = "hymba_diff_band_stream_step"


def rmsnorm(x, g, eps=EPS):
    xf = x.astype(jnp.float32)
    y = xf * lax.rsqrt(jnp.mean(xf * xf, axis=-1, keepdims=True) + eps)
    return (y * g.astype(jnp.float32)).astype(x.dtype)


def t5_bucket(rel):
    half = N_BUCKETS // 2
    n = -rel
    ret = jnp.where(n < 0, half, 0)
    n = jnp.abs(n)
    max_exact = half // 2
    nf = jnp.maximum(n, 1).astype(jnp.float32)
    large = max_exact + (jnp.log(nf / max_exact) / math.log(MAX_DISTANCE / max_exact)
                         * (half - max_exact)).astype(jnp.int32)
    large = jnp.minimum(large, half - 1)
    return ret + jnp.where(n < max_exact, n, large)


def t5_bias(table, rel):
    return jnp.transpose(table[t5_bucket(rel)].astype(jnp.float32), (2, 0, 1))


def clip_bias(table, rel):
    idx = jnp.clip(rel, -REL_CLIP, REL_CLIP) + REL_CLIP
    return jnp.transpose(table[idx].astype(jnp.float32), (2, 0, 1))


def split_proj(a, w_in):
    B, T = a.shape[:2]
    z = a @ w_in
    qa, ka, va, qb, kb, vb = jnp.split(
        z, [WIDTH_A, 2 * WIDTH_A, 3 * WIDTH_A, 3 * WIDTH_A + WIDTH_B, 3 * WIDTH_A + 2 * WIDTH_B], axis=-1)
    qa = qa.reshape(B, T, N_HEADS_A, 2, HEAD_DIM)
    ka = ka.reshape(B, T, N_HEADS_A, 2, HEAD_DIM)
    va = va.reshape(B, T, N_HEADS_A, 2 * HEAD_DIM)
    qb = qb.reshape(B, T, N_HEADS_B, HEAD_DIM)
    kb = kb.reshape(B, T, N_HEADS_B, HEAD_DIM)
    vb = vb.reshape(B, T, N_HEADS_B, HEAD_DIM)
    return qa, ka, va, qb, kb, vb


def diff_core(q, k, v, bias, mask, lam):
    s = jnp.einsum('bqhmd,bkhmd->bhmqk', q, k).astype(jnp.float32) * (HEAD_DIM ** -0.5) + bias[:, None]
    s = jnp.where(mask, s, NEG)
    p = jax.nn.softmax(s, axis=-1)
    w = p[:, :, 0] - lam * p[:, :, 1]
    return jnp.einsum('bhqk,bkhe->bqhe', w.astype(v.dtype), v)


def band_core(q, k, v, bias, mask):
    s = jnp.einsum('bqhd,bkhd->bhqk', q, k).astype(jnp.float32) * (HEAD_DIM ** -0.5) + bias[None]
    s = jnp.where(mask, s, NEG)
    p = jax.nn.softmax(s, axis=-1)
    return jnp.einsum('bhqk,bkhd->bqhd', p.astype(v.dtype), v)


def diff_attn_prompt(q, k, v, table, lam):
    B, S = q.shape[:2]
    kpos = jnp.arange(S)

    def block(bi):
        q0 = bi * Q_BLOCK
        qb = lax.dynamic_slice_in_dim(q, q0, Q_BLOCK, axis=1)
        qpos = q0 + jnp.arange(Q_BLOCK)
        mask = (kpos[None, :] // CHUNK) <= (qpos[:, None] // CHUNK)
        bias = t5_bias(table, kpos[None, :] - qpos[:, None])
        return diff_core(qb, k, v, bias, mask, lam)

    out = lax.map(block, jnp.arange(S // Q_BLOCK))
    return jnp.moveaxis(out, 0, 1).reshape(B, S, N_HEADS_A, 2 * HEAD_DIM)


def band_attn_prompt(q, k, v, table):
    B, S = q.shape[:2]
    pad = ((0, 0), (WINDOW_B, 0), (0, 0), (0, 0))
    kp = jnp.pad(k, pad)
    vp = jnp.pad(v, pad)
    j = jnp.arange(BAND_LEN)
    i = jnp.arange(CHUNK)
    bias = clip_bias(table, (j[None, :] - WINDOW_B) - i[:, None])

    def chunk(c):
        c0 = c * CHUNK
        qc = lax.dynamic_slice_in_dim(q, c0, CHUNK, axis=1)
        kc = lax.dynamic_slice_in_dim(kp, c0, BAND_LEN, axis=1)
        vc = lax.dynamic_slice_in_dim(vp, c0, BAND_LEN, axis=1)
        valid = (c0 - WINDOW_B + j) >= 0
        mask = jnp.broadcast_to(valid[None, :], (CHUNK, BAND_LEN))
        return band_core(qc, kc, vc, bias, mask)

    out = lax.map(chunk, jnp.arange(S // CHUNK))
    return jnp.moveaxis(out, 0, 1).reshape(B, S, N_HEADS_B, HEAD_DIM)


def diff_attn_sample(q, k_all, v_all, table, lam, past):
    Tn = q.shape[1]
    kpos = jnp.arange(k_all.shape[1])
    qpos = past + jnp.arange(Tn)
    mask = (kpos[None, :] // CHUNK) <= (qpos[:, None] // CHUNK)
    bias = t5_bias(table, kpos[None, :] - qpos[:, None])
    return diff_core(q, k_all, v_all, bias, mask, lam)


def band_attn_sample(q, k_all, v_all, table, past, cache_len):
    Tn = q.shape[1]
    kpos = jnp.concatenate([past - cache_len + jnp.arange(cache_len), past + jnp.arange(Tn)])
    qpos = past + jnp.arange(Tn)
    kc = kpos[None, :] // CHUNK
    qc = qpos[:, None] // CHUNK
    mask = (kc <= qc) & (kc >= qc - BAND_CHUNKS) & (kpos[None, :] >= 0)
    bias = clip_bias(table, kpos[None, :] - qpos[:, None])
    return band_core(q, k_all, v_all, bias, mask)


def merge(oa, ob, subln_g, out_norm_b, w_out, lam_init):
    B, T = oa.shape[:2]
    oa = rmsnorm(oa, subln_g, SUBLN_EPS) * (1.0 - lam_init)
    ob = rmsnorm(ob.reshape(B, T, WIDTH_B), out_norm_b)
    return jnp.concatenate([oa.reshape(B, T, WIDTH_A), ob], axis=-1) @ w_out


def ffn_ple(h, p, g_mlp, w_up, w_down, w_ple_gate, w_ple_proj):
    c = rmsnorm(h, g_mlp)
    h = h + jnp.square(jax.nn.relu(c @ w_up)) @ w_down
    return h + jax.nn.sigmoid(h @ w_ple_gate) * (p @ w_ple_proj)


def setup_inputs(seed: int = 0) -> dict:
    key = jax.random.key(seed)
    ks = jax.random.split(key, 32)
    f32 = jnp.float32
    nrm = lambda k, shape, s: jax.random.normal(k, shape, f32) * s
    cache_b_len = min(WINDOW_B, PAST_LEN)
    return {
        "x_prompt": nrm(ks[0], (BATCH, SEQ, D_MODEL), 1.0),
        "x_sample": nrm(ks[1], (DEC_BATCH, DEC_SEQ, D_MODEL), 1.0),
        "cache_a_k": nrm(ks[2], (DEPTH, DEC_BATCH, PAST_LEN, N_HEADS_A, 2, HEAD_DIM), 1.0),
        "cache_a_v": nrm(ks[3], (DEPTH, DEC_BATCH, PAST_LEN, N_HEADS_A, 2 * HEAD_DIM), 1.0),
        "cache_b_k": nrm(ks[4], (DEPTH, DEC_BATCH, cache_b_len, N_HEADS_B, HEAD_DIM), 1.0),
        "cache_b_v": nrm(ks[5], (DEPTH, DEC_BATCH, cache_b_len, N_HEADS_B, HEAD_DIM), 1.0),
        "p_prompt": nrm(ks[6], (DEPTH, BATCH, SEQ, D_PLE), 1.0),
        "p_sample": nrm(ks[7], (DEPTH, DEC_BATCH, DEC_SEQ, D_PLE), 1.0),
        "t5_table": nrm(ks[8], (N_BUCKETS, N_HEADS_A), 0.5),
        "g_attn": 1.0 + nrm(ks[9], (DEPTH, D_MODEL), 0.02),
        "w_in": nrm(ks[10], (DEPTH, D_MODEL, IN_COLS), D_MODEL ** -0.5),
        "lambda_q1": nrm(ks[11], (DEPTH, HEAD_DIM), 0.1),
        "lambda_k1": nrm(ks[12], (DEPTH, HEAD_DIM), 0.1),
        "lambda_q2": nrm(ks[13], (DEPTH, HEAD_DIM), 0.1),
        "lambda_k2": nrm(ks[14], (DEPTH, HEAD_DIM), 0.1),
        "subln_g": 1.0 + nrm(ks[15], (DEPTH, 2 * HEAD_DIM), 0.02),
        "band_table": nrm(ks[16], (DEPTH, 2 * REL_CLIP + 1, N_HEADS_B), 0.5),
        "out_norm_b": 1.0 + nrm(ks[17], (DEPTH, WIDTH_B), 0.02),
        "w_out": nrm(ks[18], (DEPTH, MIX_WIDTH, D_MODEL), MIX_WIDTH ** -0.5),
        "g_mlp": 1.0 + nrm(ks[19], (DEPTH, D_MODEL), 0.02),
        "w_up": nrm(ks[20], (DEPTH, D_MODEL, D_FF), D_MODEL ** -0.5),
        "w_down": nrm(ks[21], (DEPTH, D_FF, D_MODEL), D_FF ** -0.5),
        "w_ple_gate": nrm(ks[22], (DEPTH, D_MODEL, D_MODEL), D_MODEL ** -0.5),
        "w_ple_proj": nrm(ks[23], (DEPTH, D_PLE, D_MODEL), D_PLE ** -0.5),
        "g_final": 1.0 + nrm(ks[24], (D_MODEL,), 0.02),
    }


def reference(x_prompt, x_sample, cache_a_k, cache_a_v, cache_b_k, cache_b_v, p_prompt, p_sample,
              t5_table, g_attn, w_in, lambda_q1, lambda_k1, lambda_q2, lambda_k2, subln_g,
              band_table, out_norm_b, w_out, g_mlp, w_up, w_down, w_ple_gate, w_ple_proj, g_final):
    hp, hs = x_prompt, x_sample
    S = hp.shape[1]
    past = cache_a_k.shape[2]
    cache_b_len = cache_b_k.shape[2]
    keep = min(WINDOW_B, S)
    ak_p, av_p, bk_p, bv_p = [], [], [], []
    ak_s, av_s, bk_s, bv_s = [], [], [], []
    for i in range(DEPTH):
        lam_init = 0.8 - 0.6 * math.exp(-0.3 * i)
        lam = (jnp.exp(jnp.sum(lambda_q1[i].astype(jnp.float32) * lambda_k1[i].astype(jnp.float32)))
               - jnp.exp(jnp.sum(lambda_q2[i].astype(jnp.float32) * lambda_k2[i].astype(jnp.float32)))
               + lam_init)
        qa, ka, va, qb, kb, vb = split_proj(rmsnorm(hp, g_attn[i]), w_in[i])
        oa = diff_attn_prompt(qa, ka, va, t5_table, lam)
        ob = band_attn_prompt(qb, kb, vb, band_table[i])
        hp = hp + merge(oa, ob, subln_g[i], out_norm_b[i], w_out[i], lam_init)
        hp = ffn_ple(hp, p_prompt[i], g_mlp[i], w_up[i], w_down[i], w_ple_gate[i], w_ple_proj[i])
        ak_p.append(ka)
        av_p.append(va)
        bk_p.append(kb[:, S - keep:])
        bv_p.append(vb[:, S - keep:])
        qa, ka, va, qb, kb, vb = split_proj(rmsnorm(hs, g_attn[i]), w_in[i])
        ka_all = jnp.concatenate([cache_a_k[i], ka], axis=1)
        va_all = jnp.concatenate([cache_a_v[i], va], axis=1)
        kb_all = jnp.concatenate([cache_b_k[i], kb], axis=1)
        vb_all = jnp.concatenate([cache_b_v[i], vb], axis=1)
        oa = diff_attn_sample(qa, ka_all, va_all, t5_table, lam, past)
        ob = band_attn_sample(qb, kb_all, vb_all, band_table[i], past, cache_b_len)
        hs = hs + merge(oa, ob, subln_g[i], out_norm_b[i], w_out[i], lam_init)
        hs = ffn_ple(hs, p_sample[i], g_mlp[i], w_up[i], w_down[i], w_ple_gate[i], w_ple_proj[i])
        ak_s.append(ka)
        av_s.append(va)
        bk_s.append(kb)
        bv_s.append(vb)
    y_prompt = rmsnorm(hp, g_final)
    y_sample = rmsnorm(hs, g_final)
    return (y_prompt, y_sample,
            jnp.stack(ak_p), jnp.stack(av_p), jnp.stack(bk_p), jnp.stack(bv_p),
            jnp.stack(ak_s), jnp.stack(av_s), jnp.stack(bk_s), jnp.stack(bv_s))
```

```python
import math
import contextlib
import numpy as np
import concourse.bass as bass
import concourse.mybir as mybir
from concourse.bass_utils import run_bass_kernel_spmd

ENGS = ("pe", "act", "dve", "pool", "sp")


class Op:
    __slots__ = ("idx", "eng", "fn", "dma_key", "dma_cnt", "waits", "signal", "sigcnt", "eidx")

    def __init__(self, idx, eng, fn, dma_key):
        self.idx = idx
        self.eng = eng
        self.fn = fn
        self.dma_key = dma_key
        self.dma_cnt = 0
        self.waits = []
        self.signal = False
        self.sigcnt = 0
        self.eidx = 0


class Prog:
    def __init__(self, nc):
        self.nc = nc
        self.ops = []
        self.by_eng = {e: [] for e in ENGS}
        self.last_w = {}
        self.readers = {}
        self.seen = {e: {f: -1 for f in ENGS} for e in ENGS}
        self.seen_dma = {e: {} for e in ENGS}
        self.dma_counts = {}
        self.final_dma = []
        self.force = {}

    def op(self, eng, fn, reads=(), writes=(), dma_key=None, self_sync=True, extra=()):
        o = Op(len(self.ops), eng, fn, dma_key)
        o.eidx = len(self.by_eng[eng])
        deps = []
        for r in reads:
            w = self.last_w.get(r)
            if w is not None:
                deps.append(w)
        for w_ in writes:
            w = self.last_w.get(w_)
            if w is not None:
                deps.append(w)
            deps.extend(self.readers.get(w_, ()))
        for r in reads:
            self.readers.setdefault(r, []).append(o)
        for w_ in writes:
            self.last_w[w_] = o
            self.readers[w_] = [x for x in self.readers.get(w_, ()) if x is o]
        f = self.force.pop(eng, None)
        if f is not None:
            deps.append(f)
        deps.extend(extra)
        if dma_key is not None:
            self.dma_counts[dma_key] = self.dma_counts.get(dma_key, 0) + 16
            o.dma_cnt = self.dma_counts[dma_key]
        for d in deps:
            if d is o:
                continue
            if d.dma_key is not None:
                cur = self.seen_dma[eng].get(d.dma_key, 0)
                if cur >= d.dma_cnt:
                    continue
                self.seen_dma[eng][d.dma_key] = d.dma_cnt
                o.waits.append(("dma", d.dma_key, d.dma_cnt))
            else:
                if d.eng == eng and (eng == "pe" or not self_sync):
                    continue
                if self.seen[eng][d.eng] >= d.eidx:
                    continue
                self.seen[eng][d.eng] = d.eidx
                d.signal = True
                o.waits.append(("eng", d.eng, d))
        self.ops.append(o)
        self.by_eng[eng].append(o)
        return o

    def barrier(self, tile, skip=()):
        extra = [self.by_eng[e][-1] for e in ENGS if self.by_eng[e]]
        last_dma = {}
        for o in self.ops:
            if o.dma_key is not None and o.dma_key not in skip:
                last_dma[o.dma_key] = o
        extra.extend(last_dma.values())
        b = self.op("dve", lambda e: e.memset(tile, 0.0), extra=extra)
        for e in ENGS:
            if e != "dve":
                self.force[e] = b
        return b

    def emit(self, final_dma_keys=()):
        nc = self.nc
        import contextlib
        for o in self.ops:
            best = {}
            for w in o.waits:
                if w[0] == "dma":
                    k = ("dma", w[1])
                    if k not in best or best[k][2] < w[2]:
                        best[k] = w
                else:
                    k = ("eng", w[1])
                    if k not in best or best[k][2].eidx < w[2].eidx:
                        best[k] = w
            o.waits = list(best.values())
        for e in ENGS:
            c = 0
            for o in self.by_eng[e]:
                if o.signal:
                    c += 1
                    o.sigcnt = c
        with contextlib.ExitStack() as st:
            esem = {e: st.enter_context(nc.semaphore("s_" + e)) for e in ENGS}
            dsem = {k: st.enter_context(nc.semaphore("d_%d" % i))
                    for i, k in enumerate(sorted(self.dma_counts))}
            block = st.enter_context(nc.Block())

            def run(e, eng):
                for o in self.by_eng[e]:
                    for w in o.waits:
                        if w[0] == "dma":
                            eng.wait_ge(dsem[w[1]], w[2])
                        else:
                            eng.wait_ge(esem[w[1]], w[2].sigcnt)
                    inst = o.fn(eng)
                    if o.dma_key is not None:
                        inst.then_inc(dsem[o.dma_key], 16)
                    elif o.signal:
                        inst.then_inc(esem[e], 1)
                if e == "sp":
                    for k in final_dma_keys:
                        eng.wait_ge(dsem[k], self.dma_counts[k])

            @block.tensor
            def _(eng):
                run("pe", eng)

            @block.scalar
            def _(eng):
                run("act", eng)

            @block.vector
            def _(eng):
                run("dve", eng)

            @block.gpsimd
            def _(eng):
                run("pool", eng)

            @block.sync
            def _(eng):
                run("sp", eng)

F32 = mybir.dt.float32
BF16 = mybir.dt.bfloat16
AF = mybir.ActivationFunctionType
ALU = mybir.AluOpType
NEGM = -30000.0
NBLK = 32
NOWN = 8
SEQV = NBLK * 512


def _t5_bucket_np(rel):
    half = 16
    n = -rel
    ret = np.where(n < 0, half, 0)
    n = np.abs(n)
    max_exact = 8
    nf = np.maximum(n, 1).astype(np.float32)
    large = max_exact + (np.log(nf / np.float32(max_exact)) / np.float32(math.log(128 / max_exact))
                         * np.float32(half - max_exact)).astype(np.int32)
    large = np.minimum(large, half - 1)
    return ret + np.where(n < max_exact, n, large)


def _consts():
    ident = np.eye(128, dtype=np.float32)
    J = ident[::-1].copy()
    i = np.arange(384)
    delta = i - 255
    CA = np.zeros((32, 384), np.float32)
    bk = _t5_bucket_np(delta.astype(np.int32))
    CA[bk, i] += 1.0
    CA[15, :] -= 1.0
    CA[:, 383] = 0.0
    CB = np.zeros((384, 384), np.float32)
    idx = np.clip(delta, -128, 128) + 128
    CB[idx, i] += 1.0
    CB[0, :] -= 1.0
    CB[:, 383] = 0.0
    return ident, J, CA, CB


def build_program(phases="ABCD", nblk=32):
    global NBLK, NOWN, SEQV
    NBLK = nblk
    NOWN = nblk // 4
    SEQV = nblk * 512
    NTOK = NOWN * 512
    nc = bass.Bass("TRN2", target_bir_lowering=False)
    T = {}

    def din(name, shape, dt=F32):
        T[name] = nc.dram_tensor(name, shape, dt, kind="ExternalInput")
        return T[name]

    def dout(name, shape, dt=F32):
        T[name] = nc.dram_tensor(name, shape, dt, kind="ExternalOutput")
        return T[name]

    def dscr(name, shape, dt=BF16):
        T[name] = nc.dram_tensor(name, shape, dt, kind="Internal")
        return T[name]

    xv = din("xv", [SEQV, 1024]); pv = din("pv", [NTOK, 256])
    xsm = din("xsm", [64, 1024]); psm = din("psm", [64, 256])
    cak = din("cak", [2048, 512]); cav = din("cav", [2048, 512])
    cbk = din("cbk", [1024, 512]); cbv = din("cbv", [1024, 512])
    padmask = din("padmask", [128, 4])
    t5 = din("t5", [32, 4]); bt = din("bt", [257, 8])
    g_attn = din("g_attn", [1, 1024]); w_in = din("w_in", [1024, 3072])
    lq1 = din("lq1", [1, 64]); lk1 = din("lk1", [1, 64]); lq2 = din("lq2", [1, 64]); lk2 = din("lk2", [1, 64])
    subln = din("subln", [1, 128]); onb = din("onb", [1, 512])
    w_out = din("w_out", [1024, 1024]); g_mlp = din("g_mlp", [1, 1024])
    w_up = din("w_up", [1024, 4096]); w_down = din("w_down", [4096, 1024])
    w_gate = din("w_gate", [1024, 1024]); w_ple = din("w_ple", [256, 1024]); g_final = din("g_final", [1, 1024])
    identd = din("ident", [128, 128]); Jd = din("J", [128, 128]); CAd = din("CA", [32, 384]); CBd = din("CB", [384, 384])

    y = dout("y", [NTOK, 1024]); ys = dout("ys", [64, 1024])
    nak = dout("nak", [NTOK, 512]); nav = dout("nav", [NTOK, 512])
    nbk = dout("nbk", [512, 512]); nbv = dout("nbv", [512, 512])
    sak = dout("sak", [64, 512]); sav = dout("sav", [64, 512]); sbk = dout("sbk", [64, 512]); sbv = dout("sbv", [64, 512])

    KTs = dscr("KTs", [4, 128, SEQV]); VAs = dscr("VAs", [SEQV, 512]); QTs = dscr("QTs", [4, 128, NTOK])
    OBNs = dscr("OBNs", [NTOK, 512]); OANss = dscr("OANss", [64, 512]); OBNss = dscr("OBNss", [64, 512])
    vecA = dscr("vecA", [4, 384], F32); vecB = dscr("vecB", [8, 384], F32)
    wout_b = dscr("wout_b", [1024, 1024]); wup_b = dscr("wup_b", [1024, 4096]); wdown_b = dscr("wdown_b", [4096, 1024])
    wgate_b = dscr("wgate_b", [1024, 1024]); wple_b = dscr("wple_b", [256, 1024])

    P = Prog(nc)
    outkeys = set()
    with contextlib.ExitStack() as st:
        def sb(name, shape, dt):
            return st.enter_context(nc.sbuf_tensor(name, shape, dt))

        def psm_(name, shape, dt):
            return st.enter_context(nc.psum_tensor(name, shape, dt))

        R1 = sb("R1", [128, 33024], BF16)
        R2a = sb("R2a", [128, 8192], F32)
        R2b = sb("R2b", [128, 28800], BF16)
        idb = sb("idb", [128, 128], BF16)
        Js = sb("Js", [128, 128], F32)
        Hs = sb("Hs", [128, 128], F32)
        TA = sb("TA", [128, 4, 2, 128], F32)
        TB = sb("TB", [128, 8, 2, 128], F32)
        Tm4 = sb("Tm4", [128, 128], F32)
        chA = sb("chA", [128, 4], F32); chB = sb("chB", [128, 8], F32)
        cmA = sb("cmA", [128, 4, 3], F32); cmB = sb("cmB", [128, 8], F32)
        pmk = sb("pmk", [128, 4], F32)
        lam4 = sb("lam4", [128, 4, 64], F32)
        lcol = sb("lcol", [128, 8], F32)
        sublnbc = sb("sublnbc", [128, 128], F32)
        onbbc = sb("onbbc", [128, 512], F32)
        junk = sb("junk", [128, 1024], BF16)
        Eb = sb("Eb", [128, 2, 2, 512], BF16)
        cols = sb("cols", [128, 64], F32)
        eps6 = sb("eps6", [128, 1], F32); eps5 = sb("eps5", [128, 1], F32)
        bar = sb("bar", [128, 1], F32)
        osm = sb("osm", [128, 2, 2, 128], F32)
        t5s = sb("t5s", [32, 4], F32); bts = sb("bts", [128, 3, 8], F32)
        CAs = sb("CAs", [32, 384], F32); CBs = sb("CBs", [128, 3, 384], F32)
        vst = sb("vst", [8, 384], F32)

        TP = psm_("TP", [128, 2, 1024], BF16)
        PS = psm_("PS", [128, 6, 512], F32)
        TPf = TP.bitcast(F32)
        obanks = [(PS[:, 4, :], "PS4"), (PS[:, 5, :], "PS5"), (TPf[:, 0, :], "TP0")]

        cnt = {"ev": 0, "tp": 0, "ps": 0, "sl": 0, "ost": 0}

        KM = {"k_idb": "q1", "k_w": "q10", "k_v0": "q14", "k_v1": "q15", "k_h": "q16",
              "k_xb": "q0", "k_ktst": "q1", "k_vast": "q2", "k_qast": "q3", "k_ost0": "q4", "k_ost1": "q5", "k_obst": "q6",
              "k_win": "q7", "k_c11": "q8",
              "k_kth0": "q0", "k_kth1": "q1", "k_kth2": "q2", "k_kth3": "q3", "k_vh0": "q0", "k_vh1": "q1", "k_vh2": "q2",
              "k_vh3": "q3", "k_qth": "q4",
              "k_wr0": "q0", "k_wr1": "q1", "k_wr2": "q2", "k_wr3": "q3", "k_hres": "q4", "k_ps": "q5", "k_obn": "q6",
              "k_y": "q7", "k_c12": "q8", "k_c13": "q9",
              "k_cst": "q1", "k_vc": "q2", "k_vbc": "q3", "k_oans": "q6", "k_obns": "q10", "k_oanl": "q9"}
        for i_ in range(11):
            KM["k_c%d" % i_] = "q0"
        KM.update({"k_xb0": "q0", "k_xb1": "q11", "k_xb2": "q12", "k_xb3": "q13"})

        def dmaL(out, in_, reads=(), writes=(), key=None, eng="sp"):
            key = KM[key]
            return P.op(eng, lambda e: e.dma_start(out=out, in_=in_), reads=reads, writes=writes, dma_key=key)

        def dmaO(out, in_, reads, key):
            key = KM[key]
            outkeys.add(key)
            return P.op("sp" if "4" in phases else "pool", lambda e: e.dma_start(out=out, in_=in_), reads=reads, dma_key=key)

        def evac(out, in_, reads, writes, scale=None, eng=None):
            if eng is None:
                cnt["ev"] += 1
                eng = "act" if cnt["ev"] % 2 else "dve"
            if eng == "act":
                s = 1.0 if scale is None else scale
                return P.op("act", lambda e: e.activation(out=out, in_=in_, func=AF.Copy, scale=s), reads=reads, writes=writes)
            if scale is None:
                return P.op("dve", lambda e: e.tensor_copy(out=out, in_=in_), reads=reads, writes=writes)
            return P.op("dve", lambda e: e.tensor_scalar(out=out, in0=in_, scalar1=scale, scalar2=None, op0=ALU.mult),
                        reads=reads, writes=writes)

        pe_state = {"mode": None}

        def pe_op(K, M, fn, reads=(), writes=()):
            r = lambda x: 32 if x <= 32 else (64 if x <= 64 else 128)
            mode = (r(K), r(M))
            if pe_state["mode"] is not None and pe_state["mode"] != mode:
                P.op("pe", lambda e: e.drain())
            pe_state["mode"] = mode
            return P.op("pe", fn, reads=reads, writes=writes)

        def nextps():
            cnt["ps"] = (cnt["ps"] + 1) % 4
            return cnt["ps"]

        def rms_rstd(src, rows, F, eps_t, ci, reads, tag):
            P.op("act", lambda e: e.activation(out=junk[0:rows, 0:F], in_=src, func=AF.Square,
                                               accum_out=cols[0:rows, ci:ci + 1]),
                 reads=reads, writes=["junk", "c%d" % ci])
            P.op("act", lambda e: e.activation(out=cols[0:rows, ci + 1:ci + 2], in_=cols[0:rows, ci:ci + 1], func=AF.Sqrt,
                                               bias=eps_t[0:rows, 0:1], scale=1.0 / F),
                 reads=["c%d" % ci, "eps"], writes=["c%d" % (ci + 1)])
            P.op("dve", lambda e: e.reciprocal(out=cols[0:rows, ci + 2:ci + 3], in_=cols[0:rows, ci + 1:ci + 2]),
                 reads=["c%d" % (ci + 1)], writes=["c%d" % (ci + 2)])
            return cols[0:rows, ci + 2:ci + 3], "c%d" % (ci + 2)

        def transposes(src, rows, nch, dst, reads, writes):
            b = cnt["tp"] % 2
            cnt["tp"] += 1
            for c in range(nch):
                pe_op(rows, 128, (lambda e, c=c: e.transpose(TP[:, b, c * 128:c * 128 + rows], src[:, c * 128:(c + 1) * 128],
                                                       idb[0:rows, 0:rows])),
                     reads=list(reads) + ["idb"], writes=["TP%d" % b])
            tv = TP[:, b, :].rearrange("p (a r) -> p a r", r=128)[:, 0:nch, 0:rows]
            evac(dst, tv, reads=["TP%d" % b], writes=writes)

        def proj_fm(wfn, rhsT, n, reads):
            b = nextps()
            for c in range(8):
                pe_op(128, 128, (lambda e, c=c: e.matmul(PS[:, b, 0:n], lhsT=wfn(c), rhs=rhsT[:, c, 0:n], start=(c == 0), stop=(c == 7))),
                     reads=reads, writes=["PS%d" % b])
            return PS[:, b, 0:n], "PS%d" % b

        def proj_tm(xT, tok0, rows, wfn, reads, nk=8):
            b = nextps()
            for c in range(nk):
                pe_op(128, rows, (lambda e, c=c: e.matmul(PS[0:rows, b, :], lhsT=xT[:, c, tok0:tok0 + rows], rhs=wfn(c),
                                                    start=(c == 0), stop=(c == nk - 1))),
                     reads=reads, writes=["PS%d" % b])
            return PS[0:rows, b, :], "PS%d" % b

        osb_state = {"i": 0}

        def pair_attn(keytiles, QT, qtiles, ed, finalize, qreads, osb=None):
            nqt = len(qtiles)
            per_bank = 512 // (ed + 1)
            started = set()

            def oloc(u, qi):
                g = u * nqt + qi
                bank, slot = divmod(g, per_bank)
                ap, nm = obanks[bank]
                rows = qtiles[qi][1]
                return ap[0:rows, slot * (ed + 1):(slot + 1) * (ed + 1)], nm, bank

            def qk(kt):
                sl = cnt["sl"] % 2
                cnt["sl"] += 1
                kt["sl"] = sl
                nk = kt["nk"]
                c0 = qtiles[kt["qlo"]][0]
                c1 = qtiles[kt["qhi"]][0] + qtiles[kt["qhi"]][1]
                kt["c"] = (c0, c1)
                for u in range(2):
                    pe_op(128, nk, (lambda e, u=u: e.matmul(PS[0:nk, sl * 2 + u, c0:c1], lhsT=kt["KT"],
                                                            rhs=QT[u][:, c0:c1], start=True, stop=True)),
                         reads=list(kt["reads"]) + list(qreads), writes=["PS%d" % (sl * 2 + u)])
                for (qi, Ts) in ([] if "k" in phases else kt["adds"]):
                    q0, qr = qtiles[qi]
                    for u in range(2):
                        P.op("dve", (lambda e, u=u, q0=q0, qr=qr, Ts=Ts: e.tensor_tensor(
                            out=PS[0:nk, sl * 2 + u, q0:q0 + qr], in0=PS[0:nk, sl * 2 + u, q0:q0 + qr], in1=Ts[u], op=ALU.add)),
                            reads=["PS%d" % (sl * 2 + u), "Tt"], writes=["PS%d" % (sl * 2 + u)], self_sync=False)
                if kt["bias"][0] is kt["bias"][1]:
                    P.op("act", lambda e: e.activation(out=Eb[0:nk, sl, :, c0:c1], in_=PS[0:nk, sl * 2:sl * 2 + 2, c0:c1],
                                                       func=AF.Exp, bias=kt["bias"][0], scale=1.0),
                         reads=["PS%d" % (sl * 2), "PS%d" % (sl * 2 + 1), "bias"], writes=["E%d" % sl])
                else:
                    for u in range(2):
                        P.op("act", (lambda e, u=u: e.activation(out=Eb[0:nk, sl, u, c0:c1], in_=PS[0:nk, sl * 2 + u, c0:c1],
                                                                 func=AF.Exp, bias=kt["bias"][u], scale=1.0)),
                             reads=["PS%d" % (sl * 2 + u), "bias"], writes=["E%d" % sl])

            def pvm(kt):
                if "l" in phases:
                    return
                sl = kt["sl"]
                nk = kt["nk"]
                for u in range(2):
                    for qi in range(kt["qlo"], kt["qhi"] + 1):
                        oap, nm, bank = oloc(u, qi)
                        q0, qr = qtiles[qi]
                        first = bank not in started
                        started.add(bank)
                        pe_op(nk, qr, (lambda e, u=u, oap=oap, q0=q0, qr=qr, first=first: e.matmul(
                            oap, lhsT=Eb[0:nk, sl, u, q0:q0 + qr], rhs=kt["V"][u], start=first, stop=False,
                            skip_group_check=True)),
                            reads=["E%d" % sl] + list(kt["reads"]), writes=[nm])

            prev = None
            for kt in keytiles:
                qk(kt)
                if prev is not None:
                    pvm(prev)
                prev = kt
            pvm(prev)
            O = [[oloc(u, qi)[0] for qi in range(nqt)] for u in range(2)]
            names = sorted({oloc(u, qi)[1] for u in range(2) for qi in range(nqt)})
            if osb is not None:
                sset = osb[osb_state["i"] % len(osb)]
                osb_state["i"] += 1
                used = sorted({oloc(u, qi)[2] for u in range(2) for qi in range(nqt)})
                for bk in used:
                    bap, bnm = obanks[bk]
                    P.op("dve", (lambda e, bk=bk, bap=bap: e.tensor_copy(out=sset[0][:, bk, :], in_=bap)), reads=[bnm],
                         writes=[sset[1] + str(bk)])

                def oloc2(u, qi):
                    g = u * nqt + qi
                    bank, slot = divmod(g, per_bank)
                    rows = qtiles[qi][1]
                    return sset[0][0:rows, bank, slot * (ed + 1):(slot + 1) * (ed + 1)], sset[1] + str(bank)
                O = [[oloc2(u, qi)[0] for qi in range(nqt)] for u in range(2)]
                names = sorted({oloc2(u, qi)[1] for u in range(2) for qi in range(nqt)})
            if "m" not in phases:
                finalize(O, names)

        def load_win():
            Wv = R1[:, 0:24576].rearrange("p (c n) -> p c n", n=3072)
            for c in range(8):
                for hh in range(2):
                    dmaL(Wv[:, c, hh * 1536:(hh + 1) * 1536], w_in.ap()[c * 128:(c + 1) * 128, hh * 1536:(hh + 1) * 1536],
                         writes=["W"], key="k_win", eng="pool")
            return Wv

        Wv = load_win()
        P.op("dve", lambda e: e.memset(eps6[:], 1e-6), writes=["eps"])
        P.op("dve", lambda e: e.memset(eps5[:], 1e-5), writes=["eps"])
        dmaL(idb[:], identd.ap(), writes=["idb"], key="k_idb", eng="pool")
        dmaL(Js[:], Jd.ap(), writes=["Js"], key="k_c0")
        dmaL(t5s[:], t5.ap(), writes=["t5s"], key="k_c1")
        P.op("dve", lambda e: e.memset(bts[:], 0.0), writes=["bts"])
        dmaL(bts[:, 0:2, :], bt.ap()[0:256, :].rearrange("(a p) h -> p a h", p=128), writes=["bts"], key="k_c2")
        dmaL(bts[0:1, 2, :], bt.ap()[256:257, :], writes=["bts"], key="k_c2")
        dmaL(CAs[:], CAd.ap(), writes=["CAs"], key="k_c3")
        dmaL(CBs[:], CBd.ap().rearrange("(a p) n -> p a n", p=128), writes=["CBs"], key="k_c4")
        dmaL(chA[:], bass.AP(t5, 15 * 4, [[0, 128], [1, 4]]), writes=["chA"], key="k_c5")
        dmaL(chB[:], bass.AP(bt, 0, [[0, 128], [1, 8]]), writes=["chB"], key="k_c6")
        dmaL(pmk[:], padmask.ap(), writes=["pmk"], key="k_c7")
        for i, lt in enumerate([lq1, lk1, lq2, lk2]):
            dmaL(lam4[:, i, :], bass.AP(lt, 0, [[0, 128], [1, 64]]), writes=["lam4"], key="k_c8")
        dmaL(sublnbc[:], bass.AP(subln, 0, [[0, 128], [1, 128]]), writes=["sublnbc"], key="k_c9")
        dmaL(onbbc[:], bass.AP(onb, 0, [[0, 128], [1, 512]]), writes=["onbbc"], key="k_c10")
        P.barrier(bar[:], skip=("q7",))
        P.op("dve", lambda e: e.tensor_scalar(out=sublnbc[:], in0=sublnbc[:], scalar1=0.8, scalar2=None, op0=ALU.mult),
             reads=["sublnbc"], writes=["sublnbc"])
        P.op("dve", lambda e: e.tensor_tensor(out=lam4[:, 0, :], in0=lam4[:, 0, :], in1=lam4[:, 1, :], op=ALU.mult),
             reads=["lam4"], writes=["lam4"])
        P.op("dve", lambda e: e.tensor_tensor(out=lam4[:, 2, :], in0=lam4[:, 2, :], in1=lam4[:, 3, :], op=ALU.mult),
             reads=["lam4"], writes=["lam4"])
        P.op("dve", lambda e: e.tensor_reduce(out=lcol[:, 0:1], in_=lam4[:, 0, :], axis=mybir.AxisListType.X, op=ALU.add),
             reads=["lam4"], writes=["lcol"])
        P.op("dve", lambda e: e.tensor_reduce(out=lcol[:, 1:2], in_=lam4[:, 2, :], axis=mybir.AxisListType.X, op=ALU.add),
             reads=["lam4"], writes=["lcol"])
        P.op("act", lambda e: e.activation(out=lcol[:, 2:4], in_=lcol[:, 0:2], func=AF.Exp), reads=["lcol"], writes=["lcol"])
        P.op("dve", lambda e: e.scalar_tensor_tensor(out=lcol[:, 4:5], in0=lcol[:, 3:4], scalar=-0.2, in1=lcol[:, 2:3],
                                                     op0=ALU.add, op1=ALU.subtract), reads=["lcol"], writes=["neglam"])
        neglam = lcol[:, 4:5]
        for h in range(4):
            P.op("dve", (lambda e, h=h: e.tensor_scalar(out=cmA[:, h, :], in0=pmk[:, 0:3], scalar1=chA[:, h:h + 1], scalar2=None,
                                                        op0=ALU.add)), reads=["pmk", "chA"], writes=["bias"])
        P.op("dve", lambda e: e.tensor_scalar(out=cmB[:], in0=chB[:], scalar1=pmk[:, 2:3], scalar2=None, op0=ALU.add),
             reads=["pmk", "chB"], writes=["bias"])
        pe_op(32, 4, lambda e: e.matmul(PS[0:4, 0, 0:384], lhsT=t5s[:], rhs=CAs[:], start=True, stop=True),
             reads=["t5s", "CAs"], writes=["PS0"])
        P.op("dve", lambda e: e.tensor_copy(out=vst[0:4, :], in_=PS[0:4, 0, 0:384]), reads=["PS0"], writes=["vst"])
        dmaL(vecA.ap(), vst[0:4, :], reads=["vst"], writes=["vecA"], key="k_v0")
        for a in range(3):
            pe_op(128, 8, (lambda e, a=a: e.matmul(PS[0:8, 1, 0:384], lhsT=bts[:, a, :], rhs=CBs[:, a, :], start=(a == 0), stop=(a == 2))),
                 reads=["bts", "CBs"], writes=["PS1"])
        P.op("dve", lambda e: e.tensor_copy(out=vst[0:8, :], in_=PS[0:8, 1, 0:384]), reads=["PS1", "vecA"], writes=["vst"])
        dmaL(vecB.ap(), vst[0:8, :], reads=["vst"], writes=["vecB"], key="k_v1")
        Hall = R2a[:, 4096:7168].rearrange("p (i n) -> p i n", n=128)
        hi = 0
        hlist = []
        for (vec, Tt, nh) in ((vecA, TA, 4), (vecB, TB, 8)):
            for h in range(nh):
                for kind, base in ((0, 128), (1, 0)):
                    hank = bass.AP(vec, h * 384 + base, [[1, 128], [1, 128]])
                    dmaL(Hall[:, hi, :], hank, reads=["vecA", "vecB"], writes=["Hall"], key="k_h")
                    hlist.append((hi, Tt, h, kind))
                    hi += 1
        for (hi, Tt, h, kind) in hlist:
            bk = 2 + hi % 2
            pe_op(128, 128, (lambda e, hi=hi, bk=bk: e.matmul(PS[:, bk, 0:128], lhsT=Hall[:, hi, :], rhs=Js[:], start=True, stop=True)),
                  reads=["Hall", "Js"], writes=["PS%d" % bk])
            P.op("dve", (lambda e, Tt=Tt, h=h, kind=kind, bk=bk: e.tensor_copy(out=Tt[:, h, kind, :], in_=PS[:, bk, 0:128])),
                 reads=["PS%d" % bk], writes=["Tt"], self_sync=False)
            if kind == 0:
                P.op("dve", (lambda e, Tt=Tt, h=h: e.memset(Tt[64:128, h, 0, 0:64], NEGM)), reads=["Tt"], writes=["Tt"])
        P.op("dve", lambda e: e.memset(Tm4[:], 0.0), writes=["Tt"])
        P.op("dve", lambda e: e.memset(Tm4[0:64, 64:128], NEGM), reads=["Tt"], writes=["Tt"])
        def weight_casts():
            for (src, dst, rows, colsn) in ((w_out, wout_b, 1024, 1024), (w_up, wup_b, 1024, 4096), (w_down, wdown_b, 4096, 1024),
                                           (w_gate, wgate_b, 1024, 1024), (w_ple, wple_b, 256, 1024)):
                sv = src.ap().rearrange("r (a n) -> (r a) n", n=1024)
                dv = dst.ap().rearrange("r (a n) -> (r a) n", n=1024)
                tot = rows * colsn // 1024
                for r0 in range(0, tot, 512):
                    n_ = min(512, tot - r0)
                    P.op("pool", (lambda e, r0=r0, n_=n_, dv=dv, sv=sv: e.dma_start(out=dv[r0:r0 + n_, :], in_=sv[r0:r0 + n_, :],
                                                                                    max_dma_last_dim=2048)),
                         writes=["wscr"], dma_key=KM["k_w"])


        P.barrier(bar[:], skip=("q7",))
        xb = R2a[:, 0:4096].rearrange("p (t f) -> p t f", f=1024)
        ostg = R2a[:, 4096:5120].rearrange("p (s f) -> p s f", f=512)
        obraw = R2a[:, 5120:7168].rearrange("p (t f) -> p t f", f=512)
        gattnbc = R2a[:, 7168:8192]
        xs = R2b[:, 0:4096].rearrange("p (t f) -> p t f", f=1024)
        xsT = R2b[:, 4096:8192].rearrange("p (c n) -> p c n", n=512)
        KTst = R2b[:, 8192:10240].rearrange("p (h n) -> p h n", n=512)
        VAst = R2b[:, 10240:12288].rearrange("p (t n) -> p t n", n=512)
        KBT = R2b[:, 12288:16384].rearrange("p (s c n) -> p s c n", s=2, n=512)
        VBa = R2b[:, 16384:20608].rearrange("p (s t h e) -> p s t h e", s=2, t=4, e=66)
        QBz = R2b[:, 20608:24704].rearrange("p (u c n) -> p u c n", u=2, n=512)
        QAst = R2b[:, 24704:26752].rearrange("p (h n) -> p h n", n=512)
        OBst = R2b[:, 26752:28800].rearrange("p (t n) -> p t n", n=512)
        P.op("pool", lambda e: e.memset(QBz, 0.0), writes=["QBT"])
        dmaL(gattnbc, bass.AP(g_attn, 0, [[0, 128], [1, 1024]]), writes=["gattnbc"], key="k_c11")
        P.op("dve", lambda e: e.memset(VBa[:, :, :, :, 64:66], 1.0), writes=["VBones"])

        def out_store(dst_ap, psum_ap, psname, rows=128):
            s = cnt["ost"] % 2
            cnt["ost"] += 1
            evac(ostg[0:rows, s, :], psum_ap, reads=[psname], writes=["ostg%d" % s])
            dmaO(dst_ap, ostg[0:rows, s, :], reads=["ostg%d" % s], key="k_ost%d" % s)
            return ostg[0:rows, s, :], "ostg%d" % s

        def band_attention(p, I):
            sp_, so_ = (p - 1) % 2, p % 2
            qtl = [(i * 128, 128) for i in range(4)]
            for cb in range(4):
                kts = []
                for r in range(-4, 4):
                    slot, tk = (sp_, r + 4) if r < 0 else (so_, r)
                    qlo, qhi = max(0, r), min(3, r + 4)
                    adds = []
                    for qi in range(qlo, qhi + 1):
                        rel = r - qi
                        if rel == 0:
                            adds.append((qi, [TB[:, 2 * cb + u, 0, :] for u in range(2)]))
                        elif rel == -1:
                            adds.append((qi, [TB[:, 2 * cb + u, 1, :] for u in range(2)]))
                        elif rel == -4:
                            adds.append((qi, [Tm4[:], Tm4[:]]))
                    bsrc = cmB if (p == 3 and r < 0) else chB
                    kts.append(dict(KT=KBT[:, slot, cb, tk * 128:(tk + 1) * 128], nk=128,
                                    V=[VBa[:, slot, tk, 2 * cb + u, 0:65] for u in range(2)],
                                    bias=[bsrc[:, 2 * cb + u:2 * cb + u + 1] for u in range(2)],
                                    qlo=qlo, qhi=qhi, adds=adds, reads=["KBT%d" % slot, "VB%d" % slot, "VBones"]))

                def fin(O, names, cb=cb):
                    for u in range(2):
                        hb = 2 * cb + u
                        for qi in range(4):
                            ci = 8 + (qi * 2 + u)
                            P.op("dve", (lambda e, u=u, qi=qi, ci=ci: e.reciprocal(out=cols[:, ci:ci + 1], in_=O[u][qi][:, 64:65])),
                                 reads=names, writes=["c%d" % ci])
                            P.op("dve", (lambda e, u=u, qi=qi, ci=ci, hb=hb: e.tensor_scalar(
                                out=obraw[:, qi, hb * 64:(hb + 1) * 64], in0=O[u][qi][:, 0:64], scalar1=cols[:, ci:ci + 1],
                                scalar2=None, op0=ALU.mult)), reads=names + ["c%d" % ci], writes=["obraw"])
                pair_attn(kts, [QBz[:, 0, cb, :], QBz[:, 1, cb, :]], qtl, 64, fin, ["QBT"])
            for qi in range(4):
                rc, rn = rms_rstd(obraw[:, qi, :], 128, 512, eps6, 16 + 3 * qi, ["obraw"], "ob")
                P.op("dve", (lambda e, qi=qi, rc=rc: e.scalar_tensor_tensor(out=OBst[:, qi, :], in0=obraw[:, qi, :], scalar=rc,
                                                                              in1=onbbc[:], op0=ALU.mult, op1=ALU.mult)),
                     reads=["obraw", rn, "onbbc"], writes=["OBst"])
            dmaL(OBNs.ap()[I * 512:(I + 1) * 512, :].rearrange("(t p) n -> p t n", p=128), OBst, reads=["OBst"], writes=["OBNs"],
                 key="k_obst", eng="pool")

        def phaseA_block(p):
            own = (p % 4 == 3)
            I = p // 4
            last = (p == NBLK - 1)
            so_ = p % 2
            for t in range(4):
                dmaL(xb[:, t, :], xv.ap()[p * 512 + t * 128:p * 512 + (t + 1) * 128, :], writes=["xb%d" % t], key="k_xb%d" % t)
            for t in range(4):
                rc, rn = rms_rstd(xb[:, t, :], 128, 1024, eps6, 3 * t, ["xb%d" % t], "x")
                P.op("dve", (lambda e, t=t, rc=rc: e.scalar_tensor_tensor(out=xs[:, t, :], in0=xb[:, t, :], scalar=rc, in1=gattnbc,
                                                                            op0=ALU.mult, op1=ALU.mult)),
                     reads=["xb%d" % t, rn, "gattnbc"], writes=["xs%d" % t])
            for t in range(4):
                transposes(xs[:, t, :], 128, 8, xsT[:, :, t * 128:(t + 1) * 128], ["xs%d" % t], ["xsT"])
            for h in range(4):
                ps_, nm = proj_fm(lambda c, h=h: Wv[:, c, 512 + h * 128:512 + (h + 1) * 128], xsT, 512, ["W", "xsT"])
                evac(KTst[:, h, :], ps_, reads=[nm], writes=["KTst"])
            dmaL(KTs.ap().rearrange("h p n -> p h n")[:, :, p * 512:(p + 1) * 512], KTst, reads=["KTst"], writes=["KTs"],
                 key="k_ktst", eng="pool")
            for cb in range(4):
                ps_, nm = proj_fm(lambda c, cb=cb: Wv[:, c, 2048 + cb * 128:2048 + (cb + 1) * 128], xsT, 512, ["W", "xsT"])
                evac(KBT[:, so_, cb, :], ps_, reads=[nm], writes=["KBT%d" % so_])
            if own and "3" not in phases:
                for h in range(4):
                    ps_, nm = proj_fm(lambda c, h=h: Wv[:, c, h * 128:(h + 1) * 128], xsT, 512, ["W", "xsT"])
                    evac(QAst[:, h, :], ps_, reads=[nm], writes=["QAst"], scale=0.125)
                dmaL(QTs.ap().rearrange("h p n -> p h n")[:, :, I * 512:(I + 1) * 512], QAst, reads=["QAst"], writes=["QTs"],
                     key="k_qast", eng="pool")
                for cb in range(4):
                    ps_, nm = proj_fm(lambda c, cb=cb: Wv[:, c, 1536 + cb * 128:1536 + (cb + 1) * 128], xsT, 512, ["W", "xsT"])
                    evac(QBz[:, 0, cb, :], ps_, reads=[nm], writes=["QBT"], scale=0.125)
                    P.op("pool", (lambda e, cb=cb: e.tensor_copy(out=QBz[64:128, 1, cb, :], in_=QBz[64:128, 0, cb, :])),
                         reads=["QBT"], writes=["QBT"])
                    P.op("pool", (lambda e, cb=cb: e.memset(QBz[64:128, 0, cb, :], 0.0)), reads=["QBT"], writes=["QBT"])
            for t in range(4):
                ps_, nm = proj_tm(xsT, t * 128, 128, lambda c: Wv[:, c, 1024:1536], ["W", "xsT"])
                if own:
                    sa, sn = out_store(nav.ap()[I * 512 + t * 128:I * 512 + (t + 1) * 128, :], ps_, nm)
                    P.op("pool", (lambda e, t=t, sa=sa: e.tensor_copy(out=VAst[:, t, :], in_=sa)), reads=[sn], writes=["VAst"])
                else:
                    evac(VAst[:, t, :], ps_, reads=[nm], writes=["VAst"])
                ps_, nm = proj_tm(xsT, t * 128, 128, lambda c: Wv[:, c, 2560:3072], ["W", "xsT"])
                if last:
                    sa, sn = out_store(nbv.ap()[t * 128:(t + 1) * 128, :], ps_, nm)
                    P.op("pool", (lambda e, t=t, sa=sa: e.tensor_copy(out=VBa[:, so_, t, :, 0:64],
                                                                      in_=sa.rearrange("p (h e) -> p h e", e=64))),
                         reads=[sn], writes=["VB%d" % so_])
                else:
                    evac(VBa[:, so_, t, :, 0:64], ps_.rearrange("p (h e) -> p h e", e=64), reads=[nm], writes=["VB%d" % so_])
                if own:
                    ps_, nm = proj_tm(xsT, t * 128, 128, lambda c: Wv[:, c, 512:1024], ["W", "xsT"])
                    out_store(nak.ap()[I * 512 + t * 128:I * 512 + (t + 1) * 128, :], ps_, nm)
                if last:
                    ps_, nm = proj_tm(xsT, t * 128, 128, lambda c: Wv[:, c, 2048:2560], ["W", "xsT"])
                    out_store(nbk.ap()[t * 128:(t + 1) * 128, :], ps_, nm)
            dmaL(VAs.ap()[p * 512:(p + 1) * 512, :].rearrange("(t p) n -> p t n", p=128), VAst, reads=["VAst"], writes=["VAs"],
                 key="k_vast", eng="pool")
            if own and "1" not in phases:
                band_attention(p, I)

        if "A" in phases:
            for p in range(NBLK):
                phaseA_block(p)
        if "a" in phases:
            for p in range(4):
                phaseA_block(p)
        if "e" in phases:
            for p in range(3):
                phaseA_block(p)
        P.barrier(bar[:])

        KTh = R1[:, 0:16384]
        Vaug = R1[:, 16384:33024].rearrange("p (t e) -> p t e", e=130)
        OAN = R2b[:, 0:16384].rearrange("p (t n) -> p t n", n=512)
        QTz = R2b[:, 16384:24576].rearrange("p (u n) -> p u n", u=2)

        def finA_factory(h, dst_fn, rows):
            def fin(O, names):
                nqt = len(O[0])
                for qi in range(nqt):
                    pr = qi % 2
                    cb_ = 28 + 8 * pr
                    P.op("dve", (lambda e, qi=qi, cb_=cb_: e.reciprocal(out=cols[0:rows, cb_:cb_ + 1], in_=O[0][qi][:, 128:129])),
                         reads=names, writes=["fa%d" % pr])
                    P.op("dve", (lambda e, qi=qi, cb_=cb_: e.reciprocal(out=cols[0:rows, cb_ + 1:cb_ + 2], in_=O[1][qi][:, 128:129])),
                         reads=names + ["fa%d" % pr], writes=["fa%d" % pr])
                    P.op("dve", (lambda e, cb_=cb_: e.tensor_scalar(out=cols[0:rows, cb_ + 2:cb_ + 3], in0=cols[0:rows, cb_ + 1:cb_ + 2],
                                                                   scalar1=neglam[0:rows, :], scalar2=None, op0=ALU.mult)),
                         reads=["fa%d" % pr, "neglam"], writes=["fa%d" % pr])
                    P.op("dve", (lambda e, qi=qi, cb_=cb_, pr=pr: e.tensor_scalar(out=osm[0:rows, pr, 0, :], in0=O[1][qi][:, 0:128],
                                                                                  scalar1=cols[0:rows, cb_ + 2:cb_ + 3], scalar2=None,
                                                                                  op0=ALU.mult)),
                         reads=names + ["fa%d" % pr], writes=["osm%d" % pr])
                    P.op("dve", (lambda e, qi=qi, cb_=cb_, pr=pr: e.scalar_tensor_tensor(
                        out=osm[0:rows, pr, 1, :], in0=O[0][qi][:, 0:128], scalar=cols[0:rows, cb_:cb_ + 1], in1=osm[0:rows, pr, 0, :],
                        op0=ALU.mult, op1=ALU.add)), reads=names + ["fa%d" % pr, "osm%d" % pr], writes=["osm%d" % pr])
                    ci = cb_ + 3
                    P.op("act", (lambda e, pr=pr, ci=ci: e.activation(out=junk[0:rows, 0:128], in_=osm[0:rows, pr, 1, :], func=AF.Square,
                                                                     accum_out=cols[0:rows, ci:ci + 1])),
                         reads=["osm%d" % pr], writes=["junk", "fb%d" % pr])
                    P.op("act", (lambda e, ci=ci: e.activation(out=cols[0:rows, ci + 1:ci + 2], in_=cols[0:rows, ci:ci + 1], func=AF.Sqrt,
                                                              bias=eps5[0:rows, 0:1], scale=1.0 / 128)),
                         reads=["fb%d" % pr, "eps"], writes=["fb%d" % pr])
                    P.op("dve", (lambda e, ci=ci: e.reciprocal(out=cols[0:rows, ci + 2:ci + 3], in_=cols[0:rows, ci + 1:ci + 2])),
                         reads=["fb%d" % pr], writes=["fb%d" % pr])
                    dst, dnm = dst_fn(qi)
                    P.op("dve", (lambda e, pr=pr, ci=ci, dst=dst: e.scalar_tensor_tensor(
                        out=dst, in0=osm[0:rows, pr, 1, :], scalar=cols[0:rows, ci + 2:ci + 3], in1=sublnbc[0:rows, :],
                        op0=ALU.mult, op1=ALU.mult)), reads=["osm%d" % pr, "fb%d" % pr, "sublnbc"], writes=[dnm])
            return fin

        def phaseD1():
            xb = R2a[:, 0:1024]
            ostg = R2a[:, 4096:5120].rearrange("p (s f) -> p s f", f=512)
            obraw = R2a[:, 1024:1536]
            o = 0

            def carve(n):
                nonlocal o
                a = R2b[:, o:o + n]
                o += n
                return a
            oanl = carve(512)
            xs = carve(1024)
            xsT = carve(8 * 64).rearrange("p (c n) -> p c n", n=64)
            cst = carve(8 * 512).rearrange("p (t n) -> p t n", n=512)
            KTc = carve(4 * 1056).rearrange("p (h n) -> p h n", n=1056)
            Vc = carve(9 * 4 * 130).rearrange("p (t h e) -> p t h e", h=4, e=130)
            KBc = carve(4 * 544).rearrange("p (c n) -> p c n", n=544)
            VBc = carve(5 * 8 * 66).rearrange("p (t h e) -> p t h e", h=8, e=66)
            QAz = carve(2 * 4 * 64).rearrange("p (u h n) -> p u h n", u=2, n=64)
            QBzs = carve(2 * 4 * 64).rearrange("p (u c n) -> p u c n", u=2, n=64)
            P.op("pool", lambda e: e.memset(QAz, 0.0), writes=["QAs"])
            P.op("pool", lambda e: e.memset(QBzs, 0.0), writes=["QBs"])
            oans = carve(512)
            obns = carve(512)
            P.op("dve", lambda e: e.memset(Vc[:, :, :, 128:130], 1.0), writes=["Vc1"])
            P.op("dve", lambda e: e.memset(VBc[:, :, :, 64:66], 1.0), writes=["VBc1"])
            dmaL(xb[0:64, :], xsm.ap(), writes=["xb"], key="k_xb")
            rc, rn = rms_rstd(xb[0:64, :], 64, 1024, eps6, 0, ["xb"], "x")
            P.op("dve", (lambda e, rc=rc: e.scalar_tensor_tensor(out=xs[0:64, :], in0=xb[0:64, :], scalar=rc, in1=gattnbc[0:64, :],
                                                                  op0=ALU.mult, op1=ALU.mult)), reads=["xb", rn, "gattnbc"], writes=["xs"])
            transposes(xs[0:64, :], 64, 8, xsT[:, :, 0:64], ["xs"], ["xsT"])
            for (c0, dst) in ((512, sak), (1024, sav), (2048, sbk), (2560, sbv)):
                ps_, nm = proj_tm(xsT, 0, 64, lambda c, c0=c0: Wv[:, c, c0:c0 + 512], ["W", "xsT"])
                out_store(dst.ap(), ps_, nm, rows=64)
            for h in range(4):
                ps_, nm = proj_fm(lambda c, h=h: Wv[:, c, h * 128:(h + 1) * 128], xsT, 64, ["W", "xsT"])
                evac(QAz[:, 0, h, :], ps_, reads=[nm], writes=["QAs"], scale=0.125)
                P.op("pool", (lambda e, h=h: e.tensor_copy(out=QAz[64:128, 1, h, :], in_=QAz[64:128, 0, h, :])), reads=["QAs"], writes=["QAs"])
                P.op("pool", (lambda e, h=h: e.memset(QAz[64:128, 0, h, :], 0.0)), reads=["QAs"], writes=["QAs"])
                ps_, nm = proj_fm(lambda c, h=h: Wv[:, c, 1536 + h * 128:1536 + (h + 1) * 128], xsT, 64, ["W", "xsT"])
                evac(QBzs[:, 0, h, :], ps_, reads=[nm], writes=["QBs"], scale=0.125)
                P.op("pool", (lambda e, h=h: e.tensor_copy(out=QBzs[64:128, 1, h, :], in_=QBzs[64:128, 0, h, :])), reads=["QBs"], writes=["QBs"])
                P.op("pool", (lambda e, h=h: e.memset(QBzs[64:128, 0, h, :], 0.0)), reads=["QBs"], writes=["QBs"])
            for s in range(2):
                for h in range(4):
                    ps_, nm = proj_fm(lambda c, h=h: Wv[:, c, 512 + h * 128:512 + (h + 1) * 128], xsT[:, :, s * 32:(s + 1) * 32], 32,
                                      ["W", "xsT"])
                    evac(KTc[:, h, 1024:1056], ps_, reads=[nm], writes=["KTc"])
                    ps_, nm = proj_fm(lambda c, h=h: Wv[:, c, 2048 + h * 128:2048 + (h + 1) * 128], xsT[:, :, s * 32:(s + 1) * 32], 32,
                                      ["W", "xsT"])
                    evac(KBc[:, h, 512:544], ps_, reads=[nm], writes=["KBc"])
                ps_, nm = proj_tm(xsT, s * 32, 32, lambda c: Wv[:, c, 1024:1536], ["W", "xsT"])
                evac(Vc[0:32, 8, :, 0:128], ps_.rearrange("p (h e) -> p h e", e=128), reads=[nm], writes=["Vc"])
                ps_, nm = proj_tm(xsT, s * 32, 32, lambda c: Wv[:, c, 2560:3072], ["W", "xsT"])
                evac(VBc[0:32, 4, :, 0:64], ps_.rearrange("p (h e) -> p h e", e=64), reads=[nm], writes=["VBc"])
                dmaL(cst, cak.ap()[s * 1024:(s + 1) * 1024, :].rearrange("(t p) n -> p t n", p=128), writes=["cst"], key="k_cst",
                     eng="pool")
                for t in range(8):
                    transposes(cst[:, t, :], 128, 4, KTc[:, :, t * 128:(t + 1) * 128], ["cst"], ["KTc"])
                dmaL(cst[:, 0:4, :], cbk.ap()[s * 512:(s + 1) * 512, :].rearrange("(t p) n -> p t n", p=128), writes=["cst"],
                     key="k_cst", eng="pool")
                for t in range(4):
                    transposes(cst[:, t, :], 128, 4, KBc[:, :, t * 128:(t + 1) * 128], ["cst"], ["KBc"])
                for h_ in range(4):
                    dmaL(Vc[:, 0:8, h_, 0:128],
                         cav.ap()[s * 1024:(s + 1) * 1024, h_ * 128:(h_ + 1) * 128].rearrange("(t p) e -> p t e", p=128),
                         reads=["Vc1"], writes=["Vc"], key="k_vc", eng="pool")
                for h_ in range(8):
                    dmaL(VBc[:, 0:4, h_, 0:64],
                         cbv.ap()[s * 512:(s + 1) * 512, h_ * 64:(h_ + 1) * 64].rearrange("(t p) e -> p t e", p=128),
                         reads=["VBc1"], writes=["VBc"], key="k_vbc", eng="pool")
                for h in range(4):
                    kts = []
                    for t in range(9):
                        nk = 128 if t < 8 else 32
                        adds = []
                        if t == 7:
                            adds.append((0, [TA[:, h, 1, 0:32]] * 2))
                        if t == 8:
                            adds.append((0, [TA[0:32, h, 0, 0:32]] * 2))
                        bcol = chA[0:nk, h:h + 1]
                        kts.append(dict(KT=KTc[:, h, t * 128:t * 128 + nk], nk=nk, V=[Vc[0:nk, t, h, 0:129]] * 2, bias=[bcol, bcol],
                                        qlo=0, qhi=0, adds=adds, reads=["KTc", "Vc", "Vc1"]))
                    fin = finA_factory(h, lambda qi, h=h: (oans[0:32, h * 128:(h + 1) * 128], "oans"), 32)
                    pair_attn(kts, [QAz[:, u_, h, s * 32:(s + 1) * 32] for u_ in range(2)], [(0, 32)], 128, fin, ["QAs"])
                dmaL(OANss.ap()[s * 32:(s + 1) * 32, :], oans[0:32, :], reads=["oans"], writes=["OANss"], key="k_oans", eng="pool")
                for cb in range(4):
                    kts = []
                    for t in range(5):
                        nk = 128 if t < 4 else 32
                        adds = []
                        if t == 3:
                            adds.append((0, [TB[:, 2 * cb + u, 1, 0:32] for u in range(2)]))
                        if t == 4:
                            adds.append((0, [TB[0:32, 2 * cb + u, 0, 0:32] for u in range(2)]))
                        kts.append(dict(KT=KBc[:, cb, t * 128:t * 128 + nk], nk=nk,
                                        V=[VBc[0:nk, t, 2 * cb + u, 0:65] for u in range(2)],
                                        bias=[chB[0:nk, 2 * cb + u:2 * cb + u + 1] for u in range(2)],
                                        qlo=0, qhi=0, adds=adds, reads=["KBc", "VBc", "VBc1"]))

                    def finb(O, names, cb=cb):
                        for u in range(2):
                            hb = 2 * cb + u
                            ci = 8 + u
                            P.op("dve", (lambda e, u=u, ci=ci: e.reciprocal(out=cols[0:32, ci:ci + 1], in_=O[u][0][:, 64:65])),
                                 reads=names, writes=["c%d" % ci])
                            P.op("dve", (lambda e, u=u, ci=ci, hb=hb: e.tensor_scalar(
                                out=obraw[0:32, hb * 64:(hb + 1) * 64], in0=O[u][0][:, 0:64], scalar1=cols[0:32, ci:ci + 1],
                                scalar2=None, op0=ALU.mult)), reads=names + ["c%d" % ci], writes=["obraw"])
                    pair_attn(kts, [QBzs[:, u_, cb, s * 32:(s + 1) * 32] for u_ in range(2)], [(0, 32)], 64, finb, ["QBs"])
                rc, rn = rms_rstd(obraw[0:32, :], 32, 512, eps6, 16, ["obraw"], "ob")
                P.op("dve", (lambda e, rc=rc: e.scalar_tensor_tensor(out=obns[0:32, :], in0=obraw[0:32, :], scalar=rc, in1=onbbc[0:32, :],
                                                                      op0=ALU.mult, op1=ALU.mult)),
                     reads=["obraw", rn, "onbbc"], writes=["obns"])
                dmaL(OBNss.ap()[s * 32:(s + 1) * 32, :], obns[0:32, :], reads=["obns"], writes=["OBNs"], key="k_obns", eng="pool")
        if "D" in phases:
            phaseD1()
            P.barrier(bar[:])

        weight_casts()
        osbB = [(R2a[:, 0:1536].rearrange("p (b n) -> p b n", n=512), "osbA"),
                (R2a[:, 1536:3072].rearrange("p (b n) -> p b n", n=512), "osbB")]
        if "B" in phases:
            P.op("dve", lambda e: e.memset(Vaug[:, :, 128:130], 1.0), writes=["Vones"])
            P.op("dve", lambda e: e.memset(QTz, 0.0), writes=["QTh", "QTh0"])
            NCH = 4
            for h in range(4):
                for ch in range(NCH):
                    k0 = ch * (SEQV // NCH)
                    k1 = (ch + 1) * (SEQV // NCH)
                    dmaL(KTh[:, k0:k1], KTs.ap()[h, :, k0:k1], reads=["Vones"], writes=["KTh%d" % ch], key="k_kth%d" % ch)
                    dmaL(Vaug[:, k0 // 128:k1 // 128, 0:128],
                         VAs.ap()[k0:k1, h * 128:(h + 1) * 128].rearrange("(t p) e -> p t e", p=128),
                         reads=["Vones"], writes=["Vh%d" % ch], key="k_vh%d" % ch)
                for u_ in range(2):
                    dmaL(QTz[64 * u_:64 * u_ + 64, u_, 0:NTOK], QTs.ap()[h, 64 * u_:64 * u_ + 64, :], reads=["QTh0"], writes=["QTh"],
                         key="k_qth")
                for I in range(NOWN):
                    p = 4 * I + 3
                    kts = []
                    for kt in range(4 * p + 4):
                        r = kt - 4 * p
                        qlo = max(0, r)
                        adds = []
                        if r >= 0:
                            adds.append((r, [TA[:, h, 0, :]] * 2))
                            if r + 1 <= 3:
                                adds.append((r + 1, [TA[:, h, 1, :]] * 2))
                        elif r == -1:
                            adds.append((0, [TA[:, h, 1, :]] * 2))
                        bcol = cmA[:, h, kt // 4:kt // 4 + 1] if kt < 12 else chA[:, h:h + 1]
                        ch = kt * 128 // (SEQV // NCH)
                        kts.append(dict(KT=KTh[:, kt * 128:(kt + 1) * 128], nk=128, V=[Vaug[:, kt, 0:129]] * 2, bias=[bcol, bcol],
                                        qlo=qlo, qhi=3, adds=adds, reads=["KTh%d" % ch, "Vh%d" % ch, "Vones"]))
                    fin = finA_factory(h, lambda qi, I=I, h=h: (OAN[:, I * 4 + qi, h * 128:(h + 1) * 128], "OAN"), 128)
                    pair_attn(kts, [QTz[:, u_, I * 512:(I + 1) * 512] for u_ in range(2)], [(i * 128, 128) for i in range(4)], 128, fin,
                              ["QTh"], osb=osbB)
        P.barrier(bar[:])

        ACT_T = R1[:, 0:16384].rearrange("p (h n) -> p h n", n=512)
        WR = R1[:, 16384:32768].rearrange("p (s n) -> p s n", n=4096)
        hres = R2a[:, 0:4096].rearrange("p (t f) -> p t f", f=1024)
        p_s = R2a[:, 4096:5120].rearrange("p (t f) -> p t f", f=256)
        gs = R2a[:, 5120:5632]
        tmpf = R2a[:, 5632:6144]
        gmlpbc = R2a[:, 6144:7168]
        gfinbc = R2a[:, 7168:8192]
        cs = R2b[:, 16384:20480].rearrange("p (t f) -> p t f", f=1024)
        aT = R2b[:, 20480:24576].rearrange("p (c n) -> p c n", n=512)
        obn_s = R2b[:, 24576:26624].rearrange("p (t n) -> p t n", n=512)
        pb = R2b[:, 26624:27648].rearrange("p (t f) -> p t f", f=256)
        pT = R2b[:, 27648:28672].rearrange("p (c n) -> p c n", n=512)
        wcnt = {"i": 0}

        def wload(src_ap, shape_view):
            s = wcnt["i"] % 4
            wcnt["i"] += 1
            dst = shape_view(WR[:, s, :])
            dmaL(dst, src_ap, reads=["wscr"], writes=["WR%d" % s], key="k_wr%d" % s)
            return dst, "WR%d" % s

        def phaseC_group(tiles, x_ap, p_ap, oan_fn, obn_src, y_ap):
            NT = tiles[-1][0] + tiles[-1][1]
            nt = len(tiles)
            rows0 = tiles[0][1]
            if rows0 == 128:
                dmaL(hres[:, 0:nt, :], x_ap.rearrange("(t p) f -> p t f", p=128), writes=["hres"], key="k_hres")
                dmaL(p_s[:, 0:nt, :], p_ap.rearrange("(t p) f -> p t f", p=128), writes=["p_s"], key="k_ps")
                dmaL(obn_s[:, 0:nt, :], obn_src.rearrange("(t p) f -> p t f", p=128), reads=["OBNs"], writes=["obn_s"], key="k_obn")
            else:
                dmaL(hres[0:rows0, 0, :], x_ap, writes=["hres"], key="k_hres")
                dmaL(p_s[0:rows0, 0, :], p_ap, writes=["p_s"], key="k_ps")
                dmaL(obn_s[0:rows0, 0, :], obn_src, reads=["OBNs"], writes=["obn_s"], key="k_obn")
            for ti, (tok0, rows) in enumerate(tiles):
                oa, oan_names = oan_fn(ti)
                transposes(oa, rows, 4, aT[:, 0:4, tok0:tok0 + rows], oan_names, ["aT"])
                transposes(obn_s[0:rows, ti, :], rows, 4, aT[:, 4:8, tok0:tok0 + rows], ["obn_s"], ["aT"])
            wo = [wload(wout_b.ap()[j * 512:(j + 1) * 512, :].rearrange("(c p) n -> p c n", p=128),
                        lambda v: v.rearrange("p (c n) -> p c n", n=1024)) for j in range(2)]
            for ti, (tok0, rows) in enumerate(tiles):
                for hf in range(2):
                    ps_, nm = proj_tm(aT, tok0, rows, lambda c, hf=hf: wo[c // 4][0][:, c % 4, hf * 512:(hf + 1) * 512],
                                      ["aT", wo[0][1], wo[1][1]])
                    P.op("dve", (lambda e, ti=ti, rows=rows, hf=hf, ps_=ps_: e.tensor_tensor(
                        out=hres[0:rows, ti, hf * 512:(hf + 1) * 512], in0=ps_, in1=hres[0:rows, ti, hf * 512:(hf + 1) * 512], op=ALU.add)),
                        reads=[nm, "hres"], writes=["hres"], self_sync=False)
            for ti, (tok0, rows) in enumerate(tiles):
                rc, rn = rms_rstd(hres[0:rows, ti, :], rows, 1024, eps6, 3 * ti, ["hres"], "c")
                P.op("dve", (lambda e, ti=ti, rows=rows, rc=rc: e.scalar_tensor_tensor(out=cs[0:rows, ti, :], in0=hres[0:rows, ti, :],
                                                                                       scalar=rc, in1=gmlpbc[0:rows, :], op0=ALU.mult,
                                                                                       op1=ALU.mult)),
                     reads=["hres", rn, "gmlpbc"], writes=["cs"])
                transposes(cs[0:rows, ti, :], rows, 8, aT[:, :, tok0:tok0 + rows], ["cs"], ["aT"])
            for j in range(8):
                wu, wn = wload(wup_b.ap()[:, j * 512:(j + 1) * 512].rearrange("(c p) n -> p c n", p=128),
                               lambda v: v.rearrange("p (c n) -> p c n", n=512))
                for hl in range(4):
                    hc = j * 4 + hl
                    ps_, nm = proj_fm(lambda c, hl=hl, wu=wu: wu[:, c, hl * 128:(hl + 1) * 128], aT, NT, ["aT", wn])
                    rb, rbn = (tmpf, "tmpf") if hc % 2 else (gs, "gs")
                    P.op("act", (lambda e, ps_=ps_, rb=rb: e.activation(out=rb[:, 0:NT], in_=ps_, func=AF.Relu)),
                         reads=[nm], writes=[rbn])
                    P.op("pool", (lambda e, hc=hc, rb=rb: e.tensor_tensor(out=ACT_T[:, hc, 0:NT], in0=rb[:, 0:NT], in1=rb[:, 0:NT],
                                                                          op=ALU.mult)), reads=[rbn], writes=["ACT_T"])
            for hf in range(2):
                accs = []
                for ti in range(nt):
                    accs.append(ti)
                for j in range(4):
                    wd, wn = wload(wdown_b.ap()[j * 1024:(j + 1) * 1024, hf * 512:(hf + 1) * 512].rearrange("(c p) n -> p c n", p=128),
                                   lambda v: v.rearrange("p (c n) -> p c n", n=512))
                    for hl in range(8):
                        hc = j * 8 + hl
                        for ti, (tok0, rows) in enumerate(tiles):
                            pe_op(128, rows, (lambda e, ti=ti, tok0=tok0, rows=rows, hc=hc, hl=hl, wd=wd: e.matmul(
                                PS[0:rows, ti, :], lhsT=ACT_T[:, hc, tok0:tok0 + rows], rhs=wd[:, hl, :], start=(hc == 0), stop=(hc == 31))),
                                reads=["ACT_T", wn], writes=["PS%d" % ti])
                for ti, (tok0, rows) in enumerate(tiles):
                    P.op("dve", (lambda e, ti=ti, rows=rows, hf=hf: e.tensor_tensor(
                        out=hres[0:rows, ti, hf * 512:(hf + 1) * 512], in0=PS[0:rows, ti, :], in1=hres[0:rows, ti, hf * 512:(hf + 1) * 512],
                        op=ALU.add)), reads=["PS%d" % ti, "hres"], writes=["hres"], self_sync=False)
            for ti, (tok0, rows) in enumerate(tiles):
                P.op("act", (lambda e, ti=ti, rows=rows: e.activation(out=cs[0:rows, ti, :], in_=hres[0:rows, ti, :], func=AF.Copy, scale=1.0)),
                     reads=["hres"], writes=["cs"])
                transposes(cs[0:rows, ti, :], rows, 8, aT[:, :, tok0:tok0 + rows], ["cs"], ["aT"])
                P.op("dve", (lambda e, ti=ti, rows=rows: e.tensor_copy(out=pb[0:rows, ti, :], in_=p_s[0:rows, ti, :])),
                     reads=["p_s"], writes=["pb"])
                transposes(pb[0:rows, ti, :], rows, 2, pT[:, :, tok0:tok0 + rows], ["pb"], ["pT"])
            wg = [wload(wgate_b.ap()[j * 512:(j + 1) * 512, :].rearrange("(c p) n -> p c n", p=128),
                        lambda v: v.rearrange("p (c n) -> p c n", n=1024)) for j in range(2)]
            wp, wpn = wload(wple_b.ap().rearrange("(c p) n -> p c n", p=128),
                            lambda v: v[:, 0:2048].rearrange("p (c n) -> p c n", n=1024))
            for ti, (tok0, rows) in enumerate(tiles):
                for hf in range(2):
                    ps_, nm = proj_tm(aT, tok0, rows, lambda c, hf=hf: wg[c // 4][0][:, c % 4, hf * 512:(hf + 1) * 512],
                                      ["aT", wg[0][1], wg[1][1]])
                    P.op("act", (lambda e, rows=rows, ps_=ps_: e.activation(out=gs[0:rows, :], in_=ps_, func=AF.Sigmoid)),
                         reads=[nm], writes=["gs"])
                    ps2, nm2 = proj_tm(pT, tok0, rows, lambda c, hf=hf: wp[:, c, hf * 512:(hf + 1) * 512], ["pT", wpn], nk=2)
                    P.op("dve", (lambda e, rows=rows, ps2=ps2: e.tensor_tensor(out=tmpf[0:rows, :], in0=gs[0:rows, :], in1=ps2, op=ALU.mult)),
                         reads=["gs", nm2], writes=["tmpf"])
                    P.op("dve", (lambda e, ti=ti, rows=rows, hf=hf: e.tensor_tensor(
                        out=hres[0:rows, ti, hf * 512:(hf + 1) * 512], in0=tmpf[0:rows, :], in1=hres[0:rows, ti, hf * 512:(hf + 1) * 512],
                        op=ALU.add)), reads=["tmpf", "hres"], writes=["hres"])
            for ti, (tok0, rows) in enumerate(tiles):
                rc, rn = rms_rstd(hres[0:rows, ti, :], rows, 1024, eps6, 3 * ti, ["hres"], "f")
                P.op("dve", (lambda e, ti=ti, rows=rows, rc=rc: e.scalar_tensor_tensor(out=hres[0:rows, ti, :], in0=hres[0:rows, ti, :],
                                                                                       scalar=rc, in1=gfinbc[0:rows, :], op0=ALU.mult,
                                                                                       op1=ALU.mult)),
                     reads=["hres", rn, "gfinbc"], writes=["hres"])
            if rows0 == 128:
                dmaO(y_ap.rearrange("(t p) f -> p t f", p=128), hres[:, 0:nt, :], reads=["hres"], key="k_y")
            else:
                dmaO(y_ap, hres[0:rows0, 0, :], reads=["hres"], key="k_y")

        if "C" in phases or "D" in phases:
            dmaL(gmlpbc, bass.AP(g_mlp, 0, [[0, 128], [1, 1024]]), writes=["gmlpbc"], key="k_c12")
            dmaL(gfinbc, bass.AP(g_final, 0, [[0, 128], [1, 1024]]), writes=["gfinbc"], key="k_c13")
        if "C" in phases:
            for I in range(NOWN):
                phaseC_group([(i * 128, 128) for i in range(4)], xv.ap()[(4 * I + 3) * 512:(4 * I + 4) * 512, :],
                             pv.ap()[I * 512:(I + 1) * 512, :],
                             lambda ti, I=I: (OAN[:, I * 4 + ti, :], ["OAN"]),
                             OBNs.ap()[I * 512:(I + 1) * 512, :], y.ap()[I * 512:(I + 1) * 512, :])
        P.barrier(bar[:])

        def phaseD2():
            oanl = R2b[:, 0:512]
            dmaL(oanl[0:64, :], OANss.ap(), reads=["OANss"], writes=["oanl"], key="k_oanl")
            phaseC_group([(0, 64)], xsm.ap(), psm.ap(), lambda ti: (oanl[0:64, :], ["oanl"]), OBNss.ap(), ys.ap())

        if "D" in phases:
            phaseD2()

        P.emit(final_dma_keys=[] if "5" in phases else sorted(outkeys))
    return nc


_NC_CACHE = {}


def _run(x_prompt, x_sample, cache_a_k, cache_a_v, cache_b_k, cache_b_v, p_prompt, p_sample,
         t5_table, g_attn, w_in, lambda_q1, lambda_k1, lambda_q2, lambda_k2, subln_g,
         band_table, out_norm_b, w_out, g_mlp, w_up, w_down, w_ple_gate, w_ple_proj, g_final,
         phases="ABCD", cores=None, trace=False):
    f = lambda a: np.ascontiguousarray(np.asarray(a, dtype=np.float32))
    x_prompt = f(x_prompt); x_sample = f(x_sample); p_prompt = f(p_prompt); p_sample = f(p_sample)
    cache_a_k = f(cache_a_k); cache_a_v = f(cache_a_v); cache_b_k = f(cache_b_k); cache_b_v = f(cache_b_v)
    S = x_prompt.shape[1]
    nblk = S // 512
    nown = nblk // 4
    seqv = nblk * 512
    ident, J, CA, CB = _consts()
    ck = (phases, nblk)
    if ck not in _NC_CACHE:
        _NC_CACHE[ck] = build_program(phases, nblk)
    nc = _NC_CACHE[ck]
    common = {
        "t5": f(t5_table), "bt": f(band_table)[0], "g_attn": f(g_attn), "w_in": f(w_in)[0],
        "lq1": f(lambda_q1), "lk1": f(lambda_k1), "lq2": f(lambda_q2), "lk2": f(lambda_k2),
        "subln": f(subln_g), "onb": f(out_norm_b), "w_out": f(w_out)[0], "g_mlp": f(g_mlp),
        "w_up": f(w_up)[0], "w_down": f(w_down)[0], "w_gate": f(w_ple_gate)[0], "w_ple": f(w_ple_proj)[0],
        "g_final": f(g_final).reshape(1, 1024), "ident": ident, "J": J, "CA": CA, "CB": CB,
    }
    cores = list(range(8)) if cores is None else list(cores)
    in_maps = []
    for c in cores:
        b, j = divmod(c, 4)
        npad = 3 - j
        xvv = np.zeros((seqv, 1024), np.float32)
        nreal = (nblk - npad) * 512
        xvv[npad * 512:] = x_prompt[b, :nreal]
        pvv = np.concatenate([p_prompt[0, b, (4 * I + j) * 512:(4 * I + j + 1) * 512] for I in range(nown)], 0)
        pm = np.zeros((128, 4), np.float32)
        pm[:, :npad] = NEGM
        m = dict(common)
        m.update({
            "xv": xvv, "pv": np.ascontiguousarray(pvv),
            "xsm": x_sample[2 * c:2 * c + 2].reshape(64, 1024), "psm": p_sample[0, 2 * c:2 * c + 2].reshape(64, 256),
            "cak": cache_a_k[0, 2 * c:2 * c + 2].reshape(2048, 512), "cav": cache_a_v[0, 2 * c:2 * c + 2].reshape(2048, 512),
            "cbk": cache_b_k[0, 2 * c:2 * c + 2].reshape(1024, 512), "cbv": cache_b_v[0, 2 * c:2 * c + 2].reshape(1024, 512),
            "padmask": pm,
        })
        in_maps.append({k: np.ascontiguousarray(v) for k, v in m.items()})
    if trace:
        res = run_bass_kernel_spmd(nc, in_maps, core_ids=list(range(len(cores))), trace=True)
        print("EXEC_TIME_NS", res.exec_time_ns)
    else:
        res = run_bass_kernel_spmd(nc, in_maps, core_ids=list(range(len(cores))))
    R = res.results
    y_prompt = np.zeros((2, S, 1024), np.float32)
    nakp = np.zeros((1, 2, S, 512), np.float32)
    navp = np.zeros((1, 2, S, 512), np.float32)
    nbkp = np.zeros((1, 2, 512, 512), np.float32)
    nbvp = np.zeros((1, 2, 512, 512), np.float32)
    y_sample = np.zeros((16, 32, 1024), np.float32)
    saks = np.zeros((1, 16, 32, 512), np.float32); savs = np.zeros((1, 16, 32, 512), np.float32)
    sbks = np.zeros((1, 16, 32, 512), np.float32); sbvs = np.zeros((1, 16, 32, 512), np.float32)
    for ci, c in enumerate(cores):
        b, j = divmod(c, 4)
        r = R[ci]
        for I in range(nown):
            g0 = (4 * I + j) * 512
            y_prompt[b, g0:g0 + 512] = r["y"][I * 512:(I + 1) * 512]
            nakp[0, b, g0:g0 + 512] = r["nak"][I * 512:(I + 1) * 512]
            navp[0, b, g0:g0 + 512] = r["nav"][I * 512:(I + 1) * 512]
        if j == 3:
            nbkp[0, b] = r["nbk"]
            nbvp[0, b] = r["nbv"]
        y_sample[2 * c:2 * c + 2] = r["ys"].reshape(2, 32, 1024)
        saks[0, 2 * c:2 * c + 2] = r["sak"].reshape(2, 32, 512)
        savs[0, 2 * c:2 * c + 2] = r["sav"].reshape(2, 32, 512)
        sbks[0, 2 * c:2 * c + 2] = r["sbk"].reshape(2, 32, 512)
        sbvs[0, 2 * c:2 * c + 2] = r["sbv"].reshape(2, 32, 512)
    return (y_prompt, y_sample,
            nakp.reshape(1, 2, S, 4, 2, 64), navp.reshape(1, 2, S, 4, 128),
            nbkp.reshape(1, 2, 512, 8, 64), nbvp.reshape(1, 2, 512, 8, 64),
            saks.reshape(1, 16, 32, 4, 2, 64), savs.reshape(1, 16, 32, 4, 128),
            sbks.reshape(1, 16, 32, 8, 64), sbvs.reshape(1, 16, 32, 8, 64))


def kernel(x_prompt, x_sample, cache_a_k, cache_a_v, cache_b_k, cache_b_v, p_prompt, p_sample,
           t5_table, g_attn, w_in, lambda_q1, lambda_k1, lambda_q2, lambda_k2, subln_g,
           band_table, out_norm_b, w_out, g_mlp, w_up, w_down, w_ple_gate, w_ple_proj, g_final):
    return _run(x_prompt, x_sample, cache_a_k, cache_a_v, cache_b_k, cache_b_v, p_prompt, p_sample,
                t5_table, g_attn, w_in, lambda_q1, lambda_k1, lambda_q2, lambda_k2, subln_g,
                band_table, out_norm_b, w_out, g_mlp, w_up, w_down, w_ple_gate, w_ple_proj, g_final)
```

```python
import math
import contextlib
import numpy as np
import concourse.bass as bass
import concourse.mybir as mybir
from concourse.bass_utils import run_bass_kernel_spmd

ENGS = ("pe", "act", "dve", "pool", "sp")


class Op:
    __slots__ = ("idx", "eng", "fn", "dma_key", "dma_cnt", "waits", "signal", "sigcnt", "eidx")

    def __init__(self, idx, eng, fn, dma_key):
        self.idx = idx
        self.eng = eng
        self.fn = fn
        self.dma_key = dma_key
        self.dma_cnt = 0
        self.waits = []
        self.signal = False
        self.sigcnt = 0
        self.eidx = 0


class Prog:
    def __init__(self, nc):
        self.nc = nc
        self.ops = []
        self.by_eng = {e: [] for e in ENGS}
        self.last_w = {}
        self.readers = {}
        self.seen = {e: {f: -1 for f in ENGS} for e in ENGS}
        self.seen_dma = {e: {} for e in ENGS}
        self.dma_counts = {}
        self.final_dma = []
        self.force = {}

    def op(self, eng, fn, reads=(), writes=(), dma_key=None, self_sync=True, extra=()):
        o = Op(len(self.ops), eng, fn, dma_key)
        o.eidx = len(self.by_eng[eng])
        deps = []
        for r in reads:
            w = self.last_w.get(r)
            if w is not None:
                deps.append(w)
        for w_ in writes:
            w = self.last_w.get(w_)
            if w is not None:
                deps.append(w)
            deps.extend(self.readers.get(w_, ()))
        for r in reads:
            self.readers.setdefault(r, []).append(o)
        for w_ in writes:
            self.last_w[w_] = o
            self.readers[w_] = [x for x in self.readers.get(w_, ()) if x is o]
        f = self.force.pop(eng, None)
        if f is not None:
            deps.append(f)
        deps.extend(extra)
        if dma_key is not None:
            self.dma_counts[dma_key] = self.dma_counts.get(dma_key, 0) + 16
            o.dma_cnt = self.dma_counts[dma_key]
        for d in deps:
            if d is o:
                continue
            if d.dma_key is not None:
                cur = self.seen_dma[eng].get(d.dma_key, 0)
                if cur >= d.dma_cnt:
                    continue
                self.seen_dma[eng][d.dma_key] = d.dma_cnt
                o.waits.append(("dma", d.dma_key, d.dma_cnt))
            else:
                if d.eng == eng and (eng == "pe" or not self_sync):
                    continue
                if self.seen[eng][d.eng] >= d.eidx:
                    continue
                self.seen[eng][d.eng] = d.eidx
                d.signal = True
                o.waits.append(("eng", d.eng, d))
        self.ops.append(o)
        self.by_eng[eng].append(o)
        return o

    def barrier(self, tile, skip=()):
        extra = [self.by_eng[e][-1] for e in ENGS if self.by_eng[e]]
        last_dma = {}
        for o in self.ops:
            if o.dma_key is not None and o.dma_key not in skip:
                last_dma[o.dma_key] = o
        extra.extend(last_dma.values())
        b = self.op("dve", lambda e: e.memset(tile, 0.0), extra=extra)
        for e in ENGS:
            if e != "dve":
                self.force[e] = b
        return b

    def emit(self, final_dma_keys=()):
        nc = self.nc
        import contextlib
        for o in self.ops:
            best = {}
            for w in o.waits:
                if w[0] == "dma":
                    k = ("dma", w[1])
                    if k not in best or best[k][2] < w[2]:
                        best[k] = w
                else:
                    k = ("eng", w[1])
                    if k not in best or best[k][2].eidx < w[2].eidx:
                        best[k] = w
            o.waits = list(best.values())
        for e in ENGS:
            c = 0
            for o in self.by_eng[e]:
                if o.signal:
                    c += 1
                    o.sigcnt = c
        with contextlib.ExitStack() as st:
            esem = {e: st.enter_context(nc.semaphore("s_" + e)) for e in ENGS}
            dsem = {k: st.enter_context(nc.semaphore("d_%d" % i))
                    for i, k in enumerate(sorted(self.dma_counts))}
            block = st.enter_context(nc.Block())

            def run(e, eng):
                for o in self.by_eng[e]:
                    for w in o.waits:
                        if w[0] == "dma":
                            eng.wait_ge(dsem[w[1]], w[2])
                        else:
                            eng.wait_ge(esem[w[1]], w[2].sigcnt)
                    inst = o.fn(eng)
                    if o.dma_key is not None:
                        inst.then_inc(dsem[o.dma_key], 16)
                    elif o.signal:
                        inst.then_inc(esem[e], 1)
                if e == "sp":
                    for k in final_dma_keys:
                        eng.wait_ge(dsem[k], self.dma_counts[k])

            @block.tensor
            def _(eng):
                run("pe", eng)

            @block.scalar
            def _(eng):
                run("act", eng)

            @block.vector
            def _(eng):
                run("dve", eng)

            @block.gpsimd
            def _(eng):
                run("pool", eng)

            @block.sync
            def _(eng):
                run("sp", eng)

F32 = mybir.dt.float32
BF16 = mybir.dt.bfloat16
AF = mybir.ActivationFunctionType
ALU = mybir.AluOpType
NEGM = -30000.0
NBLK = 32
NOWN = 8
SEQV = NBLK * 512


def _t5_bucket_np(rel):
    half = 16
    n = -rel
    ret = np.where(n < 0, half, 0)
    n = np.abs(n)
    max_exact = 8
    nf = np.maximum(n, 1).astype(np.float32)
    large = max_exact + (np.log(nf / np.float32(max_exact)) / np.float32(math.log(128 / max_exact))
                         * np.float32(half - max_exact)).astype(np.int32)
    large = np.minimum(large, half - 1)
    return ret + np.where(n < max_exact, n, large)


def _consts():
    ident = np.eye(128, dtype=np.float32)
    J = ident[::-1].copy()
    i = np.arange(384)
    delta = i - 255
    CA = np.zeros((32, 384), np.float32)
    bk = _t5_bucket_np(delta.astype(np.int32))
    CA[bk, i] += 1.0
    CA[15, :] -= 1.0
    CA[:, 383] = 0.0
    CB = np.zeros((384, 384), np.float32)
    idx = np.clip(delta, -128, 128) + 128
    CB[idx, i] += 1.0
    CB[0, :] -= 1.0
    CB[:, 383] = 0.0
    return ident, J, CA, CB


def build_program(phases="ABCD", nblk=32):
    global NBLK, NOWN, SEQV
    NBLK = nblk
    NOWN = nblk // 4
    SEQV = nblk * 512
    NTOK = NOWN * 512
    nc = bass.Bass("TRN2", target_bir_lowering=False)
    T = {}

    def din(name, shape, dt=F32):
        T[name] = nc.dram_tensor(name, shape, dt, kind="ExternalInput")
        return T[name]

    def dout(name, shape, dt=F32):
        T[name] = nc.dram_tensor(name, shape, dt, kind="ExternalOutput")
        return T[name]

    def dscr(name, shape, dt=BF16):
        T[name] = nc.dram_tensor(name, shape, dt, kind="Internal")
        return T[name]

    xv = din("xv", [SEQV, 1024]); pv = din("pv", [NTOK, 256])
    xsm = din("xsm", [64, 1024]); psm = din("psm", [64, 256])
    cak = din("cak", [2048, 512]); cav = din("cav", [2048, 512])
    cbk = din("cbk", [1024, 512]); cbv = din("cbv", [1024, 512])
    padmask = din("padmask", [128, 4])
    t5 = din("t5", [32, 4]); bt = din("bt", [257, 8])
    g_attn = din("g_attn", [1, 1024]); w_in = din("w_in", [1024, 3072])
    lq1 = din("lq1", [1, 64]); lk1 = din("lk1", [1, 64]); lq2 = din("lq2", [1, 64]); lk2 = din("lk2", [1, 64])
    subln = din("subln", [1, 128]); onb = din("onb", [1, 512])
    w_out = din("w_out", [1024, 1024]); g_mlp = din("g_mlp", [1, 1024])
    w_up = din("w_up", [1024, 4096]); w_down = din("w_down", [4096, 1024])
    w_gate = din("w_gate", [1024, 1024]); w_ple = din("w_ple", [256, 1024]); g_final = din("g_final", [1, 1024])
    identd = din("ident", [128, 128]); Jd = din("J", [128, 128]); CAd = din("CA", [32, 384]); CBd = din("CB", [384, 384])

    y = dout("y", [NTOK, 1024]); ys = dout("ys", [64, 1024])
    nak = dout("nak", [NTOK, 512]); nav = dout("nav", [NTOK, 512])
    nbk = dout("nbk", [512, 512]); nbv = dout("nbv", [512, 512])
    sak = dout("sak", [64, 512]); sav = dout("sav", [64, 512]); sbk = dout("sbk", [64, 512]); sbv = dout("sbv", [64, 512])

    KTs = dscr("KTs", [4, 128, SEQV]); VAs = dscr("VAs", [SEQV, 512]); QTs = dscr("QTs", [4, 128, NTOK])
    OBNs = dscr("OBNs", [NTOK, 512]); OANss = dscr("OANss", [64, 512]); OBNss = dscr("OBNss", [64, 512])
    vecA = dscr("vecA", [4, 384], F32); vecB = dscr("vecB", [8, 384], F32)
    wout_b = dscr("wout_b", [1024, 1024]); wup_b = dscr("wup_b", [1024, 4096]); wdown_b = dscr("wdown_b", [4096, 1024])
    wgate_b = dscr("wgate_b", [1024, 1024]); wple_b = dscr("wple_b", [256, 1024])

    P = Prog(nc)
    outkeys = set()
    with contextlib.ExitStack() as st:
        def sb(name, shape, dt):
            return st.enter_context(nc.sbuf_tensor(name, shape, dt))

        def psm_(name, shape, dt):
            return st.enter_context(nc.psum_tensor(name, shape, dt))

        R1 = sb("R1", [128, 33024], BF16)
        R2a = sb("R2a", [128, 8192], F32)
        R2b = sb("R2b", [128, 28800], BF16)
        idb = sb("idb", [128, 128], BF16)
        Js = sb("Js", [128, 128], F32)
        Hs = sb("Hs", [128, 128], F32)
        TA = sb("TA", [128, 4, 2, 128], F32)
        TB = sb("TB", [128, 8, 2, 128], F32)
        Tm4 = sb("Tm4", [128, 128], F32)
        chA = sb("chA", [128, 4], F32); chB = sb("chB", [128, 8], F32)
        cmA = sb("cmA", [128, 4, 3], F32); cmB = sb("cmB", [128, 8], F32)
        pmk = sb("pmk", [128, 4], F32)
        lam4 = sb("lam4", [128, 4, 64], F32)
        lcol = sb("lcol", [128, 8], F32)
        sublnbc = sb("sublnbc", [128, 128], F32)
        onbbc = sb("onbbc", [128, 512], F32)
        junk = sb("junk", [128, 1024], BF16)
        Eb = sb("Eb", [128, 2, 2, 512], BF16)
        cols = sb("cols", [128, 64], F32)
        eps6 = sb("eps6", [128, 1], F32); eps5 = sb("eps5", [128, 1], F32)
        bar = sb("bar", [128, 1], F32)
        osm = sb("osm", [128, 2, 2, 128], F32)
        t5s = sb("t5s", [32, 4], F32); bts = sb("bts", [128, 3, 8], F32)
        CAs = sb("CAs", [32, 384], F32); CBs = sb("CBs", [128, 3, 384], F32)
        vst = sb("vst", [8, 384], F32)

        TP = psm_("TP", [128, 2, 1024], BF16)
        PS = psm_("PS", [128, 6, 512], F32)
        TPf = TP.bitcast(F32)
        obanks = [(PS[:, 4, :], "PS4"), (PS[:, 5, :], "PS5"), (TPf[:, 0, :], "TP0")]

        cnt = {"ev": 0, "tp": 0, "ps": 0, "sl": 0, "ost": 0}

        KM = {"k_idb": "q1", "k_w": "q10", "k_v0": "q14", "k_v1": "q15", "k_h": "q16",
              "k_xb": "q0", "k_ktst": "q1", "k_vast": "q2", "k_qast": "q3", "k_ost0": "q4", "k_ost1": "q5", "k_obst": "q6",
              "k_win": "q7", "k_c11": "q8",
              "k_kth0": "q0", "k_kth1": "q1", "k_kth2": "q2", "k_kth3": "q3", "k_vh0": "q0", "k_vh1": "q1", "k_vh2": "q2",
              "k_vh3": "q3", "k_qth": "q4",
              "k_wr0": "q0", "k_wr1": "q1", "k_wr2": "q2", "k_wr3": "q3", "k_hres": "q4", "k_ps": "q5", "k_obn": "q6",
              "k_y": "q7", "k_c12": "q8", "k_c13": "q9",
              "k_cst": "q1", "k_vc": "q2", "k_vbc": "q3", "k_oans": "q6", "k_obns": "q10", "k_oanl": "q9"}
        for i_ in range(11):
            KM["k_c%d" % i_] = "q0"
        KM.update({"k_xb0": "q0", "k_xb1": "q11", "k_xb2": "q12", "k_xb3": "q13"})

        def dmaL(out, in_, reads=(), writes=(), key=None, eng="sp"):
            key = KM[key]
            return P.op(eng, lambda e: e.dma_start(out=out, in_=in_), reads=reads, writes=writes, dma_key=key)

        def dmaO(out, in_, reads, key):
            key = KM[key]
            outkeys.add(key)
            return P.op("sp" if "4" in phases else "pool", lambda e: e.dma_start(out=out, in_=in_), reads=reads, dma_key=key)

        def evac(out, in_, reads, writes, scale=None, eng=None):
            if eng is None:
                cnt["ev"] += 1
                eng = "act" if cnt["ev"] % 2 else "dve"
            if eng == "act":
                s = 1.0 if scale is None else scale
                return P.op("act", lambda e: e.activation(out=out, in_=in_, func=AF.Copy, scale=s), reads=reads, writes=writes)
            if scale is None:
                return P.op("dve", lambda e: e.tensor_copy(out=out, in_=in_), reads=reads, writes=writes)
            return P.op("dve", lambda e: e.tensor_scalar(out=out, in0=in_, scalar1=scale, scalar2=None, op0=ALU.mult),
                        reads=reads, writes=writes)

        pe_state = {"mode": None}

        def pe_op(K, M, fn, reads=(), writes=()):
            r = lambda x: 32 if x <= 32 else (64 if x <= 64 else 128)
            mode = (r(K), r(M))
            if pe_state["mode"] is not None and pe_state["mode"] != mode:
                P.op("pe", lambda e: e.drain())
            pe_state["mode"] = mode
            return P.op("pe", fn, reads=reads, writes=writes)

        def nextps():
            cnt["ps"] = (cnt["ps"] + 1) % 4
            return cnt["ps"]

        def rms_rstd(src, rows, F, eps_t, ci, reads, tag):
            P.op("act", lambda e: e.activation(out=junk[0:rows, 0:F], in_=src, func=AF.Square,
                                               accum_out=cols[0:rows, ci:ci + 1]),
                 reads=reads, writes=["junk", "c%d" % ci])
            P.op("act", lambda e: e.activation(out=cols[0:rows, ci + 1:ci + 2], in_=cols[0:rows, ci:ci + 1], func=AF.Sqrt,
                                               bias=eps_t[0:rows, 0:1], scale=1.0 / F),
                 reads=["c%d" % ci, "eps"], writes=["c%d" % (ci + 1)])
            P.op("dve", lambda e: e.reciprocal(out=cols[0:rows, ci + 2:ci + 3], in_=cols[0:rows, ci + 1:ci + 2]),
                 reads=["c%d" % (ci + 1)], writes=["c%d" % (ci + 2)])
            return cols[0:rows, ci + 2:ci + 3], "c%d" % (ci + 2)

        def transposes(src, rows, nch, dst, reads, writes):
            b = cnt["tp"] % 2
            cnt["tp"] += 1
            for c in range(nch):
                pe_op(rows, 128, (lambda e, c=c: e.transpose(TP[:, b, c * 128:c * 128 + rows], src[:, c * 128:(c + 1) * 128],
                                                       idb[0:rows, 0:rows])),
                     reads=list(reads) + ["idb"], writes=["TP%d" % b])
            tv = TP[:, b, :].rearrange("p (a r) -> p a r", r=128)[:, 0:nch, 0:rows]
            evac(dst, tv, reads=["TP%d" % b], writes=writes)

        def proj_fm(wfn, rhsT, n, reads):
            b = nextps()
            for c in range(8):
                pe_op(128, 128, (lambda e, c=c: e.matmul(PS[:, b, 0:n], lhsT=wfn(c), rhs=rhsT[:, c, 0:n], start=(c == 0), stop=(c == 7))),
                     reads=reads, writes=["PS%d" % b])
            return PS[:, b, 0:n], "PS%d" % b

        def proj_tm(xT, tok0, rows, wfn, reads, nk=8):
            b = nextps()
            for c in range(nk):
                pe_op(128, rows, (lambda e, c=c: e.matmul(PS[0:rows, b, :], lhsT=xT[:, c, tok0:tok0 + rows], rhs=wfn(c),
                                                    start=(c == 0), stop=(c == nk - 1))),
                     reads=reads, writes=["PS%d" % b])
            return PS[0:rows, b, :], "PS%d" % b

        osb_state = {"i": 0}

        def pair_attn(keytiles, QT, qtiles, ed, finalize, qreads, osb=None):
            nqt = len(qtiles)
            per_bank = 512 // (ed + 1)
            started = set()

            def oloc(u, qi):
                g = u * nqt + qi
                bank, slot = divmod(g, per_bank)
                ap, nm = obanks[bank]
                rows = qtiles[qi][1]
                return ap[0:rows, slot * (ed + 1):(slot + 1) * (ed + 1)], nm, bank

            def qk(kt):
                sl = cnt["sl"] % 2
                cnt["sl"] += 1
                kt["sl"] = sl
                nk = kt["nk"]
                c0 = qtiles[kt["qlo"]][0]
                c1 = qtiles[kt["qhi"]][0] + qtiles[kt["qhi"]][1]
                kt["c"] = (c0, c1)
                for u in range(2):
                    pe_op(128, nk, (lambda e, u=u: e.matmul(PS[0:nk, sl * 2 + u, c0:c1], lhsT=kt["KT"],
                                                            rhs=QT[u][:, c0:c1], start=True, stop=True)),
                         reads=list(kt["reads"]) + list(qreads), writes=["PS%d" % (sl * 2 + u)])
                for (qi, Ts) in ([] if "k" in phases else kt["adds"]):
                    q0, qr = qtiles[qi]
                    for u in range(2):
                        P.op("dve", (lambda e, u=u, q0=q0, qr=qr, Ts=Ts: e.tensor_tensor(
                            out=PS[0:nk, sl * 2 + u, q0:q0 + qr], in0=PS[0:nk, sl * 2 + u, q0:q0 + qr], in1=Ts[u], op=ALU.add)),
                            reads=["PS%d" % (sl * 2 + u), "Tt"], writes=["PS%d" % (sl * 2 + u)], self_sync=False)
                if kt["bias"][0] is kt["bias"][1]:
                    P.op("act", lambda e: e.activation(out=Eb[0:nk, sl, :, c0:c1], in_=PS[0:nk, sl * 2:sl * 2 + 2, c0:c1],
                                                       func=AF.Exp, bias=kt["bias"][0], scale=1.0),
                         reads=["PS%d" % (sl * 2), "PS%d" % (sl * 2 + 1), "bias"], writes=["E%d" % sl])
                else:
                    for u in range(2):
                        P.op("act", (lambda e, u=u: e.activation(out=Eb[0:nk, sl, u, c0:c1], in_=PS[0:nk, sl * 2 + u, c0:c1],
                                                                 func=AF.Exp, bias=kt["bias"][u], scale=1.0)),
                             reads=["PS%d" % (sl * 2 + u), "bias"], writes=["E%d" % sl])

            def pvm(kt):
                if "l" in phases:
                    return
                sl = kt["sl"]
                nk = kt["nk"]
                for u in range(2):
                    for qi in range(kt["qlo"], kt["qhi"] + 1):
                        oap, nm, bank = oloc(u, qi)
                        q0, qr = qtiles[qi]
                        first = bank not in started
                        started.add(bank)
                        pe_op(nk, qr, (lambda e, u=u, oap=oap, q0=q0, qr=qr, first=first: e.matmul(
                            oap, lhsT=Eb[0:nk, sl, u, q0:q0 + qr], rhs=kt["V"][u], start=first, stop=False,
                            skip_group_check=True)),
                            reads=["E%d" % sl] + list(kt["reads"]), writes=[nm])

            prev = None
            for kt in keytiles:
                qk(kt)
                if prev is not None:
                    pvm(prev)
                prev = kt
            pvm(prev)
            O = [[oloc(u, qi)[0] for qi in range(nqt)] for u in range(2)]
            names = sorted({oloc(u, qi)[1] for u in range(2) for qi in range(nqt)})
            if osb is not None:
                sset = osb[osb_state["i"] % len(osb)]
                osb_state["i"] += 1
                used = sorted({oloc(u, qi)[2] for u in range(2) for qi in range(nqt)})
                for bk in used:
                    bap, bnm = obanks[bk]
                    P.op("dve", (lambda e, bk=bk, bap=bap: e.tensor_copy(out=sset[0][:, bk, :], in_=bap)), reads=[bnm],
                         writes=[sset[1] + str(bk)])

                def oloc2(u, qi):
                    g = u * nqt + qi
                    bank, slot = divmod(g, per_bank)
                    rows = qtiles[qi][1]
                    return sset[0][0:rows, bank, slot * (ed + 1):(slot + 1) * (ed + 1)], sset[1] + str(bank)
                O = [[oloc2(u, qi)[0] for qi in range(nqt)] for u in range(2)]
                names = sorted({oloc2(u, qi)[1] for u in range(2) for qi in range(nqt)})
            if "m" not in phases:
                finalize(O, names)

        def load_win():
            Wv = R1[:, 0:24576].rearrange("p (c n) -> p c n", n=3072)
            for c in range(8):
                for hh in range(2):
                    dmaL(Wv[:, c, hh * 1536:(hh + 1) * 1536], w_in.ap()[c * 128:(c + 1) * 128, hh * 1536:(hh + 1) * 1536],
                         writes=["W"], key="k_win", eng="pool")
            return Wv

        Wv = load_win()
        P.op("dve", lambda e: e.memset(eps6[:], 1e-6), writes=["eps"])
        P.op("dve", lambda e: e.memset(eps5[:], 1e-5), writes=["eps"])
        dmaL(idb[:], identd.ap(), writes=["idb"], key="k_idb", eng="pool")
        dmaL(Js[:], Jd.ap(), writes=["Js"], key="k_c0")
        dmaL(t5s[:], t5.ap(), writes=["t5s"], key="k_c1")
        P.op("dve", lambda e: e.memset(bts[:], 0.0), writes=["bts"])
        dmaL(bts[:, 0:2, :], bt.ap()[0:256, :].rearrange("(a p) h -> p a h", p=128), writes=["bts"], key="k_c2")
        dmaL(bts[0:1, 2, :], bt.ap()[256:257, :], writes=["bts"], key="k_c2")
        dmaL(CAs[:], CAd.ap(), writes=["CAs"], key="k_c3")
        dmaL(CBs[:], CBd.ap().rearrange("(a p) n -> p a n", p=128), writes=["CBs"], key="k_c4")
        dmaL(chA[:], bass.AP(t5, 15 * 4, [[0, 128], [1, 4]]), writes=["chA"], key="k_c5")
        dmaL(chB[:], bass.AP(bt, 0, [[0, 128], [1, 8]]), writes=["chB"], key="k_c6")
        dmaL(pmk[:], padmask.ap(), writes=["pmk"], key="k_c7")
        for i, lt in enumerate([lq1, lk1, lq2, lk2]):
            dmaL(lam4[:, i, :], bass.AP(lt, 0, [[0, 128], [1, 64]]), writes=["lam4"], key="k_c8")
        dmaL(sublnbc[:], bass.AP(subln, 0, [[0, 128], [1, 128]]), writes=["sublnbc"], key="k_c9")
        dmaL(onbbc[:], bass.AP(onb, 0, [[0, 128], [1, 512]]), writes=["onbbc"], key="k_c10")
        P.barrier(bar[:], skip=("q7",))
        P.op("dve", lambda e: e.tensor_scalar(out=sublnbc[:], in0=sublnbc[:], scalar1=0.8, scalar2=None, op0=ALU.mult),
             reads=["sublnbc"], writes=["sublnbc"])
        P.op("dve", lambda e: e.tensor_tensor(out=lam4[:, 0, :], in0=lam4[:, 0, :], in1=lam4[:, 1, :], op=ALU.mult),
             reads=["lam4"], writes=["lam4"])
        P.op("dve", lambda e: e.tensor_tensor(out=lam4[:, 2, :], in0=lam4[:, 2, :], in1=lam4[:, 3, :], op=ALU.mult),
             reads=["lam4"], writes=["lam4"])
        P.op("dve", lambda e: e.tensor_reduce(out=lcol[:, 0:1], in_=lam4[:, 0, :], axis=mybir.AxisListType.X, op=ALU.add),
             reads=["lam4"], writes=["lcol"])
        P.op("dve", lambda e: e.tensor_reduce(out=lcol[:, 1:2], in_=lam4[:, 2, :], axis=mybir.AxisListType.X, op=ALU.add),
             reads=["lam4"], writes=["lcol"])
        P.op("act", lambda e: e.activation(out=lcol[:, 2:4], in_=lcol[:, 0:2], func=AF.Exp), reads=["lcol"], writes=["lcol"])
        P.op("dve", lambda e: e.scalar_tensor_tensor(out=lcol[:, 4:5], in0=lcol[:, 3:4], scalar=-0.2, in1=lcol[:, 2:3],
                                                     op0=ALU.add, op1=ALU.subtract), reads=["lcol"], writes=["neglam"])
        neglam = lcol[:, 4:5]
        for h in range(4):
            P.op("dve", (lambda e, h=h: e.tensor_scalar(out=cmA[:, h, :], in0=pmk[:, 0:3], scalar1=chA[:, h:h + 1], scalar2=None,
                                                        op0=ALU.add)), reads=["pmk", "chA"], writes=["bias"])
        P.op("dve", lambda e: e.tensor_scalar(out=cmB[:], in0=chB[:], scalar1=pmk[:, 2:3], scalar2=None, op0=ALU.add),
             reads=["pmk", "chB"], writes=["bias"])
        pe_op(32, 4, lambda e: e.matmul(PS[0:4, 0, 0:384], lhsT=t5s[:], rhs=CAs[:], start=True, stop=True),
             reads=["t5s", "CAs"], writes=["PS0"])
        P.op("dve", lambda e: e.tensor_copy(out=vst[0:4, :], in_=PS[0:4, 0, 0:384]), reads=["PS0"], writes=["vst"])
        dmaL(vecA.ap(), vst[0:4, :], reads=["vst"], writes=["vecA"], key="k_v0")
        for a in range(3):
            pe_op(128, 8, (lambda e, a=a: e.matmul(PS[0:8, 1, 0:384], lhsT=bts[:, a, :], rhs=CBs[:, a, :], start=(a == 0), stop=(a == 2))),
                 reads=["bts", "CBs"], writes=["PS1"])
        P.op("dve", lambda e: e.tensor_copy(out=vst[0:8, :], in_=PS[0:8, 1, 0:384]), reads=["PS1", "vecA"], writes=["vst"])
        dmaL(vecB.ap(), vst[0:8, :], reads=["vst"], writes=["vecB"], key="k_v1")
        Hall = R2a[:, 4096:7168].rearrange("p (i n) -> p i n", n=128)
        hi = 0
        hlist = []
        for (vec, Tt, nh) in ((vecA, TA, 4), (vecB, TB, 8)):
            for h in range(nh):
                for kind, base in ((0, 128), (1, 0)):
                    hank = bass.AP(vec, h * 384 + base, [[1, 128], [1, 128]])
                    dmaL(Hall[:, hi, :], hank, reads=["vecA", "vecB"], writes=["Hall"], key="k_h")
                    hlist.append((hi, Tt, h, kind))
                    hi += 1
        for (hi, Tt, h, kind) in hlist:
            bk = 2 + hi % 2
            pe_op(128, 128, (lambda e, hi=hi, bk=bk: e.matmul(PS[:, bk, 0:128], lhsT=Hall[:, hi, :], rhs=Js[:], start=True, stop=True)),
                  reads=["Hall", "Js"], writes=["PS%d" % bk])
            P.op("dve", (lambda e, Tt=Tt, h=h, kind=kind, bk=bk: e.tensor_copy(out=Tt[:, h, kind, :], in_=PS[:, bk, 0:128])),
                 reads=["PS%d" % bk], writes=["Tt"], self_sync=False)
            if kind == 0:
                P.op("dve", (lambda e, Tt=Tt, h=h: e.memset(Tt[64:128, h, 0, 0:64], NEGM)), reads=["Tt"], writes=["Tt"])
        P.op("dve", lambda e: e.memset(Tm4[:], 0.0), writes=["Tt"])
        P.op("dve", lambda e: e.memset(Tm4[0:64, 64:128], NEGM), reads=["Tt"], writes=["Tt"])
        def weight_casts():
            for (src, dst, rows, colsn) in ((w_out, wout_b, 1024, 1024), (w_up, wup_b, 1024, 4096), (w_down, wdown_b, 4096, 1024),
                                           (w_gate, wgate_b, 1024, 1024), (w_ple, wple_b, 256, 1024)):
                sv = src.ap().rearrange("r (a n) -> (r a) n", n=1024)
                dv = dst.ap().rearrange("r (a n) -> (r a) n", n=1024)
                tot = rows * colsn // 1024
                for r0 in range(0, tot, 512):
                    n_ = min(512, tot - r0)
                    P.op("pool", (lambda e, r0=r0, n_=n_, dv=dv, sv=sv: e.dma_start(out=dv[r0:r0 + n_, :], in_=sv[r0:r0 + n_, :],
                                                                                    max_dma_last_dim=2048)),
                         writes=["wscr"], dma_key=KM["k_w"])


        P.barrier(bar[:], skip=("q7",))
        xb = R2a[:, 0:4096].rearrange("p (t f) -> p t f", f=1024)
        ostg = R2a[:, 4096:5120].rearrange("p (s f) -> p s f", f=512)
        obraw = R2a[:, 5120:7168].rearrange("p (t f) -> p t f", f=512)
        gattnbc = R2a[:, 7168:8192]
        xs = R2b[:, 0:4096].rearrange("p (t f) -> p t f", f=1024)
        xsT = R2b[:, 4096:8192].rearrange("p (c n) -> p c n", n=512)
        KTst = R2b[:, 8192:10240].rearrange("p (h n) -> p h n", n=512)
        VAst = R2b[:, 10240:12288].rearrange("p (t n) -> p t n", n=512)
        KBT = R2b[:, 12288:16384].rearrange("p (s c n) -> p s c n", s=2, n=512)
        VBa = R2b[:, 16384:20608].rearrange("p (s t h e) -> p s t h e", s=2, t=4, e=66)
        QBz = R2b[:, 20608:24704].rearrange("p (u c n) -> p u c n", u=2, n=512)
        QAst = R2b[:, 24704:26752].rearrange("p (h n) -> p h n", n=512)
        OBst = R2b[:, 26752:28800].rearrange("p (t n) -> p t n", n=512)
        P.op("pool", lambda e: e.memset(QBz, 0.0), writes=["QBT"])
        dmaL(gattnbc, bass.AP(g_attn, 0, [[0, 128], [1, 1024]]), writes=["gattnbc"], key="k_c11")
        P.op("dve", lambda e: e.memset(VBa[:, :, :, :, 64:66], 1.0), writes=["VBones"])

        def out_store(dst_ap, psum_ap, psname, rows=128):
            s = cnt["ost"] % 2
            cnt["ost"] += 1
            evac(ostg[0:rows, s, :], psum_ap, reads=[psname], writes=["ostg%d" % s])
            dmaO(dst_ap, ostg[0:rows, s, :], reads=["ostg%d" % s], key="k_ost%d" % s)
            return ostg[0:rows, s, :], "ostg%d" % s

        def band_attention(p, I):
            sp_, so_ = (p - 1) % 2, p % 2
            qtl = [(i * 128, 128) for i in range(4)]
            for cb in range(4):
                kts = []
                for r in range(-4, 4):
                    slot, tk = (sp_, r + 4) if r < 0 else (so_, r)
                    qlo, qhi = max(0, r), min(3, r + 4)
                    adds = []
                    for qi in range(qlo, qhi + 1):
                        rel = r - qi
                        if rel == 0:
                            adds.append((qi, [TB[:, 2 * cb + u, 0, :] for u in range(2)]))
                        elif rel == -1:
                            adds.append((qi, [TB[:, 2 * cb + u, 1, :] for u in range(2)]))
                        elif rel == -4:
                            adds.append((qi, [Tm4[:], Tm4[:]]))
                    bsrc = cmB if (p == 3 and r < 0) else chB
                    kts.append(dict(KT=KBT[:, slot, cb, tk * 128:(tk + 1) * 128], nk=128,
                                    V=[VBa[:, slot, tk, 2 * cb + u, 0:65] for u in range(2)],
                                    bias=[bsrc[:, 2 * cb + u:2 * cb + u + 1] for u in range(2)],
                                    qlo=qlo, qhi=qhi, adds=adds, reads=["KBT%d" % slot, "VB%d" % slot, "VBones"]))

                def fin(O, names, cb=cb):
                    for u in range(2):
                        hb = 2 * cb + u
                        for qi in range(4):
                            ci = 8 + (qi * 2 + u)
                            P.op("dve", (lambda e, u=u, qi=qi, ci=ci: e.reciprocal(out=cols[:, ci:ci + 1], in_=O[u][qi][:, 64:65])),
                                 reads=names, writes=["c%d" % ci])
                            P.op("dve", (lambda e, u=u, qi=qi, ci=ci, hb=hb: e.tensor_scalar(
                                out=obraw[:, qi, hb * 64:(hb + 1) * 64], in0=O[u][qi][:, 0:64], scalar1=cols[:, ci:ci + 1],
                                scalar2=None, op0=ALU.mult)), reads=names + ["c%d" % ci], writes=["obraw"])
                pair_attn(kts, [QBz[:, 0, cb, :], QBz[:, 1, cb, :]], qtl, 64, fin, ["QBT"])
            for qi in range(4):
                rc, rn = rms_rstd(obraw[:, qi, :], 128, 512, eps6, 16 + 3 * qi, ["obraw"], "ob")
                P.op("dve", (lambda e, qi=qi, rc=rc: e.scalar_tensor_tensor(out=OBst[:, qi, :], in0=obraw[:, qi, :], scalar=rc,
                                                                              in1=onbbc[:], op0=ALU.mult, op1=ALU.mult)),
                     reads=["obraw", rn, "onbbc"], writes=["OBst"])
            dmaL(OBNs.ap()[I * 512:(I + 1) * 512, :].rearrange("(t p) n -> p t n", p=128), OBst, reads=["OBst"], writes=["OBNs"],
                 key="k_obst", eng="pool")

        def phaseA_block(p):
            own = (p % 4 == 3)
            I = p // 4
            last = (p == NBLK - 1)
            so_ = p % 2
            for t in range(4):
                dmaL(xb[:, t, :], xv.ap()[p * 512 + t * 128:p * 512 + (t + 1) * 128, :], writes=["xb%d" % t], key="k_xb%d" % t)
            for t in range(4):
                rc, rn = rms_rstd(xb[:, t, :], 128, 1024, eps6, 3 * t, ["xb%d" % t], "x")
                P.op("dve", (lambda e, t=t, rc=rc: e.scalar_tensor_tensor(out=xs[:, t, :], in0=xb[:, t, :], scalar=rc, in1=gattnbc,
                                                                            op0=ALU.mult, op1=ALU.mult)),
                     reads=["xb%d" % t, rn, "gattnbc"], writes=["xs%d" % t])
            for t in range(4):
                transposes(xs[:, t, :], 128, 8, xsT[:, :, t * 128:(t + 1) * 128], ["xs%d" % t], ["xsT"])
            for h in range(4):
                ps_, nm = proj_fm(lambda c, h=h: Wv[:, c, 512 + h * 128:512 + (h + 1) * 128], xsT, 512, ["W", "xsT"])
                evac(KTst[:, h, :], ps_, reads=[nm], writes=["KTst"])
            dmaL(KTs.ap().rearrange("h p n -> p h n")[:, :, p * 512:(p + 1) * 512], KTst, reads=["KTst"], writes=["KTs"],
                 key="k_ktst", eng="pool")
            for cb in range(4):
                ps_, nm = proj_fm(lambda c, cb=cb: Wv[:, c, 2048 + cb * 128:2048 + (cb + 1) * 128], xsT, 512, ["W", "xsT"])
                evac(KBT[:, so_, cb, :], ps_, reads=[nm], writes=["KBT%d" % so_])
            if own and "3" not in phases:
                for h in range(4):
                    ps_, nm = proj_fm(lambda c, h=h: Wv[:, c, h * 128:(h + 1) * 128], xsT, 512, ["W", "xsT"])
                    evac(QAst[:, h, :], ps_, reads=[nm], writes=["QAst"], scale=0.125)
                dmaL(QTs.ap().rearrange("h p n -> p h n")[:, :, I * 512:(I + 1) * 512], QAst, reads=["QAst"], writes=["QTs"],
                     key="k_qast", eng="pool")
                for cb in range(4):
                    ps_, nm = proj_fm(lambda c, cb=cb: Wv[:, c, 1536 + cb * 128:1536 + (cb + 1) * 128], xsT, 512, ["W", "xsT"])
                    evac(QBz[:, 0, cb, :], ps_, reads=[nm], writes=["QBT"], scale=0.125)
                    P.op("pool", (lambda e, cb=cb: e.tensor_copy(out=QBz[64:128, 1, cb, :], in_=QBz[64:128, 0, cb, :])),
                         reads=["QBT"], writes=["QBT"])
                    P.op("pool", (lambda e, cb=cb: e.memset(QBz[64:128, 0, cb, :], 0.0)), reads=["QBT"], writes=["QBT"])
            for t in range(4):
                ps_, nm = proj_tm(xsT, t * 128, 128, lambda c: Wv[:, c, 1024:1536], ["W", "xsT"])
                if own:
                    sa, sn = out_store(nav.ap()[I * 512 + t * 128:I * 512 + (t + 1) * 128, :], ps_, nm)
                    P.op("pool", (lambda e, t=t, sa=sa: e.tensor_copy(out=VAst[:, t, :], in_=sa)), reads=[sn], writes=["VAst"])
                else:
                    evac(VAst[:, t, :], ps_, reads=[nm], writes=["VAst"])
                ps_, nm = proj_tm(xsT, t * 128, 128, lambda c: Wv[:, c, 2560:3072], ["W", "xsT"])
                if last:
                    sa, sn = out_store(nbv.ap()[t * 128:(t + 1) * 128, :], ps_, nm)
                    P.op("pool", (lambda e, t=t, sa=sa: e.tensor_copy(out=VBa[:, so_, t, :, 0:64],
                                                                      in_=sa.rearrange("p (h e) -> p h e", e=64))),
                         reads=[sn], writes=["VB%d" % so_])
                else:
                    evac(VBa[:, so_, t, :, 0:64], ps_.rearrange("p (h e) -> p h e", e=64), reads=[nm], writes=["VB%d" % so_])
                if own:
                    ps_, nm = proj_tm(xsT, t * 128, 128, lambda c: Wv[:, c, 512:1024], ["W", "xsT"])
                    out_store(nak.ap()[I * 512 + t * 128:I * 512 + (t + 1) * 128, :], ps_, nm)
                if last:
                    ps_, nm = proj_tm(xsT, t * 128, 128, lambda c: Wv[:, c, 2048:2560], ["W", "xsT"])
                    out_store(nbk.ap()[t * 128:(t + 1) * 128, :], ps_, nm)
            dmaL(VAs.ap()[p * 512:(p + 1) * 512, :].rearrange("(t p) n -> p t n", p=128), VAst, reads=["VAst"], writes=["VAs"],
                 key="k_vast", eng="pool")
            if own and "1" not in phases:
                band_attention(p, I)

        if "A" in phases:
            for p in range(NBLK):
                phaseA_block(p)
        if "a" in phases:
            for p in range(4):
                phaseA_block(p)
        if "e" in phases:
            for p in range(3):
                phaseA_block(p)
        P.barrier(bar[:])

        KTh = R1[:, 0:16384]
        Vaug = R1[:, 16384:33024].rearrange("p (t e) -> p t e", e=130)
        OAN = R2b[:, 0:16384].rearrange("p (t n) -> p t n", n=512)
        QTz = R2b[:, 16384:24576].rearrange("p (u n) -> p u n", u=2)

        def finA_factory(h, dst_fn, rows):
            def fin(O, names):
                nqt = len(O[0])
                for qi in range(nqt):
                    pr = qi % 2
                    cb_ = 28 + 8 * pr
                    P.op("dve", (lambda e, qi=qi, cb_=cb_: e.reciprocal(out=cols[0:rows, cb_:cb_ + 1], in_=O[0][qi][:, 128:129])),
                         reads=names, writes=["fa%d" % pr])
                    P.op("dve", (lambda e, qi=qi, cb_=cb_: e.reciprocal(out=cols[0:rows, cb_ + 1:cb_ + 2], in_=O[1][qi][:, 128:129])),
                         reads=names + ["fa%d" % pr], writes=["fa%d" % pr])
                    P.op("dve", (lambda e, cb_=cb_: e.tensor_scalar(out=cols[0:rows, cb_ + 2:cb_ + 3], in0=cols[0:rows, cb_ + 1:cb_ + 2],
                                                                   scalar1=neglam[0:rows, :], scalar2=None, op0=ALU.mult)),
                         reads=["fa%d" % pr, "neglam"], writes=["fa%d" % pr])
                    P.op("dve", (lambda e, qi=qi, cb_=cb_, pr=pr: e.tensor_scalar(out=osm[0:rows, pr, 0, :], in0=O[1][qi][:, 0:128],
                                                                                  scalar1=cols[0:rows, cb_ + 2:cb_ + 3], scalar2=None,
                                                                                  op0=ALU.mult)),
                         reads=names + ["fa%d" % pr], writes=["osm%d" % pr])
                    P.op("dve", (lambda e, qi=qi, cb_=cb_, pr=pr: e.scalar_tensor_tensor(
                        out=osm[0:rows, pr, 1, :], in0=O[0][qi][:, 0:128], scalar=cols[0:rows, cb_:cb_ + 1], in1=osm[0:rows, pr, 0, :],
                        op0=ALU.mult, op1=ALU.add)), reads=names + ["fa%d" % pr, "osm%d" % pr], writes=["osm%d" % pr])
                    ci = cb_ + 3
                    P.op("act", (lambda e, pr=pr, ci=ci: e.activation(out=junk[0:rows, 0:128], in_=osm[0:rows, pr, 1, :], func=AF.Square,
                                                                     accum_out=cols[0:rows, ci:ci + 1])),
                         reads=["osm%d" % pr], writes=["junk", "fb%d" % pr])
                    P.op("act", (lambda e, ci=ci: e.activation(out=cols[0:rows, ci + 1:ci + 2], in_=cols[0:rows, ci:ci + 1], func=AF.Sqrt,
                                                              bias=eps5[0:rows, 0:1], scale=1.0 / 128)),
                         reads=["fb%d" % pr, "eps"], writes=["fb%d" % pr])
                    P.op("dve", (lambda e, ci=ci: e.reciprocal(out=cols[0:rows, ci + 2:ci + 3], in_=cols[0:rows, ci + 1:ci + 2])),
                         reads=["fb%d" % pr], writes=["fb%d" % pr])
                    dst, dnm = dst_fn(qi)
                    P.op("dve", (lambda e, pr=pr, ci=ci, dst=dst: e.scalar_tensor_tensor(
                        out=dst, in0=osm[0:rows, pr, 1, :], scalar=cols[0:rows, ci + 2:ci + 3], in1=sublnbc[0:rows, :],
                        op0=ALU.mult, op1=ALU.mult)), reads=["osm%d" % pr, "fb%d" % pr, "sublnbc"], writes=[dnm])
            return fin

        def phaseD1():
            xb = R2a[:, 0:1024]
            ostg = R2a[:, 4096:5120].rearrange("p (s f) -> p s f", f=512)
            obraw = R2a[:, 1024:1536]
            o = 0

            def carve(n):
                nonlocal o
                a = R2b[:, o:o + n]
                o += n
                return a
            oanl = carve(512)
            xs = carve(1024)
            xsT = carve(8 * 64).rearrange("p (c n) -> p c n", n=64)
            cst = carve(8 * 512).rearrange("p (t n) -> p t n", n=512)
            KTc = carve(4 * 1056).rearrange("p (h n) -> p h n", n=1056)
            Vc = carve(9 * 4 * 130).rearrange("p (t h e) -> p t h e", h=4, e=130)
            KBc = carve(4 * 544).rearrange("p (c n) -> p c n", n=544)
            VBc = carve(5 * 8 * 66).rearrange("p (t h e) -> p t h e", h=8, e=66)
            QAz = carve(2 * 4 * 64).rearrange("p (u h n) -> p u h n", u=2, n=64)
            QBzs = carve(2 * 4 * 64).rearrange("p (u c n) -> p u c n", u=2, n=64)
            P.op("pool", lambda e: e.memset(QAz, 0.0), writes=["QAs"])
            P.op("pool", lambda e: e.memset(QBzs, 0.0), writes=["QBs"])
            oans = carve(512)
            obns = carve(512)
            P.op("dve", lambda e: e.memset(Vc[:, :, :, 128:130], 1.0), writes=["Vc1"])
            P.op("dve", lambda e: e.memset(VBc[:, :, :, 64:66], 1.0), writes=["VBc1"])
            dmaL(xb[0:64, :], xsm.ap(), writes=["xb"], key="k_xb")
            rc, rn = rms_rstd(xb[0:64, :], 64, 1024, eps6, 0, ["xb"], "x")
            P.op("dve", (lambda e, rc=rc: e.scalar_tensor_tensor(out=xs[0:64, :], in0=xb[0:64, :], scalar=rc, in1=gattnbc[0:64, :],
                                                                  op0=ALU.mult, op1=ALU.mult)), reads=["xb", rn, "gattnbc"], writes=["xs"])
            transposes(xs[0:64, :], 64, 8, xsT[:, :, 0:64], ["xs"], ["xsT"])
            for (c0, dst) in ((512, sak), (1024, sav), (2048, sbk), (2560, sbv)):
                ps_, nm = proj_tm(xsT, 0, 64, lambda c, c0=c0: Wv[:, c, c0:c0 + 512], ["W", "xsT"])
                out_store(dst.ap(), ps_, nm, rows=64)
            for h in range(4):
                ps_, nm = proj_fm(lambda c, h=h: Wv[:, c, h * 128:(h + 1) * 128], xsT, 64, ["W", "xsT"])
                evac(QAz[:, 0, h, :], ps_, reads=[nm], writes=["QAs"], scale=0.125)
                P.op("pool", (lambda e, h=h: e.tensor_copy(out=QAz[64:128, 1, h, :], in_=QAz[64:128, 0, h, :])), reads=["QAs"], writes=["QAs"])
                P.op("pool", (lambda e, h=h: e.memset(QAz[64:128, 0, h, :], 0.0)), reads=["QAs"], writes=["QAs"])
                ps_, nm = proj_fm(lambda c, h=h: Wv[:, c, 1536 + h * 128:1536 + (h + 1) * 128], xsT, 64, ["W", "xsT"])
                evac(QBzs[:, 0, h, :], ps_, reads=[nm], writes=["QBs"], scale=0.125)
                P.op("pool", (lambda e, h=h: e.tensor_copy(out=QBzs[64:128, 1, h, :], in_=QBzs[64:128, 0, h, :])), reads=["QBs"], writes=["QBs"])
                P.op("pool", (lambda e, h=h: e.memset(QBzs[64:128, 0, h, :], 0.0)), reads=["QBs"], writes=["QBs"])
            for s in range(2):
                for h in range(4):
                    ps_, nm = proj_fm(lambda c, h=h: Wv[:, c, 512 + h * 128:512 + (h + 1) * 128], xsT[:, :, s * 32:(s + 1) * 32], 32,
                                      ["W", "xsT"])
                    evac(KTc[:, h, 1024:1056], ps_, reads=[nm], writes=["KTc"])
                    ps_, nm = proj_fm(lambda c, h=h: Wv[:, c, 2048 + h * 128:2048 + (h + 1) * 128], xsT[:, :, s * 32:(s + 1) * 32], 32,
                                      ["W", "xsT"])
                    evac(KBc[:, h, 512:544], ps_, reads=[nm], writes=["KBc"])
                ps_, nm = proj_tm(xsT, s * 32, 32, lambda c: Wv[:, c, 1024:1536], ["W", "xsT"])
                evac(Vc[0:32, 8, :, 0:128], ps_.rearrange("p (h e) -> p h e", e=128), reads=[nm], writes=["Vc"])
                ps_, nm = proj_tm(xsT, s * 32, 32, lambda c: Wv[:, c, 2560:3072], ["W", "xsT"])
                evac(VBc[0:32, 4, :, 0:64], ps_.rearrange("p (h e) -> p h e", e=64), reads=[nm], writes=["VBc"])
                dmaL(cst, cak.ap()[s * 1024:(s + 1) * 1024, :].rearrange("(t p) n -> p t n", p=128), writes=["cst"], key="k_cst",
                     eng="pool")
                for t in range(8):
                    transposes(cst[:, t, :], 128, 4, KTc[:, :, t * 128:(t + 1) * 128], ["cst"], ["KTc"])
                dmaL(cst[:, 0:4, :], cbk.ap()[s * 512:(s + 1) * 512, :].rearrange("(t p) n -> p t n", p=128), writes=["cst"],
                     key="k_cst", eng="pool")
                for t in range(4):
                    transposes(cst[:, t, :], 128, 4, KBc[:, :, t * 128:(t + 1) * 128], ["cst"], ["KBc"])
                for h_ in range(4):
                    dmaL(Vc[:, 0:8, h_, 0:128],
                         cav.ap()[s * 1024:(s + 1) * 1024, h_ * 128:(h_ + 1) * 128].rearrange("(t p) e -> p t e", p=128),
                         reads=["Vc1"], writes=["Vc"], key="k_vc", eng="pool")
                for h_ in range(8):
                    dmaL(VBc[:, 0:4, h_, 0:64],
                         cbv.ap()[s * 512:(s + 1) * 512, h_ * 64:(h_ + 1) * 64].rearrange("(t p) e -> p t e", p=128),
                         reads=["VBc1"], writes=["VBc"], key="k_vbc", eng="pool")
                for h in range(4):
                    kts = []
                    for t in range(9):
                        nk = 128 if t < 8 else 32
                        adds = []
                        if t == 7:
                            adds.append((0, [TA[:, h, 1, 0:32]] * 2))
                        if t == 8:
                            adds.append((0, [TA[0:32, h, 0, 0:32]] * 2))
                        bcol = chA[0:nk, h:h + 1]
                        kts.append(dict(KT=KTc[:, h, t * 128:t * 128 + nk], nk=nk, V=[Vc[0:nk, t, h, 0:129]] * 2, bias=[bcol, bcol],
                                        qlo=0, qhi=0, adds=adds, reads=["KTc", "Vc", "Vc1"]))
                    fin = finA_factory(h, lambda qi, h=h: (oans[0:32, h * 128:(h + 1) * 128], "oans"), 32)
                    pair_attn(kts, [QAz[:, u_, h, s * 32:(s + 1) * 32] for u_ in range(2)], [(0, 32)], 128, fin, ["QAs"])
                dmaL(OANss.ap()[s * 32:(s + 1) * 32, :], oans[0:32, :], reads=["oans"], writes=["OANss"], key="k_oans", eng="pool")
                for cb in range(4):
                    kts = []
                    for t in range(5):
                        nk = 128 if t < 4 else 32
                        adds = []
                        if t == 3:
                            adds.append((0, [TB[:, 2 * cb + u, 1, 0:32] for u in range(2)]))
                        if t == 4:
                            adds.append((0, [TB[0:32, 2 * cb + u, 0, 0:32] for u in range(2)]))
                        kts.append(dict(KT=KBc[:, cb, t * 128:t * 128 + nk], nk=nk,
                                        V=[VBc[0:nk, t, 2 * cb + u, 0:65] for u in range(2)],
                                        bias=[chB[0:nk, 2 * cb + u:2 * cb + u + 1] for u in range(2)],
                                        qlo=0, qhi=0, adds=adds, reads=["KBc", "VBc", "VBc1"]))

                    def finb(O, names, cb=cb):
                        for u in range(2):
                            hb = 2 * cb + u
                            ci = 8 + u
                            P.op("dve", (lambda e, u=u, ci=ci: e.reciprocal(out=cols[0:32, ci:ci + 1], in_=O[u][0][:, 64:65])),
                                 reads=names, writes=["c%d" % ci])
                            P.op("dve", (lambda e, u=u, ci=ci, hb=hb: e.tensor_scalar(
                                out=obraw[0:32, hb * 64:(hb + 1) * 64], in0=O[u][0][:, 0:64], scalar1=cols[0:32, ci:ci + 1],
                                scalar2=None, op0=ALU.mult)), reads=names + ["c%d" % ci], writes=["obraw"])
                    pair_attn(kts, [QBzs[:, u_, cb, s * 32:(s + 1) * 32] for u_ in range(2)], [(0, 32)], 64, finb, ["QBs"])
                rc, rn = rms_rstd(obraw[0:32, :], 32, 512, eps6, 16, ["obraw"], "ob")
                P.op("dve", (lambda e, rc=rc: e.scalar_tensor_tensor(out=obns[0:32, :], in0=obraw[0:32, :], scalar=rc, in1=onbbc[0:32, :],
                                                                      op0=ALU.mult, op1=ALU.mult)),
                     reads=["obraw", rn, "onbbc"], writes=["obns"])
                dmaL(OBNss.ap()[s * 32:(s + 1) * 32, :], obns[0:32, :], reads=["obns"], writes=["OBNs"], key="k_obns", eng="pool")
        if "D" in phases:
            phaseD1()
            P.barrier(bar[:])

        weight_casts()
        osbB = [(R2a[:, 0:1536].rearrange("p (b n) -> p b n", n=512), "osbA"),
                (R2a[:, 1536:3072].rearrange("p (b n) -> p b n", n=512), "osbB")]
        if "B" in phases:
            P.op("dve", lambda e: e.memset(Vaug[:, :, 128:130], 1.0), writes=["Vones"])
            P.op("dve", lambda e: e.memset(QTz, 0.0), writes=["QTh", "QTh0"])
            NCH = 4
            for h in range(4):
                for ch in range(NCH):
                    k0 = ch * (SEQV // NCH)
                    k1 = (ch + 1) * (SEQV // NCH)
                    dmaL(KTh[:, k0:k1], KTs.ap()[h, :, k0:k1], reads=["Vones"], writes=["KTh%d" % ch], key="k_kth%d" % ch)
                    dmaL(Vaug[:, k0 // 128:k1 // 128, 0:128],
                         VAs.ap()[k0:k1, h * 128:(h + 1) * 128].rearrange("(t p) e -> p t e", p=128),
                         reads=["Vones"], writes=["Vh%d" % ch], key="k_vh%d" % ch)
                for u_ in range(2):
                    dmaL(QTz[64 * u_:64 * u_ + 64, u_, 0:NTOK], QTs.ap()[h, 64 * u_:64 * u_ + 64, :], reads=["QTh0"], writes=["QTh"],
                         key="k_qth")
                for I in range(NOWN):
                    p = 4 * I + 3
                    kts = []
                    for kt in range(4 * p + 4):
                        r = kt - 4 * p
                        qlo = max(0, r)
                        adds = []
                        if r >= 0:
                            adds.append((r, [TA[:, h, 0, :]] * 2))
                            if r + 1 <= 3:
                                adds.append((r + 1, [TA[:, h, 1, :]] * 2))
                        elif r == -1:
                            adds.append((0, [TA[:, h, 1, :]] * 2))
                        bcol = cmA[:, h, kt // 4:kt // 4 + 1] if kt < 12 else chA[:, h:h + 1]
                        ch = kt * 128 // (SEQV // NCH)
                        kts.append(dict(KT=KTh[:, kt * 128:(kt + 1) * 128], nk=128, V=[Vaug[:, kt, 0:129]] * 2, bias=[bcol, bcol],
                                        qlo=qlo, qhi=3, adds=adds, reads=["KTh%d" % ch, "Vh%d" % ch, "Vones"]))
                    fin = finA_factory(h, lambda qi, I=I, h=h: (OAN[:, I * 4 + qi, h * 128:(h + 1) * 128], "OAN"), 128)
                    pair_attn(kts, [QTz[:, u_, I * 512:(I + 1) * 512] for u_ in range(2)], [(i * 128, 128) for i in range(4)], 128, fin,
                              ["QTh"], osb=osbB)
        P.barrier(bar[:])

        ACT_T = R1[:, 0:16384].rearrange("p (h n) -> p h n", n=512)
        WR = R1[:, 16384:32768].rearrange("p (s n) -> p s n", n=4096)
        hres = R2a[:, 0:4096].rearrange("p (t f) -> p t f", f=1024)
        p_s = R2a[:, 4096:5120].rearrange("p (t f) -> p t f", f=256)
        gs = R2a[:, 5120:5632]
        tmpf = R2a[:, 5632:6144]
        gmlpbc = R2a[:, 6144:7168]
        gfinbc = R2a[:, 7168:8192]
        cs = R2b[:, 16384:20480].rearrange("p (t f) -> p t f", f=1024)
        aT = R2b[:, 20480:24576].rearrange("p (c n) -> p c n", n=512)
        obn_s = R2b[:, 24576:26624].rearrange("p (t n) -> p t n", n=512)
        pb = R2b[:, 26624:27648].rearrange("p (t f) -> p t f", f=256)
        pT = R2b[:, 27648:28672].rearrange("p (c n) -> p c n", n=512)
        wcnt = {"i": 0}

        def wload(src_ap, shape_view):
            s = wcnt["i"] % 4
            wcnt["i"] += 1
            dst = shape_view(WR[:, s, :])
            dmaL(dst, src_ap, reads=["wscr"], writes=["WR%d" % s], key="k_wr%d" % s)
            return dst, "WR%d" % s

        def phaseC_group(tiles, x_ap, p_ap, oan_fn, obn_src, y_ap):
            NT = tiles[-1][0] + tiles[-1][1]
            nt = len(tiles)
            rows0 = tiles[0][1]
            if rows0 == 128:
                dmaL(obn_s[:, 0:nt, :], obn_src.rearrange("(t p) f -> p t f", p=128), reads=["OBNs"], writes=["obn_s"], key="k_obn",
                     eng="pool")
                dmaL(hres[:, 0:nt, :], x_ap.rearrange("(t p) f -> p t f", p=128), writes=["hres"], key="k_hres", eng="pool")
                dmaL(p_s[:, 0:nt, :], p_ap.rearrange("(t p) f -> p t f", p=128), writes=["p_s"], key="k_ps", eng="pool")
            else:
                dmaL(obn_s[0:rows0, 0, :], obn_src, reads=["OBNs"], writes=["obn_s"], key="k_obn", eng="pool")
                dmaL(hres[0:rows0, 0, :], x_ap, writes=["hres"], key="k_hres", eng="pool")
                dmaL(p_s[0:rows0, 0, :], p_ap, writes=["p_s"], key="k_ps", eng="pool")
            for ti, (tok0, rows) in enumerate(tiles):
                oa, oan_names = oan_fn(ti)
                transposes(oa, rows, 4, aT[:, 0:4, tok0:tok0 + rows], oan_names, ["aT"])
                transposes(obn_s[0:rows, ti, :], rows, 4, aT[:, 4:8, tok0:tok0 + rows], ["obn_s"], ["aT"])
            wo = [wload(wout_b.ap()[j * 512:(j + 1) * 512, :].rearrange("(c p) n -> p c n", p=128),
                        lambda v: v.rearrange("p (c n) -> p c n", n=1024)) for j in range(2)]
            for ti, (tok0, rows) in enumerate(tiles):
                for hf in range(2):
                    ps_, nm = proj_tm(aT, tok0, rows, lambda c, hf=hf: wo[c // 4][0][:, c % 4, hf * 512:(hf + 1) * 512],
                                      ["aT", wo[0][1], wo[1][1]])
                    P.op("dve", (lambda e, ti=ti, rows=rows, hf=hf, ps_=ps_: e.tensor_tensor(
                        out=hres[0:rows, ti, hf * 512:(hf + 1) * 512], in0=ps_, in1=hres[0:rows, ti, hf * 512:(hf + 1) * 512], op=ALU.add)),
                        reads=[nm, "hres"], writes=["hres"], self_sync=False)
            for ti, (tok0, rows) in enumerate(tiles):
                rc, rn = rms_rstd(hres[0:rows, ti, :], rows, 1024, eps6, 3 * ti, ["hres"], "c")
                P.op("dve", (lambda e, ti=ti, rows=rows, rc=rc: e.scalar_tensor_tensor(out=cs[0:rows, ti, :], in0=hres[0:rows, ti, :],
                                                                                       scalar=rc, in1=gmlpbc[0:rows, :], op0=ALU.mult,
                                                                                       op1=ALU.mult)),
                     reads=["hres", rn, "gmlpbc"], writes=["cs"])
                transposes(cs[0:rows, ti, :], rows, 8, aT[:, :, tok0:tok0 + rows], ["cs"], ["aT"])
            for j in range(8):
                wu, wn = wload(wup_b.ap()[:, j * 512:(j + 1) * 512].rearrange("(c p) n -> p c n", p=128),
                               lambda v: v.rearrange("p (c n) -> p c n", n=512))
                for hl in range(4):
                    hc = j * 4 + hl
                    ps_, nm = proj_fm(lambda c, hl=hl, wu=wu: wu[:, c, hl * 128:(hl + 1) * 128], aT, NT, ["aT", wn])
                    rb, rbn = (tmpf, "tmpf") if hc % 2 else (gs, "gs")
                    P.op("act", (lambda e, ps_=ps_, rb=rb: e.activation(out=rb[:, 0:NT], in_=ps_, func=AF.Relu)),
                         reads=[nm], writes=[rbn])
                    P.op("dve" if hc % 4 != 3 else "pool",
                         (lambda e, hc=hc, rb=rb: e.tensor_tensor(out=ACT_T[:, hc, 0:NT], in0=rb[:, 0:NT], in1=rb[:, 0:NT],
                                                                  op=ALU.mult)), reads=[rbn], writes=["ACT_T"], self_sync=False)
            for hf in range(2):
                accs = []
                for ti in range(nt):
                    accs.append(ti)
                for j in range(4):
                    wd, wn = wload(wdown_b.ap()[j * 1024:(j + 1) * 1024, hf * 512:(hf + 1) * 512].rearrange("(c p) n -> p c n", p=128),
                                   lambda v: v.rearrange("p (c n) -> p c n", n=512))
                    for hl in range(8):
                        hc = j * 8 + hl
                        for ti, (tok0, rows) in enumerate(tiles):
                            pe_op(128, rows, (lambda e, ti=ti, tok0=tok0, rows=rows, hc=hc, hl=hl, wd=wd: e.matmul(
                                PS[0:rows, ti, :], lhsT=ACT_T[:, hc, tok0:tok0 + rows], rhs=wd[:, hl, :], start=(hc == 0), stop=(hc == 31))),
                                reads=["ACT_T", wn], writes=["PS%d" % ti])
                for ti, (tok0, rows) in enumerate(tiles):
                    P.op("dve", (lambda e, ti=ti, rows=rows, hf=hf: e.tensor_tensor(
                        out=hres[0:rows, ti, hf * 512:(hf + 1) * 512], in0=PS[0:rows, ti, :], in1=hres[0:rows, ti, hf * 512:(hf + 1) * 512],
                        op=ALU.add)), reads=["PS%d" % ti, "hres"], writes=["hres"], self_sync=False)
            for ti, (tok0, rows) in enumerate(tiles):
                P.op("act", (lambda e, ti=ti, rows=rows: e.activation(out=cs[0:rows, ti, :], in_=hres[0:rows, ti, :], func=AF.Copy, scale=1.0)),
                     reads=["hres"], writes=["cs"])
                transposes(cs[0:rows, ti, :], rows, 8, aT[:, :, tok0:tok0 + rows], ["cs"], ["aT"])
                P.op("dve", (lambda e, ti=ti, rows=rows: e.tensor_copy(out=pb[0:rows, ti, :], in_=p_s[0:rows, ti, :])),
                     reads=["p_s"], writes=["pb"])
                transposes(pb[0:rows, ti, :], rows, 2, pT[:, :, tok0:tok0 + rows], ["pb"], ["pT"])
            wg = [wload(wgate_b.ap()[j * 512:(j + 1) * 512, :].rearrange("(c p) n -> p c n", p=128),
                        lambda v: v.rearrange("p (c n) -> p c n", n=1024)) for j in range(2)]
            wp, wpn = wload(wple_b.ap().rearrange("(c p) n -> p c n", p=128),
                            lambda v: v[:, 0:2048].rearrange("p (c n) -> p c n", n=1024))
            for ti, (tok0, rows) in enumerate(tiles):
                for hf in range(2):
                    ps_, nm = proj_tm(aT, tok0, rows, lambda c, hf=hf: wg[c // 4][0][:, c % 4, hf * 512:(hf + 1) * 512],
                                      ["aT", wg[0][1], wg[1][1]])
                    P.op("act", (lambda e, rows=rows, ps_=ps_: e.activation(out=gs[0:rows, :], in_=ps_, func=AF.Sigmoid)),
                         reads=[nm], writes=["gs"])
                    ps2, nm2 = proj_tm(pT, tok0, rows, lambda c, hf=hf: wp[:, c, hf * 512:(hf + 1) * 512], ["pT", wpn], nk=2)
                    P.op("dve", (lambda e, rows=rows, ps2=ps2: e.tensor_tensor(out=tmpf[0:rows, :], in0=gs[0:rows, :], in1=ps2, op=ALU.mult)),
                         reads=["gs", nm2], writes=["tmpf"])
                    P.op("dve", (lambda e, ti=ti, rows=rows, hf=hf: e.tensor_tensor(
                        out=hres[0:rows, ti, hf * 512:(hf + 1) * 512], in0=tmpf[0:rows, :], in1=hres[0:rows, ti, hf * 512:(hf + 1) * 512],
                        op=ALU.add)), reads=["tmpf", "hres"], writes=["hres"])
            for ti, (tok0, rows) in enumerate(tiles):
                rc, rn = rms_rstd(hres[0:rows, ti, :], rows, 1024, eps6, 3 * ti, ["hres"], "f")
                P.op("dve", (lambda e, ti=ti, rows=rows, rc=rc: e.scalar_tensor_tensor(out=hres[0:rows, ti, :], in0=hres[0:rows, ti, :],
                                                                                       scalar=rc, in1=gfinbc[0:rows, :], op0=ALU.mult,
                                                                                       op1=ALU.mult)),
                     reads=["hres", rn, "gfinbc"], writes=["hres"])
            if rows0 == 128:
                dmaO(y_ap.rearrange("(t p) f -> p t f", p=128), hres[:, 0:nt, :], reads=["hres"], key="k_y")
            else:
                dmaO(y_ap, hres[0:rows0, 0, :], reads=["hres"], key="k_y")

        if "C" in phases or "D" in phases:
            dmaL(gmlpbc, bass.AP(g_mlp, 0, [[0, 128], [1, 1024]]), writes=["gmlpbc"], key="k_c12")
            dmaL(gfinbc, bass.AP(g_final, 0, [[0, 128], [1, 1024]]), writes=["gfinbc"], key="k_c13")
        if "C" in phases:
            for I in range(NOWN):
                phaseC_group([(i * 128, 128) for i in range(4)], xv.ap()[(4 * I + 3) * 512:(4 * I + 4) * 512, :],
                             pv.ap()[I * 512:(I + 1) * 512, :],
                             lambda ti, I=I: (OAN[:, I * 4 + ti, :], ["OAN"]),
                             OBNs.ap()[I * 512:(I + 1) * 512, :], y.ap()[I * 512:(I + 1) * 512, :])
        P.barrier(bar[:])

        def phaseD2():
            oanl = R2b[:, 0:512]
            dmaL(oanl[0:64, :], OANss.ap(), reads=["OANss"], writes=["oanl"], key="k_oanl")
            phaseC_group([(0, 64)], xsm.ap(), psm.ap(), lambda ti: (oanl[0:64, :], ["oanl"]), OBNss.ap(), ys.ap())

        if "D" in phases:
            phaseD2()

        P.emit(final_dma_keys=[] if "5" in phases else sorted(outkeys))
    return nc


_NC_CACHE = {}


def _run(x_prompt, x_sample, cache_a_k, cache_a_v, cache_b_k, cache_b_v, p_prompt, p_sample,
         t5_table, g_attn, w_in, lambda_q1, lambda_k1, lambda_q2, lambda_k2, subln_g,
         band_table, out_norm_b, w_out, g_mlp, w_up, w_down, w_ple_gate, w_ple_proj, g_final,
         phases="ABCD", cores=None, trace=False):
    f = lambda a: np.ascontiguousarray(np.asarray(a, dtype=np.float32))
    x_prompt = f(x_prompt); x_sample = f(x_sample); p_prompt = f(p_prompt); p_sample = f(p_sample)
    cache_a_k = f(cache_a_k); cache_a_v = f(cache_a_v); cache_b_k = f(cache_b_k); cache_b_v = f(cache_b_v)
    S = x_prompt.shape[1]
    nblk = S // 512
    nown = nblk // 4
    seqv = nblk * 512
    ident, J, CA, CB = _consts()
    ck = (phases, nblk)
    if ck not in _NC_CACHE:
        _NC_CACHE[ck] = build_program(phases, nblk)
    nc = _NC_CACHE[ck]
    common = {
        "t5": f(t5_table), "bt": f(band_table)[0], "g_attn": f(g_attn), "w_in": f(w_in)[0],
        "lq1": f(lambda_q1), "lk1": f(lambda_k1), "lq2": f(lambda_q2), "lk2": f(lambda_k2),
        "subln": f(subln_g), "onb": f(out_norm_b), "w_out": f(w_out)[0], "g_mlp": f(g_mlp),
        "w_up": f(w_up)[0], "w_down": f(w_down)[0], "w_gate": f(w_ple_gate)[0], "w_ple": f(w_ple_proj)[0],
        "g_final": f(g_final).reshape(1, 1024), "ident": ident, "J": J, "CA": CA, "CB": CB,
    }
    cores = list(range(8)) if cores is None else list(cores)
    in_maps = []
    for c in cores:
        b, j = divmod(c, 4)
        npad = 3 - j
        xvv = np.zeros((seqv, 1024), np.float32)
        nreal = (nblk - npad) * 512
        xvv[npad * 512:] = x_prompt[b, :nreal]
        pvv = np.concatenate([p_prompt[0, b, (4 * I + j) * 512:(4 * I + j + 1) * 512] for I in range(nown)], 0)
        pm = np.zeros((128, 4), np.float32)
        pm[:, :npad] = NEGM
        m = dict(common)
        m.update({
            "xv": xvv, "pv": np.ascontiguousarray(pvv),
            "xsm": x_sample[2 * c:2 * c + 2].reshape(64, 1024), "psm": p_sample[0, 2 * c:2 * c + 2].reshape(64, 256),
            "cak": cache_a_k[0, 2 * c:2 * c + 2].reshape(2048, 512), "cav": cache_a_v[0, 2 * c:2 * c + 2].reshape(2048, 512),
            "cbk": cache_b_k[0, 2 * c:2 * c + 2].reshape(1024, 512), "cbv": cache_b_v[0, 2 * c:2 * c + 2].reshape(1024, 512),
            "padmask": pm,
        })
        in_maps.append({k: np.ascontiguousarray(v) for k, v in m.items()})
    if trace:
        res = run_bass_kernel_spmd(nc, in_maps, core_ids=list(range(len(cores))), trace=True)
        print("EXEC_TIME_NS", res.exec_time_ns)
    else:
        res = run_bass_kernel_spmd(nc, in_maps, core_ids=list(range(len(cores))))
    R = res.results
    y_prompt = np.zeros((2, S, 1024), np.float32)
    nakp = np.zeros((1, 2, S, 512), np.float32)
    navp = np.zeros((1, 2, S, 512), np.float32)
    nbkp = np.zeros((1, 2, 512, 512), np.float32)
    nbvp = np.zeros((1, 2, 512, 512), np.float32)
    y_sample = np.zeros((16, 32, 1024), np.float32)
    saks = np.zeros((1, 16, 32, 512), np.float32); savs = np.zeros((1, 16, 32, 512), np.float32)
    sbks = np.zeros((1, 16, 32, 512), np.float32); sbvs = np.zeros((1, 16, 32, 512), np.float32)
    for ci, c in enumerate(cores):
        b, j = divmod(c, 4)
        r = R[ci]
        for I in range(nown):
            g0 = (4 * I + j) * 512
            y_prompt[b, g0:g0 + 512] = r["y"][I * 512:(I + 1) * 512]
            nakp[0, b, g0:g0 + 512] = r["nak"][I * 512:(I + 1) * 512]
            navp[0, b, g0:g0 + 512] = r["nav"][I * 512:(I + 1) * 512]
        if j == 3:
            nbkp[0, b] = r["nbk"]
            nbvp[0, b] = r["nbv"]
        y_sample[2 * c:2 * c + 2] = r["ys"].reshape(2, 32, 1024)
        saks[0, 2 * c:2 * c + 2] = r["sak"].reshape(2, 32, 512)
        savs[0, 2 * c:2 * c + 2] = r["sav"].reshape(2, 32, 512)
        sbks[0, 2 * c:2 * c + 2] = r["sbk"].reshape(2, 32, 512)
        sbvs[0, 2 * c:2 * c + 2] = r["sbv"].reshape(2, 32, 512)
    return (y_prompt, y_sample,
            nakp.reshape(1, 2, S, 4, 2, 64), navp.reshape(1, 2, S, 4, 128),
            nbkp.reshape(1, 2, 512, 8, 64), nbvp.reshape(1, 2, 512, 8, 64),
            saks.reshape(1, 16, 32, 4, 2, 64), savs.reshape(1, 16, 32, 4, 128),
            sbks.reshape(1, 16, 32, 8, 64), sbvs.reshape(1, 16, 32, 8, 64))


def kernel(x_prompt, x_sample, cache_a_k, cache_a_v, cache_b_k, cache_b_v, p_prompt, p_sample,
           t5_table, g_attn, w_in, lambda_q1, lambda_k1, lambda_q2, lambda_k2, subln_g,
           band_table, out_norm_b, w_out, g_mlp, w_up, w_down, w_ple_gate, w_ple_proj, g_final):
    return _run(x_prompt, x_sample, cache_a_k, cache_a_v, cache_b_k, cache_b_v, p_prompt, p_sample,
                t5_table, g_attn, w_in, lambda_q1, lambda_k1, lambda_q2, lambda_k2, subln_g,
                band_table, out_norm_b, w_out, g_mlp, w_up, w_down, w_ple_gate, w_ple_proj, g_final)
```

```python
import math
import contextlib
import numpy as np
import concourse.bass as bass
import concourse.mybir as mybir
from concourse.bass_utils import run_bass_kernel_spmd

ENGS = ("pe", "act", "dve", "pool", "sp")


class Op:
    __slots__ = ("idx", "eng", "fn", "dma_key", "dma_cnt", "waits", "signal", "sigcnt", "eidx")

    def __init__(self, idx, eng, fn, dma_key):
        self.idx = idx
        self.eng = eng
        self.fn = fn
        self.dma_key = dma_key
        self.dma_cnt = 0
        self.waits = []
        self.signal = False
        self.sigcnt = 0
        self.eidx = 0


class Prog:
    def __init__(self, nc):
        self.nc = nc
        self.ops = []
        self.by_eng = {e: [] for e in ENGS}
        self.last_w = {}
        self.readers = {}
        self.seen = {e: {f: -1 for f in ENGS} for e in ENGS}
        self.seen_dma = {e: {} for e in ENGS}
        self.dma_counts = {}
        self.final_dma = []
        self.force = {}

    def op(self, eng, fn, reads=(), writes=(), dma_key=None, self_sync=True, extra=()):
        o = Op(len(self.ops), eng, fn, dma_key)
        o.eidx = len(self.by_eng[eng])
        deps = []
        for r in reads:
            w = self.last_w.get(r)
            if w is not None:
                deps.append(w)
        for w_ in writes:
            w = self.last_w.get(w_)
            if w is not None:
                deps.append(w)
            deps.extend(self.readers.get(w_, ()))
        for r in reads:
            self.readers.setdefault(r, []).append(o)
        for w_ in writes:
            self.last_w[w_] = o
            self.readers[w_] = [x for x in self.readers.get(w_, ()) if x is o]
        f = self.force.pop(eng, None)
        if f is not None:
            deps.append(f)
        deps.extend(extra)
        if dma_key is not None:
            self.dma_counts[dma_key] = self.dma_counts.get(dma_key, 0) + 16
            o.dma_cnt = self.dma_counts[dma_key]
        for d in deps:
            if d is o:
                continue
            if d.dma_key is not None:
                cur = self.seen_dma[eng].get(d.dma_key, 0)
                if cur >= d.dma_cnt:
                    continue
                self.seen_dma[eng][d.dma_key] = d.dma_cnt
                o.waits.append(("dma", d.dma_key, d.dma_cnt))
            else:
                if d.eng == eng and (eng == "pe" or not self_sync):
                    continue
                if self.seen[eng][d.eng] >= d.eidx:
                    continue
                self.seen[eng][d.eng] = d.eidx
                d.signal = True
                o.waits.append(("eng", d.eng, d))
        self.ops.append(o)
        self.by_eng[eng].append(o)
        return o

    def barrier(self, tile, skip=()):
        extra = [self.by_eng[e][-1] for e in ENGS if self.by_eng[e]]
        last_dma = {}
        for o in self.ops:
            if o.dma_key is not None and o.dma_key not in skip:
                last_dma[o.dma_key] = o
        extra.extend(last_dma.values())
        b = self.op("dve", lambda e: e.memset(tile, 0.0), extra=extra)
        for e in ENGS:
            if e != "dve":
                self.force[e] = b
        return b

    def emit(self, final_dma_keys=()):
        nc = self.nc
        import contextlib
        for o in self.ops:
            best = {}
            for w in o.waits:
                if w[0] == "dma":
                    k = ("dma", w[1])
                    if k not in best or best[k][2] < w[2]:
                        best[k] = w
                else:
                    k = ("eng", w[1])
                    if k not in best or best[k][2].eidx < w[2].eidx:
                        best[k] = w
            o.waits = list(best.values())
        for e in ENGS:
            c = 0
            for o in self.by_eng[e]:
                if o.signal:
                    c += 1
                    o.sigcnt = c
        with contextlib.ExitStack() as st:
            esem = {e: st.enter_context(nc.semaphore("s_" + e)) for e in ENGS}
            dsem = {k: st.enter_context(nc.semaphore("d_%d" % i))
                    for i, k in enumerate(sorted(self.dma_counts))}
            block = st.enter_context(nc.Block())

            def run(e, eng):
                for o in self.by_eng[e]:
                    for w in o.waits:
                        if w[0] == "dma":
                            eng.wait_ge(dsem[w[1]], w[2])
                        else:
                            eng.wait_ge(esem[w[1]], w[2].sigcnt)
                    inst = o.fn(eng)
                    if o.dma_key is not None:
                        inst.then_inc(dsem[o.dma_key], 16)
                    elif o.signal:
                        inst.then_inc(esem[e], 1)
                if e == "sp":
                    for k in final_dma_keys:
                        eng.wait_ge(dsem[k], self.dma_counts[k])

            @block.tensor
            def _(eng):
                run("pe", eng)

            @block.scalar
            def _(eng):
                run("act", eng)

            @block.vector
            def _(eng):
                run("dve", eng)

            @block.gpsimd
            def _(eng):
                run("pool", eng)

            @block.sync
            def _(eng):
                run("sp", eng)

F32 = mybir.dt.float32
BF16 = mybir.dt.bfloat16
AF = mybir.ActivationFunctionType
ALU = mybir.AluOpType
NEGM = -30000.0
NBLK = 32
NOWN = 8
SEQV = NBLK * 512


def _t5_bucket_np(rel):
    half = 16
    n = -rel
    ret = np.where(n < 0, half, 0)
    n = np.abs(n)
    max_exact = 8
    nf = np.maximum(n, 1).astype(np.float32)
    large = max_exact + (np.log(nf / np.float32(max_exact)) / np.float32(math.log(128 / max_exact))
                         * np.float32(half - max_exact)).astype(np.int32)
    large = np.minimum(large, half - 1)
    return ret + np.where(n < max_exact, n, large)


def _consts():
    ident = np.eye(128, dtype=np.float32)
    J = ident[::-1].copy()
    i = np.arange(384)
    delta = i - 255
    CA = np.zeros((32, 384), np.float32)
    bk = _t5_bucket_np(delta.astype(np.int32))
    CA[bk, i] += 1.0
    CA[15, :] -= 1.0
    CA[:, 383] = 0.0
    CB = np.zeros((384, 384), np.float32)
    idx = np.clip(delta, -128, 128) + 128
    CB[idx, i] += 1.0
    CB[0, :] -= 1.0
    CB[:, 383] = 0.0
    return ident, J, CA, CB


def build_program(phases="ABCD", nblk=32):
    global NBLK, NOWN, SEQV
    NBLK = nblk
    NOWN = nblk // 4
    SEQV = nblk * 512
    NTOK = NOWN * 512
    nc = bass.Bass("TRN2", target_bir_lowering=False)
    T = {}

    def din(name, shape, dt=F32):
        T[name] = nc.dram_tensor(name, shape, dt, kind="ExternalInput")
        return T[name]

    def dout(name, shape, dt=F32):
        T[name] = nc.dram_tensor(name, shape, dt, kind="ExternalOutput")
        return T[name]

    def dscr(name, shape, dt=BF16):
        T[name] = nc.dram_tensor(name, shape, dt, kind="Internal")
        return T[name]

    xv = din("xv", [SEQV, 1024]); pv = din("pv", [NTOK, 256])
    xsm = din("xsm", [64, 1024]); psm = din("psm", [64, 256])
    cak = din("cak", [2048, 512]); cav = din("cav", [2048, 512])
    cbk = din("cbk", [1024, 512]); cbv = din("cbv", [1024, 512])
    padmask = din("padmask", [128, 4])
    t5 = din("t5", [32, 4]); bt = din("bt", [257, 8])
    g_attn = din("g_attn", [1, 1024]); w_in = din("w_in", [1024, 3072])
    lq1 = din("lq1", [1, 64]); lk1 = din("lk1", [1, 64]); lq2 = din("lq2", [1, 64]); lk2 = din("lk2", [1, 64])
    subln = din("subln", [1, 128]); onb = din("onb", [1, 512])
    w_out = din("w_out", [1024, 1024]); g_mlp = din("g_mlp", [1, 1024])
    w_up = din("w_up", [1024, 4096]); w_down = din("w_down", [4096, 1024])
    w_gate = din("w_gate", [1024, 1024]); w_ple = din("w_ple", [256, 1024]); g_final = din("g_final", [1, 1024])
    identd = din("ident", [128, 128]); Jd = din("J", [128, 128]); CAd = din("CA", [32, 384]); CBd = din("CB", [384, 384])

    y = dout("y", [NTOK, 1024]); ys = dout("ys", [64, 1024])
    nak = dout("nak", [NTOK, 512]); nav = dout("nav", [NTOK, 512])
    nbk = dout("nbk", [512, 512]); nbv = dout("nbv", [512, 512])
    sak = dout("sak", [64, 512]); sav = dout("sav", [64, 512]); sbk = dout("sbk", [64, 512]); sbv = dout("sbv", [64, 512])

    KTs = dscr("KTs", [4, 128, SEQV]); VAs = dscr("VAs", [SEQV, 512]); QTs = dscr("QTs", [4, 128, NTOK])
    OBNs = dscr("OBNs", [NTOK, 512]); OANss = dscr("OANss", [64, 512]); OBNss = dscr("OBNss", [64, 512])
    vecA = dscr("vecA", [4, 384], F32); vecB = dscr("vecB", [8, 384], F32)
    wout_b = dscr("wout_b", [1024, 1024]); wup_b = dscr("wup_b", [1024, 4096]); wdown_b = dscr("wdown_b", [4096, 1024])
    wgate_b = dscr("wgate_b", [1024, 1024]); wple_b = dscr("wple_b", [256, 1024])

    P = Prog(nc)
    outkeys = set()
    with contextlib.ExitStack() as st:
        def sb(name, shape, dt):
            return st.enter_context(nc.sbuf_tensor(name, shape, dt))

        def psm_(name, shape, dt):
            return st.enter_context(nc.psum_tensor(name, shape, dt))

        R1 = sb("R1", [128, 33024], BF16)
        R2a = sb("R2a", [128, 8192], F32)
        R2b = sb("R2b", [128, 28800], BF16)
        idb = sb("idb", [128, 128], BF16)
        Js = sb("Js", [128, 128], F32)
        Hs = sb("Hs", [128, 128], F32)
        TA = sb("TA", [128, 4, 2, 128], F32)
        TB = sb("TB", [128, 8, 2, 128], F32)
        Tm4 = sb("Tm4", [128, 128], F32)
        chA = sb("chA", [128, 4], F32); chB = sb("chB", [128, 8], F32)
        cmA = sb("cmA", [128, 4, 3], F32); cmB = sb("cmB", [128, 8], F32)
        pmk = sb("pmk", [128, 4], F32)
        lam4 = sb("lam4", [128, 4, 64], F32)
        lcol = sb("lcol", [128, 8], F32)
        sublnbc = sb("sublnbc", [128, 128], F32)
        onbbc = sb("onbbc", [128, 512], F32)
        junk = sb("junk", [128, 1024], BF16)
        Eb = sb("Eb", [128, 2, 2, 512], BF16)
        cols = sb("cols", [128, 64], F32)
        eps6 = sb("eps6", [128, 1], F32); eps5 = sb("eps5", [128, 1], F32)
        bar = sb("bar", [128, 1], F32)
        osm = sb("osm", [128, 2, 2, 128], F32)
        t5s = sb("t5s", [32, 4], F32); bts = sb("bts", [128, 3, 8], F32)
        CAs = sb("CAs", [32, 384], F32); CBs = sb("CBs", [128, 3, 384], F32)
        vst = sb("vst", [8, 384], F32)

        TP = psm_("TP", [128, 2, 1024], BF16)
        PS = psm_("PS", [128, 6, 512], F32)
        TPf = TP.bitcast(F32)
        obanks = [(PS[:, 4, :], "PS4"), (PS[:, 5, :], "PS5"), (TPf[:, 0, :], "TP0")]

        cnt = {"ev": 0, "tp": 0, "ps": 0, "sl": 0, "ost": 0}

        KM = {"k_idb": "q1", "k_w": "q10", "k_v0": "q14", "k_v1": "q15", "k_h": "q16",
              "k_xb": "q0", "k_ktst": "q1", "k_vast": "q2", "k_qast": "q3", "k_ost0": "q4", "k_ost1": "q5", "k_obst": "q6",
              "k_win": "q7", "k_c11": "q8",
              "k_kth0": "q0", "k_kth1": "q1", "k_kth2": "q2", "k_kth3": "q3", "k_vh0": "q0", "k_vh1": "q1", "k_vh2": "q2",
              "k_vh3": "q3", "k_qth": "q4",
              "k_wr0": "q0", "k_wr1": "q1", "k_wr2": "q2", "k_wr3": "q3", "k_hres": "q4", "k_ps": "q5", "k_obn": "q6",
              "k_y": "q7", "k_c12": "q8", "k_c13": "q9",
              "k_cst": "q1", "k_vc": "q2", "k_vbc": "q3", "k_oans": "q6", "k_obns": "q10", "k_oanl": "q9"}
        for i_ in range(11):
            KM["k_c%d" % i_] = "q0"
        KM.update({"k_xb0": "q0", "k_xb1": "q11", "k_xb2": "q12", "k_xb3": "q13"})

        def dmaL(out, in_, reads=(), writes=(), key=None, eng="sp"):
            key = KM[key]
            return P.op(eng, lambda e: e.dma_start(out=out, in_=in_), reads=reads, writes=writes, dma_key=key)

        def dmaO(out, in_, reads, key):
            key = KM[key]
            outkeys.add(key)
            return P.op("sp" if "4" in phases else "pool", lambda e: e.dma_start(out=out, in_=in_), reads=reads, dma_key=key)

        def evac(out, in_, reads, writes, scale=None, eng=None):
            if eng is None:
                cnt["ev"] += 1
                eng = "act" if cnt["ev"] % 2 else "dve"
            if eng == "act":
                s = 1.0 if scale is None else scale
                return P.op("act", lambda e: e.activation(out=out, in_=in_, func=AF.Copy, scale=s), reads=reads, writes=writes)
            if scale is None:
                return P.op("dve", lambda e: e.tensor_copy(out=out, in_=in_), reads=reads, writes=writes)
            return P.op("dve", lambda e: e.tensor_scalar(out=out, in0=in_, scalar1=scale, scalar2=None, op0=ALU.mult),
                        reads=reads, writes=writes)

        pe_state = {"mode": None}

        def pe_op(K, M, fn, reads=(), writes=()):
            r = lambda x: 32 if x <= 32 else (64 if x <= 64 else 128)
            mode = (r(K), r(M))
            if pe_state["mode"] is not None and pe_state["mode"] != mode:
                P.op("pe", lambda e: e.drain())
            pe_state["mode"] = mode
            return P.op("pe", fn, reads=reads, writes=writes)

        def nextps():
            cnt["ps"] = (cnt["ps"] + 1) % 4
            return cnt["ps"]

        def rms_rstd(src, rows, F, eps_t, ci, reads, tag):
            P.op("act", lambda e: e.activation(out=junk[0:rows, 0:F], in_=src, func=AF.Square,
                                               accum_out=cols[0:rows, ci:ci + 1]),
                 reads=reads, writes=["junk", "c%d" % ci])
            P.op("act", lambda e: e.activation(out=cols[0:rows, ci + 1:ci + 2], in_=cols[0:rows, ci:ci + 1], func=AF.Sqrt,
                                               bias=eps_t[0:rows, 0:1], scale=1.0 / F),
                 reads=["c%d" % ci, "eps"], writes=["c%d" % (ci + 1)])
            P.op("dve", lambda e: e.reciprocal(out=cols[0:rows, ci + 2:ci + 3], in_=cols[0:rows, ci + 1:ci + 2]),
                 reads=["c%d" % (ci + 1)], writes=["c%d" % (ci + 2)])
            return cols[0:rows, ci + 2:ci + 3], "c%d" % (ci + 2)

        def transposes(src, rows, nch, dst, reads, writes):
            b = cnt["tp"] % 2
            cnt["tp"] += 1
            for c in range(nch):
                pe_op(rows, 128, (lambda e, c=c: e.transpose(TP[:, b, c * 128:c * 128 + rows], src[:, c * 128:(c + 1) * 128],
                                                       idb[0:rows, 0:rows])),
                     reads=list(reads) + ["idb"], writes=["TP%d" % b])
            tv = TP[:, b, :].rearrange("p (a r) -> p a r", r=128)[:, 0:nch, 0:rows]
            evac(dst, tv, reads=["TP%d" % b], writes=writes)

        def proj_fm(wfn, rhsT, n, reads):
            b = nextps()
            for c in range(8):
                pe_op(128, 128, (lambda e, c=c: e.matmul(PS[:, b, 0:n], lhsT=wfn(c), rhs=rhsT[:, c, 0:n], start=(c == 0), stop=(c == 7))),
                     reads=reads, writes=["PS%d" % b])
            return PS[:, b, 0:n], "PS%d" % b

        def proj_tm(xT, tok0, rows, wfn, reads, nk=8):
            b = nextps()
            for c in range(nk):
                pe_op(128, rows, (lambda e, c=c: e.matmul(PS[0:rows, b, :], lhsT=xT[:, c, tok0:tok0 + rows], rhs=wfn(c),
                                                    start=(c == 0), stop=(c == nk - 1))),
                     reads=reads, writes=["PS%d" % b])
            return PS[0:rows, b, :], "PS%d" % b

        osb_state = {"i": 0}

        def pair_attn(keytiles, QT, qtiles, ed, finalize, qreads, osb=None):
            nqt = len(qtiles)
            per_bank = 512 // (ed + 1)
            started = set()

            def oloc(u, qi):
                g = u * nqt + qi
                bank, slot = divmod(g, per_bank)
                ap, nm = obanks[bank]
                rows = qtiles[qi][1]
                return ap[0:rows, slot * (ed + 1):(slot + 1) * (ed + 1)], nm, bank

            def qk(kt):
                sl = cnt["sl"] % 2
                cnt["sl"] += 1
                kt["sl"] = sl
                nk = kt["nk"]
                c0 = qtiles[kt["qlo"]][0]
                c1 = qtiles[kt["qhi"]][0] + qtiles[kt["qhi"]][1]
                kt["c"] = (c0, c1)
                for u in range(2):
                    pe_op(128, nk, (lambda e, u=u: e.matmul(PS[0:nk, sl * 2 + u, c0:c1], lhsT=kt["KT"],
                                                            rhs=QT[u][:, c0:c1], start=True, stop=True)),
                         reads=list(kt["reads"]) + list(qreads), writes=["PS%d" % (sl * 2 + u)])
                for (qi, Ts) in ([] if "k" in phases else kt["adds"]):
                    q0, qr = qtiles[qi]
                    for u in range(2):
                        P.op("dve", (lambda e, u=u, q0=q0, qr=qr, Ts=Ts: e.tensor_tensor(
                            out=PS[0:nk, sl * 2 + u, q0:q0 + qr], in0=PS[0:nk, sl * 2 + u, q0:q0 + qr], in1=Ts[u], op=ALU.add)),
                            reads=["PS%d" % (sl * 2 + u), "Tt"], writes=["PS%d" % (sl * 2 + u)], self_sync=False)
                if kt["bias"][0] is kt["bias"][1]:
                    P.op("act", lambda e: e.activation(out=Eb[0:nk, sl, :, c0:c1], in_=PS[0:nk, sl * 2:sl * 2 + 2, c0:c1],
                                                       func=AF.Exp, bias=kt["bias"][0], scale=1.0),
                         reads=["PS%d" % (sl * 2), "PS%d" % (sl * 2 + 1), "bias"], writes=["E%d" % sl])
                else:
                    for u in range(2):
                        P.op("act", (lambda e, u=u: e.activation(out=Eb[0:nk, sl, u, c0:c1], in_=PS[0:nk, sl * 2 + u, c0:c1],
                                                                 func=AF.Exp, bias=kt["bias"][u], scale=1.0)),
                             reads=["PS%d" % (sl * 2 + u), "bias"], writes=["E%d" % sl])

            def pvm(kt):
                if "l" in phases:
                    return
                sl = kt["sl"]
                nk = kt["nk"]
                for u in range(2):
                    for qi in range(kt["qlo"], kt["qhi"] + 1):
                        oap, nm, bank = oloc(u, qi)
                        q0, qr = qtiles[qi]
                        first = bank not in started
                        started.add(bank)
                        pe_op(nk, qr, (lambda e, u=u, oap=oap, q0=q0, qr=qr, first=first: e.matmul(
                            oap, lhsT=Eb[0:nk, sl, u, q0:q0 + qr], rhs=kt["V"][u], start=first, stop=False,
                            skip_group_check=True)),
                            reads=["E%d" % sl] + list(kt["reads"]), writes=[nm])

            prev = None
            for kt in keytiles:
                qk(kt)
                if prev is not None:
                    pvm(prev)
                prev = kt
            pvm(prev)
            O = [[oloc(u, qi)[0] for qi in range(nqt)] for u in range(2)]
            names = sorted({oloc(u, qi)[1] for u in range(2) for qi in range(nqt)})
            if osb is not None:
                sset = osb[osb_state["i"] % len(osb)]
                osb_state["i"] += 1
                used = sorted({oloc(u, qi)[2] for u in range(2) for qi in range(nqt)})
                for bk in used:
                    bap, bnm = obanks[bk]
                    P.op("dve", (lambda e, bk=bk, bap=bap: e.tensor_copy(out=sset[0][:, bk, :], in_=bap)), reads=[bnm],
                         writes=[sset[1] + str(bk)])

                def oloc2(u, qi):
                    g = u * nqt + qi
                    bank, slot = divmod(g, per_bank)
                    rows = qtiles[qi][1]
                    return sset[0][0:rows, bank, slot * (ed + 1):(slot + 1) * (ed + 1)], sset[1] + str(bank)
                O = [[oloc2(u, qi)[0] for qi in range(nqt)] for u in range(2)]
                names = sorted({oloc2(u, qi)[1] for u in range(2) for qi in range(nqt)})
            if "m" not in phases:
                finalize(O, names)

        def load_win():
            Wv = R1[:, 0:24576].rearrange("p (c n) -> p c n", n=3072)
            for c in range(8):
                for hh in range(2):
                    dmaL(Wv[:, c, hh * 1536:(hh + 1) * 1536], w_in.ap()[c * 128:(c + 1) * 128, hh * 1536:(hh + 1) * 1536],
                         writes=["W"], key="k_win", eng="pool")
            return Wv

        Wv = load_win()
        P.op("dve", lambda e: e.memset(eps6[:], 1e-6), writes=["eps"])
        P.op("dve", lambda e: e.memset(eps5[:], 1e-5), writes=["eps"])
        dmaL(idb[:], identd.ap(), writes=["idb"], key="k_idb", eng="pool")
        dmaL(Js[:], Jd.ap(), writes=["Js"], key="k_c0")
        dmaL(t5s[:], t5.ap(), writes=["t5s"], key="k_c1")
        P.op("dve", lambda e: e.memset(bts[:], 0.0), writes=["bts"])
        dmaL(bts[:, 0:2, :], bt.ap()[0:256, :].rearrange("(a p) h -> p a h", p=128), writes=["bts"], key="k_c2")
        dmaL(bts[0:1, 2, :], bt.ap()[256:257, :], writes=["bts"], key="k_c2")
        dmaL(CAs[:], CAd.ap(), writes=["CAs"], key="k_c3")
        dmaL(CBs[:], CBd.ap().rearrange("(a p) n -> p a n", p=128), writes=["CBs"], key="k_c4")
        dmaL(chA[:], bass.AP(t5, 15 * 4, [[0, 128], [1, 4]]), writes=["chA"], key="k_c5")
        dmaL(chB[:], bass.AP(bt, 0, [[0, 128], [1, 8]]), writes=["chB"], key="k_c6")
        dmaL(pmk[:], padmask.ap(), writes=["pmk"], key="k_c7")
        for i, lt in enumerate([lq1, lk1, lq2, lk2]):
            dmaL(lam4[:, i, :], bass.AP(lt, 0, [[0, 128], [1, 64]]), writes=["lam4"], key="k_c8")
        dmaL(sublnbc[:], bass.AP(subln, 0, [[0, 128], [1, 128]]), writes=["sublnbc"], key="k_c9")
        dmaL(onbbc[:], bass.AP(onb, 0, [[0, 128], [1, 512]]), writes=["onbbc"], key="k_c10")
        P.barrier(bar[:], skip=("q7",))
        P.op("dve", lambda e: e.tensor_scalar(out=sublnbc[:], in0=sublnbc[:], scalar1=0.8, scalar2=None, op0=ALU.mult),
             reads=["sublnbc"], writes=["sublnbc"])
        P.op("dve", lambda e: e.tensor_tensor(out=lam4[:, 0, :], in0=lam4[:, 0, :], in1=lam4[:, 1, :], op=ALU.mult),
             reads=["lam4"], writes=["lam4"])
        P.op("dve", lambda e: e.tensor_tensor(out=lam4[:, 2, :], in0=lam4[:, 2, :], in1=lam4[:, 3, :], op=ALU.mult),
             reads=["lam4"], writes=["lam4"])
        P.op("dve", lambda e: e.tensor_reduce(out=lcol[:, 0:1], in_=lam4[:, 0, :], axis=mybir.AxisListType.X, op=ALU.add),
             reads=["lam4"], writes=["lcol"])
        P.op("dve", lambda e: e.tensor_reduce(out=lcol[:, 1:2], in_=lam4[:, 2, :], axis=mybir.AxisListType.X, op=ALU.add),
             reads=["lam4"], writes=["lcol"])
        P.op("act", lambda e: e.activation(out=lcol[:, 2:4], in_=lcol[:, 0:2], func=AF.Exp), reads=["lcol"], writes=["lcol"])
        P.op("dve", lambda e: e.scalar_tensor_tensor(out=lcol[:, 4:5], in0=lcol[:, 3:4], scalar=-0.2, in1=lcol[:, 2:3],
                                                     op0=ALU.add, op1=ALU.subtract), reads=["lcol"], writes=["neglam"])
        neglam = lcol[:, 4:5]
        for h in range(4):
            P.op("dve", (lambda e, h=h: e.tensor_scalar(out=cmA[:, h, :], in0=pmk[:, 0:3], scalar1=chA[:, h:h + 1], scalar2=None,
                                                        op0=ALU.add)), reads=["pmk", "chA"], writes=["bias"])
        P.op("dve", lambda e: e.tensor_scalar(out=cmB[:], in0=chB[:], scalar1=pmk[:, 2:3], scalar2=None, op0=ALU.add),
             reads=["pmk", "chB"], writes=["bias"])
        pe_op(32, 4, lambda e: e.matmul(PS[0:4, 0, 0:384], lhsT=t5s[:], rhs=CAs[:], start=True, stop=True),
             reads=["t5s", "CAs"], writes=["PS0"])
        P.op("dve", lambda e: e.tensor_copy(out=vst[0:4, :], in_=PS[0:4, 0, 0:384]), reads=["PS0"], writes=["vst"])
        dmaL(vecA.ap(), vst[0:4, :], reads=["vst"], writes=["vecA"], key="k_v0")
        for a in range(3):
            pe_op(128, 8, (lambda e, a=a: e.matmul(PS[0:8, 1, 0:384], lhsT=bts[:, a, :], rhs=CBs[:, a, :], start=(a == 0), stop=(a == 2))),
                 reads=["bts", "CBs"], writes=["PS1"])
        P.op("dve", lambda e: e.tensor_copy(out=vst[0:8, :], in_=PS[0:8, 1, 0:384]), reads=["PS1", "vecA"], writes=["vst"])
        dmaL(vecB.ap(), vst[0:8, :], reads=["vst"], writes=["vecB"], key="k_v1")
        Hall = R2a[:, 4096:7168].rearrange("p (i n) -> p i n", n=128)
        hi = 0
        hlist = []
        for (vec, Tt, nh) in ((vecA, TA, 4), (vecB, TB, 8)):
            for h in range(nh):
                for kind, base in ((0, 128), (1, 0)):
                    hank = bass.AP(vec, h * 384 + base, [[1, 128], [1, 128]])
                    dmaL(Hall[:, hi, :], hank, reads=["vecA", "vecB"], writes=["Hall"], key="k_h")
                    hlist.append((hi, Tt, h, kind))
                    hi += 1
        for (hi, Tt, h, kind) in hlist:
            bk = 2 + hi % 2
            pe_op(128, 128, (lambda e, hi=hi, bk=bk: e.matmul(PS[:, bk, 0:128], lhsT=Hall[:, hi, :], rhs=Js[:], start=True, stop=True)),
                  reads=["Hall", "Js"], writes=["PS%d" % bk])
            P.op("dve", (lambda e, Tt=Tt, h=h, kind=kind, bk=bk: e.tensor_copy(out=Tt[:, h, kind, :], in_=PS[:, bk, 0:128])),
                 reads=["PS%d" % bk], writes=["Tt"], self_sync=False)
            if kind == 0:
                P.op("dve", (lambda e, Tt=Tt, h=h: e.memset(Tt[64:128, h, 0, 0:64], NEGM)), reads=["Tt"], writes=["Tt"])
        P.op("dve", lambda e: e.memset(Tm4[:], 0.0), writes=["Tt"])
        P.op("dve", lambda e: e.memset(Tm4[0:64, 64:128], NEGM), reads=["Tt"], writes=["Tt"])
        def weight_casts():
            for (src, dst, rows, colsn) in ((w_out, wout_b, 1024, 1024), (w_up, wup_b, 1024, 4096), (w_down, wdown_b, 4096, 1024),
                                           (w_gate, wgate_b, 1024, 1024), (w_ple, wple_b, 256, 1024)):
                sv = src.ap().rearrange("r (a n) -> (r a) n", n=1024)
                dv = dst.ap().rearrange("r (a n) -> (r a) n", n=1024)
                tot = rows * colsn // 1024
                for r0 in range(0, tot, 512):
                    n_ = min(512, tot - r0)
                    P.op("pool", (lambda e, r0=r0, n_=n_, dv=dv, sv=sv: e.dma_start(out=dv[r0:r0 + n_, :], in_=sv[r0:r0 + n_, :],
                                                                                    max_dma_last_dim=2048)),
                         writes=["wscr"], dma_key=KM["k_w"])


        P.barrier(bar[:], skip=("q7",))
        xb = R2a[:, 0:4096].rearrange("p (t f) -> p t f", f=1024)
        ostg = R2a[:, 4096:5120].rearrange("p (s f) -> p s f", f=512)
        obraw = R2a[:, 5120:7168].rearrange("p (t f) -> p t f", f=512)
        gattnbc = R2a[:, 7168:8192]
        xs = R2b[:, 0:4096].rearrange("p (t f) -> p t f", f=1024)
        xsT = R2b[:, 4096:8192].rearrange("p (c n) -> p c n", n=512)
        KTst = R2b[:, 8192:10240].rearrange("p (h n) -> p h n", n=512)
        VAst = R2b[:, 10240:12288].rearrange("p (t n) -> p t n", n=512)
        KBT = R2b[:, 12288:16384].rearrange("p (s c n) -> p s c n", s=2, n=512)
        VBa = R2b[:, 16384:20608].rearrange("p (s t h e) -> p s t h e", s=2, t=4, e=66)
        QBz = R2b[:, 20608:24704].rearrange("p (u c n) -> p u c n", u=2, n=512)
        QAst = R2b[:, 24704:26752].rearrange("p (h n) -> p h n", n=512)
        OBst = R2b[:, 26752:28800].rearrange("p (t n) -> p t n", n=512)
        P.op("pool", lambda e: e.memset(QBz, 0.0), writes=["QBT"])
        dmaL(gattnbc, bass.AP(g_attn, 0, [[0, 128], [1, 1024]]), writes=["gattnbc"], key="k_c11")
        P.op("dve", lambda e: e.memset(VBa[:, :, :, :, 64:66], 1.0), writes=["VBones"])

        def out_store(dst_ap, psum_ap, psname, rows=128):
            s = cnt["ost"] % 2
            cnt["ost"] += 1
            evac(ostg[0:rows, s, :], psum_ap, reads=[psname], writes=["ostg%d" % s])
            dmaO(dst_ap, ostg[0:rows, s, :], reads=["ostg%d" % s], key="k_ost%d" % s)
            return ostg[0:rows, s, :], "ostg%d" % s

        def band_attention(p, I):
            sp_, so_ = (p - 1) % 2, p % 2
            qtl = [(i * 128, 128) for i in range(4)]
            for cb in range(4):
                kts = []
                for r in range(-4, 4):
                    slot, tk = (sp_, r + 4) if r < 0 else (so_, r)
                    qlo, qhi = max(0, r), min(3, r + 4)
                    adds = []
                    for qi in range(qlo, qhi + 1):
                        rel = r - qi
                        if rel == 0:
                            adds.append((qi, [TB[:, 2 * cb + u, 0, :] for u in range(2)]))
                        elif rel == -1:
                            adds.append((qi, [TB[:, 2 * cb + u, 1, :] for u in range(2)]))
                        elif rel == -4:
                            adds.append((qi, [Tm4[:], Tm4[:]]))
                    bsrc = cmB if (p == 3 and r < 0) else chB
                    kts.append(dict(KT=KBT[:, slot, cb, tk * 128:(tk + 1) * 128], nk=128,
                                    V=[VBa[:, slot, tk, 2 * cb + u, 0:65] for u in range(2)],
                                    bias=[bsrc[:, 2 * cb + u:2 * cb + u + 1] for u in range(2)],
                                    qlo=qlo, qhi=qhi, adds=adds, reads=["KBT%d" % slot, "VB%d" % slot, "VBones"]))

                def fin(O, names, cb=cb):
                    for u in range(2):
                        hb = 2 * cb + u
                        for qi in range(4):
                            ci = 8 + (qi * 2 + u)
                            P.op("dve", (lambda e, u=u, qi=qi, ci=ci: e.reciprocal(out=cols[:, ci:ci + 1], in_=O[u][qi][:, 64:65])),
                                 reads=names, writes=["c%d" % ci])
                            P.op("dve", (lambda e, u=u, qi=qi, ci=ci, hb=hb: e.tensor_scalar(
                                out=obraw[:, qi, hb * 64:(hb + 1) * 64], in0=O[u][qi][:, 0:64], scalar1=cols[:, ci:ci + 1],
                                scalar2=None, op0=ALU.mult)), reads=names + ["c%d" % ci], writes=["obraw"])
                pair_attn(kts, [QBz[:, 0, cb, :], QBz[:, 1, cb, :]], qtl, 64, fin, ["QBT"])
            for qi in range(4):
                rc, rn = rms_rstd(obraw[:, qi, :], 128, 512, eps6, 16 + 3 * qi, ["obraw"], "ob")
                P.op("dve", (lambda e, qi=qi, rc=rc: e.scalar_tensor_tensor(out=OBst[:, qi, :], in0=obraw[:, qi, :], scalar=rc,
                                                                              in1=onbbc[:], op0=ALU.mult, op1=ALU.mult)),
                     reads=["obraw", rn, "onbbc"], writes=["OBst"])
            dmaL(OBNs.ap()[I * 512:(I + 1) * 512, :].rearrange("(t p) n -> p t n", p=128), OBst, reads=["OBst"], writes=["OBNs"],
                 key="k_obst", eng="pool")

        def phaseA_block(p):
            own = (p % 4 == 3)
            I = p // 4
            last = (p == NBLK - 1)
            so_ = p % 2
            for t in range(4):
                dmaL(xb[:, t, :], xv.ap()[p * 512 + t * 128:p * 512 + (t + 1) * 128, :], writes=["xb%d" % t], key="k_xb%d" % t)
            for t in range(4):
                rc, rn = rms_rstd(xb[:, t, :], 128, 1024, eps6, 3 * t, ["xb%d" % t], "x")
                P.op("dve", (lambda e, t=t, rc=rc: e.scalar_tensor_tensor(out=xs[:, t, :], in0=xb[:, t, :], scalar=rc, in1=gattnbc,
                                                                            op0=ALU.mult, op1=ALU.mult)),
                     reads=["xb%d" % t, rn, "gattnbc"], writes=["xs%d" % t])
            for t in range(4):
                transposes(xs[:, t, :], 128, 8, xsT[:, :, t * 128:(t + 1) * 128], ["xs%d" % t], ["xsT"])
            for h in range(4):
                ps_, nm = proj_fm(lambda c, h=h: Wv[:, c, 512 + h * 128:512 + (h + 1) * 128], xsT, 512, ["W", "xsT"])
                evac(KTst[:, h, :], ps_, reads=[nm], writes=["KTst"])
            dmaL(KTs.ap().rearrange("h p n -> p h n")[:, :, p * 512:(p + 1) * 512], KTst, reads=["KTst"], writes=["KTs"],
                 key="k_ktst", eng="pool")
            needb = (p % 4 >= 2)
            for cb in (range(4) if needb else ()):
                ps_, nm = proj_fm(lambda c, cb=cb: Wv[:, c, 2048 + cb * 128:2048 + (cb + 1) * 128], xsT, 512, ["W", "xsT"])
                evac(KBT[:, so_, cb, :], ps_, reads=[nm], writes=["KBT%d" % so_])
            if own and "3" not in phases:
                for h in range(4):
                    ps_, nm = proj_fm(lambda c, h=h: Wv[:, c, h * 128:(h + 1) * 128], xsT, 512, ["W", "xsT"])
                    evac(QAst[:, h, :], ps_, reads=[nm], writes=["QAst"], scale=0.125)
                dmaL(QTs.ap().rearrange("h p n -> p h n")[:, :, I * 512:(I + 1) * 512], QAst, reads=["QAst"], writes=["QTs"],
                     key="k_qast", eng="pool")
                for cb in range(4):
                    ps_, nm = proj_fm(lambda c, cb=cb: Wv[:, c, 1536 + cb * 128:1536 + (cb + 1) * 128], xsT, 512, ["W", "xsT"])
                    evac(QBz[:, 0, cb, :], ps_, reads=[nm], writes=["QBT"], scale=0.125)
                    P.op("pool", (lambda e, cb=cb: e.tensor_copy(out=QBz[64:128, 1, cb, :], in_=QBz[64:128, 0, cb, :])),
                         reads=["QBT"], writes=["QBT"])
                    P.op("pool", (lambda e, cb=cb: e.memset(QBz[64:128, 0, cb, :], 0.0)), reads=["QBT"], writes=["QBT"])
            for t in range(4):
                ps_, nm = proj_tm(xsT, t * 128, 128, lambda c: Wv[:, c, 1024:1536], ["W", "xsT"])
                if own:
                    sa, sn = out_store(nav.ap()[I * 512 + t * 128:I * 512 + (t + 1) * 128, :], ps_, nm)
                    P.op("pool", (lambda e, t=t, sa=sa: e.tensor_copy(out=VAst[:, t, :], in_=sa)), reads=[sn], writes=["VAst"])
                else:
                    evac(VAst[:, t, :], ps_, reads=[nm], writes=["VAst"])
                if not needb:
                    continue
                ps_, nm = proj_tm(xsT, t * 128, 128, lambda c: Wv[:, c, 2560:3072], ["W", "xsT"])
                if last:
                    sa, sn = out_store(nbv.ap()[t * 128:(t + 1) * 128, :], ps_, nm)
                    P.op("pool", (lambda e, t=t, sa=sa: e.tensor_copy(out=VBa[:, so_, t, :, 0:64],
                                                                      in_=sa.rearrange("p (h e) -> p h e", e=64))),
                         reads=[sn], writes=["VB%d" % so_])
                else:
                    evac(VBa[:, so_, t, :, 0:64], ps_.rearrange("p (h e) -> p h e", e=64), reads=[nm], writes=["VB%d" % so_])
                if own:
                    ps_, nm = proj_tm(xsT, t * 128, 128, lambda c: Wv[:, c, 512:1024], ["W", "xsT"])
                    out_store(nak.ap()[I * 512 + t * 128:I * 512 + (t + 1) * 128, :], ps_, nm)
                if last:
                    ps_, nm = proj_tm(xsT, t * 128, 128, lambda c: Wv[:, c, 2048:2560], ["W", "xsT"])
                    out_store(nbk.ap()[t * 128:(t + 1) * 128, :], ps_, nm)
            dmaL(VAs.ap()[p * 512:(p + 1) * 512, :].rearrange("(t p) n -> p t n", p=128), VAst, reads=["VAst"], writes=["VAs"],
                 key="k_vast", eng="pool")
            if own and "1" not in phases:
                band_attention(p, I)

        if "A" in phases:
            for p in range(NBLK):
                phaseA_block(p)
        if "a" in phases:
            for p in range(4):
                phaseA_block(p)
        if "e" in phases:
            for p in range(3):
                phaseA_block(p)
        P.barrier(bar[:])

        KTh = R1[:, 0:16384]
        Vaug = R1[:, 16384:33024].rearrange("p (t e) -> p t e", e=130)
        OAN = R2b[:, 0:16384].rearrange("p (t n) -> p t n", n=512)
        QTz = R2b[:, 16384:24576].rearrange("p (u n) -> p u n", u=2)

        def finA_factory(h, dst_fn, rows):
            def fin(O, names):
                nqt = len(O[0])
                for qi in range(nqt):
                    pr = qi % 2
                    cb_ = 28 + 8 * pr
                    P.op("dve", (lambda e, qi=qi, cb_=cb_: e.reciprocal(out=cols[0:rows, cb_:cb_ + 1], in_=O[0][qi][:, 128:129])),
                         reads=names, writes=["fa%d" % pr])
                    P.op("dve", (lambda e, qi=qi, cb_=cb_: e.reciprocal(out=cols[0:rows, cb_ + 1:cb_ + 2], in_=O[1][qi][:, 128:129])),
                         reads=names + ["fa%d" % pr], writes=["fa%d" % pr])
                    P.op("dve", (lambda e, cb_=cb_: e.tensor_scalar(out=cols[0:rows, cb_ + 2:cb_ + 3], in0=cols[0:rows, cb_ + 1:cb_ + 2],
                                                                   scalar1=neglam[0:rows, :], scalar2=None, op0=ALU.mult)),
                         reads=["fa%d" % pr, "neglam"], writes=["fa%d" % pr])
                    P.op("dve", (lambda e, qi=qi, cb_=cb_, pr=pr: e.tensor_scalar(out=osm[0:rows, pr, 0, :], in0=O[1][qi][:, 0:128],
                                                                                  scalar1=cols[0:rows, cb_ + 2:cb_ + 3], scalar2=None,
                                                                                  op0=ALU.mult)),
                         reads=names + ["fa%d" % pr], writes=["osm%d" % pr])
                    P.op("dve", (lambda e, qi=qi, cb_=cb_, pr=pr: e.scalar_tensor_tensor(
                        out=osm[0:rows, pr, 1, :], in0=O[0][qi][:, 0:128], scalar=cols[0:rows, cb_:cb_ + 1], in1=osm[0:rows, pr, 0, :],
                        op0=ALU.mult, op1=ALU.add)), reads=names + ["fa%d" % pr, "osm%d" % pr], writes=["osm%d" % pr])
                    ci = cb_ + 3
                    P.op("dve", (lambda e, pr=pr, ci=ci: e.scalar_tensor_tensor(
                        out=osm[0:rows, pr, 0, :], in0=osm[0:rows, pr, 1, :], scalar=1.0, in1=osm[0:rows, pr, 1, :],
                        op0=ALU.mult, op1=ALU.mult, accum_out=cols[0:rows, ci:ci + 1])),
                        reads=["osm%d" % pr], writes=["osm%d" % pr, "fb%d" % pr])
                    P.op("act", (lambda e, ci=ci: e.activation(out=cols[0:rows, ci + 1:ci + 2], in_=cols[0:rows, ci:ci + 1], func=AF.Ln,
                                                              bias=eps5[0:rows, 0:1], scale=1.0 / 128)),
                         reads=["fb%d" % pr, "eps"], writes=["fb%d" % pr])
                    P.op("act", (lambda e, ci=ci: e.activation(out=cols[0:rows, ci + 2:ci + 3], in_=cols[0:rows, ci + 1:ci + 2],
                                                              func=AF.Exp, scale=-0.5)),
                         reads=["fb%d" % pr], writes=["fb%d" % pr])
                    dst, dnm = dst_fn(qi)
                    P.op("dve", (lambda e, pr=pr, ci=ci, dst=dst: e.scalar_tensor_tensor(
                        out=dst, in0=osm[0:rows, pr, 1, :], scalar=cols[0:rows, ci + 2:ci + 3], in1=sublnbc[0:rows, :],
                        op0=ALU.mult, op1=ALU.mult)), reads=["osm%d" % pr, "fb%d" % pr, "sublnbc"], writes=[dnm])
            return fin

        def phaseD1():
            xb = R2a[:, 0:1024]
            ostg = R2a[:, 4096:5120].rearrange("p (s f) -> p s f", f=512)
            obraw = R2a[:, 1024:1536]
            o = 0

            def carve(n):
                nonlocal o
                a = R2b[:, o:o + n]
                o += n
                return a
            oanl = carve(512)
            xs = carve(1024)
            xsT = carve(8 * 64).rearrange("p (c n) -> p c n", n=64)
            cst = carve(8 * 512).rearrange("p (t n) -> p t n", n=512)
            KTc = carve(4 * 1056).rearrange("p (h n) -> p h n", n=1056)
            Vc = carve(9 * 4 * 130).rearrange("p (t h e) -> p t h e", h=4, e=130)
            KBc = carve(4 * 544).rearrange("p (c n) -> p c n", n=544)
            VBc = carve(5 * 8 * 66).rearrange("p (t h e) -> p t h e", h=8, e=66)
            QAz = carve(2 * 4 * 64).rearrange("p (u h n) -> p u h n", u=2, n=64)
            QBzs = carve(2 * 4 * 64).rearrange("p (u c n) -> p u c n", u=2, n=64)
            P.op("pool", lambda e: e.memset(QAz, 0.0), writes=["QAs"])
            P.op("pool", lambda e: e.memset(QBzs, 0.0), writes=["QBs"])
            oans = carve(512)
            obns = carve(512)
            P.op("dve", lambda e: e.memset(Vc[:, :, :, 128:130], 1.0), writes=["Vc1"])
            P.op("dve", lambda e: e.memset(VBc[:, :, :, 64:66], 1.0), writes=["VBc1"])
            dmaL(xb[0:64, :], xsm.ap(), writes=["xb"], key="k_xb")
            rc, rn = rms_rstd(xb[0:64, :], 64, 1024, eps6, 0, ["xb"], "x")
            P.op("dve", (lambda e, rc=rc: e.scalar_tensor_tensor(out=xs[0:64, :], in0=xb[0:64, :], scalar=rc, in1=gattnbc[0:64, :],
                                                                  op0=ALU.mult, op1=ALU.mult)), reads=["xb", rn, "gattnbc"], writes=["xs"])
            transposes(xs[0:64, :], 64, 8, xsT[:, :, 0:64], ["xs"], ["xsT"])
            for (c0, dst) in ((512, sak), (1024, sav), (2048, sbk), (2560, sbv)):
                ps_, nm = proj_tm(xsT, 0, 64, lambda c, c0=c0: Wv[:, c, c0:c0 + 512], ["W", "xsT"])
                out_store(dst.ap(), ps_, nm, rows=64)
            for h in range(4):
                ps_, nm = proj_fm(lambda c, h=h: Wv[:, c, h * 128:(h + 1) * 128], xsT, 64, ["W", "xsT"])
                evac(QAz[:, 0, h, :], ps_, reads=[nm], writes=["QAs"], scale=0.125)
                P.op("pool", (lambda e, h=h: e.tensor_copy(out=QAz[64:128, 1, h, :], in_=QAz[64:128, 0, h, :])), reads=["QAs"], writes=["QAs"])
                P.op("pool", (lambda e, h=h: e.memset(QAz[64:128, 0, h, :], 0.0)), reads=["QAs"], writes=["QAs"])
                ps_, nm = proj_fm(lambda c, h=h: Wv[:, c, 1536 + h * 128:1536 + (h + 1) * 128], xsT, 64, ["W", "xsT"])
                evac(QBzs[:, 0, h, :], ps_, reads=[nm], writes=["QBs"], scale=0.125)
                P.op("pool", (lambda e, h=h: e.tensor_copy(out=QBzs[64:128, 1, h, :], in_=QBzs[64:128, 0, h, :])), reads=["QBs"], writes=["QBs"])
                P.op("pool", (lambda e, h=h: e.memset(QBzs[64:128, 0, h, :], 0.0)), reads=["QBs"], writes=["QBs"])
            for s in range(2):
                for h in range(4):
                    ps_, nm = proj_fm(lambda c, h=h: Wv[:, c, 512 + h * 128:512 + (h + 1) * 128], xsT[:, :, s * 32:(s + 1) * 32], 32,
                                      ["W", "xsT"])
                    evac(KTc[:, h, 1024:1056], ps_, reads=[nm], writes=["KTc"])
                    ps_, nm = proj_fm(lambda c, h=h: Wv[:, c, 2048 + h * 128:2048 + (h + 1) * 128], xsT[:, :, s * 32:(s + 1) * 32], 32,
                                      ["W", "xsT"])
                    evac(KBc[:, h, 512:544], ps_, reads=[nm], writes=["KBc"])
                ps_, nm = proj_tm(xsT, s * 32, 32, lambda c: Wv[:, c, 1024:1536], ["W", "xsT"])
                evac(Vc[0:32, 8, :, 0:128], ps_.rearrange("p (h e) -> p h e", e=128), reads=[nm], writes=["Vc"])
                ps_, nm = proj_tm(xsT, s * 32, 32, lambda c: Wv[:, c, 2560:3072], ["W", "xsT"])
                evac(VBc[0:32, 4, :, 0:64], ps_.rearrange("p (h e) -> p h e", e=64), reads=[nm], writes=["VBc"])
                dmaL(cst, cak.ap()[s * 1024:(s + 1) * 1024, :].rearrange("(t p) n -> p t n", p=128), writes=["cst"], key="k_cst",
                     eng="pool")
                for t in range(8):
                    transposes(cst[:, t, :], 128, 4, KTc[:, :, t * 128:(t + 1) * 128], ["cst"], ["KTc"])
                dmaL(cst[:, 0:4, :], cbk.ap()[s * 512:(s + 1) * 512, :].rearrange("(t p) n -> p t n", p=128), writes=["cst"],
                     key="k_cst", eng="pool")
                for t in range(4):
                    transposes(cst[:, t, :], 128, 4, KBc[:, :, t * 128:(t + 1) * 128], ["cst"], ["KBc"])
                for h_ in range(4):
                    dmaL(Vc[:, 0:8, h_, 0:128],
                         cav.ap()[s * 1024:(s + 1) * 1024, h_ * 128:(h_ + 1) * 128].rearrange("(t p) e -> p t e", p=128),
                         reads=["Vc1"], writes=["Vc"], key="k_vc", eng="pool")
                for h_ in range(8):
                    dmaL(VBc[:, 0:4, h_, 0:64],
                         cbv.ap()[s * 512:(s + 1) * 512, h_ * 64:(h_ + 1) * 64].rearrange("(t p) e -> p t e", p=128),
                         reads=["VBc1"], writes=["VBc"], key="k_vbc", eng="pool")
                for h in range(4):
                    kts = []
                    for t in range(9):
                        nk = 128 if t < 8 else 32
                        adds = []
                        if t == 7:
                            adds.append((0, [TA[:, h, 1, 0:32]] * 2))
                        if t == 8:
                            adds.append((0, [TA[0:32, h, 0, 0:32]] * 2))
                        bcol = chA[0:nk, h:h + 1]
                        kts.append(dict(KT=KTc[:, h, t * 128:t * 128 + nk], nk=nk, V=[Vc[0:nk, t, h, 0:129]] * 2, bias=[bcol, bcol],
                                        qlo=0, qhi=0, adds=adds, reads=["KTc", "Vc", "Vc1"]))
                    fin = finA_factory(h, lambda qi, h=h: (oans[0:32, h * 128:(h + 1) * 128], "oans"), 32)
                    pair_attn(kts, [QAz[:, u_, h, s * 32:(s + 1) * 32] for u_ in range(2)], [(0, 32)], 128, fin, ["QAs"])
                dmaL(OANss.ap()[s * 32:(s + 1) * 32, :], oans[0:32, :], reads=["oans"], writes=["OANss"], key="k_oans", eng="pool")
                for cb in range(4):
                    kts = []
                    for t in range(5):
                        nk = 128 if t < 4 else 32
                        adds = []
                        if t == 3:
                            adds.append((0, [TB[:, 2 * cb + u, 1, 0:32] for u in range(2)]))
                        if t == 4:
                            adds.append((0, [TB[0:32, 2 * cb + u, 0, 0:32] for u in range(2)]))
                        kts.append(dict(KT=KBc[:, cb, t * 128:t * 128 + nk], nk=nk,
                                        V=[VBc[0:nk, t, 2 * cb + u, 0:65] for u in range(2)],
                                        bias=[chB[0:nk, 2 * cb + u:2 * cb + u + 1] for u in range(2)],
                                        qlo=0, qhi=0, adds=adds, reads=["KBc", "VBc", "VBc1"]))

                    def finb(O, names, cb=cb):
                        for u in range(2):
                            hb = 2 * cb + u
                            ci = 8 + u
                            P.op("dve", (lambda e, u=u, ci=ci: e.reciprocal(out=cols[0:32, ci:ci + 1], in_=O[u][0][:, 64:65])),
                                 reads=names, writes=["c%d" % ci])
                            P.op("dve", (lambda e, u=u, ci=ci, hb=hb: e.tensor_scalar(
                                out=obraw[0:32, hb * 64:(hb + 1) * 64], in0=O[u][0][:, 0:64], scalar1=cols[0:32, ci:ci + 1],
                                scalar2=None, op0=ALU.mult)), reads=names + ["c%d" % ci], writes=["obraw"])
                    pair_attn(kts, [QBzs[:, u_, cb, s * 32:(s + 1) * 32] for u_ in range(2)], [(0, 32)], 64, finb, ["QBs"])
                rc, rn = rms_rstd(obraw[0:32, :], 32, 512, eps6, 16, ["obraw"], "ob")
                P.op("dve", (lambda e, rc=rc: e.scalar_tensor_tensor(out=obns[0:32, :], in0=obraw[0:32, :], scalar=rc, in1=onbbc[0:32, :],
                                                                      op0=ALU.mult, op1=ALU.mult)),
                     reads=["obraw", rn, "onbbc"], writes=["obns"])
                dmaL(OBNss.ap()[s * 32:(s + 1) * 32, :], obns[0:32, :], reads=["obns"], writes=["OBNs"], key="k_obns", eng="pool")
        if "D" in phases:
            phaseD1()
            P.barrier(bar[:])

        weight_casts()
        osbB = [(R2a[:, 0:1536].rearrange("p (b n) -> p b n", n=512), "osbA"),
                (R2a[:, 1536:3072].rearrange("p (b n) -> p b n", n=512), "osbB")]
        if "B" in phases:
            P.op("dve", lambda e: e.memset(Vaug[:, :, 128:130], 1.0), writes=["Vones"])
            P.op("dve", lambda e: e.memset(QTz, 0.0), writes=["QTh", "QTh0"])
            NCH = 4
            for h in range(4):
                for ch in range(NCH):
                    k0 = ch * (SEQV // NCH)
                    k1 = (ch + 1) * (SEQV // NCH)
                    dmaL(KTh[:, k0:k1], KTs.ap()[h, :, k0:k1], reads=["Vones"], writes=["KTh%d" % ch], key="k_kth%d" % ch)
                    dmaL(Vaug[:, k0 // 128:k1 // 128, 0:128],
                         VAs.ap()[k0:k1, h * 128:(h + 1) * 128].rearrange("(t p) e -> p t e", p=128),
                         reads=["Vones"], writes=["Vh%d" % ch], key="k_vh%d" % ch)
                for u_ in range(2):
                    dmaL(QTz[64 * u_:64 * u_ + 64, u_, 0:NTOK], QTs.ap()[h, 64 * u_:64 * u_ + 64, :], reads=["QTh0"], writes=["QTh"],
                         key="k_qth")
                for I in range(NOWN):
                    p = 4 * I + 3
                    kts = []
                    for kt in range(4 * p + 4):
                        r = kt - 4 * p
                        qlo = max(0, r)
                        adds = []
                        if r >= 0:
                            adds.append((r, [TA[:, h, 0, :]] * 2))
                            if r + 1 <= 3:
                                adds.append((r + 1, [TA[:, h, 1, :]] * 2))
                        elif r == -1:
                            adds.append((0, [TA[:, h, 1, :]] * 2))
                        bcol = cmA[:, h, kt // 4:kt // 4 + 1] if kt < 12 else chA[:, h:h + 1]
                        ch = kt * 128 // (SEQV // NCH)
                        kts.append(dict(KT=KTh[:, kt * 128:(kt + 1) * 128], nk=128, V=[Vaug[:, kt, 0:129]] * 2, bias=[bcol, bcol],
                                        qlo=qlo, qhi=3, adds=adds, reads=["KTh%d" % ch, "Vh%d" % ch, "Vones"]))
                    fin = finA_factory(h, lambda qi, I=I, h=h: (OAN[:, I * 4 + qi, h * 128:(h + 1) * 128], "OAN"), 128)
                    pair_attn(kts, [QTz[:, u_, I * 512:(I + 1) * 512] for u_ in range(2)], [(i * 128, 128) for i in range(4)], 128, fin,
                              ["QTh"], osb=osbB)
        P.barrier(bar[:])

        ACT_T = R1[:, 0:16384].rearrange("p (h n) -> p h n", n=512)
        WR = R1[:, 16384:32768].rearrange("p (s n) -> p s n", n=4096)
        hres = R2a[:, 0:4096].rearrange("p (t f) -> p t f", f=1024)
        p_s = R2a[:, 4096:5120].rearrange("p (t f) -> p t f", f=256)
        gs = R2a[:, 5120:5632]
        tmpf = R2a[:, 5632:6144]
        gmlpbc = R2a[:, 6144:7168]
        gfinbc = R2a[:, 7168:8192]
        cs = R2b[:, 16384:20480].rearrange("p (t f) -> p t f", f=1024)
        aT = R2b[:, 20480:24576].rearrange("p (c n) -> p c n", n=512)
        obn_s = R2b[:, 24576:26624].rearrange("p (t n) -> p t n", n=512)
        pb = R2b[:, 26624:27648].rearrange("p (t f) -> p t f", f=256)
        pT = R2b[:, 27648:28672].rearrange("p (c n) -> p c n", n=512)
        wcnt = {"i": 0}

        def wload(src_ap, shape_view):
            s = wcnt["i"] % 4
            wcnt["i"] += 1
            dst = shape_view(WR[:, s, :])
            dmaL(dst, src_ap, reads=["wscr"], writes=["WR%d" % s], key="k_wr%d" % s)
            return dst, "WR%d" % s

        def phaseC_group(tiles, x_ap, p_ap, oan_fn, obn_src, y_ap):
            NT = tiles[-1][0] + tiles[-1][1]
            nt = len(tiles)
            rows0 = tiles[0][1]
            if rows0 == 128:
                dmaL(hres[:, 0:nt, :], x_ap.rearrange("(t p) f -> p t f", p=128), writes=["hres"], key="k_hres")
                dmaL(p_s[:, 0:nt, :], p_ap.rearrange("(t p) f -> p t f", p=128), writes=["p_s"], key="k_ps")
                dmaL(obn_s[:, 0:nt, :], obn_src.rearrange("(t p) f -> p t f", p=128), reads=["OBNs"], writes=["obn_s"], key="k_obn")
            else:
                dmaL(hres[0:rows0, 0, :], x_ap, writes=["hres"], key="k_hres")
                dmaL(p_s[0:rows0, 0, :], p_ap, writes=["p_s"], key="k_ps")
                dmaL(obn_s[0:rows0, 0, :], obn_src, reads=["OBNs"], writes=["obn_s"], key="k_obn")
            for ti, (tok0, rows) in enumerate(tiles):
                oa, oan_names = oan_fn(ti)
                transposes(oa, rows, 4, aT[:, 0:4, tok0:tok0 + rows], oan_names, ["aT"])
                transposes(obn_s[0:rows, ti, :], rows, 4, aT[:, 4:8, tok0:tok0 + rows], ["obn_s"], ["aT"])
            wo = [wload(wout_b.ap()[j * 512:(j + 1) * 512, :].rearrange("(c p) n -> p c n", p=128),
                        lambda v: v.rearrange("p (c n) -> p c n", n=1024)) for j in range(2)]
            for ti, (tok0, rows) in enumerate(tiles):
                for hf in range(2):
                    ps_, nm = proj_tm(aT, tok0, rows, lambda c, hf=hf: wo[c // 4][0][:, c % 4, hf * 512:(hf + 1) * 512],
                                      ["aT", wo[0][1], wo[1][1]])
                    P.op("dve", (lambda e, ti=ti, rows=rows, hf=hf, ps_=ps_: e.tensor_tensor(
                        out=hres[0:rows, ti, hf * 512:(hf + 1) * 512], in0=ps_, in1=hres[0:rows, ti, hf * 512:(hf + 1) * 512], op=ALU.add)),
                        reads=[nm, "hres"], writes=["hres"], self_sync=False)
            for ti, (tok0, rows) in enumerate(tiles):
                rc, rn = rms_rstd(hres[0:rows, ti, :], rows, 1024, eps6, 3 * ti, ["hres"], "c")
                P.op("dve", (lambda e, ti=ti, rows=rows, rc=rc: e.scalar_tensor_tensor(out=cs[0:rows, ti, :], in0=hres[0:rows, ti, :],
                                                                                       scalar=rc, in1=gmlpbc[0:rows, :], op0=ALU.mult,
                                                                                       op1=ALU.mult)),
                     reads=["hres", rn, "gmlpbc"], writes=["cs"])
                transposes(cs[0:rows, ti, :], rows, 8, aT[:, :, tok0:tok0 + rows], ["cs"], ["aT"])
            for j in range(8):
                wu, wn = wload(wup_b.ap()[:, j * 512:(j + 1) * 512].rearrange("(c p) n -> p c n", p=128),
                               lambda v: v.rearrange("p (c n) -> p c n", n=512))
                for hl in range(4):
                    hc = j * 4 + hl
                    ps_, nm = proj_fm(lambda c, hl=hl, wu=wu: wu[:, c, hl * 128:(hl + 1) * 128], aT, NT, ["aT", wn])
                    rb, rbn = (tmpf, "tmpf") if hc % 2 else (gs, "gs")
                    P.op("act", (lambda e, ps_=ps_, rb=rb: e.activation(out=rb[:, 0:NT], in_=ps_, func=AF.Relu)),
                         reads=[nm], writes=[rbn])
                    P.op("pool", (lambda e, hc=hc, rb=rb: e.tensor_tensor(out=ACT_T[:, hc, 0:NT], in0=rb[:, 0:NT], in1=rb[:, 0:NT],
                                                                          op=ALU.mult)), reads=[rbn], writes=["ACT_T"])
            for hf in range(2):
                accs = []
                for ti in range(nt):
                    accs.append(ti)
                for j in range(4):
                    wd, wn = wload(wdown_b.ap()[j * 1024:(j + 1) * 1024, hf * 512:(hf + 1) * 512].rearrange("(c p) n -> p c n", p=128),
                                   lambda v: v.rearrange("p (c n) -> p c n", n=512))
                    for hl in range(8):
                        hc = j * 8 + hl
                        for ti, (tok0, rows) in enumerate(tiles):
                            pe_op(128, rows, (lambda e, ti=ti, tok0=tok0, rows=rows, hc=hc, hl=hl, wd=wd: e.matmul(
                                PS[0:rows, ti, :], lhsT=ACT_T[:, hc, tok0:tok0 + rows], rhs=wd[:, hl, :], start=(hc == 0), stop=(hc == 31))),
                                reads=["ACT_T", wn], writes=["PS%d" % ti])
                for ti, (tok0, rows) in enumerate(tiles):
                    P.op("dve", (lambda e, ti=ti, rows=rows, hf=hf: e.tensor_tensor(
                        out=hres[0:rows, ti, hf * 512:(hf + 1) * 512], in0=PS[0:rows, ti, :], in1=hres[0:rows, ti, hf * 512:(hf + 1) * 512],
                        op=ALU.add)), reads=["PS%d" % ti, "hres"], writes=["hres"], self_sync=False)
            for ti, (tok0, rows) in enumerate(tiles):
                P.op("act", (lambda e, ti=ti, rows=rows: e.activation(out=cs[0:rows, ti, :], in_=hres[0:rows, ti, :], func=AF.Copy, scale=1.0)),
                     reads=["hres"], writes=["cs"])
                transposes(cs[0:rows, ti, :], rows, 8, aT[:, :, tok0:tok0 + rows], ["cs"], ["aT"])
                P.op("dve", (lambda e, ti=ti, rows=rows: e.tensor_copy(out=pb[0:rows, ti, :], in_=p_s[0:rows, ti, :])),
                     reads=["p_s"], writes=["pb"])
                transposes(pb[0:rows, ti, :], rows, 2, pT[:, :, tok0:tok0 + rows], ["pb"], ["pT"])
            wg = [wload(wgate_b.ap()[j * 512:(j + 1) * 512, :].rearrange("(c p) n -> p c n", p=128),
                        lambda v: v.rearrange("p (c n) -> p c n", n=1024)) for j in range(2)]
            wp, wpn = wload(wple_b.ap().rearrange("(c p) n -> p c n", p=128),
                            lambda v: v[:, 0:2048].rearrange("p (c n) -> p c n", n=1024))
            for ti, (tok0, rows) in enumerate(tiles):
                for hf in range(2):
                    ps_, nm = proj_tm(aT, tok0, rows, lambda c, hf=hf: wg[c // 4][0][:, c % 4, hf * 512:(hf + 1) * 512],
                                      ["aT", wg[0][1], wg[1][1]])
                    P.op("act", (lambda e, rows=rows, ps_=ps_: e.activation(out=gs[0:rows, :], in_=ps_, func=AF.Sigmoid)),
                         reads=[nm], writes=["gs"])
                    ps2, nm2 = proj_tm(pT, tok0, rows, lambda c, hf=hf: wp[:, c, hf * 512:(hf + 1) * 512], ["pT", wpn], nk=2)
                    P.op("dve", (lambda e, rows=rows, ps2=ps2: e.tensor_tensor(out=tmpf[0:rows, :], in0=gs[0:rows, :], in1=ps2, op=ALU.mult)),
                         reads=["gs", nm2], writes=["tmpf"])
                    P.op("dve", (lambda e, ti=ti, rows=rows, hf=hf: e.tensor_tensor(
                        out=hres[0:rows, ti, hf * 512:(hf + 1) * 512], in0=tmpf[0:rows, :], in1=hres[0:rows, ti, hf * 512:(hf + 1) * 512],
                        op=ALU.add)), reads=["tmpf", "hres"], writes=["hres"])
            for ti, (tok0, rows) in enumerate(tiles):
                rc, rn = rms_rstd(hres[0:rows, ti, :], rows, 1024, eps6, 3 * ti, ["hres"], "f")
                P.op("dve", (lambda e, ti=ti, rows=rows, rc=rc: e.scalar_tensor_tensor(out=hres[0:rows, ti, :], in0=hres[0:rows, ti, :],
                                                                                       scalar=rc, in1=gfinbc[0:rows, :], op0=ALU.mult,
                                                                                       op1=ALU.mult)),
                     reads=["hres", rn, "gfinbc"], writes=["hres"])
            if rows0 == 128:
                dmaO(y_ap.rearrange("(t p) f -> p t f", p=128), hres[:, 0:nt, :], reads=["hres"], key="k_y")
            else:
                dmaO(y_ap, hres[0:rows0, 0, :], reads=["hres"], key="k_y")

        if "C" in phases or "D" in phases:
            dmaL(gmlpbc, bass.AP(g_mlp, 0, [[0, 128], [1, 1024]]), writes=["gmlpbc"], key="k_c12")
            dmaL(gfinbc, bass.AP(g_final, 0, [[0, 128], [1, 1024]]), writes=["gfinbc"], key="k_c13")
        if "C" in phases:
            for I in range(NOWN):
                phaseC_group([(i * 128, 128) for i in range(4)], xv.ap()[(4 * I + 3) * 512:(4 * I + 4) * 512, :],
                             pv.ap()[I * 512:(I + 1) * 512, :],
                             lambda ti, I=I: (OAN[:, I * 4 + ti, :], ["OAN"]),
                             OBNs.ap()[I * 512:(I + 1) * 512, :], y.ap()[I * 512:(I + 1) * 512, :])
        P.barrier(bar[:])

        def phaseD2():
            oanl = R2b[:, 0:512]
            dmaL(oanl[0:64, :], OANss.ap(), reads=["OANss"], writes=["oanl"], key="k_oanl")
            phaseC_group([(0, 64)], xsm.ap(), psm.ap(), lambda ti: (oanl[0:64, :], ["oanl"]), OBNss.ap(), ys.ap())

        if "D" in phases:
            phaseD2()

        P.emit(final_dma_keys=[] if "5" in phases else sorted(outkeys))
    return nc


_NC_CACHE = {}


def _run(x_prompt, x_sample, cache_a_k, cache_a_v, cache_b_k, cache_b_v, p_prompt, p_sample,
         t5_table, g_attn, w_in, lambda_q1, lambda_k1, lambda_q2, lambda_k2, subln_g,
         band_table, out_norm_b, w_out, g_mlp, w_up, w_down, w_ple_gate, w_ple_proj, g_final,
         phases="ABCD", cores=None, trace=False):
    f = lambda a: np.ascontiguousarray(np.asarray(a, dtype=np.float32))
    x_prompt = f(x_prompt); x_sample = f(x_sample); p_prompt = f(p_prompt); p_sample = f(p_sample)
    cache_a_k = f(cache_a_k); cache_a_v = f(cache_a_v); cache_b_k = f(cache_b_k); cache_b_v = f(cache_b_v)
    S = x_prompt.shape[1]
    nblk = S // 512
    nown = nblk // 4
    seqv = nblk * 512
    ident, J, CA, CB = _consts()
    ck = (phases, nblk)
    if ck not in _NC_CACHE:
        _NC_CACHE[ck] = build_program(phases, nblk)
    nc = _NC_CACHE[ck]
    common = {
        "t5": f(t5_table), "bt": f(band_table)[0], "g_attn": f(g_attn), "w_in": f(w_in)[0],
        "lq1": f(lambda_q1), "lk1": f(lambda_k1), "lq2": f(lambda_q2), "lk2": f(lambda_k2),
        "subln": f(subln_g), "onb": f(out_norm_b), "w_out": f(w_out)[0], "g_mlp": f(g_mlp),
        "w_up": f(w_up)[0], "w_down": f(w_down)[0], "w_gate": f(w_ple_gate)[0], "w_ple": f(w_ple_proj)[0],
        "g_final": f(g_final).reshape(1, 1024), "ident": ident, "J": J, "CA": CA, "CB": CB,
    }
    cores = list(range(8)) if cores is None else list(cores)
    in_maps = []
    for c in cores:
        b, j = divmod(c, 4)
        npad = 3 - j
        xvv = np.zeros((seqv, 1024), np.float32)
        nreal = (nblk - npad) * 512
        xvv[npad * 512:] = x_prompt[b, :nreal]
        pvv = np.concatenate([p_prompt[0, b, (4 * I + j) * 512:(4 * I + j + 1) * 512] for I in range(nown)], 0)
        pm = np.zeros((128, 4), np.float32)
        pm[:, :npad] = NEGM
        m = dict(common)
        m.update({
            "xv": xvv, "pv": np.ascontiguousarray(pvv),
            "xsm": x_sample[2 * c:2 * c + 2].reshape(64, 1024), "psm": p_sample[0, 2 * c:2 * c + 2].reshape(64, 256),
            "cak": cache_a_k[0, 2 * c:2 * c + 2].reshape(2048, 512), "cav": cache_a_v[0, 2 * c:2 * c + 2].reshape(2048, 512),
            "cbk": cache_b_k[0, 2 * c:2 * c + 2].reshape(1024, 512), "cbv": cache_b_v[0, 2 * c:2 * c + 2].reshape(1024, 512),
            "padmask": pm,
        })
        in_maps.append({k: np.ascontiguousarray(v) for k, v in m.items()})
    if trace:
        res = run_bass_kernel_spmd(nc, in_maps, core_ids=list(range(len(cores))), trace=True)
        print("EXEC_TIME_NS", res.exec_time_ns)
    else:
        res = run_bass_kernel_spmd(nc, in_maps, core_ids=list(range(len(cores))))
    R = res.results
    y_prompt = np.zeros((2, S, 1024), np.float32)
    nakp = np.zeros((1, 2, S, 512), np.float32)
    navp = np.zeros((1, 2, S, 512), np.float32)
    nbkp = np.zeros((1, 2, 512, 512), np.float32)
    nbvp = np.zeros((1, 2, 512, 512), np.float32)
    y_sample = np.zeros((16, 32, 1024), np.float32)
    saks = np.zeros((1, 16, 32, 512), np.float32); savs = np.zeros((1, 16, 32, 512), np.float32)
    sbks = np.zeros((1, 16, 32, 512), np.float32); sbvs = np.zeros((1, 16, 32, 512), np.float32)
    for ci, c in enumerate(cores):
        b, j = divmod(c, 4)
        r = R[ci]
        for I in range(nown):
            g0 = (4 * I + j) * 512
            y_prompt[b, g0:g0 + 512] = r["y"][I * 512:(I + 1) * 512]
            nakp[0, b, g0:g0 + 512] = r["nak"][I * 512:(I + 1) * 512]
            navp[0, b, g0:g0 + 512] = r["nav"][I * 512:(I + 1) * 512]
        if j == 3:
            nbkp[0, b] = r["nbk"]
            nbvp[0, b] = r["nbv"]
        y_sample[2 * c:2 * c + 2] = r["ys"].reshape(2, 32, 1024)
        saks[0, 2 * c:2 * c + 2] = r["sak"].reshape(2, 32, 512)
        savs[0, 2 * c:2 * c + 2] = r["sav"].reshape(2, 32, 512)
        sbks[0, 2 * c:2 * c + 2] = r["sbk"].reshape(2, 32, 512)
        sbvs[0, 2 * c:2 * c + 2] = r["sbv"].reshape(2, 32, 512)
    return (y_prompt, y_sample,
            nakp.reshape(1, 2, S, 4, 2, 64), navp.reshape(1, 2, S, 4, 128),
            nbkp.reshape(1, 2, 512, 8, 64), nbvp.reshape(1, 2, 512, 8, 64),
            saks.reshape(1, 16, 32, 4, 2, 64), savs.reshape(1, 16, 32, 4, 128),
            sbks.reshape(1, 16, 32, 8, 64), sbvs.reshape(1, 16, 32, 8, 64))


def kernel(x_prompt, x_sample, cache_a_k, cache_a_v, cache_b_k, cache_b_v, p_prompt, p_sample,
           t5_table, g_attn, w_in, lambda_q1, lambda_k1, lambda_q2, lambda_k2, subln_g,
           band_table, out_norm_b, w_out, g_mlp, w_up, w_down, w_ple_gate, w_ple_proj, g_final):
    return _run(x_prompt, x_sample, cache_a_k, cache_a_v, cache_b_k, cache_b_v, p_prompt, p_sample,
                t5_table, g_attn, w_in, lambda_q1, lambda_k1, lambda_q2, lambda_k2, subln_g,
                band_table, out_norm_b, w_out, g_mlp, w_up, w_down, w_ple_gate, w_ple_proj, g_final)
```

```python
import math
import contextlib
import numpy as np
import concourse.bass as bass
import concourse.mybir as mybir
from concourse.bass_utils import run_bass_kernel_spmd

ENGS = ("pe", "act", "dve", "pool", "sp")


class Op:
    __slots__ = ("idx", "eng", "fn", "dma_key", "dma_cnt", "waits", "signal", "sigcnt", "eidx")

    def __init__(self, idx, eng, fn, dma_key):
        self.idx = idx
        self.eng = eng
        self.fn = fn
        self.dma_key = dma_key
        self.dma_cnt = 0
        self.waits = []
        self.signal = False
        self.sigcnt = 0
        self.eidx = 0


class Prog:
    def __init__(self, nc):
        self.nc = nc
        self.ops = []
        self.by_eng = {e: [] for e in ENGS}
        self.last_w = {}
        self.readers = {}
        self.seen = {e: {f: -1 for f in ENGS} for e in ENGS}
        self.seen_dma = {e: {} for e in ENGS}
        self.dma_counts = {}
        self.final_dma = []
        self.force = {}

    def op(self, eng, fn, reads=(), writes=(), dma_key=None, self_sync=True, extra=()):
        o = Op(len(self.ops), eng, fn, dma_key)
        o.eidx = len(self.by_eng[eng])
        deps = []
        for r in reads:
            w = self.last_w.get(r)
            if w is not None:
                deps.append(w)
        for w_ in writes:
            w = self.last_w.get(w_)
            if w is not None:
                deps.append(w)
            deps.extend(self.readers.get(w_, ()))
        for r in reads:
            self.readers.setdefault(r, []).append(o)
        for w_ in writes:
            self.last_w[w_] = o
            self.readers[w_] = [x for x in self.readers.get(w_, ()) if x is o]
        f = self.force.pop(eng, None)
        if f is not None:
            deps.append(f)
        deps.extend(extra)
        if dma_key is not None:
            self.dma_counts[dma_key] = self.dma_counts.get(dma_key, 0) + 16
            o.dma_cnt = self.dma_counts[dma_key]
        for d in deps:
            if d is o:
                continue
            if d.dma_key is not None:
                cur = self.seen_dma[eng].get(d.dma_key, 0)
                if cur >= d.dma_cnt:
                    continue
                self.seen_dma[eng][d.dma_key] = d.dma_cnt
                o.waits.append(("dma", d.dma_key, d.dma_cnt))
            else:
                if d.eng == eng and (eng == "pe" or not self_sync):
                    continue
                if self.seen[eng][d.eng] >= d.eidx:
                    continue
                self.seen[eng][d.eng] = d.eidx
                d.signal = True
                o.waits.append(("eng", d.eng, d))
        self.ops.append(o)
        self.by_eng[eng].append(o)
        return o

    def barrier(self, tile, skip=()):
        extra = [self.by_eng[e][-1] for e in ENGS if self.by_eng[e]]
        last_dma = {}
        for o in self.ops:
            if o.dma_key is not None and o.dma_key not in skip:
                last_dma[o.dma_key] = o
        extra.extend(last_dma.values())
        b = self.op("dve", lambda e: e.memset(tile, 0.0), extra=extra)
        for e in ENGS:
            if e != "dve":
                self.force[e] = b
        return b

    def emit(self, final_dma_keys=()):
        nc = self.nc
        import contextlib
        for o in self.ops:
            best = {}
            for w in o.waits:
                if w[0] == "dma":
                    k = ("dma", w[1])
                    if k not in best or best[k][2] < w[2]:
                        best[k] = w
                else:
                    k = ("eng", w[1])
                    if k not in best or best[k][2].eidx < w[2].eidx:
                        best[k] = w
            o.waits = list(best.values())
        for e in ENGS:
            c = 0
            for o in self.by_eng[e]:
                if o.signal:
                    c += 1
                    o.sigcnt = c
        with contextlib.ExitStack() as st:
            esem = {e: st.enter_context(nc.semaphore("s_" + e)) for e in ENGS}
            dsem = {k: st.enter_context(nc.semaphore("d_%d" % i))
                    for i, k in enumerate(sorted(self.dma_counts))}
            block = st.enter_context(nc.Block())

            def run(e, eng):
                for o in self.by_eng[e]:
                    for w in o.waits:
                        if w[0] == "dma":
                            eng.wait_ge(dsem[w[1]], w[2])
                        else:
                            eng.wait_ge(esem[w[1]], w[2].sigcnt)
                    inst = o.fn(eng)
                    if o.dma_key is not None:
                        inst.then_inc(dsem[o.dma_key], 16)
                    elif o.signal:
                        inst.then_inc(esem[e], 1)
                if e == "sp":
                    for k in final_dma_keys:
                        eng.wait_ge(dsem[k], self.dma_counts[k])

            @block.tensor
            def _(eng):
                run("pe", eng)

            @block.scalar
            def _(eng):
                run("act", eng)

            @block.vector
            def _(eng):
                run("dve", eng)

            @block.gpsimd
            def _(eng):
                run("pool", eng)

            @block.sync
            def _(eng):
                run("sp", eng)

F32 = mybir.dt.float32
BF16 = mybir.dt.bfloat16
AF = mybir.ActivationFunctionType
ALU = mybir.AluOpType
NEGM = -30000.0
NBLK = 32
NOWN = 8
SEQV = NBLK * 512


def _t5_bucket_np(rel):
    half = 16
    n = -rel
    ret = np.where(n < 0, half, 0)
    n = np.abs(n)
    max_exact = 8
    nf = np.maximum(n, 1).astype(np.float32)
    large = max_exact + (np.log(nf / np.float32(max_exact)) / np.float32(math.log(128 / max_exact))
                         * np.float32(half - max_exact)).astype(np.int32)
    large = np.minimum(large, half - 1)
    return ret + np.where(n < max_exact, n, large)


def _consts():
    ident = np.eye(128, dtype=np.float32)
    J = ident[::-1].copy()
    i = np.arange(384)
    delta = i - 255
    CA = np.zeros((32, 384), np.float32)
    bk = _t5_bucket_np(delta.astype(np.int32))
    CA[bk, i] += 1.0
    CA[15, :] -= 1.0
    CA[:, 383] = 0.0
    CB = np.zeros((384, 384), np.float32)
    idx = np.clip(delta, -128, 128) + 128
    CB[idx, i] += 1.0
    CB[0, :] -= 1.0
    CB[:, 383] = 0.0
    return ident, J, CA, CB


def build_program(phases="ABCD", nblk=32):
    global NBLK, NOWN, SEQV
    NBLK = nblk
    NOWN = nblk // 4
    SEQV = nblk * 512
    NTOK = NOWN * 512
    nc = bass.Bass("TRN2", target_bir_lowering=False)
    T = {}

    def din(name, shape, dt=F32):
        T[name] = nc.dram_tensor(name, shape, dt, kind="ExternalInput")
        return T[name]

    def dout(name, shape, dt=F32):
        T[name] = nc.dram_tensor(name, shape, dt, kind="ExternalOutput")
        return T[name]

    def dscr(name, shape, dt=BF16):
        T[name] = nc.dram_tensor(name, shape, dt, kind="Internal")
        return T[name]

    xv = din("xv", [SEQV, 1024]); pv = din("pv", [NTOK, 256])
    xsm = din("xsm", [64, 1024]); psm = din("psm", [64, 256])
    cak = din("cak", [2048, 512]); cav = din("cav", [2048, 512])
    cbk = din("cbk", [1024, 512]); cbv = din("cbv", [1024, 512])
    padmask = din("padmask", [128, 4])
    t5 = din("t5", [32, 4]); bt = din("bt", [257, 8])
    g_attn = din("g_attn", [1, 1024]); w_in = din("w_in", [1024, 3072])
    lq1 = din("lq1", [1, 64]); lk1 = din("lk1", [1, 64]); lq2 = din("lq2", [1, 64]); lk2 = din("lk2", [1, 64])
    subln = din("subln", [1, 128]); onb = din("onb", [1, 512])
    w_out = din("w_out", [1024, 1024]); g_mlp = din("g_mlp", [1, 1024])
    w_up = din("w_up", [1024, 4096]); w_down = din("w_down", [4096, 1024])
    w_gate = din("w_gate", [1024, 1024]); w_ple = din("w_ple", [256, 1024]); g_final = din("g_final", [1, 1024])
    identd = din("ident", [128, 128]); Jd = din("J", [128, 128]); CAd = din("CA", [32, 384]); CBd = din("CB", [384, 384])

    y = dout("y", [NTOK, 1024]); ys = dout("ys", [64, 1024])
    nak = dout("nak", [NTOK, 512]); nav = dout("nav", [NTOK, 512])
    nbk = dout("nbk", [512, 512]); nbv = dout("nbv", [512, 512])
    sak = dout("sak", [64, 512]); sav = dout("sav", [64, 512]); sbk = dout("sbk", [64, 512]); sbv = dout("sbv", [64, 512])

    KTs = dscr("KTs", [4, 128, SEQV]); VAs = dscr("VAs", [SEQV, 512]); QTs = dscr("QTs", [4, 128, NTOK])
    OBNs = dscr("OBNs", [NTOK, 512]); OANss = dscr("OANss", [64, 512]); OBNss = dscr("OBNss", [64, 512])
    vecA = dscr("vecA", [4, 384], F32); vecB = dscr("vecB", [8, 384], F32)
    wout_b = dscr("wout_b", [1024, 1024]); wup_b = dscr("wup_b", [1024, 4096]); wdown_b = dscr("wdown_b", [4096, 1024])
    wgate_b = dscr("wgate_b", [1024, 1024]); wple_b = dscr("wple_b", [256, 1024])

    P = Prog(nc)
    outkeys = set()
    with contextlib.ExitStack() as st:
        def sb(name, shape, dt):
            return st.enter_context(nc.sbuf_tensor(name, shape, dt))

        def psm_(name, shape, dt):
            return st.enter_context(nc.psum_tensor(name, shape, dt))

        R1 = sb("R1", [128, 33024], BF16)
        R2a = sb("R2a", [128, 8192], F32)
        R2b = sb("R2b", [128, 28800], BF16)
        idb = sb("idb", [128, 128], BF16)
        Js = sb("Js", [128, 128], F32)
        Hs = sb("Hs", [128, 128], F32)
        TA = sb("TA", [128, 4, 2, 128], F32)
        TB = sb("TB", [128, 8, 2, 128], F32)
        Tm4 = sb("Tm4", [128, 128], F32)
        chA = sb("chA", [128, 4], F32); chB = sb("chB", [128, 8], F32)
        cmA = sb("cmA", [128, 4, 3], F32); cmB = sb("cmB", [128, 8], F32)
        pmk = sb("pmk", [128, 4], F32)
        lam4 = sb("lam4", [128, 4, 64], F32)
        lcol = sb("lcol", [128, 8], F32)
        sublnbc = sb("sublnbc", [128, 128], F32)
        onbbc = sb("onbbc", [128, 512], F32)
        junk = sb("junk", [128, 1024], BF16)
        Eb = sb("Eb", [128, 2, 2, 512], BF16)
        cols = sb("cols", [128, 64], F32)
        eps6 = sb("eps6", [128, 1], F32); eps5 = sb("eps5", [128, 1], F32)
        bar = sb("bar", [128, 1], F32)
        osm = sb("osm", [128, 2, 2, 128], F32)
        t5s = sb("t5s", [32, 4], F32); bts = sb("bts", [128, 3, 8], F32)
        CAs = sb("CAs", [32, 384], F32); CBs = sb("CBs", [128, 3, 384], F32)
        vst = sb("vst", [8, 384], F32)

        TP = psm_("TP", [128, 2, 1024], BF16)
        PS = psm_("PS", [128, 6, 512], F32)
        TPf = TP.bitcast(F32)
        obanks = [(PS[:, 4, :], "PS4"), (PS[:, 5, :], "PS5"), (TPf[:, 0, :], "TP0")]

        cnt = {"ev": 0, "tp": 0, "ps": 0, "sl": 0, "ost": 0}

        KM = {"k_idb": "q1", "k_w": "q10", "k_v0": "q14", "k_v1": "q15", "k_h": "q16",
              "k_xb": "q0", "k_ktst": "q1", "k_vast": "q2", "k_qast": "q3", "k_ost0": "q4", "k_ost1": "q5", "k_obst": "q6",
              "k_win": "q7", "k_c11": "q8",
              "k_kth0": "q0", "k_kth1": "q1", "k_kth2": "q2", "k_kth3": "q3", "k_vh0": "q0", "k_vh1": "q1", "k_vh2": "q2",
              "k_vh3": "q3", "k_qth": "q4",
              "k_wr0": "q0", "k_wr1": "q1", "k_wr2": "q2", "k_wr3": "q3", "k_hres": "q4", "k_ps": "q5", "k_obn": "q6",
              "k_y": "q7", "k_c12": "q8", "k_c13": "q9",
              "k_cst": "q1", "k_vc": "q2", "k_vbc": "q3", "k_oans": "q6", "k_obns": "q10", "k_oanl": "q9"}
        for i_ in range(11):
            KM["k_c%d" % i_] = "q0"
        KM.update({"k_xb0": "q0", "k_xb1": "q11", "k_xb2": "q12", "k_xb3": "q13"})

        def dmaL(out, in_, reads=(), writes=(), key=None, eng="sp"):
            key = KM[key]
            return P.op(eng, lambda e: e.dma_start(out=out, in_=in_), reads=reads, writes=writes, dma_key=key)

        def dmaO(out, in_, reads, key):
            key = KM[key]
            outkeys.add(key)
            return P.op("sp" if "4" in phases else "pool", lambda e: e.dma_start(out=out, in_=in_), reads=reads, dma_key=key)

        def evac(out, in_, reads, writes, scale=None, eng=None):
            if eng is None:
                cnt["ev"] += 1
                eng = "act" if cnt["ev"] % 2 else "dve"
            if eng == "act":
                s = 1.0 if scale is None else scale
                return P.op("act", lambda e: e.activation(out=out, in_=in_, func=AF.Copy, scale=s), reads=reads, writes=writes)
            if scale is None:
                return P.op("dve", lambda e: e.tensor_copy(out=out, in_=in_), reads=reads, writes=writes)
            return P.op("dve", lambda e: e.tensor_scalar(out=out, in0=in_, scalar1=scale, scalar2=None, op0=ALU.mult),
                        reads=reads, writes=writes)

        pe_state = {"mode": None}

        def pe_op(K, M, fn, reads=(), writes=()):
            r = lambda x: 32 if x <= 32 else (64 if x <= 64 else 128)
            mode = (r(K), r(M))
            if pe_state["mode"] is not None and pe_state["mode"] != mode:
                P.op("pe", lambda e: e.drain())
            pe_state["mode"] = mode
            return P.op("pe", fn, reads=reads, writes=writes)

        def nextps():
            cnt["ps"] = (cnt["ps"] + 1) % 4
            return cnt["ps"]

        def rms_rstd(src, rows, F, eps_t, ci, reads, tag):
            P.op("act", lambda e: e.activation(out=junk[0:rows, 0:F], in_=src, func=AF.Square,
                                               accum_out=cols[0:rows, ci:ci + 1]),
                 reads=reads, writes=["junk", "c%d" % ci])
            P.op("act", lambda e: e.activation(out=cols[0:rows, ci + 1:ci + 2], in_=cols[0:rows, ci:ci + 1], func=AF.Sqrt,
                                               bias=eps_t[0:rows, 0:1], scale=1.0 / F),
                 reads=["c%d" % ci, "eps"], writes=["c%d" % (ci + 1)])
            P.op("dve", lambda e: e.reciprocal(out=cols[0:rows, ci + 2:ci + 3], in_=cols[0:rows, ci + 1:ci + 2]),
                 reads=["c%d" % (ci + 1)], writes=["c%d" % (ci + 2)])
            return cols[0:rows, ci + 2:ci + 3], "c%d" % (ci + 2)

        def transposes(src, rows, nch, dst, reads, writes):
            b = cnt["tp"] % 2
            cnt["tp"] += 1
            for c in range(nch):
                pe_op(rows, 128, (lambda e, c=c: e.transpose(TP[:, b, c * 128:c * 128 + rows], src[:, c * 128:(c + 1) * 128],
                                                       idb[0:rows, 0:rows])),
                     reads=list(reads) + ["idb"], writes=["TP%d" % b])
            tv = TP[:, b, :].rearrange("p (a r) -> p a r", r=128)[:, 0:nch, 0:rows]
            evac(dst, tv, reads=["TP%d" % b], writes=writes)

        def proj_fm(wfn, rhsT, n, reads):
            b = nextps()
            for c in range(8):
                pe_op(128, 128, (lambda e, c=c: e.matmul(PS[:, b, 0:n], lhsT=wfn(c), rhs=rhsT[:, c, 0:n], start=(c == 0), stop=(c == 7))),
                     reads=reads, writes=["PS%d" % b])
            return PS[:, b, 0:n], "PS%d" % b

        def proj_tm(xT, tok0, rows, wfn, reads, nk=8):
            b = nextps()
            for c in range(nk):
                pe_op(128, rows, (lambda e, c=c: e.matmul(PS[0:rows, b, :], lhsT=xT[:, c, tok0:tok0 + rows], rhs=wfn(c),
                                                    start=(c == 0), stop=(c == nk - 1))),
                     reads=reads, writes=["PS%d" % b])
            return PS[0:rows, b, :], "PS%d" % b

        osb_state = {"i": 0}

        def pair_attn(keytiles, QT, qtiles, ed, finalize, qreads, osb=None):
            nqt = len(qtiles)
            per_bank = 512 // (ed + 1)
            started = set()

            def oloc(u, qi):
                g = u * nqt + qi
                bank, slot = divmod(g, per_bank)
                ap, nm = obanks[bank]
                rows = qtiles[qi][1]
                return ap[0:rows, slot * (ed + 1):(slot + 1) * (ed + 1)], nm, bank

            def qk(kt):
                sl = cnt["sl"] % 2
                cnt["sl"] += 1
                kt["sl"] = sl
                nk = kt["nk"]
                c0 = qtiles[kt["qlo"]][0]
                c1 = qtiles[kt["qhi"]][0] + qtiles[kt["qhi"]][1]
                kt["c"] = (c0, c1)
                for u in range(2):
                    pe_op(128, nk, (lambda e, u=u: e.matmul(PS[0:nk, sl * 2 + u, c0:c1], lhsT=kt["KT"],
                                                            rhs=QT[u][:, c0:c1], start=True, stop=True)),
                         reads=list(kt["reads"]) + list(qreads), writes=["PS%d" % (sl * 2 + u)])
                for (qi, Ts) in ([] if "k" in phases else kt["adds"]):
                    q0, qr = qtiles[qi]
                    for u in range(2):
                        P.op("dve", (lambda e, u=u, q0=q0, qr=qr, Ts=Ts: e.tensor_tensor(
                            out=PS[0:nk, sl * 2 + u, q0:q0 + qr], in0=PS[0:nk, sl * 2 + u, q0:q0 + qr], in1=Ts[u], op=ALU.add)),
                            reads=["PS%d" % (sl * 2 + u), "Tt"], writes=["PS%d" % (sl * 2 + u)], self_sync=False)
                if kt["bias"][0] is kt["bias"][1]:
                    P.op("act", lambda e: e.activation(out=Eb[0:nk, sl, :, c0:c1], in_=PS[0:nk, sl * 2:sl * 2 + 2, c0:c1],
                                                       func=AF.Exp, bias=kt["bias"][0], scale=1.0),
                         reads=["PS%d" % (sl * 2), "PS%d" % (sl * 2 + 1), "bias"], writes=["E%d" % sl])
                else:
                    for u in range(2):
                        P.op("act", (lambda e, u=u: e.activation(out=Eb[0:nk, sl, u, c0:c1], in_=PS[0:nk, sl * 2 + u, c0:c1],
                                                                 func=AF.Exp, bias=kt["bias"][u], scale=1.0)),
                             reads=["PS%d" % (sl * 2 + u), "bias"], writes=["E%d" % sl])

            def pvm(kt):
                if "l" in phases:
                    return
                sl = kt["sl"]
                nk = kt["nk"]
                for u in range(2):
                    for qi in range(kt["qlo"], kt["qhi"] + 1):
                        oap, nm, bank = oloc(u, qi)
                        q0, qr = qtiles[qi]
                        first = bank not in started
                        started.add(bank)
                        pe_op(nk, qr, (lambda e, u=u, oap=oap, q0=q0, qr=qr, first=first: e.matmul(
                            oap, lhsT=Eb[0:nk, sl, u, q0:q0 + qr], rhs=kt["V"][u], start=first, stop=False,
                            skip_group_check=True)),
                            reads=["E%d" % sl] + list(kt["reads"]), writes=[nm])

            prev = None
            for kt in keytiles:
                qk(kt)
                if prev is not None:
                    pvm(prev)
                prev = kt
            pvm(prev)
            O = [[oloc(u, qi)[0] for qi in range(nqt)] for u in range(2)]
            names = sorted({oloc(u, qi)[1] for u in range(2) for qi in range(nqt)})
            if osb is not None:
                sset = osb[osb_state["i"] % len(osb)]
                osb_state["i"] += 1
                used = sorted({oloc(u, qi)[2] for u in range(2) for qi in range(nqt)})
                for bk in used:
                    bap, bnm = obanks[bk]
                    P.op("dve", (lambda e, bk=bk, bap=bap: e.tensor_copy(out=sset[0][:, bk, :], in_=bap)), reads=[bnm],
                         writes=[sset[1] + str(bk)])

                def oloc2(u, qi):
                    g = u * nqt + qi
                    bank, slot = divmod(g, per_bank)
                    rows = qtiles[qi][1]
                    return sset[0][0:rows, bank, slot * (ed + 1):(slot + 1) * (ed + 1)], sset[1] + str(bank)
                O = [[oloc2(u, qi)[0] for qi in range(nqt)] for u in range(2)]
                names = sorted({oloc2(u, qi)[1] for u in range(2) for qi in range(nqt)})
            if "m" not in phases:
                finalize(O, names)

        def load_win():
            Wv = R1[:, 0:24576].rearrange("p (c n) -> p c n", n=3072)
            for c in range(8):
                for hh in range(2):
                    dmaL(Wv[:, c, hh * 1536:(hh + 1) * 1536], w_in.ap()[c * 128:(c + 1) * 128, hh * 1536:(hh + 1) * 1536],
                         writes=["W"], key="k_win", eng="pool")
            return Wv

        Wv = load_win()
        P.op("dve", lambda e: e.memset(eps6[:], 1e-6), writes=["eps"])
        P.op("dve", lambda e: e.memset(eps5[:], 1e-5), writes=["eps"])
        dmaL(idb[:], identd.ap(), writes=["idb"], key="k_idb", eng="pool")
        dmaL(Js[:], Jd.ap(), writes=["Js"], key="k_c0")
        dmaL(t5s[:], t5.ap(), writes=["t5s"], key="k_c1")
        P.op("dve", lambda e: e.memset(bts[:], 0.0), writes=["bts"])
        dmaL(bts[:, 0:2, :], bt.ap()[0:256, :].rearrange("(a p) h -> p a h", p=128), writes=["bts"], key="k_c2")
        dmaL(bts[0:1, 2, :], bt.ap()[256:257, :], writes=["bts"], key="k_c2")
        dmaL(CAs[:], CAd.ap(), writes=["CAs"], key="k_c3")
        dmaL(CBs[:], CBd.ap().rearrange("(a p) n -> p a n", p=128), writes=["CBs"], key="k_c4")
        dmaL(chA[:], bass.AP(t5, 15 * 4, [[0, 128], [1, 4]]), writes=["chA"], key="k_c5")
        dmaL(chB[:], bass.AP(bt, 0, [[0, 128], [1, 8]]), writes=["chB"], key="k_c6")
        dmaL(pmk[:], padmask.ap(), writes=["pmk"], key="k_c7")
        for i, lt in enumerate([lq1, lk1, lq2, lk2]):
            dmaL(lam4[:, i, :], bass.AP(lt, 0, [[0, 128], [1, 64]]), writes=["lam4"], key="k_c8")
        dmaL(sublnbc[:], bass.AP(subln, 0, [[0, 128], [1, 128]]), writes=["sublnbc"], key="k_c9")
        dmaL(onbbc[:], bass.AP(onb, 0, [[0, 128], [1, 512]]), writes=["onbbc"], key="k_c10")
        P.barrier(bar[:], skip=("q7",))
        P.op("dve", lambda e: e.tensor_scalar(out=sublnbc[:], in0=sublnbc[:], scalar1=0.8, scalar2=None, op0=ALU.mult),
             reads=["sublnbc"], writes=["sublnbc"])
        P.op("dve", lambda e: e.tensor_tensor(out=lam4[:, 0, :], in0=lam4[:, 0, :], in1=lam4[:, 1, :], op=ALU.mult),
             reads=["lam4"], writes=["lam4"])
        P.op("dve", lambda e: e.tensor_tensor(out=lam4[:, 2, :], in0=lam4[:, 2, :], in1=lam4[:, 3, :], op=ALU.mult),
             reads=["lam4"], writes=["lam4"])
        P.op("dve", lambda e: e.tensor_reduce(out=lcol[:, 0:1], in_=lam4[:, 0, :], axis=mybir.AxisListType.X, op=ALU.add),
             reads=["lam4"], writes=["lcol"])
        P.op("dve", lambda e: e.tensor_reduce(out=lcol[:, 1:2], in_=lam4[:, 2, :], axis=mybir.AxisListType.X, op=ALU.add),
             reads=["lam4"], writes=["lcol"])
        P.op("act", lambda e: e.activation(out=lcol[:, 2:4], in_=lcol[:, 0:2], func=AF.Exp), reads=["lcol"], writes=["lcol"])
        P.op("dve", lambda e: e.scalar_tensor_tensor(out=lcol[:, 4:5], in0=lcol[:, 3:4], scalar=-0.2, in1=lcol[:, 2:3],
                                                     op0=ALU.add, op1=ALU.subtract), reads=["lcol"], writes=["neglam"])
        neglam = lcol[:, 4:5]
        for h in range(4):
            P.op("dve", (lambda e, h=h: e.tensor_scalar(out=cmA[:, h, :], in0=pmk[:, 0:3], scalar1=chA[:, h:h + 1], scalar2=None,
                                                        op0=ALU.add)), reads=["pmk", "chA"], writes=["bias"])
        P.op("dve", lambda e: e.tensor_scalar(out=cmB[:], in0=chB[:], scalar1=pmk[:, 2:3], scalar2=None, op0=ALU.add),
             reads=["pmk", "chB"], writes=["bias"])
        pe_op(32, 4, lambda e: e.matmul(PS[0:4, 0, 0:384], lhsT=t5s[:], rhs=CAs[:], start=True, stop=True),
             reads=["t5s", "CAs"], writes=["PS0"])
        P.op("dve", lambda e: e.tensor_copy(out=vst[0:4, :], in_=PS[0:4, 0, 0:384]), reads=["PS0"], writes=["vst"])
        dmaL(vecA.ap(), vst[0:4, :], reads=["vst"], writes=["vecA"], key="k_v0")
        for a in range(3):
            pe_op(128, 8, (lambda e, a=a: e.matmul(PS[0:8, 1, 0:384], lhsT=bts[:, a, :], rhs=CBs[:, a, :], start=(a == 0), stop=(a == 2))),
                 reads=["bts", "CBs"], writes=["PS1"])
        P.op("dve", lambda e: e.tensor_copy(out=vst[0:8, :], in_=PS[0:8, 1, 0:384]), reads=["PS1", "vecA"], writes=["vst"])
        dmaL(vecB.ap(), vst[0:8, :], reads=["vst"], writes=["vecB"], key="k_v1")
        Hall = R2a[:, 4096:7168].rearrange("p (i n) -> p i n", n=128)
        hi = 0
        hlist = []
        for (vec, Tt, nh) in ((vecA, TA, 4), (vecB, TB, 8)):
            for h in range(nh):
                for kind, base in ((0, 128), (1, 0)):
                    hank = bass.AP(vec, h * 384 + base, [[1, 128], [1, 128]])
                    dmaL(Hall[:, hi, :], hank, reads=["vecA", "vecB"], writes=["Hall"], key="k_h")
                    hlist.append((hi, Tt, h, kind))
                    hi += 1
        for (hi, Tt, h, kind) in hlist:
            bk = 2 + hi % 2
            pe_op(128, 128, (lambda e, hi=hi, bk=bk: e.matmul(PS[:, bk, 0:128], lhsT=Hall[:, hi, :], rhs=Js[:], start=True, stop=True)),
                  reads=["Hall", "Js"], writes=["PS%d" % bk])
            P.op("dve", (lambda e, Tt=Tt, h=h, kind=kind, bk=bk: e.tensor_copy(out=Tt[:, h, kind, :], in_=PS[:, bk, 0:128])),
                 reads=["PS%d" % bk], writes=["Tt"], self_sync=False)
            if kind == 0:
                P.op("dve", (lambda e, Tt=Tt, h=h: e.memset(Tt[64:128, h, 0, 0:64], NEGM)), reads=["Tt"], writes=["Tt"])
        P.op("dve", lambda e: e.memset(Tm4[:], 0.0), writes=["Tt"])
        P.op("dve", lambda e: e.memset(Tm4[0:64, 64:128], NEGM), reads=["Tt"], writes=["Tt"])
        def weight_casts():
            for (src, dst, rows, colsn) in ((w_out, wout_b, 1024, 1024), (w_up, wup_b, 1024, 4096), (w_down, wdown_b, 4096, 1024),
                                           (w_gate, wgate_b, 1024, 1024), (w_ple, wple_b, 256, 1024)):
                sv = src.ap().rearrange("r (a n) -> (r a) n", n=1024)
                dv = dst.ap().rearrange("r (a n) -> (r a) n", n=1024)
                tot = rows * colsn // 1024
                for r0 in range(0, tot, 512):
                    n_ = min(512, tot - r0)
                    P.op("pool", (lambda e, r0=r0, n_=n_, dv=dv, sv=sv: e.dma_start(out=dv[r0:r0 + n_, :], in_=sv[r0:r0 + n_, :],
                                                                                    max_dma_last_dim=2048)),
                         writes=["wscr"], dma_key=KM["k_w"])


        P.barrier(bar[:], skip=("q7",))
        xb = R2a[:, 0:4096].rearrange("p (t f) -> p t f", f=1024)
        ostg = R2a[:, 4096:5120].rearrange("p (s f) -> p s f", f=512)
        obraw = R2a[:, 5120:7168].rearrange("p (t f) -> p t f", f=512)
        gattnbc = R2a[:, 7168:8192]
        xs = R2b[:, 0:4096].rearrange("p (t f) -> p t f", f=1024)
        xsT = R2b[:, 4096:8192].rearrange("p (c n) -> p c n", n=512)
        KTst = R2b[:, 8192:10240].rearrange("p (h n) -> p h n", n=512)
        VAst = R2b[:, 10240:12288].rearrange("p (t n) -> p t n", n=512)
        KBT = R2b[:, 12288:16384].rearrange("p (s c n) -> p s c n", s=2, n=512)
        VBa = R2b[:, 16384:20608].rearrange("p (s t h e) -> p s t h e", s=2, t=4, e=66)
        QBz = R2b[:, 20608:24704].rearrange("p (u c n) -> p u c n", u=2, n=512)
        QAst = R2b[:, 24704:26752].rearrange("p (h n) -> p h n", n=512)
        OBst = R2b[:, 26752:28800].rearrange("p (t n) -> p t n", n=512)
        P.op("pool", lambda e: e.memset(QBz, 0.0), writes=["QBT"])
        dmaL(gattnbc, bass.AP(g_attn, 0, [[0, 128], [1, 1024]]), writes=["gattnbc"], key="k_c11")
        P.op("dve", lambda e: e.memset(VBa[:, :, :, :, 64:66], 1.0), writes=["VBones"])

        def out_store(dst_ap, psum_ap, psname, rows=128):
            s = cnt["ost"] % 2
            cnt["ost"] += 1
            evac(ostg[0:rows, s, :], psum_ap, reads=[psname], writes=["ostg%d" % s])
            dmaO(dst_ap, ostg[0:rows, s, :], reads=["ostg%d" % s], key="k_ost%d" % s)
            return ostg[0:rows, s, :], "ostg%d" % s

        def band_attention(p, I):
            sp_, so_ = (p - 1) % 2, p % 2
            qtl = [(i * 128, 128) for i in range(4)]
            for cb in range(4):
                kts = []
                for r in range(-4, 4):
                    slot, tk = (sp_, r + 4) if r < 0 else (so_, r)
                    qlo, qhi = max(0, r), min(3, r + 4)
                    adds = []
                    for qi in range(qlo, qhi + 1):
                        rel = r - qi
                        if rel == 0:
                            adds.append((qi, [TB[:, 2 * cb + u, 0, :] for u in range(2)]))
                        elif rel == -1:
                            adds.append((qi, [TB[:, 2 * cb + u, 1, :] for u in range(2)]))
                        elif rel == -4:
                            adds.append((qi, [Tm4[:], Tm4[:]]))
                    bsrc = cmB if (p == 3 and r < 0) else chB
                    kts.append(dict(KT=KBT[:, slot, cb, tk * 128:(tk + 1) * 128], nk=128,
                                    V=[VBa[:, slot, tk, 2 * cb + u, 0:65] for u in range(2)],
                                    bias=[bsrc[:, 2 * cb + u:2 * cb + u + 1] for u in range(2)],
                                    qlo=qlo, qhi=qhi, adds=adds, reads=["KBT%d" % slot, "VB%d" % slot, "VBones"]))

                def fin(O, names, cb=cb):
                    for u in range(2):
                        hb = 2 * cb + u
                        for qi in range(4):
                            ci = 8 + (qi * 2 + u)
                            P.op("dve", (lambda e, u=u, qi=qi, ci=ci: e.reciprocal(out=cols[:, ci:ci + 1], in_=O[u][qi][:, 64:65])),
                                 reads=names, writes=["c%d" % ci])
                            P.op("dve", (lambda e, u=u, qi=qi, ci=ci, hb=hb: e.tensor_scalar(
                                out=obraw[:, qi, hb * 64:(hb + 1) * 64], in0=O[u][qi][:, 0:64], scalar1=cols[:, ci:ci + 1],
                                scalar2=None, op0=ALU.mult)), reads=names + ["c%d" % ci], writes=["obraw"])
                pair_attn(kts, [QBz[:, 0, cb, :], QBz[:, 1, cb, :]], qtl, 64, fin, ["QBT"])
            for qi in range(4):
                rc, rn = rms_rstd(obraw[:, qi, :], 128, 512, eps6, 16 + 3 * qi, ["obraw"], "ob")
                P.op("dve", (lambda e, qi=qi, rc=rc: e.scalar_tensor_tensor(out=OBst[:, qi, :], in0=obraw[:, qi, :], scalar=rc,
                                                                              in1=onbbc[:], op0=ALU.mult, op1=ALU.mult)),
                     reads=["obraw", rn, "onbbc"], writes=["OBst"])
            dmaL(OBNs.ap()[I * 512:(I + 1) * 512, :].rearrange("(t p) n -> p t n", p=128), OBst, reads=["OBst"], writes=["OBNs"],
                 key="k_obst", eng="pool")

        def phaseA_block(p):
            own = (p % 4 == 3)
            I = p // 4
            last = (p == NBLK - 1)
            so_ = p % 2
            for t in range(4):
                dmaL(xb[:, t, :], xv.ap()[p * 512 + t * 128:p * 512 + (t + 1) * 128, :], writes=["xb%d" % t], key="k_xb%d" % t)
            for t in range(4):
                rc, rn = rms_rstd(xb[:, t, :], 128, 1024, eps6, 3 * t, ["xb%d" % t], "x")
                P.op("dve", (lambda e, t=t, rc=rc: e.scalar_tensor_tensor(out=xs[:, t, :], in0=xb[:, t, :], scalar=rc, in1=gattnbc,
                                                                            op0=ALU.mult, op1=ALU.mult)),
                     reads=["xb%d" % t, rn, "gattnbc"], writes=["xs%d" % t])
            for t in range(4):
                transposes(xs[:, t, :], 128, 8, xsT[:, :, t * 128:(t + 1) * 128], ["xs%d" % t], ["xsT"])
            for h in range(4):
                ps_, nm = proj_fm(lambda c, h=h: Wv[:, c, 512 + h * 128:512 + (h + 1) * 128], xsT, 512, ["W", "xsT"])
                evac(KTst[:, h, :], ps_, reads=[nm], writes=["KTst"])
            dmaL(KTs.ap().rearrange("h p n -> p h n")[:, :, p * 512:(p + 1) * 512], KTst, reads=["KTst"], writes=["KTs"],
                 key="k_ktst", eng="pool")
            needb = (p % 4 >= 2)
            for cb in (range(4) if needb else ()):
                ps_, nm = proj_fm(lambda c, cb=cb: Wv[:, c, 2048 + cb * 128:2048 + (cb + 1) * 128], xsT, 512, ["W", "xsT"])
                evac(KBT[:, so_, cb, :], ps_, reads=[nm], writes=["KBT%d" % so_])
            if own and "3" not in phases:
                for h in range(4):
                    ps_, nm = proj_fm(lambda c, h=h: Wv[:, c, h * 128:(h + 1) * 128], xsT, 512, ["W", "xsT"])
                    evac(QAst[:, h, :], ps_, reads=[nm], writes=["QAst"], scale=0.125)
                dmaL(QTs.ap().rearrange("h p n -> p h n")[:, :, I * 512:(I + 1) * 512], QAst, reads=["QAst"], writes=["QTs"],
                     key="k_qast", eng="pool")
                for cb in range(4):
                    ps_, nm = proj_fm(lambda c, cb=cb: Wv[:, c, 1536 + cb * 128:1536 + (cb + 1) * 128], xsT, 512, ["W", "xsT"])
                    evac(QBz[:, 0, cb, :], ps_, reads=[nm], writes=["QBT"], scale=0.125)
                    P.op("pool", (lambda e, cb=cb: e.tensor_copy(out=QBz[64:128, 1, cb, :], in_=QBz[64:128, 0, cb, :])),
                         reads=["QBT"], writes=["QBT"])
                    P.op("pool", (lambda e, cb=cb: e.memset(QBz[64:128, 0, cb, :], 0.0)), reads=["QBT"], writes=["QBT"])
            for t in range(4):
                ps_, nm = proj_tm(xsT, t * 128, 128, lambda c: Wv[:, c, 1024:1536], ["W", "xsT"])
                if own:
                    sa, sn = out_store(nav.ap()[I * 512 + t * 128:I * 512 + (t + 1) * 128, :], ps_, nm)
                    P.op("pool", (lambda e, t=t, sa=sa: e.tensor_copy(out=VAst[:, t, :], in_=sa)), reads=[sn], writes=["VAst"])
                else:
                    evac(VAst[:, t, :], ps_, reads=[nm], writes=["VAst"])
                if not needb:
                    continue
                ps_, nm = proj_tm(xsT, t * 128, 128, lambda c: Wv[:, c, 2560:3072], ["W", "xsT"])
                if last:
                    sa, sn = out_store(nbv.ap()[t * 128:(t + 1) * 128, :], ps_, nm)
                    P.op("pool", (lambda e, t=t, sa=sa: e.tensor_copy(out=VBa[:, so_, t, :, 0:64],
                                                                      in_=sa.rearrange("p (h e) -> p h e", e=64))),
                         reads=[sn], writes=["VB%d" % so_])
                else:
                    evac(VBa[:, so_, t, :, 0:64], ps_.rearrange("p (h e) -> p h e", e=64), reads=[nm], writes=["VB%d" % so_])
                if own:
                    ps_, nm = proj_tm(xsT, t * 128, 128, lambda c: Wv[:, c, 512:1024], ["W", "xsT"])
                    out_store(nak.ap()[I * 512 + t * 128:I * 512 + (t + 1) * 128, :], ps_, nm)
                if last:
                    ps_, nm = proj_tm(xsT, t * 128, 128, lambda c: Wv[:, c, 2048:2560], ["W", "xsT"])
                    out_store(nbk.ap()[t * 128:(t + 1) * 128, :], ps_, nm)
            dmaL(VAs.ap()[p * 512:(p + 1) * 512, :].rearrange("(t p) n -> p t n", p=128), VAst, reads=["VAst"], writes=["VAs"],
                 key="k_vast", eng="pool")
            if own and "1" not in phases:
                band_attention(p, I)

        if "A" in phases:
            for p in range(NBLK):
                phaseA_block(p)
        if "a" in phases:
            for p in range(4):
                phaseA_block(p)
        if "e" in phases:
            for p in range(3):
                phaseA_block(p)
        P.barrier(bar[:])

        KTh = R1[:, 0:16384]
        Vaug = R1[:, 16384:33024].rearrange("p (t e) -> p t e", e=130)
        OAN = R2b[:, 0:16384].rearrange("p (t n) -> p t n", n=512)
        QTz = R2b[:, 16384:24576].rearrange("p (u n) -> p u n", u=2)

        def finA_factory(h, dst_fn, rows):
            def fin(O, names):
                nqt = len(O[0])
                for qi in range(nqt):
                    pr = qi % 2
                    cb_ = 28 + 8 * pr
                    P.op("dve", (lambda e, qi=qi, cb_=cb_: e.reciprocal(out=cols[0:rows, cb_:cb_ + 1], in_=O[0][qi][:, 128:129])),
                         reads=names, writes=["fa%d" % pr])
                    P.op("dve", (lambda e, qi=qi, cb_=cb_: e.reciprocal(out=cols[0:rows, cb_ + 1:cb_ + 2], in_=O[1][qi][:, 128:129])),
                         reads=names + ["fa%d" % pr], writes=["fa%d" % pr])
                    P.op("dve", (lambda e, cb_=cb_: e.tensor_scalar(out=cols[0:rows, cb_ + 2:cb_ + 3], in0=cols[0:rows, cb_ + 1:cb_ + 2],
                                                                   scalar1=neglam[0:rows, :], scalar2=None, op0=ALU.mult)),
                         reads=["fa%d" % pr, "neglam"], writes=["fa%d" % pr])
                    P.op("dve", (lambda e, qi=qi, cb_=cb_, pr=pr: e.tensor_scalar(out=osm[0:rows, pr, 0, :], in0=O[1][qi][:, 0:128],
                                                                                  scalar1=cols[0:rows, cb_ + 2:cb_ + 3], scalar2=None,
                                                                                  op0=ALU.mult)),
                         reads=names + ["fa%d" % pr], writes=["osm%d" % pr])
                    P.op("dve", (lambda e, qi=qi, cb_=cb_, pr=pr: e.scalar_tensor_tensor(
                        out=osm[0:rows, pr, 1, :], in0=O[0][qi][:, 0:128], scalar=cols[0:rows, cb_:cb_ + 1], in1=osm[0:rows, pr, 0, :],
                        op0=ALU.mult, op1=ALU.add)), reads=names + ["fa%d" % pr, "osm%d" % pr], writes=["osm%d" % pr])
                    ci = cb_ + 3
                    P.op("dve", (lambda e, pr=pr, ci=ci: e.scalar_tensor_tensor(
                        out=osm[0:rows, pr, 0, :], in0=osm[0:rows, pr, 1, :], scalar=1.0, in1=osm[0:rows, pr, 1, :],
                        op0=ALU.mult, op1=ALU.mult, accum_out=cols[0:rows, ci:ci + 1])),
                        reads=["osm%d" % pr], writes=["osm%d" % pr, "fb%d" % pr])
                    P.op("act", (lambda e, ci=ci: e.activation(out=cols[0:rows, ci + 1:ci + 2], in_=cols[0:rows, ci:ci + 1], func=AF.Ln,
                                                              bias=eps5[0:rows, 0:1], scale=1.0 / 128)),
                         reads=["fb%d" % pr, "eps"], writes=["fb%d" % pr])
                    P.op("act", (lambda e, ci=ci: e.activation(out=cols[0:rows, ci + 2:ci + 3], in_=cols[0:rows, ci + 1:ci + 2],
                                                              func=AF.Exp, scale=-0.5)),
                         reads=["fb%d" % pr], writes=["fb%d" % pr])
                    dst, dnm = dst_fn(qi)
                    P.op("dve", (lambda e, pr=pr, ci=ci, dst=dst: e.scalar_tensor_tensor(
                        out=dst, in0=osm[0:rows, pr, 1, :], scalar=cols[0:rows, ci + 2:ci + 3], in1=sublnbc[0:rows, :],
                        op0=ALU.mult, op1=ALU.mult)), reads=["osm%d" % pr, "fb%d" % pr, "sublnbc"], writes=[dnm])
            return fin

        def phaseD1():
            xb = R2a[:, 0:1024]
            ostg = R2a[:, 4096:5120].rearrange("p (s f) -> p s f", f=512)
            obraw = R2a[:, 1024:1536]
            o = 0

            def carve(n):
                nonlocal o
                a = R2b[:, o:o + n]
                o += n
                return a
            oanl = carve(512)
            xs = carve(1024)
            xsT = carve(8 * 64).rearrange("p (c n) -> p c n", n=64)
            cst = carve(8 * 512).rearrange("p (t n) -> p t n", n=512)
            KTc = carve(4 * 1056).rearrange("p (h n) -> p h n", n=1056)
            Vc = carve(9 * 4 * 130).rearrange("p (t h e) -> p t h e", h=4, e=130)
            KBc = carve(4 * 544).rearrange("p (c n) -> p c n", n=544)
            VBc = carve(5 * 8 * 66).rearrange("p (t h e) -> p t h e", h=8, e=66)
            QAz = carve(2 * 4 * 64).rearrange("p (u h n) -> p u h n", u=2, n=64)
            QBzs = carve(2 * 4 * 64).rearrange("p (u c n) -> p u c n", u=2, n=64)
            P.op("pool", lambda e: e.memset(QAz, 0.0), writes=["QAs"])
            P.op("pool", lambda e: e.memset(QBzs, 0.0), writes=["QBs"])
            oans = carve(512)
            obns = carve(512)
            P.op("dve", lambda e: e.memset(Vc[:, :, :, 128:130], 1.0), writes=["Vc1"])
            P.op("dve", lambda e: e.memset(VBc[:, :, :, 64:66], 1.0), writes=["VBc1"])
            dmaL(xb[0:64, :], xsm.ap(), writes=["xb"], key="k_xb")
            rc, rn = rms_rstd(xb[0:64, :], 64, 1024, eps6, 0, ["xb"], "x")
            P.op("dve", (lambda e, rc=rc: e.scalar_tensor_tensor(out=xs[0:64, :], in0=xb[0:64, :], scalar=rc, in1=gattnbc[0:64, :],
                                                                  op0=ALU.mult, op1=ALU.mult)), reads=["xb", rn, "gattnbc"], writes=["xs"])
            transposes(xs[0:64, :], 64, 8, xsT[:, :, 0:64], ["xs"], ["xsT"])
            for (c0, dst) in ((512, sak), (1024, sav), (2048, sbk), (2560, sbv)):
                ps_, nm = proj_tm(xsT, 0, 64, lambda c, c0=c0: Wv[:, c, c0:c0 + 512], ["W", "xsT"])
                out_store(dst.ap(), ps_, nm, rows=64)
            for h in range(4):
                ps_, nm = proj_fm(lambda c, h=h: Wv[:, c, h * 128:(h + 1) * 128], xsT, 64, ["W", "xsT"])
                evac(QAz[:, 0, h, :], ps_, reads=[nm], writes=["QAs"], scale=0.125)
                P.op("pool", (lambda e, h=h: e.tensor_copy(out=QAz[64:128, 1, h, :], in_=QAz[64:128, 0, h, :])), reads=["QAs"], writes=["QAs"])
                P.op("pool", (lambda e, h=h: e.memset(QAz[64:128, 0, h, :], 0.0)), reads=["QAs"], writes=["QAs"])
                ps_, nm = proj_fm(lambda c, h=h: Wv[:, c, 1536 + h * 128:1536 + (h + 1) * 128], xsT, 64, ["W", "xsT"])
                evac(QBzs[:, 0, h, :], ps_, reads=[nm], writes=["QBs"], scale=0.125)
                P.op("pool", (lambda e, h=h: e.tensor_copy(out=QBzs[64:128, 1, h, :], in_=QBzs[64:128, 0, h, :])), reads=["QBs"], writes=["QBs"])
                P.op("pool", (lambda e, h=h: e.memset(QBzs[64:128, 0, h, :], 0.0)), reads=["QBs"], writes=["QBs"])
            for s in range(2):
                for h in range(4):
                    ps_, nm = proj_fm(lambda c, h=h: Wv[:, c, 512 + h * 128:512 + (h + 1) * 128], xsT[:, :, s * 32:(s + 1) * 32], 32,
                                      ["W", "xsT"])
                    evac(KTc[:, h, 1024:1056], ps_, reads=[nm], writes=["KTc"])
                    ps_, nm = proj_fm(lambda c, h=h: Wv[:, c, 2048 + h * 128:2048 + (h + 1) * 128], xsT[:, :, s * 32:(s + 1) * 32], 32,
                                      ["W", "xsT"])
                    evac(KBc[:, h, 512:544], ps_, reads=[nm], writes=["KBc"])
                ps_, nm = proj_tm(xsT, s * 32, 32, lambda c: Wv[:, c, 1024:1536], ["W", "xsT"])
                evac(Vc[0:32, 8, :, 0:128], ps_.rearrange("p (h e) -> p h e", e=128), reads=[nm], writes=["Vc"])
                ps_, nm = proj_tm(xsT, s * 32, 32, lambda c: Wv[:, c, 2560:3072], ["W", "xsT"])
                evac(VBc[0:32, 4, :, 0:64], ps_.rearrange("p (h e) -> p h e", e=64), reads=[nm], writes=["VBc"])
                dmaL(cst, cak.ap()[s * 1024:(s + 1) * 1024, :].rearrange("(t p) n -> p t n", p=128), writes=["cst"], key="k_cst",
                     eng="pool")
                for t in range(8):
                    transposes(cst[:, t, :], 128, 4, KTc[:, :, t * 128:(t + 1) * 128], ["cst"], ["KTc"])
                dmaL(cst[:, 0:4, :], cbk.ap()[s * 512:(s + 1) * 512, :].rearrange("(t p) n -> p t n", p=128), writes=["cst"],
                     key="k_cst", eng="pool")
                for t in range(4):
                    transposes(cst[:, t, :], 128, 4, KBc[:, :, t * 128:(t + 1) * 128], ["cst"], ["KBc"])
                for h_ in range(4):
                    dmaL(Vc[:, 0:8, h_, 0:128],
                         cav.ap()[s * 1024:(s + 1) * 1024, h_ * 128:(h_ + 1) * 128].rearrange("(t p) e -> p t e", p=128),
                         reads=["Vc1"], writes=["Vc"], key="k_vc", eng="pool")
                for h_ in range(8):
                    dmaL(VBc[:, 0:4, h_, 0:64],
                         cbv.ap()[s * 512:(s + 1) * 512, h_ * 64:(h_ + 1) * 64].rearrange("(t p) e -> p t e", p=128),
                         reads=["VBc1"], writes=["VBc"], key="k_vbc", eng="pool")
                for h in range(4):
                    kts = []
                    for t in range(9):
                        nk = 128 if t < 8 else 32
                        adds = []
                        if t == 7:
                            adds.append((0, [TA[:, h, 1, 0:32]] * 2))
                        if t == 8:
                            adds.append((0, [TA[0:32, h, 0, 0:32]] * 2))
                        bcol = chA[0:nk, h:h + 1]
                        kts.append(dict(KT=KTc[:, h, t * 128:t * 128 + nk], nk=nk, V=[Vc[0:nk, t, h, 0:129]] * 2, bias=[bcol, bcol],
                                        qlo=0, qhi=0, adds=adds, reads=["KTc", "Vc", "Vc1"]))
                    fin = finA_factory(h, lambda qi, h=h: (oans[0:32, h * 128:(h + 1) * 128], "oans"), 32)
                    pair_attn(kts, [QAz[:, u_, h, s * 32:(s + 1) * 32] for u_ in range(2)], [(0, 32)], 128, fin, ["QAs"])
                dmaL(OANss.ap()[s * 32:(s + 1) * 32, :], oans[0:32, :], reads=["oans"], writes=["OANss"], key="k_oans", eng="pool")
                for cb in range(4):
                    kts = []
                    for t in range(5):
                        nk = 128 if t < 4 else 32
                        adds = []
                        if t == 3:
                            adds.append((0, [TB[:, 2 * cb + u, 1, 0:32] for u in range(2)]))
                        if t == 4:
                            adds.append((0, [TB[0:32, 2 * cb + u, 0, 0:32] for u in range(2)]))
                        kts.append(dict(KT=KBc[:, cb, t * 128:t * 128 + nk], nk=nk,
                                        V=[VBc[0:nk, t, 2 * cb + u, 0:65] for u in range(2)],
                                        bias=[chB[0:nk, 2 * cb + u:2 * cb + u + 1] for u in range(2)],
                                        qlo=0, qhi=0, adds=adds, reads=["KBc", "VBc", "VBc1"]))

                    def finb(O, names, cb=cb):
                        for u in range(2):
                            hb = 2 * cb + u
                            ci = 8 + u
                            P.op("dve", (lambda e, u=u, ci=ci: e.reciprocal(out=cols[0:32, ci:ci + 1], in_=O[u][0][:, 64:65])),
                                 reads=names, writes=["c%d" % ci])
                            P.op("dve", (lambda e, u=u, ci=ci, hb=hb: e.tensor_scalar(
                                out=obraw[0:32, hb * 64:(hb + 1) * 64], in0=O[u][0][:, 0:64], scalar1=cols[0:32, ci:ci + 1],
                                scalar2=None, op0=ALU.mult)), reads=names + ["c%d" % ci], writes=["obraw"])
                    pair_attn(kts, [QBzs[:, u_, cb, s * 32:(s + 1) * 32] for u_ in range(2)], [(0, 32)], 64, finb, ["QBs"])
                rc, rn = rms_rstd(obraw[0:32, :], 32, 512, eps6, 16, ["obraw"], "ob")
                P.op("dve", (lambda e, rc=rc: e.scalar_tensor_tensor(out=obns[0:32, :], in0=obraw[0:32, :], scalar=rc, in1=onbbc[0:32, :],
                                                                      op0=ALU.mult, op1=ALU.mult)),
                     reads=["obraw", rn, "onbbc"], writes=["obns"])
                dmaL(OBNss.ap()[s * 32:(s + 1) * 32, :], obns[0:32, :], reads=["obns"], writes=["OBNs"], key="k_obns", eng="pool")
        if "D" in phases:
            phaseD1()
            P.barrier(bar[:])

        weight_casts()
        osbB = [(R2a[:, 0:1536].rearrange("p (b n) -> p b n", n=512), "osbA"),
                (R2a[:, 1536:3072].rearrange("p (b n) -> p b n", n=512), "osbB")]
        if "B" in phases:
            P.op("dve", lambda e: e.memset(Vaug[:, :, 128:130], 1.0), writes=["Vones"])
            P.op("dve", lambda e: e.memset(QTz, 0.0), writes=["QTh", "QTh0"])
            NCH = 4
            for h in range(4):
                for ch in range(NCH):
                    k0 = ch * (SEQV // NCH)
                    k1 = (ch + 1) * (SEQV // NCH)
                    dmaL(KTh[:, k0:k1], KTs.ap()[h, :, k0:k1], reads=["Vones"], writes=["KTh%d" % ch], key="k_kth%d" % ch)
                    dmaL(Vaug[:, k0 // 128:k1 // 128, 0:128],
                         VAs.ap()[k0:k1, h * 128:(h + 1) * 128].rearrange("(t p) e -> p t e", p=128),
                         reads=["Vones"], writes=["Vh%d" % ch], key="k_vh%d" % ch)
                for u_ in range(2):
                    dmaL(QTz[64 * u_:64 * u_ + 64, u_, 0:NTOK], QTs.ap()[h, 64 * u_:64 * u_ + 64, :], reads=["QTh0"], writes=["QTh"],
                         key="k_qth")
                for I in range(NOWN):
                    p = 4 * I + 3
                    kts = []
                    for kt in range(4 * p + 4):
                        r = kt - 4 * p
                        qlo = max(0, r)
                        adds = []
                        if r >= 0:
                            adds.append((r, [TA[:, h, 0, :]] * 2))
                            if r + 1 <= 3:
                                adds.append((r + 1, [TA[:, h, 1, :]] * 2))
                        elif r == -1:
                            adds.append((0, [TA[:, h, 1, :]] * 2))
                        bcol = cmA[:, h, kt // 4:kt // 4 + 1] if kt < 12 else chA[:, h:h + 1]
                        ch = kt * 128 // (SEQV // NCH)
                        kts.append(dict(KT=KTh[:, kt * 128:(kt + 1) * 128], nk=128, V=[Vaug[:, kt, 0:129]] * 2, bias=[bcol, bcol],
                                        qlo=qlo, qhi=3, adds=adds, reads=["KTh%d" % ch, "Vh%d" % ch, "Vones"]))
                    fin = finA_factory(h, lambda qi, I=I, h=h: (OAN[:, I * 4 + qi, h * 128:(h + 1) * 128], "OAN"), 128)
                    pair_attn(kts, [QTz[:, u_, I * 512:(I + 1) * 512] for u_ in range(2)], [(i * 128, 128) for i in range(4)], 128, fin,
                              ["QTh"], osb=osbB)
        P.barrier(bar[:])

        ACT_T = R1[:, 0:16384].rearrange("p (h n) -> p h n", n=512)
        WR = R1[:, 16384:32768].rearrange("p (s n) -> p s n", n=4096)
        hres = R2a[:, 0:4096].rearrange("p (t f) -> p t f", f=1024)
        p_s = R2a[:, 4096:5120].rearrange("p (t f) -> p t f", f=256)
        gs = R2a[:, 5120:5632]
        tmpf = R2a[:, 5632:6144]
        gmlpbc = R2a[:, 6144:7168]
        gfinbc = R2a[:, 7168:8192]
        cs = R2b[:, 16384:20480].rearrange("p (t f) -> p t f", f=1024)
        aT = R2b[:, 20480:24576].rearrange("p (c n) -> p c n", n=512)
        obn_s = R2b[:, 24576:26624].rearrange("p (t n) -> p t n", n=512)
        pb = R2b[:, 26624:27648].rearrange("p (t f) -> p t f", f=256)
        pT = R2b[:, 27648:28672].rearrange("p (c n) -> p c n", n=512)
        wcnt = {"i": 0}

        def wload(src_ap, shape_view):
            s = wcnt["i"] % 4
            wcnt["i"] += 1
            dst = shape_view(WR[:, s, :])
            dmaL(dst, src_ap, reads=["wscr"], writes=["WR%d" % s], key="k_wr%d" % s)
            return dst, "WR%d" % s

        def phaseC_group(tiles, x_ap, p_ap, oan_fn, obn_src, y_ap):
            NT = tiles[-1][0] + tiles[-1][1]
            nt = len(tiles)
            HR = ["hres%d" % t_ for t_ in range(nt)]
            PSN = ["p_s%d" % t_ for t_ in range(nt)]
            ATN = ["aT%d" % t_ for t_ in range(nt)]
            rows0 = tiles[0][1]
            if rows0 == 128:
                dmaL(hres[:, 0:nt, :], x_ap.rearrange("(t p) f -> p t f", p=128), writes=HR, key="k_hres")
                dmaL(p_s[:, 0:nt, :], p_ap.rearrange("(t p) f -> p t f", p=128), writes=PSN, key="k_ps")
                dmaL(obn_s[:, 0:nt, :], obn_src.rearrange("(t p) f -> p t f", p=128), reads=["OBNs"], writes=["obn_s"], key="k_obn")
            else:
                dmaL(hres[0:rows0, 0, :], x_ap, writes=HR, key="k_hres")
                dmaL(p_s[0:rows0, 0, :], p_ap, writes=PSN, key="k_ps")
                dmaL(obn_s[0:rows0, 0, :], obn_src, reads=["OBNs"], writes=["obn_s"], key="k_obn")
            for ti, (tok0, rows) in enumerate(tiles):
                oa, oan_names = oan_fn(ti)
                transposes(oa, rows, 4, aT[:, 0:4, tok0:tok0 + rows], oan_names, ["aT%d" % ti])
                transposes(obn_s[0:rows, ti, :], rows, 4, aT[:, 4:8, tok0:tok0 + rows], ["obn_s"], ["aT%d" % ti])
            wo = [wload(wout_b.ap()[j * 512:(j + 1) * 512, :].rearrange("(c p) n -> p c n", p=128),
                        lambda v: v.rearrange("p (c n) -> p c n", n=1024)) for j in range(2)]
            for ti, (tok0, rows) in enumerate(tiles):
                for hf in range(2):
                    ps_, nm = proj_tm(aT, tok0, rows, lambda c, hf=hf: wo[c // 4][0][:, c % 4, hf * 512:(hf + 1) * 512],
                                      ["aT%d" % ti, wo[0][1], wo[1][1]])
                    P.op("dve", (lambda e, ti=ti, rows=rows, hf=hf, ps_=ps_: e.tensor_tensor(
                        out=hres[0:rows, ti, hf * 512:(hf + 1) * 512], in0=ps_, in1=hres[0:rows, ti, hf * 512:(hf + 1) * 512], op=ALU.add)),
                        reads=[nm, "hres%d" % ti], writes=["hres%d" % ti], self_sync=False)
            for ti, (tok0, rows) in enumerate(tiles):
                rc, rn = rms_rstd(hres[0:rows, ti, :], rows, 1024, eps6, 3 * ti, ["hres%d" % ti], "c")
                P.op("dve", (lambda e, ti=ti, rows=rows, rc=rc: e.scalar_tensor_tensor(out=cs[0:rows, ti, :], in0=hres[0:rows, ti, :],
                                                                                       scalar=rc, in1=gmlpbc[0:rows, :], op0=ALU.mult,
                                                                                       op1=ALU.mult)),
                     reads=["hres%d" % ti, rn, "gmlpbc"], writes=["cs%d" % ti])
                transposes(cs[0:rows, ti, :], rows, 8, aT[:, :, tok0:tok0 + rows], ["cs%d" % ti], ["aT%d" % ti])
            for j in range(8):
                wu, wn = wload(wup_b.ap()[:, j * 512:(j + 1) * 512].rearrange("(c p) n -> p c n", p=128),
                               lambda v: v.rearrange("p (c n) -> p c n", n=512))
                for hl in range(4):
                    hc = j * 4 + hl
                    ps_, nm = proj_fm(lambda c, hl=hl, wu=wu: wu[:, c, hl * 128:(hl + 1) * 128], aT, NT, ATN + [wn])
                    rb, rbn = (tmpf, "tmpf") if hc % 2 else (gs, "gs")
                    P.op("act", (lambda e, ps_=ps_, rb=rb: e.activation(out=rb[:, 0:NT], in_=ps_, func=AF.Relu)),
                         reads=[nm], writes=[rbn])
                    P.op("pool", (lambda e, hc=hc, rb=rb: e.tensor_tensor(out=ACT_T[:, hc, 0:NT], in0=rb[:, 0:NT], in1=rb[:, 0:NT],
                                                                          op=ALU.mult)), reads=[rbn], writes=["ACT_T"])
            for hf in range(2):
                accs = []
                for ti in range(nt):
                    accs.append(ti)
                for j in range(4):
                    wd, wn = wload(wdown_b.ap()[j * 1024:(j + 1) * 1024, hf * 512:(hf + 1) * 512].rearrange("(c p) n -> p c n", p=128),
                                   lambda v: v.rearrange("p (c n) -> p c n", n=512))
                    for hl in range(8):
                        hc = j * 8 + hl
                        for ti, (tok0, rows) in enumerate(tiles):
                            pe_op(128, rows, (lambda e, ti=ti, tok0=tok0, rows=rows, hc=hc, hl=hl, wd=wd: e.matmul(
                                PS[0:rows, ti, :], lhsT=ACT_T[:, hc, tok0:tok0 + rows], rhs=wd[:, hl, :], start=(hc == 0), stop=(hc == 31))),
                                reads=["ACT_T", wn], writes=["PS%d" % ti])
                for ti, (tok0, rows) in enumerate(tiles):
                    P.op("dve", (lambda e, ti=ti, rows=rows, hf=hf: e.tensor_tensor(
                        out=hres[0:rows, ti, hf * 512:(hf + 1) * 512], in0=PS[0:rows, ti, :], in1=hres[0:rows, ti, hf * 512:(hf + 1) * 512],
                        op=ALU.add)), reads=["PS%d" % ti, "hres%d" % ti], writes=["hres%d" % ti], self_sync=False)
            for ti, (tok0, rows) in enumerate(tiles):
                P.op("act", (lambda e, ti=ti, rows=rows: e.activation(out=cs[0:rows, ti, :], in_=hres[0:rows, ti, :], func=AF.Copy, scale=1.0)),
                     reads=["hres%d" % ti], writes=["cs%d" % ti])
                transposes(cs[0:rows, ti, :], rows, 8, aT[:, :, tok0:tok0 + rows], ["cs%d" % ti], ["aT%d" % ti])
                P.op("dve", (lambda e, ti=ti, rows=rows: e.tensor_copy(out=pb[0:rows, ti, :], in_=p_s[0:rows, ti, :])),
                     reads=["p_s%d" % ti], writes=["pb%d" % ti])
                transposes(pb[0:rows, ti, :], rows, 2, pT[:, :, tok0:tok0 + rows], ["pb%d" % ti], ["pT%d" % ti])
            wg = [wload(wgate_b.ap()[j * 512:(j + 1) * 512, :].rearrange("(c p) n -> p c n", p=128),
                        lambda v: v.rearrange("p (c n) -> p c n", n=1024)) for j in range(2)]
            wp, wpn = wload(wple_b.ap().rearrange("(c p) n -> p c n", p=128),
                            lambda v: v[:, 0:2048].rearrange("p (c n) -> p c n", n=1024))
            for ti, (tok0, rows) in enumerate(tiles):
                for hf in range(2):
                    ps_, nm = proj_tm(aT, tok0, rows, lambda c, hf=hf: wg[c // 4][0][:, c % 4, hf * 512:(hf + 1) * 512],
                                      ["aT%d" % ti, wg[0][1], wg[1][1]])
                    P.op("act", (lambda e, rows=rows, ps_=ps_: e.activation(out=gs[0:rows, :], in_=ps_, func=AF.Sigmoid)),
                         reads=[nm], writes=["gs"])
                    ps2, nm2 = proj_tm(pT, tok0, rows, lambda c, hf=hf: wp[:, c, hf * 512:(hf + 1) * 512], ["pT%d" % ti, wpn], nk=2)
                    P.op("dve", (lambda e, rows=rows, ps2=ps2: e.tensor_tensor(out=tmpf[0:rows, :], in0=gs[0:rows, :], in1=ps2, op=ALU.mult)),
                         reads=["gs", nm2], writes=["tmpf"])
                    P.op("dve", (lambda e, ti=ti, rows=rows, hf=hf: e.tensor_tensor(
                        out=hres[0:rows, ti, hf * 512:(hf + 1) * 512], in0=tmpf[0:rows, :], in1=hres[0:rows, ti, hf * 512:(hf + 1) * 512],
                        op=ALU.add)), reads=["tmpf", "hres%d" % ti], writes=["hres%d" % ti])
            for ti, (tok0, rows) in enumerate(tiles):
                rc, rn = rms_rstd(hres[0:rows, ti, :], rows, 1024, eps6, 3 * ti, ["hres%d" % ti], "f")
                P.op("dve", (lambda e, ti=ti, rows=rows, rc=rc: e.scalar_tensor_tensor(out=hres[0:rows, ti, :], in0=hres[0:rows, ti, :],
                                                                                       scalar=rc, in1=gfinbc[0:rows, :], op0=ALU.mult,
                                                                                       op1=ALU.mult)),
                     reads=["hres%d" % ti, rn, "gfinbc"], writes=["hres%d" % ti])
            if rows0 == 128:
                dmaO(y_ap.rearrange("(t p) f -> p t f", p=128), hres[:, 0:nt, :], reads=HR, key="k_y")
            else:
                dmaO(y_ap, hres[0:rows0, 0, :], reads=HR, key="k_y")

        if "C" in phases or "D" in phases:
            dmaL(gmlpbc, bass.AP(g_mlp, 0, [[0, 128], [1, 1024]]), writes=["gmlpbc"], key="k_c12")
            dmaL(gfinbc, bass.AP(g_final, 0, [[0, 128], [1, 1024]]), writes=["gfinbc"], key="k_c13")
        if "C" in phases:
            for I in range(NOWN):
                phaseC_group([(i * 128, 128) for i in range(4)], xv.ap()[(4 * I + 3) * 512:(4 * I + 4) * 512, :],
                             pv.ap()[I * 512:(I + 1) * 512, :],
                             lambda ti, I=I: (OAN[:, I * 4 + ti, :], ["OAN"]),
                             OBNs.ap()[I * 512:(I + 1) * 512, :], y.ap()[I * 512:(I + 1) * 512, :])
        P.barrier(bar[:])

        def phaseD2():
            oanl = R2b[:, 0:512]
            dmaL(oanl[0:64, :], OANss.ap(), reads=["OANss"], writes=["oanl"], key="k_oanl")
            phaseC_group([(0, 64)], xsm.ap(), psm.ap(), lambda ti: (oanl[0:64, :], ["oanl"]), OBNss.ap(), ys.ap())

        if "D" in phases:
            phaseD2()

        P.emit(final_dma_keys=[] if "5" in phases else sorted(outkeys))
    return nc


_NC_CACHE = {}


def _run(x_prompt, x_sample, cache_a_k, cache_a_v, cache_b_k, cache_b_v, p_prompt, p_sample,
         t5_table, g_attn, w_in, lambda_q1, lambda_k1, lambda_q2, lambda_k2, subln_g,
         band_table, out_norm_b, w_out, g_mlp, w_up, w_down, w_ple_gate, w_ple_proj, g_final,
         phases="ABCD", cores=None, trace=False):
    f = lambda a: np.ascontiguousarray(np.asarray(a, dtype=np.float32))
    x_prompt = f(x_prompt); x_sample = f(x_sample); p_prompt = f(p_prompt); p_sample = f(p_sample)
    cache_a_k = f(cache_a_k); cache_a_v = f(cache_a_v); cache_b_k = f(cache_b_k); cache_b_v = f(cache_b_v)
    S = x_prompt.shape[1]
    nblk = S // 512
    nown = nblk // 4
    seqv = nblk * 512
    ident, J, CA, CB = _consts()
    ck = (phases, nblk)
    if ck not in _NC_CACHE:
        _NC_CACHE[ck] = build_program(phases, nblk)
    nc = _NC_CACHE[ck]
    common = {
        "t5": f(t5_table), "bt": f(band_table)[0], "g_attn": f(g_attn), "w_in": f(w_in)[0],
        "lq1": f(lambda_q1), "lk1": f(lambda_k1), "lq2": f(lambda_q2), "lk2": f(lambda_k2),
        "subln": f(subln_g), "onb": f(out_norm_b), "w_out": f(w_out)[0], "g_mlp": f(g_mlp),
        "w_up": f(w_up)[0], "w_down": f(w_down)[0], "w_gate": f(w_ple_gate)[0], "w_ple": f(w_ple_proj)[0],
        "g_final": f(g_final).reshape(1, 1024), "ident": ident, "J": J, "CA": CA, "CB": CB,
    }
    cores = list(range(8)) if cores is None else list(cores)
    in_maps = []
    for c in cores:
        b, j = divmod(c, 4)
        npad = 3 - j
        xvv = np.zeros((seqv, 1024), np.float32)
        nreal = (nblk - npad) * 512
        xvv[npad * 512:] = x_prompt[b, :nreal]
        pvv = np.concatenate([p_prompt[0, b, (4 * I + j) * 512:(4 * I + j + 1) * 512] for I in range(nown)], 0)
        pm = np.zeros((128, 4), np.float32)
        pm[:, :npad] = NEGM
        m = dict(common)
        m.update({
            "xv": xvv, "pv": np.ascontiguousarray(pvv),
            "xsm": x_sample[2 * c:2 * c + 2].reshape(64, 1024), "psm": p_sample[0, 2 * c:2 * c + 2].reshape(64, 256),
            "cak": cache_a_k[0, 2 * c:2 * c + 2].reshape(2048, 512), "cav": cache_a_v[0, 2 * c:2 * c + 2].reshape(2048, 512),
            "cbk": cache_b_k[0, 2 * c:2 * c + 2].reshape(1024, 512), "cbv": cache_b_v[0, 2 * c:2 * c + 2].reshape(1024, 512),
            "padmask": pm,
        })
        in_maps.append({k: np.ascontiguousarray(v) for k, v in m.items()})
    if trace:
        res = run_bass_kernel_spmd(nc, in_maps, core_ids=list(range(len(cores))), trace=True)
        print("EXEC_TIME_NS", res.exec_time_ns)
    else:
        res = run_bass_kernel_spmd(nc, in_maps, core_ids=list(range(len(cores))))
    R = res.results
    y_prompt = np.zeros((2, S, 1024), np.float32)
    nakp = np.zeros((1, 2, S, 512), np.float32)
    navp = np.zeros((1, 2, S, 512), np.float32)
    nbkp = np.zeros((1, 2, 512, 512), np.float32)
    nbvp = np.zeros((1, 2, 512, 512), np.float32)
    y_sample = np.zeros((16, 32, 1024), np.float32)
    saks = np.zeros((1, 16, 32, 512), np.float32); savs = np.zeros((1, 16, 32, 512), np.float32)
    sbks = np.zeros((1, 16, 32, 512), np.float32); sbvs = np.zeros((1, 16, 32, 512), np.float32)
    for ci, c in enumerate(cores):
        b, j = divmod(c, 4)
        r = R[ci]
        for I in range(nown):
            g0 = (4 * I + j) * 512
            y_prompt[b, g0:g0 + 512] = r["y"][I * 512:(I + 1) * 512]
            nakp[0, b, g0:g0 + 512] = r["nak"][I * 512:(I + 1) * 512]
            navp[0, b, g0:g0 + 512] = r["nav"][I * 512:(I + 1) * 512]
        if j == 3:
            nbkp[0, b] = r["nbk"]
            nbvp[0, b] = r["nbv"]
        y_sample[2 * c:2 * c + 2] = r["ys"].reshape(2, 32, 1024)
        saks[0, 2 * c:2 * c + 2] = r["sak"].reshape(2, 32, 512)
        savs[0, 2 * c:2 * c + 2] = r["sav"].reshape(2, 32, 512)
        sbks[0, 2 * c:2 * c + 2] = r["sbk"].reshape(2, 32, 512)
        sbvs[0, 2 * c:2 * c + 2] = r["sbv"].reshape(2, 32, 512)
    return (y_prompt, y_sample,
            nakp.reshape(1, 2, S, 4, 2, 64), navp.reshape(1, 2, S, 4, 128),
            nbkp.reshape(1, 2, 512, 8, 64), nbvp.reshape(1, 2, 512, 8, 64),
            saks.reshape(1, 16, 32, 4, 2, 64), savs.reshape(1, 16, 32, 4, 128),
            sbks.reshape(1, 16, 32, 8, 64), sbvs.reshape(1, 16, 32, 8, 64))


def kernel(x_prompt, x_sample, cache_a_k, cache_a_v, cache_b_k, cache_b_v, p_prompt, p_sample,
           t5_table, g_attn, w_in, lambda_q1, lambda_k1, lambda_q2, lambda_k2, subln_g,
           band_table, out_norm_b, w_out, g_mlp, w_up, w_down, w_ple_gate, w_ple_proj, g_final):
    return _run(x_prompt, x_sample, cache_a_k, cache_a_v, cache_b_k, cache_b_v, p_prompt, p_sample,
                t5_table, g_attn, w_in, lambda_q1, lambda_k1, lambda_q2, lambda_k2, subln_g,
                band_table, out_norm_b, w_out, g_mlp, w_up, w_down, w_ple_gate, w_ple_proj, g_final)
```

```python
import math
import contextlib
import numpy as np
import concourse.bass as bass
import concourse.mybir as mybir
from concourse.bass_utils import run_bass_kernel_spmd

ENGS = ("pe", "act", "dve", "pool", "sp")


class Op:
    __slots__ = ("idx", "eng", "fn", "dma_key", "dma_cnt", "waits", "signal", "sigcnt", "eidx")

    def __init__(self, idx, eng, fn, dma_key):
        self.idx = idx
        self.eng = eng
        self.fn = fn
        self.dma_key = dma_key
        self.dma_cnt = 0
        self.waits = []
        self.signal = False
        self.sigcnt = 0
        self.eidx = 0


class Prog:
    def __init__(self, nc):
        self.nc = nc
        self.ops = []
        self.by_eng = {e: [] for e in ENGS}
        self.last_w = {}
        self.readers = {}
        self.seen = {e: {f: -1 for f in ENGS} for e in ENGS}
        self.seen_dma = {e: {} for e in ENGS}
        self.dma_counts = {}
        self.final_dma = []
        self.force = {}

    def op(self, eng, fn, reads=(), writes=(), dma_key=None, self_sync=True, extra=()):
        o = Op(len(self.ops), eng, fn, dma_key)
        o.eidx = len(self.by_eng[eng])
        deps = []
        for r in reads:
            w = self.last_w.get(r)
            if w is not None:
                deps.append(w)
        for w_ in writes:
            w = self.last_w.get(w_)
            if w is not None:
                deps.append(w)
            deps.extend(self.readers.get(w_, ()))
        for r in reads:
            self.readers.setdefault(r, []).append(o)
        for w_ in writes:
            self.last_w[w_] = o
            self.readers[w_] = [x for x in self.readers.get(w_, ()) if x is o]
        f = self.force.pop(eng, None)
        if f is not None:
            deps.append(f)
        deps.extend(extra)
        if dma_key is not None:
            self.dma_counts[dma_key] = self.dma_counts.get(dma_key, 0) + 16
            o.dma_cnt = self.dma_counts[dma_key]
        for d in deps:
            if d is o:
                continue
            if d.dma_key is not None:
                cur = self.seen_dma[eng].get(d.dma_key, 0)
                if cur >= d.dma_cnt:
                    continue
                self.seen_dma[eng][d.dma_key] = d.dma_cnt
                o.waits.append(("dma", d.dma_key, d.dma_cnt))
            else:
                if d.eng == eng and (eng == "pe" or not self_sync):
                    continue
                if self.seen[eng][d.eng] >= d.eidx:
                    continue
                self.seen[eng][d.eng] = d.eidx
                d.signal = True
                o.waits.append(("eng", d.eng, d))
        self.ops.append(o)
        self.by_eng[eng].append(o)
        return o

    def barrier(self, tile, skip=()):
        extra = [self.by_eng[e][-1] for e in ENGS if self.by_eng[e]]
        last_dma = {}
        for o in self.ops:
            if o.dma_key is not None and o.dma_key not in skip:
                last_dma[o.dma_key] = o
        extra.extend(last_dma.values())
        b = self.op("dve", lambda e: e.memset(tile, 0.0), extra=extra)
        for e in ENGS:
            if e != "dve":
                self.force[e] = b
        return b

    def emit(self, final_dma_keys=()):
        nc = self.nc
        import contextlib
        for o in self.ops:
            best = {}
            for w in o.waits:
                if w[0] == "dma":
                    k = ("dma", w[1])
                    if k not in best or best[k][2] < w[2]:
                        best[k] = w
                else:
                    k = ("eng", w[1])
                    if k not in best or best[k][2].eidx < w[2].eidx:
                        best[k] = w
            o.waits = list(best.values())
        for e in ENGS:
            c = 0
            for o in self.by_eng[e]:
                if o.signal:
                    c += 1
                    o.sigcnt = c
        with contextlib.ExitStack() as st:
            esem = {e: st.enter_context(nc.semaphore("s_" + e)) for e in ENGS}
            dsem = {k: st.enter_context(nc.semaphore("d_%d" % i))
                    for i, k in enumerate(sorted(self.dma_counts))}
            block = st.enter_context(nc.Block())

            def run(e, eng):
                for o in self.by_eng[e]:
                    for w in o.waits:
                        if w[0] == "dma":
                            eng.wait_ge(dsem[w[1]], w[2])
                        else:
                            eng.wait_ge(esem[w[1]], w[2].sigcnt)
                    inst = o.fn(eng)
                    if o.dma_key is not None:
                        inst.then_inc(dsem[o.dma_key], 16)
                    elif o.signal:
                        inst.then_inc(esem[e], 1)
                if e == "sp":
                    for k in final_dma_keys:
                        eng.wait_ge(dsem[k], self.dma_counts[k])

            @block.tensor
            def _(eng):
                run("pe", eng)

            @block.scalar
            def _(eng):
                run("act", eng)

            @block.vector
            def _(eng):
                run("dve", eng)

            @block.gpsimd
            def _(eng):
                run("pool", eng)

            @block.sync
            def _(eng):
                run("sp", eng)

F32 = mybir.dt.float32
BF16 = mybir.dt.bfloat16
AF = mybir.ActivationFunctionType
ALU = mybir.AluOpType
NEGM = -30000.0
NBLK = 32
NOWN = 8
SEQV = NBLK * 512


def _t5_bucket_np(rel):
    half = 16
    n = -rel
    ret = np.where(n < 0, half, 0)
    n = np.abs(n)
    max_exact = 8
    nf = np.maximum(n, 1).astype(np.float32)
    large = max_exact + (np.log(nf / np.float32(max_exact)) / np.float32(math.log(128 / max_exact))
                         * np.float32(half - max_exact)).astype(np.int32)
    large = np.minimum(large, half - 1)
    return ret + np.where(n < max_exact, n, large)


def _consts():
    ident = np.eye(128, dtype=np.float32)
    J = ident[::-1].copy()
    i = np.arange(384)
    delta = i - 255
    CA = np.zeros((32, 384), np.float32)
    bk = _t5_bucket_np(delta.astype(np.int32))
    CA[bk, i] += 1.0
    CA[15, :] -= 1.0
    CA[:, 383] = 0.0
    CB = np.zeros((384, 384), np.float32)
    idx = np.clip(delta, -128, 128) + 128
    CB[idx, i] += 1.0
    CB[0, :] -= 1.0
    CB[:, 383] = 0.0
    return ident, J, CA, CB


def build_program(phases="ABCD", nblk=32):
    global NBLK, NOWN, SEQV
    NBLK = nblk
    NOWN = nblk // 4
    SEQV = nblk * 512
    NTOK = NOWN * 512
    nc = bass.Bass("TRN2", target_bir_lowering=False)
    T = {}

    def din(name, shape, dt=F32):
        T[name] = nc.dram_tensor(name, shape, dt, kind="ExternalInput")
        return T[name]

    def dout(name, shape, dt=F32):
        T[name] = nc.dram_tensor(name, shape, dt, kind="ExternalOutput")
        return T[name]

    def dscr(name, shape, dt=BF16):
        T[name] = nc.dram_tensor(name, shape, dt, kind="Internal")
        return T[name]

    xv = din("xv", [SEQV, 1024]); pv = din("pv", [NTOK, 256])
    xsm = din("xsm", [64, 1024]); psm = din("psm", [64, 256])
    cak = din("cak", [2048, 512]); cav = din("cav", [2048, 512])
    cbk = din("cbk", [1024, 512]); cbv = din("cbv", [1024, 512])
    padmask = din("padmask", [128, 4])
    t5 = din("t5", [32, 4]); bt = din("bt", [257, 8])
    g_attn = din("g_attn", [1, 1024]); w_in = din("w_in", [1024, 3072])
    lq1 = din("lq1", [1, 64]); lk1 = din("lk1", [1, 64]); lq2 = din("lq2", [1, 64]); lk2 = din("lk2", [1, 64])
    subln = din("subln", [1, 128]); onb = din("onb", [1, 512])
    w_out = din("w_out", [1024, 1024]); g_mlp = din("g_mlp", [1, 1024])
    w_up = din("w_up", [1024, 4096]); w_down = din("w_down", [4096, 1024])
    w_gate = din("w_gate", [1024, 1024]); w_ple = din("w_ple", [256, 1024]); g_final = din("g_final", [1, 1024])
    identd = din("ident", [128, 128]); Jd = din("J", [128, 128]); CAd = din("CA", [32, 384]); CBd = din("CB", [384, 384])

    y = dout("y", [NTOK, 1024]); ys = dout("ys", [64, 1024])
    nak = dout("nak", [NTOK, 512]); nav = dout("nav", [NTOK, 512])
    nbk = dout("nbk", [512, 512]); nbv = dout("nbv", [512, 512])
    sak = dout("sak", [64, 512]); sav = dout("sav", [64, 512]); sbk = dout("sbk", [64, 512]); sbv = dout("sbv", [64, 512])

    KTs = dscr("KTs", [4, 128, SEQV]); VAs = dscr("VAs", [SEQV, 512]); QTs = dscr("QTs", [4, 128, NTOK])
    OBNs = dscr("OBNs", [NTOK, 512]); OANss = dscr("OANss", [64, 512]); OBNss = dscr("OBNss", [64, 512])
    vecA = dscr("vecA", [4, 384], F32); vecB = dscr("vecB", [8, 384], F32)
    wout_b = dscr("wout_b", [1024, 1024]); wup_b = dscr("wup_b", [1024, 4096]); wdown_b = dscr("wdown_b", [4096, 1024])
    wgate_b = dscr("wgate_b", [1024, 1024]); wple_b = dscr("wple_b", [256, 1024])

    P = Prog(nc)
    outkeys = set()
    with contextlib.ExitStack() as st:
        def sb(name, shape, dt):
            return st.enter_context(nc.sbuf_tensor(name, shape, dt))

        def psm_(name, shape, dt):
            return st.enter_context(nc.psum_tensor(name, shape, dt))

        R1 = sb("R1", [128, 33024], BF16)
        R2a = sb("R2a", [128, 8192], F32)
        R2b = sb("R2b", [128, 28800], BF16)
        idb = sb("idb", [128, 128], BF16)
        Js = sb("Js", [128, 128], F32)
        Hs = sb("Hs", [128, 128], F32)
        TA = sb("TA", [128, 4, 2, 128], F32)
        TB = sb("TB", [128, 8, 2, 128], F32)
        Tm4 = sb("Tm4", [128, 128], F32)
        chA = sb("chA", [128, 4], F32); chB = sb("chB", [128, 8], F32)
        cmA = sb("cmA", [128, 4, 3], F32); cmB = sb("cmB", [128, 8], F32)
        pmk = sb("pmk", [128, 4], F32)
        lam4 = sb("lam4", [128, 4, 64], F32)
        lcol = sb("lcol", [128, 8], F32)
        sublnbc = sb("sublnbc", [128, 128], F32)
        onbbc = sb("onbbc", [128, 512], F32)
        junk = sb("junk", [128, 1024], BF16)
        Eb = sb("Eb", [128, 2, 2, 512], BF16)
        cols = sb("cols", [128, 64], F32)
        eps6 = sb("eps6", [128, 1], F32); eps5 = sb("eps5", [128, 1], F32)
        bar = sb("bar", [128, 1], F32)
        osm = sb("osm", [128, 2, 2, 128], F32)
        t5s = sb("t5s", [32, 4], F32); bts = sb("bts", [128, 3, 8], F32)
        CAs = sb("CAs", [32, 384], F32); CBs = sb("CBs", [128, 3, 384], F32)
        vst = sb("vst", [8, 384], F32)

        TP = psm_("TP", [128, 2, 1024], BF16)
        PS = psm_("PS", [128, 6, 512], F32)
        TPf = TP.bitcast(F32)
        obanks = [(PS[:, 4, :], "PS4"), (PS[:, 5, :], "PS5"), (TPf[:, 0, :], "TP0")]

        cnt = {"ev": 0, "tp": 0, "ps": 0, "sl": 0, "ost": 0}

        KM = {"k_idb": "q1", "k_w": "q10", "k_v0": "q14", "k_v1": "q15", "k_h": "q16",
              "k_xb": "q0", "k_ktst": "q1", "k_vast": "q2", "k_qast": "q3", "k_ost0": "q4", "k_ost1": "q5", "k_obst": "q6",
              "k_win": "q7", "k_c11": "q8",
              "k_kth0": "q0", "k_kth1": "q1", "k_kth2": "q2", "k_kth3": "q3", "k_vh0": "q0", "k_vh1": "q1", "k_vh2": "q2",
              "k_vh3": "q3", "k_qth": "q4",
              "k_wr0": "q0", "k_wr1": "q1", "k_wr2": "q2", "k_wr3": "q3", "k_hres": "q4", "k_ps": "q5", "k_obn": "q6",
              "k_y": "q7", "k_c12": "q8", "k_c13": "q9",
              "k_cst": "q1", "k_vc": "q2", "k_vbc": "q3", "k_oans": "q6", "k_obns": "q10", "k_oanl": "q9"}
        for i_ in range(11):
            KM["k_c%d" % i_] = "q0"
        KM.update({"k_xb0": "q0", "k_xb1": "q11", "k_xb2": "q12", "k_xb3": "q13"})

        def dmaL(out, in_, reads=(), writes=(), key=None, eng="sp"):
            key = KM[key]
            return P.op(eng, lambda e: e.dma_start(out=out, in_=in_), reads=reads, writes=writes, dma_key=key)

        def dmaO(out, in_, reads, key):
            key = KM[key]
            outkeys.add(key)
            return P.op("sp" if "4" in phases else "pool", lambda e: e.dma_start(out=out, in_=in_), reads=reads, dma_key=key)

        def evac(out, in_, reads, writes, scale=None, eng=None):
            if eng is None:
                cnt["ev"] += 1
                eng = "act" if cnt["ev"] % 2 else "dve"
            if eng == "act":
                s = 1.0 if scale is None else scale
                return P.op("act", lambda e: e.activation(out=out, in_=in_, func=AF.Copy, scale=s), reads=reads, writes=writes)
            if scale is None:
                return P.op("dve", lambda e: e.tensor_copy(out=out, in_=in_), reads=reads, writes=writes)
            return P.op("dve", lambda e: e.tensor_scalar(out=out, in0=in_, scalar1=scale, scalar2=None, op0=ALU.mult),
                        reads=reads, writes=writes)

        pe_state = {"mode": None}

        def pe_op(K, M, fn, reads=(), writes=()):
            r = lambda x: 32 if x <= 32 else (64 if x <= 64 else 128)
            mode = (r(K), r(M))
            if pe_state["mode"] is not None and pe_state["mode"] != mode:
                P.op("pe", lambda e: e.drain())
            pe_state["mode"] = mode
            return P.op("pe", fn, reads=reads, writes=writes)

        def nextps():
            cnt["ps"] = (cnt["ps"] + 1) % 4
            return cnt["ps"]

        def rms_rstd(src, rows, F, eps_t, ci, reads, tag):
            P.op("act", lambda e: e.activation(out=junk[0:rows, 0:F], in_=src, func=AF.Square,
                                               accum_out=cols[0:rows, ci:ci + 1]),
                 reads=reads, writes=["junk", "c%d" % ci])
            P.op("act", lambda e: e.activation(out=cols[0:rows, ci + 1:ci + 2], in_=cols[0:rows, ci:ci + 1], func=AF.Sqrt,
                                               bias=eps_t[0:rows, 0:1], scale=1.0 / F),
                 reads=["c%d" % ci, "eps"], writes=["c%d" % (ci + 1)])
            P.op("dve", lambda e: e.reciprocal(out=cols[0:rows, ci + 2:ci + 3], in_=cols[0:rows, ci + 1:ci + 2]),
                 reads=["c%d" % (ci + 1)], writes=["c%d" % (ci + 2)])
            return cols[0:rows, ci + 2:ci + 3], "c%d" % (ci + 2)

        def transposes(src, rows, nch, dst, reads, writes):
            b = cnt["tp"] % 2
            cnt["tp"] += 1
            for c in range(nch):
                pe_op(rows, 128, (lambda e, c=c: e.transpose(TP[:, b, c * 128:c * 128 + rows], src[:, c * 128:(c + 1) * 128],
                                                       idb[0:rows, 0:rows])),
                     reads=list(reads) + ["idb"], writes=["TP%d" % b])
            tv = TP[:, b, :].rearrange("p (a r) -> p a r", r=128)[:, 0:nch, 0:rows]
            evac(dst, tv, reads=["TP%d" % b], writes=writes)

        def proj_fm(wfn, rhsT, n, reads):
            b = nextps()
            for c in range(8):
                pe_op(128, 128, (lambda e, c=c: e.matmul(PS[:, b, 0:n], lhsT=wfn(c), rhs=rhsT[:, c, 0:n], start=(c == 0), stop=(c == 7))),
                     reads=reads, writes=["PS%d" % b])
            return PS[:, b, 0:n], "PS%d" % b

        def proj_tm(xT, tok0, rows, wfn, reads, nk=8):
            b = nextps()
            for c in range(nk):
                pe_op(128, rows, (lambda e, c=c: e.matmul(PS[0:rows, b, :], lhsT=xT[:, c, tok0:tok0 + rows], rhs=wfn(c),
                                                    start=(c == 0), stop=(c == nk - 1))),
                     reads=reads, writes=["PS%d" % b])
            return PS[0:rows, b, :], "PS%d" % b

        osb_state = {"i": 0}

        def pair_attn(keytiles, QT, qtiles, ed, finalize, qreads, osb=None):
            nqt = len(qtiles)
            per_bank = 512 // (ed + 1)
            started = set()

            def oloc(u, qi):
                g = u * nqt + qi
                bank, slot = divmod(g, per_bank)
                ap, nm = obanks[bank]
                rows = qtiles[qi][1]
                return ap[0:rows, slot * (ed + 1):(slot + 1) * (ed + 1)], nm, bank

            def qk(kt):
                sl = cnt["sl"] % 2
                cnt["sl"] += 1
                kt["sl"] = sl
                nk = kt["nk"]
                c0 = qtiles[kt["qlo"]][0]
                c1 = qtiles[kt["qhi"]][0] + qtiles[kt["qhi"]][1]
                kt["c"] = (c0, c1)
                for u in range(2):
                    pe_op(128, nk, (lambda e, u=u: e.matmul(PS[0:nk, sl * 2 + u, c0:c1], lhsT=kt["KT"],
                                                            rhs=QT[u][:, c0:c1], start=True, stop=True)),
                         reads=list(kt["reads"]) + list(qreads), writes=["PS%d" % (sl * 2 + u)])
            def qk2(kt):
                sl = kt["sl"]
                nk = kt["nk"]
                c0, c1 = kt["c"]
                for (qi, Ts) in ([] if "k" in phases else kt["adds"]):
                    q0, qr = qtiles[qi]
                    for u in range(2):
                        P.op("dve", (lambda e, u=u, q0=q0, qr=qr, Ts=Ts: e.tensor_tensor(
                            out=PS[0:nk, sl * 2 + u, q0:q0 + qr], in0=PS[0:nk, sl * 2 + u, q0:q0 + qr], in1=Ts[u], op=ALU.add)),
                            reads=["PS%d" % (sl * 2 + u), "Tt"], writes=["PS%d" % (sl * 2 + u)], self_sync=False)
                if kt["bias"][0] is kt["bias"][1]:
                    P.op("act", lambda e: e.activation(out=Eb[0:nk, sl, :, c0:c1], in_=PS[0:nk, sl * 2:sl * 2 + 2, c0:c1],
                                                       func=AF.Exp, bias=kt["bias"][0], scale=1.0),
                         reads=["PS%d" % (sl * 2), "PS%d" % (sl * 2 + 1), "bias"], writes=["E%d" % sl])
                else:
                    for u in range(2):
                        P.op("act", (lambda e, u=u: e.activation(out=Eb[0:nk, sl, u, c0:c1], in_=PS[0:nk, sl * 2 + u, c0:c1],
                                                                 func=AF.Exp, bias=kt["bias"][u], scale=1.0)),
                             reads=["PS%d" % (sl * 2 + u), "bias"], writes=["E%d" % sl])

            def pvm(kt):
                if "l" in phases:
                    return
                sl = kt["sl"]
                nk = kt["nk"]
                for u in range(2):
                    for qi in range(kt["qlo"], kt["qhi"] + 1):
                        oap, nm, bank = oloc(u, qi)
                        q0, qr = qtiles[qi]
                        first = bank not in started
                        started.add(bank)
                        pe_op(nk, qr, (lambda e, u=u, oap=oap, q0=q0, qr=qr, first=first: e.matmul(
                            oap, lhsT=Eb[0:nk, sl, u, q0:q0 + qr], rhs=kt["V"][u], start=first, stop=False,
                            skip_group_check=True)),
                            reads=["E%d" % sl] + list(kt["reads"]), writes=[nm])

            pend = []
            for kt in keytiles:
                qk(kt)
                pend.append(kt)
                if len(pend) > 2:
                    pvm(pend.pop(0))
                qk2(kt)
            for kt in pend:
                pvm(kt)
            O = [[oloc(u, qi)[0] for qi in range(nqt)] for u in range(2)]
            names = sorted({oloc(u, qi)[1] for u in range(2) for qi in range(nqt)})
            if osb is not None:
                sset = osb[osb_state["i"] % len(osb)]
                osb_state["i"] += 1
                used = sorted({oloc(u, qi)[2] for u in range(2) for qi in range(nqt)})
                for bk in used:
                    bap, bnm = obanks[bk]
                    P.op("dve", (lambda e, bk=bk, bap=bap: e.tensor_copy(out=sset[0][:, bk, :], in_=bap)), reads=[bnm],
                         writes=[sset[1] + str(bk)])

                def oloc2(u, qi):
                    g = u * nqt + qi
                    bank, slot = divmod(g, per_bank)
                    rows = qtiles[qi][1]
                    return sset[0][0:rows, bank, slot * (ed + 1):(slot + 1) * (ed + 1)], sset[1] + str(bank)
                O = [[oloc2(u, qi)[0] for qi in range(nqt)] for u in range(2)]
                names = sorted({oloc2(u, qi)[1] for u in range(2) for qi in range(nqt)})
            if "m" not in phases:
                finalize(O, names)

        def load_win():
            Wv = R1[:, 0:24576].rearrange("p (c n) -> p c n", n=3072)
            for c in range(8):
                for hh in range(2):
                    dmaL(Wv[:, c, hh * 1536:(hh + 1) * 1536], w_in.ap()[c * 128:(c + 1) * 128, hh * 1536:(hh + 1) * 1536],
                         writes=["W"], key="k_win", eng="pool")
            return Wv

        Wv = load_win()
        P.op("dve", lambda e: e.memset(eps6[:], 1e-6), writes=["eps"])
        P.op("dve", lambda e: e.memset(eps5[:], 1e-5), writes=["eps"])
        dmaL(idb[:], identd.ap(), writes=["idb"], key="k_idb", eng="pool")
        dmaL(Js[:], Jd.ap(), writes=["Js"], key="k_c0")
        dmaL(t5s[:], t5.ap(), writes=["t5s"], key="k_c1")
        P.op("dve", lambda e: e.memset(bts[:], 0.0), writes=["bts"])
        dmaL(bts[:, 0:2, :], bt.ap()[0:256, :].rearrange("(a p) h -> p a h", p=128), writes=["bts"], key="k_c2")
        dmaL(bts[0:1, 2, :], bt.ap()[256:257, :], writes=["bts"], key="k_c2")
        dmaL(CAs[:], CAd.ap(), writes=["CAs"], key="k_c3")
        dmaL(CBs[:], CBd.ap().rearrange("(a p) n -> p a n", p=128), writes=["CBs"], key="k_c4")
        dmaL(chA[:], bass.AP(t5, 15 * 4, [[0, 128], [1, 4]]), writes=["chA"], key="k_c5")
        dmaL(chB[:], bass.AP(bt, 0, [[0, 128], [1, 8]]), writes=["chB"], key="k_c6")
        dmaL(pmk[:], padmask.ap(), writes=["pmk"], key="k_c7")
        for i, lt in enumerate([lq1, lk1, lq2, lk2]):
            dmaL(lam4[:, i, :], bass.AP(lt, 0, [[0, 128], [1, 64]]), writes=["lam4"], key="k_c8")
        dmaL(sublnbc[:], bass.AP(subln, 0, [[0, 128], [1, 128]]), writes=["sublnbc"], key="k_c9")
        dmaL(onbbc[:], bass.AP(onb, 0, [[0, 128], [1, 512]]), writes=["onbbc"], key="k_c10")
        P.barrier(bar[:], skip=("q7",))
        P.op("dve", lambda e: e.tensor_scalar(out=sublnbc[:], in0=sublnbc[:], scalar1=0.8, scalar2=None, op0=ALU.mult),
             reads=["sublnbc"], writes=["sublnbc"])
        P.op("dve", lambda e: e.tensor_tensor(out=lam4[:, 0, :], in0=lam4[:, 0, :], in1=lam4[:, 1, :], op=ALU.mult),
             reads=["lam4"], writes=["lam4"])
        P.op("dve", lambda e: e.tensor_tensor(out=lam4[:, 2, :], in0=lam4[:, 2, :], in1=lam4[:, 3, :], op=ALU.mult),
             reads=["lam4"], writes=["lam4"])
        P.op("dve", lambda e: e.tensor_reduce(out=lcol[:, 0:1], in_=lam4[:, 0, :], axis=mybir.AxisListType.X, op=ALU.add),
             reads=["lam4"], writes=["lcol"])
        P.op("dve", lambda e: e.tensor_reduce(out=lcol[:, 1:2], in_=lam4[:, 2, :], axis=mybir.AxisListType.X, op=ALU.add),
             reads=["lam4"], writes=["lcol"])
        P.op("act", lambda e: e.activation(out=lcol[:, 2:4], in_=lcol[:, 0:2], func=AF.Exp), reads=["lcol"], writes=["lcol"])
        P.op("dve", lambda e: e.scalar_tensor_tensor(out=lcol[:, 4:5], in0=lcol[:, 3:4], scalar=-0.2, in1=lcol[:, 2:3],
                                                     op0=ALU.add, op1=ALU.subtract), reads=["lcol"], writes=["neglam"])
        neglam = lcol[:, 4:5]
        for h in range(4):
            P.op("dve", (lambda e, h=h: e.tensor_scalar(out=cmA[:, h, :], in0=pmk[:, 0:3], scalar1=chA[:, h:h + 1], scalar2=None,
                                                        op0=ALU.add)), reads=["pmk", "chA"], writes=["bias"])
        P.op("dve", lambda e: e.tensor_scalar(out=cmB[:], in0=chB[:], scalar1=pmk[:, 2:3], scalar2=None, op0=ALU.add),
             reads=["pmk", "chB"], writes=["bias"])
        pe_op(32, 4, lambda e: e.matmul(PS[0:4, 0, 0:384], lhsT=t5s[:], rhs=CAs[:], start=True, stop=True),
             reads=["t5s", "CAs"], writes=["PS0"])
        P.op("dve", lambda e: e.tensor_copy(out=vst[0:4, :], in_=PS[0:4, 0, 0:384]), reads=["PS0"], writes=["vst"])
        dmaL(vecA.ap(), vst[0:4, :], reads=["vst"], writes=["vecA"], key="k_v0")
        for a in range(3):
            pe_op(128, 8, (lambda e, a=a: e.matmul(PS[0:8, 1, 0:384], lhsT=bts[:, a, :], rhs=CBs[:, a, :], start=(a == 0), stop=(a == 2))),
                 reads=["bts", "CBs"], writes=["PS1"])
        P.op("dve", lambda e: e.tensor_copy(out=vst[0:8, :], in_=PS[0:8, 1, 0:384]), reads=["PS1", "vecA"], writes=["vst"])
        dmaL(vecB.ap(), vst[0:8, :], reads=["vst"], writes=["vecB"], key="k_v1")
        Hall = R2a[:, 4096:7168].rearrange("p (i n) -> p i n", n=128)
        hi = 0
        hlist = []
        for (vec, Tt, nh) in ((vecA, TA, 4), (vecB, TB, 8)):
            for h in range(nh):
                for kind, base in ((0, 128), (1, 0)):
                    hank = bass.AP(vec, h * 384 + base, [[1, 128], [1, 128]])
                    dmaL(Hall[:, hi, :], hank, reads=["vecA", "vecB"], writes=["Hall"], key="k_h")
                    hlist.append((hi, Tt, h, kind))
                    hi += 1
        for (hi, Tt, h, kind) in hlist:
            bk = 2 + hi % 2
            pe_op(128, 128, (lambda e, hi=hi, bk=bk: e.matmul(PS[:, bk, 0:128], lhsT=Hall[:, hi, :], rhs=Js[:], start=True, stop=True)),
                  reads=["Hall", "Js"], writes=["PS%d" % bk])
            P.op("dve", (lambda e, Tt=Tt, h=h, kind=kind, bk=bk: e.tensor_copy(out=Tt[:, h, kind, :], in_=PS[:, bk, 0:128])),
                 reads=["PS%d" % bk], writes=["Tt"], self_sync=False)
            if kind == 0:
                P.op("dve", (lambda e, Tt=Tt, h=h: e.memset(Tt[64:128, h, 0, 0:64], NEGM)), reads=["Tt"], writes=["Tt"])
        P.op("dve", lambda e: e.memset(Tm4[:], 0.0), writes=["Tt"])
        P.op("dve", lambda e: e.memset(Tm4[0:64, 64:128], NEGM), reads=["Tt"], writes=["Tt"])
        def weight_casts():
            for (src, dst, rows, colsn) in ((w_out, wout_b, 1024, 1024), (w_up, wup_b, 1024, 4096), (w_down, wdown_b, 4096, 1024),
                                           (w_gate, wgate_b, 1024, 1024), (w_ple, wple_b, 256, 1024)):
                sv = src.ap().rearrange("r (a n) -> (r a) n", n=1024)
                dv = dst.ap().rearrange("r (a n) -> (r a) n", n=1024)
                tot = rows * colsn // 1024
                for r0 in range(0, tot, 512):
                    n_ = min(512, tot - r0)
                    P.op("pool", (lambda e, r0=r0, n_=n_, dv=dv, sv=sv: e.dma_start(out=dv[r0:r0 + n_, :], in_=sv[r0:r0 + n_, :],
                                                                                    max_dma_last_dim=2048)),
                         writes=["wscr"], dma_key=KM["k_w"])


        P.barrier(bar[:], skip=("q7",))
        xb = R2a[:, 0:4096].rearrange("p (t f) -> p t f", f=1024)
        ostg = R2a[:, 4096:5120].rearrange("p (s f) -> p s f", f=512)
        obraw = R2a[:, 5120:7168].rearrange("p (t f) -> p t f", f=512)
        gattnbc = R2a[:, 7168:8192]
        xs = R2b[:, 0:4096].rearrange("p (t f) -> p t f", f=1024)
        xsT = R2b[:, 4096:8192].rearrange("p (c n) -> p c n", n=512)
        KTst = R2b[:, 8192:10240].rearrange("p (h n) -> p h n", n=512)
        VAst = R2b[:, 10240:12288].rearrange("p (t n) -> p t n", n=512)
        KBT = R2b[:, 12288:16384].rearrange("p (s c n) -> p s c n", s=2, n=512)
        VBa = R2b[:, 16384:20608].rearrange("p (s t h e) -> p s t h e", s=2, t=4, e=66)
        QBz = R2b[:, 20608:24704].rearrange("p (u c n) -> p u c n", u=2, n=512)
        QAst = R2b[:, 24704:26752].rearrange("p (h n) -> p h n", n=512)
        OBst = R2b[:, 26752:28800].rearrange("p (t n) -> p t n", n=512)
        P.op("pool", lambda e: e.memset(QBz, 0.0), writes=["QBT"])
        dmaL(gattnbc, bass.AP(g_attn, 0, [[0, 128], [1, 1024]]), writes=["gattnbc"], key="k_c11")
        P.op("dve", lambda e: e.memset(VBa[:, :, :, :, 64:66], 1.0), writes=["VBones"])

        def out_store(dst_ap, psum_ap, psname, rows=128):
            s = cnt["ost"] % 2
            cnt["ost"] += 1
            evac(ostg[0:rows, s, :], psum_ap, reads=[psname], writes=["ostg%d" % s])
            dmaO(dst_ap, ostg[0:rows, s, :], reads=["ostg%d" % s], key="k_ost%d" % s)
            return ostg[0:rows, s, :], "ostg%d" % s

        def band_attention(p, I):
            sp_, so_ = (p - 1) % 2, p % 2
            qtl = [(i * 128, 128) for i in range(4)]
            for cb in range(4):
                kts = []
                for r in range(-4, 4):
                    slot, tk = (sp_, r + 4) if r < 0 else (so_, r)
                    qlo, qhi = max(0, r), min(3, r + 4)
                    adds = []
                    for qi in range(qlo, qhi + 1):
                        rel = r - qi
                        if rel == 0:
                            adds.append((qi, [TB[:, 2 * cb + u, 0, :] for u in range(2)]))
                        elif rel == -1:
                            adds.append((qi, [TB[:, 2 * cb + u, 1, :] for u in range(2)]))
                        elif rel == -4:
                            adds.append((qi, [Tm4[:], Tm4[:]]))
                    bsrc = cmB if (p == 3 and r < 0) else chB
                    kts.append(dict(KT=KBT[:, slot, cb, tk * 128:(tk + 1) * 128], nk=128,
                                    V=[VBa[:, slot, tk, 2 * cb + u, 0:65] for u in range(2)],
                                    bias=[bsrc[:, 2 * cb + u:2 * cb + u + 1] for u in range(2)],
                                    qlo=qlo, qhi=qhi, adds=adds, reads=["KBT%d" % slot, "VB%d" % slot, "VBones"]))

                def fin(O, names, cb=cb):
                    for u in range(2):
                        hb = 2 * cb + u
                        for qi in range(4):
                            ci = 8 + (qi * 2 + u)
                            P.op("dve", (lambda e, u=u, qi=qi, ci=ci: e.reciprocal(out=cols[:, ci:ci + 1], in_=O[u][qi][:, 64:65])),
                                 reads=names, writes=["c%d" % ci])
                            P.op("dve", (lambda e, u=u, qi=qi, ci=ci, hb=hb: e.tensor_scalar(
                                out=obraw[:, qi, hb * 64:(hb + 1) * 64], in0=O[u][qi][:, 0:64], scalar1=cols[:, ci:ci + 1],
                                scalar2=None, op0=ALU.mult)), reads=names + ["c%d" % ci], writes=["obraw"])
                pair_attn(kts, [QBz[:, 0, cb, :], QBz[:, 1, cb, :]], qtl, 64, fin, ["QBT"])
            for qi in range(4):
                rc, rn = rms_rstd(obraw[:, qi, :], 128, 512, eps6, 16 + 3 * qi, ["obraw"], "ob")
                P.op("dve", (lambda e, qi=qi, rc=rc: e.scalar_tensor_tensor(out=OBst[:, qi, :], in0=obraw[:, qi, :], scalar=rc,
                                                                              in1=onbbc[:], op0=ALU.mult, op1=ALU.mult)),
                     reads=["obraw", rn, "onbbc"], writes=["OBst"])
            dmaL(OBNs.ap()[I * 512:(I + 1) * 512, :].rearrange("(t p) n -> p t n", p=128), OBst, reads=["OBst"], writes=["OBNs"],
                 key="k_obst", eng="pool")

        def phaseA_block(p):
            own = (p % 4 == 3)
            I = p // 4
            last = (p == NBLK - 1)
            so_ = p % 2
            for t in range(4):
                dmaL(xb[:, t, :], xv.ap()[p * 512 + t * 128:p * 512 + (t + 1) * 128, :], writes=["xb%d" % t], key="k_xb%d" % t)
            for t in range(4):
                rc, rn = rms_rstd(xb[:, t, :], 128, 1024, eps6, 3 * t, ["xb%d" % t], "x")
                P.op("dve", (lambda e, t=t, rc=rc: e.scalar_tensor_tensor(out=xs[:, t, :], in0=xb[:, t, :], scalar=rc, in1=gattnbc,
                                                                            op0=ALU.mult, op1=ALU.mult)),
                     reads=["xb%d" % t, rn, "gattnbc"], writes=["xs%d" % t])
            for t in range(4):
                transposes(xs[:, t, :], 128, 8, xsT[:, :, t * 128:(t + 1) * 128], ["xs%d" % t], ["xsT"])
            for h in range(4):
                ps_, nm = proj_fm(lambda c, h=h: Wv[:, c, 512 + h * 128:512 + (h + 1) * 128], xsT, 512, ["W", "xsT"])
                evac(KTst[:, h, :], ps_, reads=[nm], writes=["KTst"])
            dmaL(KTs.ap().rearrange("h p n -> p h n")[:, :, p * 512:(p + 1) * 512], KTst, reads=["KTst"], writes=["KTs"],
                 key="k_ktst", eng="pool")
            needb = (p % 4 >= 2)
            for cb in (range(4) if needb else ()):
                ps_, nm = proj_fm(lambda c, cb=cb: Wv[:, c, 2048 + cb * 128:2048 + (cb + 1) * 128], xsT, 512, ["W", "xsT"])
                evac(KBT[:, so_, cb, :], ps_, reads=[nm], writes=["KBT%d" % so_])
            if own and "3" not in phases:
                for h in range(4):
                    ps_, nm = proj_fm(lambda c, h=h: Wv[:, c, h * 128:(h + 1) * 128], xsT, 512, ["W", "xsT"])
                    evac(QAst[:, h, :], ps_, reads=[nm], writes=["QAst"], scale=0.125)
                dmaL(QTs.ap().rearrange("h p n -> p h n")[:, :, I * 512:(I + 1) * 512], QAst, reads=["QAst"], writes=["QTs"],
                     key="k_qast", eng="pool")
                for cb in range(4):
                    ps_, nm = proj_fm(lambda c, cb=cb: Wv[:, c, 1536 + cb * 128:1536 + (cb + 1) * 128], xsT, 512, ["W", "xsT"])
                    evac(QBz[:, 0, cb, :], ps_, reads=[nm], writes=["QBT"], scale=0.125)
                    P.op("pool", (lambda e, cb=cb: e.tensor_copy(out=QBz[64:128, 1, cb, :], in_=QBz[64:128, 0, cb, :])),
                         reads=["QBT"], writes=["QBT"])
                    P.op("pool", (lambda e, cb=cb: e.memset(QBz[64:128, 0, cb, :], 0.0)), reads=["QBT"], writes=["QBT"])
            for t in range(4):
                ps_, nm = proj_tm(xsT, t * 128, 128, lambda c: Wv[:, c, 1024:1536], ["W", "xsT"])
                if own:
                    sa, sn = out_store(nav.ap()[I * 512 + t * 128:I * 512 + (t + 1) * 128, :], ps_, nm)
                    P.op("pool", (lambda e, t=t, sa=sa: e.tensor_copy(out=VAst[:, t, :], in_=sa)), reads=[sn], writes=["VAst"])
                else:
                    evac(VAst[:, t, :], ps_, reads=[nm], writes=["VAst"])
                if not needb:
                    continue
                ps_, nm = proj_tm(xsT, t * 128, 128, lambda c: Wv[:, c, 2560:3072], ["W", "xsT"])
                if last:
                    sa, sn = out_store(nbv.ap()[t * 128:(t + 1) * 128, :], ps_, nm)
                    P.op("pool", (lambda e, t=t, sa=sa: e.tensor_copy(out=VBa[:, so_, t, :, 0:64],
                                                                      in_=sa.rearrange("p (h e) -> p h e", e=64))),
                         reads=[sn], writes=["VB%d" % so_])
                else:
                    evac(VBa[:, so_, t, :, 0:64], ps_.rearrange("p (h e) -> p h e", e=64), reads=[nm], writes=["VB%d" % so_])
                if own:
                    ps_, nm = proj_tm(xsT, t * 128, 128, lambda c: Wv[:, c, 512:1024], ["W", "xsT"])
                    out_store(nak.ap()[I * 512 + t * 128:I * 512 + (t + 1) * 128, :], ps_, nm)
                if last:
                    ps_, nm = proj_tm(xsT, t * 128, 128, lambda c: Wv[:, c, 2048:2560], ["W", "xsT"])
                    out_store(nbk.ap()[t * 128:(t + 1) * 128, :], ps_, nm)
            dmaL(VAs.ap()[p * 512:(p + 1) * 512, :].rearrange("(t p) n -> p t n", p=128), VAst, reads=["VAst"], writes=["VAs"],
                 key="k_vast", eng="pool")
            if own and "1" not in phases:
                band_attention(p, I)

        if "A" in phases:
            for p in range(NBLK):
                phaseA_block(p)
        if "a" in phases:
            for p in range(4):
                phaseA_block(p)
        if "e" in phases:
            for p in range(3):
                phaseA_block(p)
        P.barrier(bar[:])

        KTh = R1[:, 0:16384]
        Vaug = R1[:, 16384:33024].rearrange("p (t e) -> p t e", e=130)
        OAN = R2b[:, 0:16384].rearrange("p (t n) -> p t n", n=512)
        QTz = R2b[:, 16384:24576].rearrange("p (u n) -> p u n", u=2)

        def finA_factory(h, dst_fn, rows):
            def fin(O, names):
                nqt = len(O[0])
                for qi in range(nqt):
                    pr = qi % 2
                    cb_ = 28 + 8 * pr
                    P.op("dve", (lambda e, qi=qi, cb_=cb_: e.reciprocal(out=cols[0:rows, cb_:cb_ + 1], in_=O[0][qi][:, 128:129])),
                         reads=names, writes=["fa%d" % pr])
                    P.op("dve", (lambda e, qi=qi, cb_=cb_: e.reciprocal(out=cols[0:rows, cb_ + 1:cb_ + 2], in_=O[1][qi][:, 128:129])),
                         reads=names + ["fa%d" % pr], writes=["fa%d" % pr])
                    P.op("dve", (lambda e, cb_=cb_: e.tensor_scalar(out=cols[0:rows, cb_ + 2:cb_ + 3], in0=cols[0:rows, cb_ + 1:cb_ + 2],
                                                                   scalar1=neglam[0:rows, :], scalar2=None, op0=ALU.mult)),
                         reads=["fa%d" % pr, "neglam"], writes=["fa%d" % pr])
                    P.op("dve", (lambda e, qi=qi, cb_=cb_, pr=pr: e.tensor_scalar(out=osm[0:rows, pr, 0, :], in0=O[1][qi][:, 0:128],
                                                                                  scalar1=cols[0:rows, cb_ + 2:cb_ + 3], scalar2=None,
                                                                                  op0=ALU.mult)),
                         reads=names + ["fa%d" % pr], writes=["osm%d" % pr])
                    P.op("dve", (lambda e, qi=qi, cb_=cb_, pr=pr: e.scalar_tensor_tensor(
                        out=osm[0:rows, pr, 1, :], in0=O[0][qi][:, 0:128], scalar=cols[0:rows, cb_:cb_ + 1], in1=osm[0:rows, pr, 0, :],
                        op0=ALU.mult, op1=ALU.add)), reads=names + ["fa%d" % pr, "osm%d" % pr], writes=["osm%d" % pr])
                    ci = cb_ + 3
                    P.op("dve", (lambda e, pr=pr, ci=ci: e.scalar_tensor_tensor(
                        out=osm[0:rows, pr, 0, :], in0=osm[0:rows, pr, 1, :], scalar=1.0, in1=osm[0:rows, pr, 1, :],
                        op0=ALU.mult, op1=ALU.mult, accum_out=cols[0:rows, ci:ci + 1])),
                        reads=["osm%d" % pr], writes=["osm%d" % pr, "fb%d" % pr])
                    P.op("act", (lambda e, ci=ci: e.activation(out=cols[0:rows, ci + 1:ci + 2], in_=cols[0:rows, ci:ci + 1], func=AF.Ln,
                                                              bias=eps5[0:rows, 0:1], scale=1.0 / 128)),
                         reads=["fb%d" % pr, "eps"], writes=["fb%d" % pr])
                    P.op("act", (lambda e, ci=ci: e.activation(out=cols[0:rows, ci + 2:ci + 3], in_=cols[0:rows, ci + 1:ci + 2],
                                                              func=AF.Exp, scale=-0.5)),
                         reads=["fb%d" % pr], writes=["fb%d" % pr])
                    dst, dnm = dst_fn(qi)
                    P.op("dve", (lambda e, pr=pr, ci=ci, dst=dst: e.scalar_tensor_tensor(
                        out=dst, in0=osm[0:rows, pr, 1, :], scalar=cols[0:rows, ci + 2:ci + 3], in1=sublnbc[0:rows, :],
                        op0=ALU.mult, op1=ALU.mult)), reads=["osm%d" % pr, "fb%d" % pr, "sublnbc"], writes=[dnm])
            return fin

        def phaseD1():
            xb = R2a[:, 0:1024]
            ostg = R2a[:, 4096:5120].rearrange("p (s f) -> p s f", f=512)
            obraw = R2a[:, 1024:1536]
            o = 0

            def carve(n):
                nonlocal o
                a = R2b[:, o:o + n]
                o += n
                return a
            oanl = carve(512)
            xs = carve(1024)
            xsT = carve(8 * 64).rearrange("p (c n) -> p c n", n=64)
            cst = carve(8 * 512).rearrange("p (t n) -> p t n", n=512)
            KTc = carve(4 * 1056).rearrange("p (h n) -> p h n", n=1056)
            Vc = carve(9 * 4 * 130).rearrange("p (t h e) -> p t h e", h=4, e=130)
            KBc = carve(4 * 544).rearrange("p (c n) -> p c n", n=544)
            VBc = carve(5 * 8 * 66).rearrange("p (t h e) -> p t h e", h=8, e=66)
            QAz = carve(2 * 4 * 64).rearrange("p (u h n) -> p u h n", u=2, n=64)
            QBzs = carve(2 * 4 * 64).rearrange("p (u c n) -> p u c n", u=2, n=64)
            P.op("pool", lambda e: e.memset(QAz, 0.0), writes=["QAs"])
            P.op("pool", lambda e: e.memset(QBzs, 0.0), writes=["QBs"])
            oans = carve(512)
            obns = carve(512)
            P.op("dve", lambda e: e.memset(Vc[:, :, :, 128:130], 1.0), writes=["Vc1"])
            P.op("dve", lambda e: e.memset(VBc[:, :, :, 64:66], 1.0), writes=["VBc1"])
            dmaL(xb[0:64, :], xsm.ap(), writes=["xb"], key="k_xb")
            rc, rn = rms_rstd(xb[0:64, :], 64, 1024, eps6, 0, ["xb"], "x")
            P.op("dve", (lambda e, rc=rc: e.scalar_tensor_tensor(out=xs[0:64, :], in0=xb[0:64, :], scalar=rc, in1=gattnbc[0:64, :],
                                                                  op0=ALU.mult, op1=ALU.mult)), reads=["xb", rn, "gattnbc"], writes=["xs"])
            transposes(xs[0:64, :], 64, 8, xsT[:, :, 0:64], ["xs"], ["xsT"])
            for (c0, dst) in ((512, sak), (1024, sav), (2048, sbk), (2560, sbv)):
                ps_, nm = proj_tm(xsT, 0, 64, lambda c, c0=c0: Wv[:, c, c0:c0 + 512], ["W", "xsT"])
                out_store(dst.ap(), ps_, nm, rows=64)
            for h in range(4):
                ps_, nm = proj_fm(lambda c, h=h: Wv[:, c, h * 128:(h + 1) * 128], xsT, 64, ["W", "xsT"])
                evac(QAz[:, 0, h, :], ps_, reads=[nm], writes=["QAs"], scale=0.125)
                P.op("pool", (lambda e, h=h: e.tensor_copy(out=QAz[64:128, 1, h, :], in_=QAz[64:128, 0, h, :])), reads=["QAs"], writes=["QAs"])
                P.op("pool", (lambda e, h=h: e.memset(QAz[64:128, 0, h, :], 0.0)), reads=["QAs"], writes=["QAs"])
                ps_, nm = proj_fm(lambda c, h=h: Wv[:, c, 1536 + h * 128:1536 + (h + 1) * 128], xsT, 64, ["W", "xsT"])
                evac(QBzs[:, 0, h, :], ps_, reads=[nm], writes=["QBs"], scale=0.125)
                P.op("pool", (lambda e, h=h: e.tensor_copy(out=QBzs[64:128, 1, h, :], in_=QBzs[64:128, 0, h, :])), reads=["QBs"], writes=["QBs"])
                P.op("pool", (lambda e, h=h: e.memset(QBzs[64:128, 0, h, :], 0.0)), reads=["QBs"], writes=["QBs"])
            for s in range(2):
                for h in range(4):
                    ps_, nm = proj_fm(lambda c, h=h: Wv[:, c, 512 + h * 128:512 + (h + 1) * 128], xsT[:, :, s * 32:(s + 1) * 32], 32,
                                      ["W", "xsT"])
                    evac(KTc[:, h, 1024:1056], ps_, reads=[nm], writes=["KTc"])
                    ps_, nm = proj_fm(lambda c, h=h: Wv[:, c, 2048 + h * 128:2048 + (h + 1) * 128], xsT[:, :, s * 32:(s + 1) * 32], 32,
                                      ["W", "xsT"])
                    evac(KBc[:, h, 512:544], ps_, reads=[nm], writes=["KBc"])
                ps_, nm = proj_tm(xsT, s * 32, 32, lambda c: Wv[:, c, 1024:1536], ["W", "xsT"])
                evac(Vc[0:32, 8, :, 0:128], ps_.rearrange("p (h e) -> p h e", e=128), reads=[nm], writes=["Vc"])
                ps_, nm = proj_tm(xsT, s * 32, 32, lambda c: Wv[:, c, 2560:3072], ["W", "xsT"])
                evac(VBc[0:32, 4, :, 0:64], ps_.rearrange("p (h e) -> p h e", e=64), reads=[nm], writes=["VBc"])
                dmaL(cst, cak.ap()[s * 1024:(s + 1) * 1024, :].rearrange("(t p) n -> p t n", p=128), writes=["cst"], key="k_cst",
                     eng="pool")
                for t in range(8):
                    transposes(cst[:, t, :], 128, 4, KTc[:, :, t * 128:(t + 1) * 128], ["cst"], ["KTc"])
                dmaL(cst[:, 0:4, :], cbk.ap()[s * 512:(s + 1) * 512, :].rearrange("(t p) n -> p t n", p=128), writes=["cst"],
                     key="k_cst", eng="pool")
                for t in range(4):
                    transposes(cst[:, t, :], 128, 4, KBc[:, :, t * 128:(t + 1) * 128], ["cst"], ["KBc"])
                for h_ in range(4):
                    dmaL(Vc[:, 0:8, h_, 0:128],
                         cav.ap()[s * 1024:(s + 1) * 1024, h_ * 128:(h_ + 1) * 128].rearrange("(t p) e -> p t e", p=128),
                         reads=["Vc1"], writes=["Vc"], key="k_vc", eng="pool")
                for h_ in range(8):
                    dmaL(VBc[:, 0:4, h_, 0:64],
                         cbv.ap()[s * 512:(s + 1) * 512, h_ * 64:(h_ + 1) * 64].rearrange("(t p) e -> p t e", p=128),
                         reads=["VBc1"], writes=["VBc"], key="k_vbc", eng="pool")
                for h in range(4):
                    kts = []
                    for t in range(9):
                        nk = 128 if t < 8 else 32
                        adds = []
                        if t == 7:
                            adds.append((0, [TA[:, h, 1, 0:32]] * 2))
                        if t == 8:
                            adds.append((0, [TA[0:32, h, 0, 0:32]] * 2))
                        bcol = chA[0:nk, h:h + 1]
                        kts.append(dict(KT=KTc[:, h, t * 128:t * 128 + nk], nk=nk, V=[Vc[0:nk, t, h, 0:129]] * 2, bias=[bcol, bcol],
                                        qlo=0, qhi=0, adds=adds, reads=["KTc", "Vc", "Vc1"]))
                    fin = finA_factory(h, lambda qi, h=h: (oans[0:32, h * 128:(h + 1) * 128], "oans"), 32)
                    pair_attn(kts, [QAz[:, u_, h, s * 32:(s + 1) * 32] for u_ in range(2)], [(0, 32)], 128, fin, ["QAs"])
                dmaL(OANss.ap()[s * 32:(s + 1) * 32, :], oans[0:32, :], reads=["oans"], writes=["OANss"], key="k_oans", eng="pool")
                for cb in range(4):
                    kts = []
                    for t in range(5):
                        nk = 128 if t < 4 else 32
                        adds = []
                        if t == 3:
                            adds.append((0, [TB[:, 2 * cb + u, 1, 0:32] for u in range(2)]))
                        if t == 4:
                            adds.append((0, [TB[0:32, 2 * cb + u, 0, 0:32] for u in range(2)]))
                        kts.append(dict(KT=KBc[:, cb, t * 128:t * 128 + nk], nk=nk,
                                        V=[VBc[0:nk, t, 2 * cb + u, 0:65] for u in range(2)],
                                        bias=[chB[0:nk, 2 * cb + u:2 * cb + u + 1] for u in range(2)],
                                        qlo=0, qhi=0, adds=adds, reads=["KBc", "VBc", "VBc1"]))

                    def finb(O, names, cb=cb):
                        for u in range(2):
                            hb = 2 * cb + u
                            ci = 8 + u
                            P.op("dve", (lambda e, u=u, ci=ci: e.reciprocal(out=cols[0:32, ci:ci + 1], in_=O[u][0][:, 64:65])),
                                 reads=names, writes=["c%d" % ci])
                            P.op("dve", (lambda e, u=u, ci=ci, hb=hb: e.tensor_scalar(
                                out=obraw[0:32, hb * 64:(hb + 1) * 64], in0=O[u][0][:, 0:64], scalar1=cols[0:32, ci:ci + 1],
                                scalar2=None, op0=ALU.mult)), reads=names + ["c%d" % ci], writes=["obraw"])
                    pair_attn(kts, [QBzs[:, u_, cb, s * 32:(s + 1) * 32] for u_ in range(2)], [(0, 32)], 64, finb, ["QBs"])
                rc, rn = rms_rstd(obraw[0:32, :], 32, 512, eps6, 16, ["obraw"], "ob")
                P.op("dve", (lambda e, rc=rc: e.scalar_tensor_tensor(out=obns[0:32, :], in0=obraw[0:32, :], scalar=rc, in1=onbbc[0:32, :],
                                                                      op0=ALU.mult, op1=ALU.mult)),
                     reads=["obraw", rn, "onbbc"], writes=["obns"])
                dmaL(OBNss.ap()[s * 32:(s + 1) * 32, :], obns[0:32, :], reads=["obns"], writes=["OBNs"], key="k_obns", eng="pool")
        if "D" in phases:
            phaseD1()
            P.barrier(bar[:])

        weight_casts()
        osbB = [(R2a[:, 0:1536].rearrange("p (b n) -> p b n", n=512), "osbA"),
                (R2a[:, 1536:3072].rearrange("p (b n) -> p b n", n=512), "osbB")]
        if "B" in phases:
            P.op("dve", lambda e: e.memset(Vaug[:, :, 128:130], 1.0), writes=["Vones"])
            P.op("dve", lambda e: e.memset(QTz, 0.0), writes=["QTh", "QTh0"])
            NCH = 4
            for h in range(4):
                for ch in range(NCH):
                    k0 = ch * (SEQV // NCH)
                    k1 = (ch + 1) * (SEQV // NCH)
                    dmaL(KTh[:, k0:k1], KTs.ap()[h, :, k0:k1], reads=["Vones"], writes=["KTh%d" % ch], key="k_kth%d" % ch)
                    dmaL(Vaug[:, k0 // 128:k1 // 128, 0:128],
                         VAs.ap()[k0:k1, h * 128:(h + 1) * 128].rearrange("(t p) e -> p t e", p=128),
                         reads=["Vones"], writes=["Vh%d" % ch], key="k_vh%d" % ch)
                for u_ in range(2):
                    dmaL(QTz[64 * u_:64 * u_ + 64, u_, 0:NTOK], QTs.ap()[h, 64 * u_:64 * u_ + 64, :], reads=["QTh0"], writes=["QTh"],
                         key="k_qth")
                for I in range(NOWN):
                    p = 4 * I + 3
                    kts = []
                    for kt in range(4 * p + 4):
                        r = kt - 4 * p
                        qlo = max(0, r)
                        adds = []
                        if r >= 0:
                            adds.append((r, [TA[:, h, 0, :]] * 2))
                            if r + 1 <= 3:
                                adds.append((r + 1, [TA[:, h, 1, :]] * 2))
                        elif r == -1:
                            adds.append((0, [TA[:, h, 1, :]] * 2))
                        bcol = cmA[:, h, kt // 4:kt // 4 + 1] if kt < 12 else chA[:, h:h + 1]
                        ch = kt * 128 // (SEQV // NCH)
                        kts.append(dict(KT=KTh[:, kt * 128:(kt + 1) * 128], nk=128, V=[Vaug[:, kt, 0:129]] * 2, bias=[bcol, bcol],
                                        qlo=qlo, qhi=3, adds=adds, reads=["KTh%d" % ch, "Vh%d" % ch, "Vones"]))
                    fin = finA_factory(h, lambda qi, I=I, h=h: (OAN[:, I * 4 + qi, h * 128:(h + 1) * 128], "OAN"), 128)
                    pair_attn(kts, [QTz[:, u_, I * 512:(I + 1) * 512] for u_ in range(2)], [(i * 128, 128) for i in range(4)], 128, fin,
                              ["QTh"], osb=osbB)
        P.barrier(bar[:])

        ACT_T = R1[:, 0:16384].rearrange("p (h n) -> p h n", n=512)
        WR = R1[:, 16384:32768].rearrange("p (s n) -> p s n", n=4096)
        hres = R2a[:, 0:4096].rearrange("p (t f) -> p t f", f=1024)
        p_s = R2a[:, 4096:5120].rearrange("p (t f) -> p t f", f=256)
        gs = R2a[:, 5120:5632]
        tmpf = R2a[:, 5632:6144]
        gmlpbc = R2a[:, 6144:7168]
        gfinbc = R2a[:, 7168:8192]
        cs = R2b[:, 16384:20480].rearrange("p (t f) -> p t f", f=1024)
        aT = R2b[:, 20480:24576].rearrange("p (c n) -> p c n", n=512)
        obn_s = R2b[:, 24576:26624].rearrange("p (t n) -> p t n", n=512)
        pb = R2b[:, 26624:27648].rearrange("p (t f) -> p t f", f=256)
        pT = R2b[:, 27648:28672].rearrange("p (c n) -> p c n", n=512)
        wcnt = {"i": 0}

        def wload(src_ap, shape_view):
            s = wcnt["i"] % 4
            wcnt["i"] += 1
            dst = shape_view(WR[:, s, :])
            dmaL(dst, src_ap, reads=["wscr"], writes=["WR%d" % s], key="k_wr%d" % s)
            return dst, "WR%d" % s

        def phaseC_group(tiles, x_ap, p_ap, oan_fn, obn_src, y_ap):
            NT = tiles[-1][0] + tiles[-1][1]
            nt = len(tiles)
            HR = ["hres%d" % t_ for t_ in range(nt)]
            PSN = ["p_s%d" % t_ for t_ in range(nt)]
            ATN = ["aT%d" % t_ for t_ in range(nt)]
            rows0 = tiles[0][1]
            if rows0 == 128:
                dmaL(hres[:, 0:nt, :], x_ap.rearrange("(t p) f -> p t f", p=128), writes=HR, key="k_hres")
                dmaL(p_s[:, 0:nt, :], p_ap.rearrange("(t p) f -> p t f", p=128), writes=PSN, key="k_ps")
                dmaL(obn_s[:, 0:nt, :], obn_src.rearrange("(t p) f -> p t f", p=128), reads=["OBNs"], writes=["obn_s"], key="k_obn")
            else:
                dmaL(hres[0:rows0, 0, :], x_ap, writes=HR, key="k_hres")
                dmaL(p_s[0:rows0, 0, :], p_ap, writes=PSN, key="k_ps")
                dmaL(obn_s[0:rows0, 0, :], obn_src, reads=["OBNs"], writes=["obn_s"], key="k_obn")
            for ti, (tok0, rows) in enumerate(tiles):
                oa, oan_names = oan_fn(ti)
                transposes(oa, rows, 4, aT[:, 0:4, tok0:tok0 + rows], oan_names, ["aT%d" % ti])
                transposes(obn_s[0:rows, ti, :], rows, 4, aT[:, 4:8, tok0:tok0 + rows], ["obn_s"], ["aT%d" % ti])
            wo = [wload(wout_b.ap()[j * 512:(j + 1) * 512, :].rearrange("(c p) n -> p c n", p=128),
                        lambda v: v.rearrange("p (c n) -> p c n", n=1024)) for j in range(2)]
            for ti, (tok0, rows) in enumerate(tiles):
                for hf in range(2):
                    ps_, nm = proj_tm(aT, tok0, rows, lambda c, hf=hf: wo[c // 4][0][:, c % 4, hf * 512:(hf + 1) * 512],
                                      ["aT%d" % ti, wo[0][1], wo[1][1]])
                    P.op("dve", (lambda e, ti=ti, rows=rows, hf=hf, ps_=ps_: e.tensor_tensor(
                        out=hres[0:rows, ti, hf * 512:(hf + 1) * 512], in0=ps_, in1=hres[0:rows, ti, hf * 512:(hf + 1) * 512], op=ALU.add)),
                        reads=[nm, "hres%d" % ti], writes=["hres%d" % ti], self_sync=False)
            for ti, (tok0, rows) in enumerate(tiles):
                rc, rn = rms_rstd(hres[0:rows, ti, :], rows, 1024, eps6, 3 * ti, ["hres%d" % ti], "c")
                P.op("dve", (lambda e, ti=ti, rows=rows, rc=rc: e.scalar_tensor_tensor(out=cs[0:rows, ti, :], in0=hres[0:rows, ti, :],
                                                                                       scalar=rc, in1=gmlpbc[0:rows, :], op0=ALU.mult,
                                                                                       op1=ALU.mult)),
                     reads=["hres%d" % ti, rn, "gmlpbc"], writes=["cs%d" % ti])
                transposes(cs[0:rows, ti, :], rows, 8, aT[:, :, tok0:tok0 + rows], ["cs%d" % ti], ["aT%d" % ti])
            for j in range(8):
                wu, wn = wload(wup_b.ap()[:, j * 512:(j + 1) * 512].rearrange("(c p) n -> p c n", p=128),
                               lambda v: v.rearrange("p (c n) -> p c n", n=512))
                for hl in range(4):
                    hc = j * 4 + hl
                    ps_, nm = proj_fm(lambda c, hl=hl, wu=wu: wu[:, c, hl * 128:(hl + 1) * 128], aT, NT, ATN + [wn])
                    rb, rbn = (tmpf, "tmpf") if hc % 2 else (gs, "gs")
                    P.op("act", (lambda e, ps_=ps_, rb=rb: e.activation(out=rb[:, 0:NT], in_=ps_, func=AF.Relu)),
                         reads=[nm], writes=[rbn])
                    P.op("pool", (lambda e, hc=hc, rb=rb: e.tensor_tensor(out=ACT_T[:, hc, 0:NT], in0=rb[:, 0:NT], in1=rb[:, 0:NT],
                                                                          op=ALU.mult)), reads=[rbn], writes=["ACT_T"])
            for hf in range(2):
                accs = []
                for ti in range(nt):
                    accs.append(ti)
                for j in range(4):
                    wd, wn = wload(wdown_b.ap()[j * 1024:(j + 1) * 1024, hf * 512:(hf + 1) * 512].rearrange("(c p) n -> p c n", p=128),
                                   lambda v: v.rearrange("p (c n) -> p c n", n=512))
                    for hl in range(8):
                        hc = j * 8 + hl
                        for ti, (tok0, rows) in enumerate(tiles):
                            pe_op(128, rows, (lambda e, ti=ti, tok0=tok0, rows=rows, hc=hc, hl=hl, wd=wd: e.matmul(
                                PS[0:rows, ti, :], lhsT=ACT_T[:, hc, tok0:tok0 + rows], rhs=wd[:, hl, :], start=(hc == 0), stop=(hc == 31))),
                                reads=["ACT_T", wn], writes=["PS%d" % ti])
                for ti, (tok0, rows) in enumerate(tiles):
                    P.op("dve", (lambda e, ti=ti, rows=rows, hf=hf: e.tensor_tensor(
                        out=hres[0:rows, ti, hf * 512:(hf + 1) * 512], in0=PS[0:rows, ti, :], in1=hres[0:rows, ti, hf * 512:(hf + 1) * 512],
                        op=ALU.add)), reads=["PS%d" % ti, "hres%d" % ti], writes=["hres%d" % ti], self_sync=False)
            for ti, (tok0, rows) in enumerate(tiles):
                P.op("act", (lambda e, ti=ti, rows=rows: e.activation(out=cs[0:rows, ti, :], in_=hres[0:rows, ti, :], func=AF.Copy, scale=1.0)),
                     reads=["hres%d" % ti], writes=["cs%d" % ti])
                transposes(cs[0:rows, ti, :], rows, 8, aT[:, :, tok0:tok0 + rows], ["cs%d" % ti], ["aT%d" % ti])
                P.op("dve", (lambda e, ti=ti, rows=rows: e.tensor_copy(out=pb[0:rows, ti, :], in_=p_s[0:rows, ti, :])),
                     reads=["p_s%d" % ti], writes=["pb%d" % ti])
                transposes(pb[0:rows, ti, :], rows, 2, pT[:, :, tok0:tok0 + rows], ["pb%d" % ti], ["pT%d" % ti])
            wg = [wload(wgate_b.ap()[j * 512:(j + 1) * 512, :].rearrange("(c p) n -> p c n", p=128),
                        lambda v: v.rearrange("p (c n) -> p c n", n=1024)) for j in range(2)]
            wp, wpn = wload(wple_b.ap().rearrange("(c p) n -> p c n", p=128),
                            lambda v: v[:, 0:2048].rearrange("p (c n) -> p c n", n=1024))
            for ti, (tok0, rows) in enumerate(tiles):
                for hf in range(2):
                    ps_, nm = proj_tm(aT, tok0, rows, lambda c, hf=hf: wg[c // 4][0][:, c % 4, hf * 512:(hf + 1) * 512],
                                      ["aT%d" % ti, wg[0][1], wg[1][1]])
                    P.op("act", (lambda e, rows=rows, ps_=ps_: e.activation(out=gs[0:rows, :], in_=ps_, func=AF.Sigmoid)),
                         reads=[nm], writes=["gs"])
                    ps2, nm2 = proj_tm(pT, tok0, rows, lambda c, hf=hf: wp[:, c, hf * 512:(hf + 1) * 512], ["pT%d" % ti, wpn], nk=2)
                    P.op("dve", (lambda e, rows=rows, ps2=ps2: e.tensor_tensor(out=tmpf[0:rows, :], in0=gs[0:rows, :], in1=ps2, op=ALU.mult)),
                         reads=["gs", nm2], writes=["tmpf"])
                    P.op("dve", (lambda e, ti=ti, rows=rows, hf=hf: e.tensor_tensor(
                        out=hres[0:rows, ti, hf * 512:(hf + 1) * 512], in0=tmpf[0:rows, :], in1=hres[0:rows, ti, hf * 512:(hf + 1) * 512],
                        op=ALU.add)), reads=["tmpf", "hres%d" % ti], writes=["hres%d" % ti])
            for ti, (tok0, rows) in enumerate(tiles):
                rc, rn = rms_rstd(hres[0:rows, ti, :], rows, 1024, eps6, 3 * ti, ["hres%d" % ti], "f")
                P.op("dve", (lambda e, ti=ti, rows=rows, rc=rc: e.scalar_tensor_tensor(out=hres[0:rows, ti, :], in0=hres[0:rows, ti, :],
                                                                                       scalar=rc, in1=gfinbc[0:rows, :], op0=ALU.mult,
                                                                                       op1=ALU.mult)),
                     reads=["hres%d" % ti, rn, "gfinbc"], writes=["hres%d" % ti])
            if rows0 == 128:
                dmaO(y_ap.rearrange("(t p) f -> p t f", p=128), hres[:, 0:nt, :], reads=HR, key="k_y")
            else:
                dmaO(y_ap, hres[0:rows0, 0, :], reads=HR, key="k_y")

        if "C" in phases or "D" in phases:
            dmaL(gmlpbc, bass.AP(g_mlp, 0, [[0, 128], [1, 1024]]), writes=["gmlpbc"], key="k_c12")
            dmaL(gfinbc, bass.AP(g_final, 0, [[0, 128], [1, 1024]]), writes=["gfinbc"], key="k_c13")
        if "C" in phases:
            for I in range(NOWN):
                phaseC_group([(i * 128, 128) for i in range(4)], xv.ap()[(4 * I + 3) * 512:(4 * I + 4) * 512, :],
                             pv.ap()[I * 512:(I + 1) * 512, :],
                             lambda ti, I=I: (OAN[:, I * 4 + ti, :], ["OAN"]),
                             OBNs.ap()[I * 512:(I + 1) * 512, :], y.ap()[I * 512:(I + 1) * 512, :])
        P.barrier(bar[:])

        def phaseD2():
            oanl = R2b[:, 0:512]
            dmaL(oanl[0:64, :], OANss.ap(), reads=["OANss"], writes=["oanl"], key="k_oanl")
            phaseC_group([(0, 64)], xsm.ap(), psm.ap(), lambda ti: (oanl[0:64, :], ["oanl"]), OBNss.ap(), ys.ap())

        if "D" in phases:
            phaseD2()

        P.emit(final_dma_keys=[] if "5" in phases else sorted(outkeys))
    return nc


_NC_CACHE = {}


def _run(x_prompt, x_sample, cache_a_k, cache_a_v, cache_b_k, cache_b_v, p_prompt, p_sample,
         t5_table, g_attn, w_in, lambda_q1, lambda_k1, lambda_q2, lambda_k2, subln_g,
         band_table, out_norm_b, w_out, g_mlp, w_up, w_down, w_ple_gate, w_ple_proj, g_final,
         phases="ABCD", cores=None, trace=False):
    f = lambda a: np.ascontiguousarray(np.asarray(a, dtype=np.float32))
    x_prompt = f(x_prompt); x_sample = f(x_sample); p_prompt = f(p_prompt); p_sample = f(p_sample)
    cache_a_k = f(cache_a_k); cache_a_v = f(cache_a_v); cache_b_k = f(cache_b_k); cache_b_v = f(cache_b_v)
    S = x_prompt.shape[1]
    nblk = S // 512
    nown = nblk // 4
    seqv = nblk * 512
    ident, J, CA, CB = _consts()
    ck = (phases, nblk)
    if ck not in _NC_CACHE:
        _NC_CACHE[ck] = build_program(phases, nblk)
    nc = _NC_CACHE[ck]
    common = {
        "t5": f(t5_table), "bt": f(band_table)[0], "g_attn": f(g_attn), "w_in": f(w_in)[0],
        "lq1": f(lambda_q1), "lk1": f(lambda_k1), "lq2": f(lambda_q2), "lk2": f(lambda_k2),
        "subln": f(subln_g), "onb": f(out_norm_b), "w_out": f(w_out)[0], "g_mlp": f(g_mlp),
        "w_up": f(w_up)[0], "w_down": f(w_down)[0], "w_gate": f(w_ple_gate)[0], "w_ple": f(w_ple_proj)[0],
        "g_final": f(g_final).reshape(1, 1024), "ident": ident, "J": J, "CA": CA, "CB": CB,
    }
    cores = list(range(8)) if cores is None else list(cores)
    in_maps = []
    for c in cores:
        b, j = divmod(c, 4)
        npad = 3 - j
        xvv = np.zeros((seqv, 1024), np.float32)
        nreal = (nblk - npad) * 512
        xvv[npad * 512:] = x_prompt[b, :nreal]
        pvv = np.concatenate([p_prompt[0, b, (4 * I + j) * 512:(4 * I + j + 1) * 512] for I in range(nown)], 0)
        pm = np.zeros((128, 4), np.float32)
        pm[:, :npad] = NEGM
        m = dict(common)
        m.update({
            "xv": xvv, "pv": np.ascontiguousarray(pvv),
            "xsm": x_sample[2 * c:2 * c + 2].reshape(64, 1024), "psm": p_sample[0, 2 * c:2 * c + 2].reshape(64, 256),
            "cak": cache_a_k[0, 2 * c:2 * c + 2].reshape(2048, 512), "cav": cache_a_v[0, 2 * c:2 * c + 2].reshape(2048, 512),
            "cbk": cache_b_k[0, 2 * c:2 * c + 2].reshape(1024, 512), "cbv": cache_b_v[0, 2 * c:2 * c + 2].reshape(1024, 512),
            "padmask": pm,
        })
        in_maps.append({k: np.ascontiguousarray(v) for k, v in m.items()})
    if trace:
        res = run_bass_kernel_spmd(nc, in_maps, core_ids=list(range(len(cores))), trace=True)
        print("EXEC_TIME_NS", res.exec_time_ns)
    else:
        res = run_bass_kernel_spmd(nc, in_maps, core_ids=list(range(len(cores))))
    R = res.results
    y_prompt = np.zeros((2, S, 1024), np.float32)
    nakp = np.zeros((1, 2, S, 512), np.float32)
    navp = np.zeros((1, 2, S, 512), np.float32)
    nbkp = np.zeros((1, 2, 512, 512), np.float32)
    nbvp = np.zeros((1, 2, 512, 512), np.float32)
    y_sample = np.zeros((16, 32, 1024), np.float32)
    saks = np.zeros((1, 16, 32, 512), np.float32); savs = np.zeros((1, 16, 32, 512), np.float32)
    sbks = np.zeros((1, 16, 32, 512), np.float32); sbvs = np.zeros((1, 16, 32, 512), np.float32)
    for ci, c in enumerate(cores):
        b, j = divmod(c, 4)
        r = R[ci]
        for I in range(nown):
            g0 = (4 * I + j) * 512
            y_prompt[b, g0:g0 + 512] = r["y"][I * 512:(I + 1) * 512]
            nakp[0, b, g0:g0 + 512] = r["nak"][I * 512:(I + 1) * 512]
            navp[0, b, g0:g0 + 512] = r["nav"][I * 512:(I + 1) * 512]
        if j == 3:
            nbkp[0, b] = r["nbk"]
            nbvp[0, b] = r["nbv"]
        y_sample[2 * c:2 * c + 2] = r["ys"].reshape(2, 32, 1024)
        saks[0, 2 * c:2 * c + 2] = r["sak"].reshape(2, 32, 512)
        savs[0, 2 * c:2 * c + 2] = r["sav"].reshape(2, 32, 512)
        sbks[0, 2 * c:2 * c + 2] = r["sbk"].reshape(2, 32, 512)
        sbvs[0, 2 * c:2 * c + 2] = r["sbv"].reshape(2, 32, 512)
    return (y_prompt, y_sample,
            nakp.reshape(1, 2, S, 4, 2, 64), navp.reshape(1, 2, S, 4, 128),
            nbkp.reshape(1, 2, 512, 8, 64), nbvp.reshape(1, 2, 512, 8, 64),
            saks.reshape(1, 16, 32, 4, 2, 64), savs.reshape(1, 16, 32, 4, 128),
            sbks.reshape(1, 16, 32, 8, 64), sbvs.reshape(1, 16, 32, 8, 64))


def kernel(x_prompt, x_sample, cache_a_k, cache_a_v, cache_b_k, cache_b_v, p_prompt, p_sample,
           t5_table, g_attn, w_in, lambda_q1, lambda_k1, lambda_q2, lambda_k2, subln_g,
           band_table, out_norm_b, w_out, g_mlp, w_up, w_down, w_ple_gate, w_ple_proj, g_final):
    return _run(x_prompt, x_sample, cache_a_k, cache_a_v, cache_b_k, cache_b_v, p_prompt, p_sample,
                t5_table, g_attn, w_in, lambda_q1, lambda_k1, lambda_q2, lambda_k2, subln_g,
                band_table, out_norm_b, w_out, g_mlp, w_up, w_down, w_ple_gate, w_ple_proj, g_final)
```

```python
import math
import contextlib
import numpy as np
import concourse.bass as bass
import concourse.mybir as mybir
from concourse.bass_utils import run_bass_kernel_spmd

ENGS = ("pe", "act", "dve", "pool", "sp")


class Op:
    __slots__ = ("idx", "eng", "fn", "dma_key", "dma_cnt", "waits", "signal", "sigcnt", "eidx")

    def __init__(self, idx, eng, fn, dma_key):
        self.idx = idx
        self.eng = eng
        self.fn = fn
        self.dma_key = dma_key
        self.dma_cnt = 0
        self.waits = []
        self.signal = False
        self.sigcnt = 0
        self.eidx = 0


class Prog:
    def __init__(self, nc):
        self.nc = nc
        self.ops = []
        self.by_eng = {e: [] for e in ENGS}
        self.last_w = {}
        self.readers = {}
        self.seen = {e: {f: -1 for f in ENGS} for e in ENGS}
        self.seen_dma = {e: {} for e in ENGS}
        self.dma_counts = {}
        self.final_dma = []
        self.force = {}

    def op(self, eng, fn, reads=(), writes=(), dma_key=None, self_sync=True, extra=()):
        o = Op(len(self.ops), eng, fn, dma_key)
        o.eidx = len(self.by_eng[eng])
        deps = []
        for r in reads:
            w = self.last_w.get(r)
            if w is not None:
                deps.append(w)
        for w_ in writes:
            w = self.last_w.get(w_)
            if w is not None:
                deps.append(w)
            deps.extend(self.readers.get(w_, ()))
        for r in reads:
            self.readers.setdefault(r, []).append(o)
        for w_ in writes:
            self.last_w[w_] = o
            self.readers[w_] = [x for x in self.readers.get(w_, ()) if x is o]
        f = self.force.pop(eng, None)
        if f is not None:
            deps.append(f)
        deps.extend(extra)
        if dma_key is not None:
            self.dma_counts[dma_key] = self.dma_counts.get(dma_key, 0) + 16
            o.dma_cnt = self.dma_counts[dma_key]
        for d in deps:
            if d is o:
                continue
            if d.dma_key is not None:
                cur = self.seen_dma[eng].get(d.dma_key, 0)
                if cur >= d.dma_cnt:
                    continue
                self.seen_dma[eng][d.dma_key] = d.dma_cnt
                o.waits.append(("dma", d.dma_key, d.dma_cnt))
            else:
                if d.eng == eng and (eng == "pe" or not self_sync):
                    continue
                if self.seen[eng][d.eng] >= d.eidx:
                    continue
                self.seen[eng][d.eng] = d.eidx
                d.signal = True
                o.waits.append(("eng", d.eng, d))
        self.ops.append(o)
        self.by_eng[eng].append(o)
        return o

    def barrier(self, tile, skip=()):
        extra = [self.by_eng[e][-1] for e in ENGS if self.by_eng[e]]
        last_dma = {}
        for o in self.ops:
            if o.dma_key is not None and o.dma_key not in skip:
                last_dma[o.dma_key] = o
        extra.extend(last_dma.values())
        b = self.op("dve", lambda e: e.memset(tile, 0.0), extra=extra)
        for e in ENGS:
            if e != "dve":
                self.force[e] = b
        return b

    def emit(self, final_dma_keys=()):
        nc = self.nc
        import contextlib
        for o in self.ops:
            best = {}
            for w in o.waits:
                if w[0] == "dma":
                    k = ("dma", w[1])
                    if k not in best or best[k][2] < w[2]:
                        best[k] = w
                else:
                    k = ("eng", w[1])
                    if k not in best or best[k][2].eidx < w[2].eidx:
                        best[k] = w
            o.waits = list(best.values())
        for e in ENGS:
            c = 0
            for o in self.by_eng[e]:
                if o.signal:
                    c += 1
                    o.sigcnt = c
        with contextlib.ExitStack() as st:
            esem = {e: st.enter_context(nc.semaphore("s_" + e)) for e in ENGS}
            dsem = {k: st.enter_context(nc.semaphore("d_%d" % i))
                    for i, k in enumerate(sorted(self.dma_counts))}
            block = st.enter_context(nc.Block())

            def run(e, eng):
                for o in self.by_eng[e]:
                    for w in o.waits:
                        if w[0] == "dma":
                            eng.wait_ge(dsem[w[1]], w[2])
                        else:
                            eng.wait_ge(esem[w[1]], w[2].sigcnt)
                    inst = o.fn(eng)
                    if o.dma_key is not None:
                        inst.then_inc(dsem[o.dma_key], 16)
                    elif o.signal:
                        inst.then_inc(esem[e], 1)
                if e == "sp":
                    for k in final_dma_keys:
                        eng.wait_ge(dsem[k], self.dma_counts[k])

            @block.tensor
            def _(eng):
                run("pe", eng)

            @block.scalar
            def _(eng):
                run("act", eng)

            @block.vector
            def _(eng):
                run("dve", eng)

            @block.gpsimd
            def _(eng):
                run("pool", eng)

            @block.sync
            def _(eng):
                run("sp", eng)

F32 = mybir.dt.float32
BF16 = mybir.dt.bfloat16
AF = mybir.ActivationFunctionType
ALU = mybir.AluOpType
NEGM = -30000.0
NBLK = 32
NOWN = 8
SEQV = NBLK * 512


def _t5_bucket_np(rel):
    half = 16
    n = -rel
    ret = np.where(n < 0, half, 0)
    n = np.abs(n)
    max_exact = 8
    nf = np.maximum(n, 1).astype(np.float32)
    large = max_exact + (np.log(nf / np.float32(max_exact)) / np.float32(math.log(128 / max_exact))
                         * np.float32(half - max_exact)).astype(np.int32)
    large = np.minimum(large, half - 1)
    return ret + np.where(n < max_exact, n, large)


def _consts():
    ident = np.eye(128, dtype=np.float32)
    J = ident[::-1].copy()
    i = np.arange(384)
    delta = i - 255
    CA = np.zeros((32, 384), np.float32)
    bk = _t5_bucket_np(delta.astype(np.int32))
    CA[bk, i] += 1.0
    CA[15, :] -= 1.0
    CA[:, 383] = 0.0
    CB = np.zeros((384, 384), np.float32)
    idx = np.clip(delta, -128, 128) + 128
    CB[idx, i] += 1.0
    CB[0, :] -= 1.0
    CB[:, 383] = 0.0
    return ident, J, CA, CB


def build_program(phases="ABCD", nblk=32):
    global NBLK, NOWN, SEQV
    NBLK = nblk
    NOWN = nblk // 4
    SEQV = nblk * 512
    NTOK = NOWN * 512
    nc = bass.Bass("TRN2", target_bir_lowering=False)
    T = {}

    def din(name, shape, dt=F32):
        T[name] = nc.dram_tensor(name, shape, dt, kind="ExternalInput")
        return T[name]

    def dout(name, shape, dt=F32):
        T[name] = nc.dram_tensor(name, shape, dt, kind="ExternalOutput")
        return T[name]

    def dscr(name, shape, dt=BF16):
        T[name] = nc.dram_tensor(name, shape, dt, kind="Internal")
        return T[name]

    xv = din("xv", [SEQV, 1024]); pv = din("pv", [NTOK, 256])
    xsm = din("xsm", [64, 1024]); psm = din("psm", [64, 256])
    cak = din("cak", [2048, 512]); cav = din("cav", [2048, 512])
    cbk = din("cbk", [1024, 512]); cbv = din("cbv", [1024, 512])
    padmask = din("padmask", [128, 4])
    t5 = din("t5", [32, 4]); bt = din("bt", [257, 8])
    g_attn = din("g_attn", [1, 1024]); w_in = din("w_in", [1024, 3072])
    lq1 = din("lq1", [1, 64]); lk1 = din("lk1", [1, 64]); lq2 = din("lq2", [1, 64]); lk2 = din("lk2", [1, 64])
    subln = din("subln", [1, 128]); onb = din("onb", [1, 512])
    w_out = din("w_out", [1024, 1024]); g_mlp = din("g_mlp", [1, 1024])
    w_up = din("w_up", [1024, 4096]); w_down = din("w_down", [4096, 1024])
    w_gate = din("w_gate", [1024, 1024]); w_ple = din("w_ple", [256, 1024]); g_final = din("g_final", [1, 1024])
    identd = din("ident", [128, 128]); Jd = din("J", [128, 128]); CAd = din("CA", [32, 384]); CBd = din("CB", [384, 384])

    y = dout("y", [NTOK, 1024]); ys = dout("ys", [64, 1024])
    nak = dout("nak", [NTOK, 512]); nav = dout("nav", [NTOK, 512])
    nbk = dout("nbk", [512, 512]); nbv = dout("nbv", [512, 512])
    sak = dout("sak", [64, 512]); sav = dout("sav", [64, 512]); sbk = dout("sbk", [64, 512]); sbv = dout("sbv", [64, 512])

    KTs = dscr("KTs", [4, 128, SEQV]); VAs = dscr("VAs", [SEQV, 512]); QTs = dscr("QTs", [4, 128, NTOK])
    OBNs = dscr("OBNs", [NTOK, 512]); OANss = dscr("OANss", [64, 512]); OBNss = dscr("OBNss", [64, 512])
    vecA = dscr("vecA", [4, 384], F32); vecB = dscr("vecB", [8, 384], F32)
    wout_b = dscr("wout_b", [1024, 1024]); wup_b = dscr("wup_b", [1024, 4096]); wdown_b = dscr("wdown_b", [4096, 1024])
    wgate_b = dscr("wgate_b", [1024, 1024]); wple_b = dscr("wple_b", [256, 1024])

    P = Prog(nc)
    outkeys = set()
    with contextlib.ExitStack() as st:
        def sb(name, shape, dt):
            return st.enter_context(nc.sbuf_tensor(name, shape, dt))

        def psm_(name, shape, dt):
            return st.enter_context(nc.psum_tensor(name, shape, dt))

        R1 = sb("R1", [128, 33024], BF16)
        R2a = sb("R2a", [128, 8192], F32)
        R2b = sb("R2b", [128, 28800], BF16)
        idb = sb("idb", [128, 128], BF16)
        Js = sb("Js", [128, 128], F32)
        Hs = sb("Hs", [128, 128], F32)
        TA = sb("TA", [128, 4, 2, 128], F32)
        TB = sb("TB", [128, 8, 2, 128], F32)
        Tm4 = sb("Tm4", [128, 128], F32)
        chA = sb("chA", [128, 4], F32); chB = sb("chB", [128, 8], F32)
        cmA = sb("cmA", [128, 4, 3], F32); cmB = sb("cmB", [128, 8], F32)
        pmk = sb("pmk", [128, 4], F32)
        lam4 = sb("lam4", [128, 4, 64], F32)
        lcol = sb("lcol", [128, 8], F32)
        sublnbc = sb("sublnbc", [128, 128], F32)
        onbbc = sb("onbbc", [128, 512], F32)
        junk = sb("junk", [128, 1024], BF16)
        Eb = sb("Eb", [128, 3, 2, 512], BF16)
        cols = sb("cols", [128, 64], F32)
        eps6 = sb("eps6", [128, 1], F32); eps5 = sb("eps5", [128, 1], F32)
        bar = sb("bar", [128, 1], F32)
        osm = sb("osm", [128, 2, 2, 128], F32)
        t5s = sb("t5s", [32, 4], F32); bts = sb("bts", [128, 3, 8], F32)
        CAs = sb("CAs", [32, 384], F32); CBs = sb("CBs", [128, 3, 384], F32)
        vst = sb("vst", [8, 384], F32)

        TP = psm_("TP", [128, 2, 1024], BF16)
        PS = psm_("PS", [128, 6, 512], F32)
        TPf = TP.bitcast(F32)
        obanks = [(PS[:, 4, :], "PS4"), (PS[:, 5, :], "PS5"), (TPf[:, 0, :], "TP0")]

        cnt = {"ev": 0, "tp": 0, "ps": 0, "sl": 0, "ost": 0, "el": 0}

        KM = {"k_idb": "q1", "k_w": "q10", "k_v0": "q14", "k_v1": "q15", "k_h": "q16",
              "k_xb": "q0", "k_ktst": "q1", "k_vast": "q2", "k_qast": "q3", "k_ost0": "q4", "k_ost1": "q5", "k_obst": "q6",
              "k_win": "q7", "k_c11": "q8",
              "k_kth0": "q0", "k_kth1": "q1", "k_kth2": "q2", "k_kth3": "q3", "k_vh0": "q0", "k_vh1": "q1", "k_vh2": "q2",
              "k_vh3": "q3", "k_qth": "q4",
              "k_wr0": "q0", "k_wr1": "q1", "k_wr2": "q2", "k_wr3": "q3", "k_hres": "q4", "k_ps": "q5", "k_obn": "q6",
              "k_y": "q7", "k_c12": "q8", "k_c13": "q9",
              "k_cst": "q1", "k_vc": "q2", "k_vbc": "q3", "k_oans": "q6", "k_obns": "q10", "k_oanl": "q9"}
        for i_ in range(11):
            KM["k_c%d" % i_] = "q0"
        KM.update({"k_xb0": "q0", "k_xb1": "q11", "k_xb2": "q12", "k_xb3": "q13"})

        def dmaL(out, in_, reads=(), writes=(), key=None, eng="sp"):
            key = KM[key]
            return P.op(eng, lambda e: e.dma_start(out=out, in_=in_), reads=reads, writes=writes, dma_key=key)

        def dmaO(out, in_, reads, key):
            key = KM[key]
            outkeys.add(key)
            return P.op("sp" if "4" in phases else "pool", lambda e: e.dma_start(out=out, in_=in_), reads=reads, dma_key=key)

        def evac(out, in_, reads, writes, scale=None, eng=None):
            if eng is None:
                cnt["ev"] += 1
                eng = "act" if cnt["ev"] % 2 else "dve"
            if eng == "act":
                s = 1.0 if scale is None else scale
                return P.op("act", lambda e: e.activation(out=out, in_=in_, func=AF.Copy, scale=s), reads=reads, writes=writes)
            if scale is None:
                return P.op("dve", lambda e: e.tensor_copy(out=out, in_=in_), reads=reads, writes=writes)
            return P.op("dve", lambda e: e.tensor_scalar(out=out, in0=in_, scalar1=scale, scalar2=None, op0=ALU.mult),
                        reads=reads, writes=writes)

        pe_state = {"mode": None}

        def pe_op(K, M, fn, reads=(), writes=()):
            r = lambda x: 32 if x <= 32 else (64 if x <= 64 else 128)
            mode = (r(K), r(M))
            if pe_state["mode"] is not None and pe_state["mode"] != mode:
                P.op("pe", lambda e: e.drain())
            pe_state["mode"] = mode
            return P.op("pe", fn, reads=reads, writes=writes)

        def nextps():
            cnt["ps"] = (cnt["ps"] + 1) % 4
            return cnt["ps"]

        def rms_rstd(src, rows, F, eps_t, ci, reads, tag):
            P.op("act", lambda e: e.activation(out=junk[0:rows, 0:F], in_=src, func=AF.Square,
                                               accum_out=cols[0:rows, ci:ci + 1]),
                 reads=reads, writes=["junk", "c%d" % ci])
            P.op("act", lambda e: e.activation(out=cols[0:rows, ci + 1:ci + 2], in_=cols[0:rows, ci:ci + 1], func=AF.Sqrt,
                                               bias=eps_t[0:rows, 0:1], scale=1.0 / F),
                 reads=["c%d" % ci, "eps"], writes=["c%d" % (ci + 1)])
            P.op("dve", lambda e: e.reciprocal(out=cols[0:rows, ci + 2:ci + 3], in_=cols[0:rows, ci + 1:ci + 2]),
                 reads=["c%d" % (ci + 1)], writes=["c%d" % (ci + 2)])
            return cols[0:rows, ci + 2:ci + 3], "c%d" % (ci + 2)

        def transposes(src, rows, nch, dst, reads, writes):
            b = cnt["tp"] % 2
            cnt["tp"] += 1
            for c in range(nch):
                pe_op(rows, 128, (lambda e, c=c: e.transpose(TP[:, b, c * 128:c * 128 + rows], src[:, c * 128:(c + 1) * 128],
                                                       idb[0:rows, 0:rows])),
                     reads=list(reads) + ["idb"], writes=["TP%d" % b])
            tv = TP[:, b, :].rearrange("p (a r) -> p a r", r=128)[:, 0:nch, 0:rows]
            evac(dst, tv, reads=["TP%d" % b], writes=writes)

        def proj_fm(wfn, rhsT, n, reads):
            b = nextps()
            for c in range(8):
                pe_op(128, 128, (lambda e, c=c: e.matmul(PS[:, b, 0:n], lhsT=wfn(c), rhs=rhsT[:, c, 0:n], start=(c == 0), stop=(c == 7))),
                     reads=reads, writes=["PS%d" % b])
            return PS[:, b, 0:n], "PS%d" % b

        def proj_tm(xT, tok0, rows, wfn, reads, nk=8):
            b = nextps()
            for c in range(nk):
                pe_op(128, rows, (lambda e, c=c: e.matmul(PS[0:rows, b, :], lhsT=xT[:, c, tok0:tok0 + rows], rhs=wfn(c),
                                                    start=(c == 0), stop=(c == nk - 1))),
                     reads=reads, writes=["PS%d" % b])
            return PS[0:rows, b, :], "PS%d" % b

        osb_state = {"i": 0}

        def pair_attn(keytiles, QT, qtiles, ed, finalize, qreads, osb=None):
            nqt = len(qtiles)
            per_bank = 512 // (ed + 1)
            started = set()

            def oloc(u, qi):
                g = u * nqt + qi
                bank, slot = divmod(g, per_bank)
                ap, nm = obanks[bank]
                rows = qtiles[qi][1]
                return ap[0:rows, slot * (ed + 1):(slot + 1) * (ed + 1)], nm, bank

            def qk(kt):
                sl = cnt["sl"] % 2
                cnt["sl"] += 1
                kt["sl"] = sl
                kt["el"] = cnt["el"] % 3
                cnt["el"] += 1
                nk = kt["nk"]
                c0 = qtiles[kt["qlo"]][0]
                c1 = qtiles[kt["qhi"]][0] + qtiles[kt["qhi"]][1]
                kt["c"] = (c0, c1)
                for u in range(2):
                    pe_op(128, nk, (lambda e, u=u: e.matmul(PS[0:nk, sl * 2 + u, c0:c1], lhsT=kt["KT"],
                                                            rhs=QT[u][:, c0:c1], start=True, stop=True)),
                         reads=list(kt["reads"]) + list(qreads), writes=["PS%d" % (sl * 2 + u)])
            def qk2(kt):
                sl = kt["sl"]
                el = kt["el"]
                nk = kt["nk"]
                c0, c1 = kt["c"]
                for (qi, Ts) in ([] if "k" in phases else kt["adds"]):
                    q0, qr = qtiles[qi]
                    for u in range(2):
                        P.op("dve", (lambda e, u=u, q0=q0, qr=qr, Ts=Ts: e.tensor_tensor(
                            out=PS[0:nk, sl * 2 + u, q0:q0 + qr], in0=PS[0:nk, sl * 2 + u, q0:q0 + qr], in1=Ts[u], op=ALU.add)),
                            reads=["PS%d" % (sl * 2 + u), "Tt"], writes=["PS%d" % (sl * 2 + u)], self_sync=False)
                if kt["bias"][0] is kt["bias"][1]:
                    P.op("act", lambda e: e.activation(out=Eb[0:nk, el, :, c0:c1], in_=PS[0:nk, sl * 2:sl * 2 + 2, c0:c1],
                                                       func=AF.Exp, bias=kt["bias"][0], scale=1.0),
                         reads=["PS%d" % (sl * 2), "PS%d" % (sl * 2 + 1), "bias"], writes=["E%d" % el])
                else:
                    for u in range(2):
                        P.op("act", (lambda e, u=u: e.activation(out=Eb[0:nk, el, u, c0:c1], in_=PS[0:nk, sl * 2 + u, c0:c1],
                                                                 func=AF.Exp, bias=kt["bias"][u], scale=1.0)),
                             reads=["PS%d" % (sl * 2 + u), "bias"], writes=["E%d" % el])

            def pvm(kt):
                if "l" in phases:
                    return
                sl = kt["el"]
                nk = kt["nk"]
                for u in range(2):
                    for qi in range(kt["qlo"], kt["qhi"] + 1):
                        oap, nm, bank = oloc(u, qi)
                        q0, qr = qtiles[qi]
                        first = bank not in started
                        started.add(bank)
                        pe_op(nk, qr, (lambda e, u=u, oap=oap, q0=q0, qr=qr, first=first: e.matmul(
                            oap, lhsT=Eb[0:nk, sl, u, q0:q0 + qr], rhs=kt["V"][u], start=first, stop=False,
                            skip_group_check=True)),
                            reads=["E%d" % sl] + list(kt["reads"]), writes=[nm])

            pend = []
            for kt in keytiles:
                qk(kt)
                pend.append(kt)
                if len(pend) > 2:
                    pvm(pend.pop(0))
                qk2(kt)
            for kt in pend:
                pvm(kt)
            O = [[oloc(u, qi)[0] for qi in range(nqt)] for u in range(2)]
            names = sorted({oloc(u, qi)[1] for u in range(2) for qi in range(nqt)})
            if osb is not None:
                sset = osb[osb_state["i"] % len(osb)]
                osb_state["i"] += 1
                used = sorted({oloc(u, qi)[2] for u in range(2) for qi in range(nqt)})
                for bk in used:
                    bap, bnm = obanks[bk]
                    P.op("dve", (lambda e, bk=bk, bap=bap: e.tensor_copy(out=sset[0][:, bk, :], in_=bap)), reads=[bnm],
                         writes=[sset[1] + str(bk)])

                def oloc2(u, qi):
                    g = u * nqt + qi
                    bank, slot = divmod(g, per_bank)
                    rows = qtiles[qi][1]
                    return sset[0][0:rows, bank, slot * (ed + 1):(slot + 1) * (ed + 1)], sset[1] + str(bank)
                O = [[oloc2(u, qi)[0] for qi in range(nqt)] for u in range(2)]
                names = sorted({oloc2(u, qi)[1] for u in range(2) for qi in range(nqt)})
            if "m" not in phases:
                finalize(O, names)

        def load_win():
            Wv = R1[:, 0:24576].rearrange("p (c n) -> p c n", n=3072)
            for c in range(8):
                for hh in range(2):
                    dmaL(Wv[:, c, hh * 1536:(hh + 1) * 1536], w_in.ap()[c * 128:(c + 1) * 128, hh * 1536:(hh + 1) * 1536],
                         writes=["W"], key="k_win", eng="pool")
            return Wv

        Wv = load_win()
        P.op("dve", lambda e: e.memset(eps6[:], 1e-6), writes=["eps"])
        P.op("dve", lambda e: e.memset(eps5[:], 1e-5), writes=["eps"])
        dmaL(idb[:], identd.ap(), writes=["idb"], key="k_idb", eng="pool")
        dmaL(Js[:], Jd.ap(), writes=["Js"], key="k_c0")
        dmaL(t5s[:], t5.ap(), writes=["t5s"], key="k_c1")
        P.op("dve", lambda e: e.memset(bts[:], 0.0), writes=["bts"])
        dmaL(bts[:, 0:2, :], bt.ap()[0:256, :].rearrange("(a p) h -> p a h", p=128), writes=["bts"], key="k_c2")
        dmaL(bts[0:1, 2, :], bt.ap()[256:257, :], writes=["bts"], key="k_c2")
        dmaL(CAs[:], CAd.ap(), writes=["CAs"], key="k_c3")
        dmaL(CBs[:], CBd.ap().rearrange("(a p) n -> p a n", p=128), writes=["CBs"], key="k_c4")
        dmaL(chA[:], bass.AP(t5, 15 * 4, [[0, 128], [1, 4]]), writes=["chA"], key="k_c5")
        dmaL(chB[:], bass.AP(bt, 0, [[0, 128], [1, 8]]), writes=["chB"], key="k_c6")
        dmaL(pmk[:], padmask.ap(), writes=["pmk"], key="k_c7")
        for i, lt in enumerate([lq1, lk1, lq2, lk2]):
            dmaL(lam4[:, i, :], bass.AP(lt, 0, [[0, 128], [1, 64]]), writes=["lam4"], key="k_c8")
        dmaL(sublnbc[:], bass.AP(subln, 0, [[0, 128], [1, 128]]), writes=["sublnbc"], key="k_c9")
        dmaL(onbbc[:], bass.AP(onb, 0, [[0, 128], [1, 512]]), writes=["onbbc"], key="k_c10")
        P.barrier(bar[:], skip=("q7",))
        P.op("dve", lambda e: e.tensor_scalar(out=sublnbc[:], in0=sublnbc[:], scalar1=0.8, scalar2=None, op0=ALU.mult),
             reads=["sublnbc"], writes=["sublnbc"])
        P.op("dve", lambda e: e.tensor_tensor(out=lam4[:, 0, :], in0=lam4[:, 0, :], in1=lam4[:, 1, :], op=ALU.mult),
             reads=["lam4"], writes=["lam4"])
        P.op("dve", lambda e: e.tensor_tensor(out=lam4[:, 2, :], in0=lam4[:, 2, :], in1=lam4[:, 3, :], op=ALU.mult),
             reads=["lam4"], writes=["lam4"])
        P.op("dve", lambda e: e.tensor_reduce(out=lcol[:, 0:1], in_=lam4[:, 0, :], axis=mybir.AxisListType.X, op=ALU.add),
             reads=["lam4"], writes=["lcol"])
        P.op("dve", lambda e: e.tensor_reduce(out=lcol[:, 1:2], in_=lam4[:, 2, :], axis=mybir.AxisListType.X, op=ALU.add),
             reads=["lam4"], writes=["lcol"])
        P.op("act", lambda e: e.activation(out=lcol[:, 2:4], in_=lcol[:, 0:2], func=AF.Exp), reads=["lcol"], writes=["lcol"])
        P.op("dve", lambda e: e.scalar_tensor_tensor(out=lcol[:, 4:5], in0=lcol[:, 3:4], scalar=-0.2, in1=lcol[:, 2:3],
                                                     op0=ALU.add, op1=ALU.subtract), reads=["lcol"], writes=["neglam"])
        neglam = lcol[:, 4:5]
        for h in range(4):
            P.op("dve", (lambda e, h=h: e.tensor_scalar(out=cmA[:, h, :], in0=pmk[:, 0:3], scalar1=chA[:, h:h + 1], scalar2=None,
                                                        op0=ALU.add)), reads=["pmk", "chA"], writes=["bias"])
        P.op("dve", lambda e: e.tensor_scalar(out=cmB[:], in0=chB[:], scalar1=pmk[:, 2:3], scalar2=None, op0=ALU.add),
             reads=["pmk", "chB"], writes=["bias"])
        pe_op(32, 4, lambda e: e.matmul(PS[0:4, 0, 0:384], lhsT=t5s[:], rhs=CAs[:], start=True, stop=True),
             reads=["t5s", "CAs"], writes=["PS0"])
        P.op("dve", lambda e: e.tensor_copy(out=vst[0:4, :], in_=PS[0:4, 0, 0:384]), reads=["PS0"], writes=["vst"])
        dmaL(vecA.ap(), vst[0:4, :], reads=["vst"], writes=["vecA"], key="k_v0")
        for a in range(3):
            pe_op(128, 8, (lambda e, a=a: e.matmul(PS[0:8, 1, 0:384], lhsT=bts[:, a, :], rhs=CBs[:, a, :], start=(a == 0), stop=(a == 2))),
                 reads=["bts", "CBs"], writes=["PS1"])
        P.op("dve", lambda e: e.tensor_copy(out=vst[0:8, :], in_=PS[0:8, 1, 0:384]), reads=["PS1", "vecA"], writes=["vst"])
        dmaL(vecB.ap(), vst[0:8, :], reads=["vst"], writes=["vecB"], key="k_v1")
        Hall = R2a[:, 4096:7168].rearrange("p (i n) -> p i n", n=128)
        hi = 0
        hlist = []
        for (vec, Tt, nh) in ((vecA, TA, 4), (vecB, TB, 8)):
            for h in range(nh):
                for kind, base in ((0, 128), (1, 0)):
                    hank = bass.AP(vec, h * 384 + base, [[1, 128], [1, 128]])
                    dmaL(Hall[:, hi, :], hank, reads=["vecA", "vecB"], writes=["Hall"], key="k_h")
                    hlist.append((hi, Tt, h, kind))
                    hi += 1
        for (hi, Tt, h, kind) in hlist:
            bk = 2 + hi % 2
            pe_op(128, 128, (lambda e, hi=hi, bk=bk: e.matmul(PS[:, bk, 0:128], lhsT=Hall[:, hi, :], rhs=Js[:], start=True, stop=True)),
                  reads=["Hall", "Js"], writes=["PS%d" % bk])
            P.op("dve", (lambda e, Tt=Tt, h=h, kind=kind, bk=bk: e.tensor_copy(out=Tt[:, h, kind, :], in_=PS[:, bk, 0:128])),
                 reads=["PS%d" % bk], writes=["Tt"], self_sync=False)
            if kind == 0:
                P.op("dve", (lambda e, Tt=Tt, h=h: e.memset(Tt[64:128, h, 0, 0:64], NEGM)), reads=["Tt"], writes=["Tt"])
        P.op("dve", lambda e: e.memset(Tm4[:], 0.0), writes=["Tt"])
        P.op("dve", lambda e: e.memset(Tm4[0:64, 64:128], NEGM), reads=["Tt"], writes=["Tt"])
        def weight_casts():
            for (src, dst, rows, colsn) in ((w_out, wout_b, 1024, 1024), (w_up, wup_b, 1024, 4096), (w_down, wdown_b, 4096, 1024),
                                           (w_gate, wgate_b, 1024, 1024), (w_ple, wple_b, 256, 1024)):
                sv = src.ap().rearrange("r (a n) -> (r a) n", n=1024)
                dv = dst.ap().rearrange("r (a n) -> (r a) n", n=1024)
                tot = rows * colsn // 1024
                for r0 in range(0, tot, 512):
                    n_ = min(512, tot - r0)
                    P.op("pool", (lambda e, r0=r0, n_=n_, dv=dv, sv=sv: e.dma_start(out=dv[r0:r0 + n_, :], in_=sv[r0:r0 + n_, :],
                                                                                    max_dma_last_dim=2048)),
                         writes=["wscr"], dma_key=KM["k_w"])


        P.barrier(bar[:], skip=("q7",))
        xb = R2a[:, 0:4096].rearrange("p (t f) -> p t f", f=1024)
        ostg = R2a[:, 4096:5120].rearrange("p (s f) -> p s f", f=512)
        obraw = R2a[:, 5120:7168].rearrange("p (t f) -> p t f", f=512)
        gattnbc = R2a[:, 7168:8192]
        xs = R2b[:, 0:4096].rearrange("p (t f) -> p t f", f=1024)
        xsT = R2b[:, 4096:8192].rearrange("p (c n) -> p c n", n=512)
        KTst = R2b[:, 8192:10240].rearrange("p (h n) -> p h n", n=512)
        VAst = R2b[:, 10240:12288].rearrange("p (t n) -> p t n", n=512)
        KBT = R2b[:, 12288:16384].rearrange("p (s c n) -> p s c n", s=2, n=512)
        VBa = R2b[:, 16384:20608].rearrange("p (s t h e) -> p s t h e", s=2, t=4, e=66)
        QBz = R2b[:, 20608:24704].rearrange("p (u c n) -> p u c n", u=2, n=512)
        QAst = R2b[:, 24704:26752].rearrange("p (h n) -> p h n", n=512)
        OBst = R2b[:, 26752:28800].rearrange("p (t n) -> p t n", n=512)
        P.op("pool", lambda e: e.memset(QBz, 0.0), writes=["QBT"])
        dmaL(gattnbc, bass.AP(g_attn, 0, [[0, 128], [1, 1024]]), writes=["gattnbc"], key="k_c11")
        P.op("dve", lambda e: e.memset(VBa[:, :, :, :, 64:66], 1.0), writes=["VBones"])

        def out_store(dst_ap, psum_ap, psname, rows=128):
            s = cnt["ost"] % 2
            cnt["ost"] += 1
            evac(ostg[0:rows, s, :], psum_ap, reads=[psname], writes=["ostg%d" % s])
            dmaO(dst_ap, ostg[0:rows, s, :], reads=["ostg%d" % s], key="k_ost%d" % s)
            return ostg[0:rows, s, :], "ostg%d" % s

        def band_attention(p, I):
            sp_, so_ = (p - 1) % 2, p % 2
            qtl = [(i * 128, 128) for i in range(4)]
            for cb in range(4):
                kts = []
                for r in range(-4, 4):
                    slot, tk = (sp_, r + 4) if r < 0 else (so_, r)
                    qlo, qhi = max(0, r), min(3, r + 4)
                    adds = []
                    for qi in range(qlo, qhi + 1):
                        rel = r - qi
                        if rel == 0:
                            adds.append((qi, [TB[:, 2 * cb + u, 0, :] for u in range(2)]))
                        elif rel == -1:
                            adds.append((qi, [TB[:, 2 * cb + u, 1, :] for u in range(2)]))
                        elif rel == -4:
                            adds.append((qi, [Tm4[:], Tm4[:]]))
                    bsrc = cmB if (p == 3 and r < 0) else chB
                    kts.append(dict(KT=KBT[:, slot, cb, tk * 128:(tk + 1) * 128], nk=128,
                                    V=[VBa[:, slot, tk, 2 * cb + u, 0:65] for u in range(2)],
                                    bias=[bsrc[:, 2 * cb + u:2 * cb + u + 1] for u in range(2)],
                                    qlo=qlo, qhi=qhi, adds=adds, reads=["KBT%d" % slot, "VB%d" % slot, "VBones"]))

                def fin(O, names, cb=cb):
                    for u in range(2):
                        hb = 2 * cb + u
                        for qi in range(4):
                            ci = 8 + (qi * 2 + u)
                            P.op("dve", (lambda e, u=u, qi=qi, ci=ci: e.reciprocal(out=cols[:, ci:ci + 1], in_=O[u][qi][:, 64:65])),
                                 reads=names, writes=["c%d" % ci])
                            P.op("dve", (lambda e, u=u, qi=qi, ci=ci, hb=hb: e.tensor_scalar(
                                out=obraw[:, qi, hb * 64:(hb + 1) * 64], in0=O[u][qi][:, 0:64], scalar1=cols[:, ci:ci + 1],
                                scalar2=None, op0=ALU.mult)), reads=names + ["c%d" % ci], writes=["obraw"])
                pair_attn(kts, [QBz[:, 0, cb, :], QBz[:, 1, cb, :]], qtl, 64, fin, ["QBT"])
            for qi in range(4):
                rc, rn = rms_rstd(obraw[:, qi, :], 128, 512, eps6, 16 + 3 * qi, ["obraw"], "ob")
                P.op("dve", (lambda e, qi=qi, rc=rc: e.scalar_tensor_tensor(out=OBst[:, qi, :], in0=obraw[:, qi, :], scalar=rc,
                                                                              in1=onbbc[:], op0=ALU.mult, op1=ALU.mult)),
                     reads=["obraw", rn, "onbbc"], writes=["OBst"])
            dmaL(OBNs.ap()[I * 512:(I + 1) * 512, :].rearrange("(t p) n -> p t n", p=128), OBst, reads=["OBst"], writes=["OBNs"],
                 key="k_obst", eng="pool")

        def phaseA_block(p):
            own = (p % 4 == 3)
            I = p // 4
            last = (p == NBLK - 1)
            so_ = p % 2
            for t in range(4):
                dmaL(xb[:, t, :], xv.ap()[p * 512 + t * 128:p * 512 + (t + 1) * 128, :], writes=["xb%d" % t], key="k_xb%d" % t)
            for t in range(4):
                rc, rn = rms_rstd(xb[:, t, :], 128, 1024, eps6, 3 * t, ["xb%d" % t], "x")
                P.op("dve", (lambda e, t=t, rc=rc: e.scalar_tensor_tensor(out=xs[:, t, :], in0=xb[:, t, :], scalar=rc, in1=gattnbc,
                                                                            op0=ALU.mult, op1=ALU.mult)),
                     reads=["xb%d" % t, rn, "gattnbc"], writes=["xs%d" % t])
            for t in range(4):
                transposes(xs[:, t, :], 128, 8, xsT[:, :, t * 128:(t + 1) * 128], ["xs%d" % t], ["xsT"])
            for h in range(4):
                ps_, nm = proj_fm(lambda c, h=h: Wv[:, c, 512 + h * 128:512 + (h + 1) * 128], xsT, 512, ["W", "xsT"])
                evac(KTst[:, h, :], ps_, reads=[nm], writes=["KTst"])
            dmaL(KTs.ap().rearrange("h p n -> p h n")[:, :, p * 512:(p + 1) * 512], KTst, reads=["KTst"], writes=["KTs"],
                 key="k_ktst", eng="pool")
            needb = (p % 4 >= 2)
            for cb in (range(4) if needb else ()):
                ps_, nm = proj_fm(lambda c, cb=cb: Wv[:, c, 2048 + cb * 128:2048 + (cb + 1) * 128], xsT, 512, ["W", "xsT"])
                evac(KBT[:, so_, cb, :], ps_, reads=[nm], writes=["KBT%d" % so_])
            if own and "3" not in phases:
                for h in range(4):
                    ps_, nm = proj_fm(lambda c, h=h: Wv[:, c, h * 128:(h + 1) * 128], xsT, 512, ["W", "xsT"])
                    evac(QAst[:, h, :], ps_, reads=[nm], writes=["QAst"], scale=0.125)
                dmaL(QTs.ap().rearrange("h p n -> p h n")[:, :, I * 512:(I + 1) * 512], QAst, reads=["QAst"], writes=["QTs"],
                     key="k_qast", eng="pool")
                for cb in range(4):
                    ps_, nm = proj_fm(lambda c, cb=cb: Wv[:, c, 1536 + cb * 128:1536 + (cb + 1) * 128], xsT, 512, ["W", "xsT"])
                    evac(QBz[:, 0, cb, :], ps_, reads=[nm], writes=["QBT"], scale=0.125)
                    P.op("pool", (lambda e, cb=cb: e.tensor_copy(out=QBz[64:128, 1, cb, :], in_=QBz[64:128, 0, cb, :])),
                         reads=["QBT"], writes=["QBT"])
                    P.op("pool", (lambda e, cb=cb: e.memset(QBz[64:128, 0, cb, :], 0.0)), reads=["QBT"], writes=["QBT"])
            for t in range(4):
                ps_, nm = proj_tm(xsT, t * 128, 128, lambda c: Wv[:, c, 1024:1536], ["W", "xsT"])
                if own:
                    sa, sn = out_store(nav.ap()[I * 512 + t * 128:I * 512 + (t + 1) * 128, :], ps_, nm)
                    P.op("pool", (lambda e, t=t, sa=sa: e.tensor_copy(out=VAst[:, t, :], in_=sa)), reads=[sn], writes=["VAst"])
                else:
                    evac(VAst[:, t, :], ps_, reads=[nm], writes=["VAst"])
                if not needb:
                    continue
                ps_, nm = proj_tm(xsT, t * 128, 128, lambda c: Wv[:, c, 2560:3072], ["W", "xsT"])
                if last:
                    sa, sn = out_store(nbv.ap()[t * 128:(t + 1) * 128, :], ps_, nm)
                    P.op("pool", (lambda e, t=t, sa=sa: e.tensor_copy(out=VBa[:, so_, t, :, 0:64],
                                                                      in_=sa.rearrange("p (h e) -> p h e", e=64))),
                         reads=[sn], writes=["VB%d" % so_])
                else:
                    evac(VBa[:, so_, t, :, 0:64], ps_.rearrange("p (h e) -> p h e", e=64), reads=[nm], writes=["VB%d" % so_])
                if own:
                    ps_, nm = proj_tm(xsT, t * 128, 128, lambda c: Wv[:, c, 512:1024], ["W", "xsT"])
                    out_store(nak.ap()[I * 512 + t * 128:I * 512 + (t + 1) * 128, :], ps_, nm)
                if last:
                    ps_, nm = proj_tm(xsT, t * 128, 128, lambda c: Wv[:, c, 2048:2560], ["W", "xsT"])
                    out_store(nbk.ap()[t * 128:(t + 1) * 128, :], ps_, nm)
            dmaL(VAs.ap()[p * 512:(p + 1) * 512, :].rearrange("(t p) n -> p t n", p=128), VAst, reads=["VAst"], writes=["VAs"],
                 key="k_vast", eng="pool")
            if own and "1" not in phases:
                band_attention(p, I)

        if "A" in phases:
            for p in range(NBLK):
                phaseA_block(p)
        if "a" in phases:
            for p in range(4):
                phaseA_block(p)
        if "e" in phases:
            for p in range(3):
                phaseA_block(p)
        P.barrier(bar[:])

        KTh = R1[:, 0:16384]
        Vaug = R1[:, 16384:33024].rearrange("p (t e) -> p t e", e=130)
        OAN = R2b[:, 0:16384].rearrange("p (t n) -> p t n", n=512)
        QTz = R2b[:, 16384:24576].rearrange("p (u n) -> p u n", u=2)

        def finA_factory(h, dst_fn, rows):
            def fin(O, names):
                nqt = len(O[0])
                for qi in range(nqt):
                    pr = qi % 2
                    cb_ = 28 + 8 * pr
                    P.op("dve", (lambda e, qi=qi, cb_=cb_: e.reciprocal(out=cols[0:rows, cb_:cb_ + 1], in_=O[0][qi][:, 128:129])),
                         reads=names, writes=["fa%d" % pr])
                    P.op("dve", (lambda e, qi=qi, cb_=cb_: e.reciprocal(out=cols[0:rows, cb_ + 1:cb_ + 2], in_=O[1][qi][:, 128:129])),
                         reads=names + ["fa%d" % pr], writes=["fa%d" % pr])
                    P.op("dve", (lambda e, cb_=cb_: e.tensor_scalar(out=cols[0:rows, cb_ + 2:cb_ + 3], in0=cols[0:rows, cb_ + 1:cb_ + 2],
                                                                   scalar1=neglam[0:rows, :], scalar2=None, op0=ALU.mult)),
                         reads=["fa%d" % pr, "neglam"], writes=["fa%d" % pr])
                    P.op("dve", (lambda e, qi=qi, cb_=cb_, pr=pr: e.tensor_scalar(out=osm[0:rows, pr, 0, :], in0=O[1][qi][:, 0:128],
                                                                                  scalar1=cols[0:rows, cb_ + 2:cb_ + 3], scalar2=None,
                                                                                  op0=ALU.mult)),
                         reads=names + ["fa%d" % pr], writes=["osm%d" % pr])
                    P.op("dve", (lambda e, qi=qi, cb_=cb_, pr=pr: e.scalar_tensor_tensor(
                        out=osm[0:rows, pr, 1, :], in0=O[0][qi][:, 0:128], scalar=cols[0:rows, cb_:cb_ + 1], in1=osm[0:rows, pr, 0, :],
                        op0=ALU.mult, op1=ALU.add)), reads=names + ["fa%d" % pr, "osm%d" % pr], writes=["osm%d" % pr])
                    ci = cb_ + 3
                    P.op("dve", (lambda e, pr=pr, ci=ci: e.scalar_tensor_tensor(
                        out=osm[0:rows, pr, 0, :], in0=osm[0:rows, pr, 1, :], scalar=1.0, in1=osm[0:rows, pr, 1, :],
                        op0=ALU.mult, op1=ALU.mult, accum_out=cols[0:rows, ci:ci + 1])),
                        reads=["osm%d" % pr], writes=["osm%d" % pr, "fb%d" % pr])
                    P.op("act", (lambda e, ci=ci: e.activation(out=cols[0:rows, ci + 1:ci + 2], in_=cols[0:rows, ci:ci + 1], func=AF.Ln,
                                                              bias=eps5[0:rows, 0:1], scale=1.0 / 128)),
                         reads=["fb%d" % pr, "eps"], writes=["fb%d" % pr])
                    P.op("act", (lambda e, ci=ci: e.activation(out=cols[0:rows, ci + 2:ci + 3], in_=cols[0:rows, ci + 1:ci + 2],
                                                              func=AF.Exp, scale=-0.5)),
                         reads=["fb%d" % pr], writes=["fb%d" % pr])
                    dst, dnm = dst_fn(qi)
                    P.op("dve", (lambda e, pr=pr, ci=ci, dst=dst: e.scalar_tensor_tensor(
                        out=dst, in0=osm[0:rows, pr, 1, :], scalar=cols[0:rows, ci + 2:ci + 3], in1=sublnbc[0:rows, :],
                        op0=ALU.mult, op1=ALU.mult)), reads=["osm%d" % pr, "fb%d" % pr, "sublnbc"], writes=[dnm])
            return fin

        def phaseD1():
            xb = R2a[:, 0:1024]
            ostg = R2a[:, 4096:5120].rearrange("p (s f) -> p s f", f=512)
            obraw = R2a[:, 1024:1536]
            o = 0

            def carve(n):
                nonlocal o
                a = R2b[:, o:o + n]
                o += n
                return a
            oanl = carve(512)
            xs = carve(1024)
            xsT = carve(8 * 64).rearrange("p (c n) -> p c n", n=64)
            cst = carve(8 * 512).rearrange("p (t n) -> p t n", n=512)
            KTc = carve(4 * 1056).rearrange("p (h n) -> p h n", n=1056)
            Vc = carve(9 * 4 * 130).rearrange("p (t h e) -> p t h e", h=4, e=130)
            KBc = carve(4 * 544).rearrange("p (c n) -> p c n", n=544)
            VBc = carve(5 * 8 * 66).rearrange("p (t h e) -> p t h e", h=8, e=66)
            QAz = carve(2 * 4 * 64).rearrange("p (u h n) -> p u h n", u=2, n=64)
            QBzs = carve(2 * 4 * 64).rearrange("p (u c n) -> p u c n", u=2, n=64)
            P.op("pool", lambda e: e.memset(QAz, 0.0), writes=["QAs"])
            P.op("pool", lambda e: e.memset(QBzs, 0.0), writes=["QBs"])
            oans = carve(512)
            obns = carve(512)
            P.op("dve", lambda e: e.memset(Vc[:, :, :, 128:130], 1.0), writes=["Vc1"])
            P.op("dve", lambda e: e.memset(VBc[:, :, :, 64:66], 1.0), writes=["VBc1"])
            dmaL(xb[0:64, :], xsm.ap(), writes=["xb"], key="k_xb")
            rc, rn = rms_rstd(xb[0:64, :], 64, 1024, eps6, 0, ["xb"], "x")
            P.op("dve", (lambda e, rc=rc: e.scalar_tensor_tensor(out=xs[0:64, :], in0=xb[0:64, :], scalar=rc, in1=gattnbc[0:64, :],
                                                                  op0=ALU.mult, op1=ALU.mult)), reads=["xb", rn, "gattnbc"], writes=["xs"])
            transposes(xs[0:64, :], 64, 8, xsT[:, :, 0:64], ["xs"], ["xsT"])
            for (c0, dst) in ((512, sak), (1024, sav), (2048, sbk), (2560, sbv)):
                ps_, nm = proj_tm(xsT, 0, 64, lambda c, c0=c0: Wv[:, c, c0:c0 + 512], ["W", "xsT"])
                out_store(dst.ap(), ps_, nm, rows=64)
            for h in range(4):
                ps_, nm = proj_fm(lambda c, h=h: Wv[:, c, h * 128:(h + 1) * 128], xsT, 64, ["W", "xsT"])
                evac(QAz[:, 0, h, :], ps_, reads=[nm], writes=["QAs"], scale=0.125)
                P.op("pool", (lambda e, h=h: e.tensor_copy(out=QAz[64:128, 1, h, :], in_=QAz[64:128, 0, h, :])), reads=["QAs"], writes=["QAs"])
                P.op("pool", (lambda e, h=h: e.memset(QAz[64:128, 0, h, :], 0.0)), reads=["QAs"], writes=["QAs"])
                ps_, nm = proj_fm(lambda c, h=h: Wv[:, c, 1536 + h * 128:1536 + (h + 1) * 128], xsT, 64, ["W", "xsT"])
                evac(QBzs[:, 0, h, :], ps_, reads=[nm], writes=["QBs"], scale=0.125)
                P.op("pool", (lambda e, h=h: e.tensor_copy(out=QBzs[64:128, 1, h, :], in_=QBzs[64:128, 0, h, :])), reads=["QBs"], writes=["QBs"])
                P.op("pool", (lambda e, h=h: e.memset(QBzs[64:128, 0, h, :], 0.0)), reads=["QBs"], writes=["QBs"])
            for s in range(2):
                for h in range(4):
                    ps_, nm = proj_fm(lambda c, h=h: Wv[:, c, 512 + h * 128:512 + (h + 1) * 128], xsT[:, :, s * 32:(s + 1) * 32], 32,
                                      ["W", "xsT"])
                    evac(KTc[:, h, 1024:1056], ps_, reads=[nm], writes=["KTc"])
                    ps_, nm = proj_fm(lambda c, h=h: Wv[:, c, 2048 + h * 128:2048 + (h + 1) * 128], xsT[:, :, s * 32:(s + 1) * 32], 32,
                                      ["W", "xsT"])
                    evac(KBc[:, h, 512:544], ps_, reads=[nm], writes=["KBc"])
                ps_, nm = proj_tm(xsT, s * 32, 32, lambda c: Wv[:, c, 1024:1536], ["W", "xsT"])
                evac(Vc[0:32, 8, :, 0:128], ps_.rearrange("p (h e) -> p h e", e=128), reads=[nm], writes=["Vc"])
                ps_, nm = proj_tm(xsT, s * 32, 32, lambda c: Wv[:, c, 2560:3072], ["W", "xsT"])
                evac(VBc[0:32, 4, :, 0:64], ps_.rearrange("p (h e) -> p h e", e=64), reads=[nm], writes=["VBc"])
                dmaL(cst, cak.ap()[s * 1024:(s + 1) * 1024, :].rearrange("(t p) n -> p t n", p=128), writes=["cst"], key="k_cst",
                     eng="pool")
                for t in range(8):
                    transposes(cst[:, t, :], 128, 4, KTc[:, :, t * 128:(t + 1) * 128], ["cst"], ["KTc"])
                dmaL(cst[:, 0:4, :], cbk.ap()[s * 512:(s + 1) * 512, :].rearrange("(t p) n -> p t n", p=128), writes=["cst"],
                     key="k_cst", eng="pool")
                for t in range(4):
                    transposes(cst[:, t, :], 128, 4, KBc[:, :, t * 128:(t + 1) * 128], ["cst"], ["KBc"])
                for h_ in range(4):
                    dmaL(Vc[:, 0:8, h_, 0:128],
                         cav.ap()[s * 1024:(s + 1) * 1024, h_ * 128:(h_ + 1) * 128].rearrange("(t p) e -> p t e", p=128),
                         reads=["Vc1"], writes=["Vc"], key="k_vc", eng="pool")
                for h_ in range(8):
                    dmaL(VBc[:, 0:4, h_, 0:64],
                         cbv.ap()[s * 512:(s + 1) * 512, h_ * 64:(h_ + 1) * 64].rearrange("(t p) e -> p t e", p=128),
                         reads=["VBc1"], writes=["VBc"], key="k_vbc", eng="pool")
                for h in range(4):
                    kts = []
                    for t in range(9):
                        nk = 128 if t < 8 else 32
                        adds = []
                        if t == 7:
                            adds.append((0, [TA[:, h, 1, 0:32]] * 2))
                        if t == 8:
                            adds.append((0, [TA[0:32, h, 0, 0:32]] * 2))
                        bcol = chA[0:nk, h:h + 1]
                        kts.append(dict(KT=KTc[:, h, t * 128:t * 128 + nk], nk=nk, V=[Vc[0:nk, t, h, 0:129]] * 2, bias=[bcol, bcol],
                                        qlo=0, qhi=0, adds=adds, reads=["KTc", "Vc", "Vc1"]))
                    fin = finA_factory(h, lambda qi, h=h: (oans[0:32, h * 128:(h + 1) * 128], "oans"), 32)
                    pair_attn(kts, [QAz[:, u_, h, s * 32:(s + 1) * 32] for u_ in range(2)], [(0, 32)], 128, fin, ["QAs"])
                dmaL(OANss.ap()[s * 32:(s + 1) * 32, :], oans[0:32, :], reads=["oans"], writes=["OANss"], key="k_oans", eng="pool")
                for cb in range(4):
                    kts = []
                    for t in range(5):
                        nk = 128 if t < 4 else 32
                        adds = []
                        if t == 3:
                            adds.append((0, [TB[:, 2 * cb + u, 1, 0:32] for u in range(2)]))
                        if t == 4:
                            adds.append((0, [TB[0:32, 2 * cb + u, 0, 0:32] for u in range(2)]))
                        kts.append(dict(KT=KBc[:, cb, t * 128:t * 128 + nk], nk=nk,
                                        V=[VBc[0:nk, t, 2 * cb + u, 0:65] for u in range(2)],
                                        bias=[chB[0:nk, 2 * cb + u:2 * cb + u + 1] for u in range(2)],
                                        qlo=0, qhi=0, adds=adds, reads=["KBc", "VBc", "VBc1"]))

                    def finb(O, names, cb=cb):
                        for u in range(2):
                            hb = 2 * cb + u
                            ci = 8 + u
                            P.op("dve", (lambda e, u=u, ci=ci: e.reciprocal(out=cols[0:32, ci:ci + 1], in_=O[u][0][:, 64:65])),
                                 reads=names, writes=["c%d" % ci])
                            P.op("dve", (lambda e, u=u, ci=ci, hb=hb: e.tensor_scalar(
                                out=obraw[0:32, hb * 64:(hb + 1) * 64], in0=O[u][0][:, 0:64], scalar1=cols[0:32, ci:ci + 1],
                                scalar2=None, op0=ALU.mult)), reads=names + ["c%d" % ci], writes=["obraw"])
                    pair_attn(kts, [QBzs[:, u_, cb, s * 32:(s + 1) * 32] for u_ in range(2)], [(0, 32)], 64, finb, ["QBs"])
                rc, rn = rms_rstd(obraw[0:32, :], 32, 512, eps6, 16, ["obraw"], "ob")
                P.op("dve", (lambda e, rc=rc: e.scalar_tensor_tensor(out=obns[0:32, :], in0=obraw[0:32, :], scalar=rc, in1=onbbc[0:32, :],
                                                                      op0=ALU.mult, op1=ALU.mult)),
                     reads=["obraw", rn, "onbbc"], writes=["obns"])
                dmaL(OBNss.ap()[s * 32:(s + 1) * 32, :], obns[0:32, :], reads=["obns"], writes=["OBNs"], key="k_obns", eng="pool")
        if "D" in phases:
            phaseD1()
            P.barrier(bar[:])

        weight_casts()
        osbB = [(R2a[:, 0:1536].rearrange("p (b n) -> p b n", n=512), "osbA"),
                (R2a[:, 1536:3072].rearrange("p (b n) -> p b n", n=512), "osbB")]
        if "B" in phases:
            P.op("dve", lambda e: e.memset(Vaug[:, :, 128:130], 1.0), writes=["Vones"])
            P.op("dve", lambda e: e.memset(QTz, 0.0), writes=["QTh", "QTh0"])
            NCH = 4
            for h in range(4):
                for ch in range(NCH):
                    k0 = ch * (SEQV // NCH)
                    k1 = (ch + 1) * (SEQV // NCH)
                    dmaL(KTh[:, k0:k1], KTs.ap()[h, :, k0:k1], reads=["Vones"], writes=["KTh%d" % ch], key="k_kth%d" % ch)
                    dmaL(Vaug[:, k0 // 128:k1 // 128, 0:128],
                         VAs.ap()[k0:k1, h * 128:(h + 1) * 128].rearrange("(t p) e -> p t e", p=128),
                         reads=["Vones"], writes=["Vh%d" % ch], key="k_vh%d" % ch)
                for u_ in range(2):
                    dmaL(QTz[64 * u_:64 * u_ + 64, u_, 0:NTOK], QTs.ap()[h, 64 * u_:64 * u_ + 64, :], reads=["QTh0"], writes=["QTh"],
                         key="k_qth")
                for I in range(NOWN):
                    p = 4 * I + 3
                    kts = []
                    for kt in range(4 * p + 4):
                        r = kt - 4 * p
                        qlo = max(0, r)
                        adds = []
                        if r >= 0:
                            adds.append((r, [TA[:, h, 0, :]] * 2))
                            if r + 1 <= 3:
                                adds.append((r + 1, [TA[:, h, 1, :]] * 2))
                        elif r == -1:
                            adds.append((0, [TA[:, h, 1, :]] * 2))
                        bcol = cmA[:, h, kt // 4:kt // 4 + 1] if kt < 12 else chA[:, h:h + 1]
                        ch = kt * 128 // (SEQV // NCH)
                        kts.append(dict(KT=KTh[:, kt * 128:(kt + 1) * 128], nk=128, V=[Vaug[:, kt, 0:129]] * 2, bias=[bcol, bcol],
                                        qlo=qlo, qhi=3, adds=adds, reads=["KTh%d" % ch, "Vh%d" % ch, "Vones"]))
                    fin = finA_factory(h, lambda qi, I=I, h=h: (OAN[:, I * 4 + qi, h * 128:(h + 1) * 128], "OAN"), 128)
                    pair_attn(kts, [QTz[:, u_, I * 512:(I + 1) * 512] for u_ in range(2)], [(i * 128, 128) for i in range(4)], 128, fin,
                              ["QTh"], osb=osbB)
        P.barrier(bar[:])

        ACT_T = R1[:, 0:16384].rearrange("p (h n) -> p h n", n=512)
        WR = R1[:, 16384:32768].rearrange("p (s n) -> p s n", n=4096)
        hres = R2a[:, 0:4096].rearrange("p (t f) -> p t f", f=1024)
        p_s = R2a[:, 4096:5120].rearrange("p (t f) -> p t f", f=256)
        gs = R2a[:, 5120:5632]
        tmpf = R2a[:, 5632:6144]
        gmlpbc = R2a[:, 6144:7168]
        gfinbc = R2a[:, 7168:8192]
        cs = R2b[:, 16384:20480].rearrange("p (t f) -> p t f", f=1024)
        aT = R2b[:, 20480:24576].rearrange("p (c n) -> p c n", n=512)
        obn_s = R2b[:, 24576:26624].rearrange("p (t n) -> p t n", n=512)
        pb = R2b[:, 26624:27648].rearrange("p (t f) -> p t f", f=256)
        pT = R2b[:, 27648:28672].rearrange("p (c n) -> p c n", n=512)
        wcnt = {"i": 0}

        def wload(src_ap, shape_view):
            s = wcnt["i"] % 4
            wcnt["i"] += 1
            dst = shape_view(WR[:, s, :])
            dmaL(dst, src_ap, reads=["wscr"], writes=["WR%d" % s], key="k_wr%d" % s)
            return dst, "WR%d" % s

        def phaseC_group(tiles, x_ap, p_ap, oan_fn, obn_src, y_ap):
            NT = tiles[-1][0] + tiles[-1][1]
            nt = len(tiles)
            HR = ["hres%d" % t_ for t_ in range(nt)]
            PSN = ["p_s%d" % t_ for t_ in range(nt)]
            ATN = ["aT%d" % t_ for t_ in range(nt)]
            rows0 = tiles[0][1]
            if rows0 == 128:
                dmaL(hres[:, 0:nt, :], x_ap.rearrange("(t p) f -> p t f", p=128), writes=HR, key="k_hres")
                dmaL(p_s[:, 0:nt, :], p_ap.rearrange("(t p) f -> p t f", p=128), writes=PSN, key="k_ps")
                dmaL(obn_s[:, 0:nt, :], obn_src.rearrange("(t p) f -> p t f", p=128), reads=["OBNs"], writes=["obn_s"], key="k_obn")
            else:
                dmaL(hres[0:rows0, 0, :], x_ap, writes=HR, key="k_hres")
                dmaL(p_s[0:rows0, 0, :], p_ap, writes=PSN, key="k_ps")
                dmaL(obn_s[0:rows0, 0, :], obn_src, reads=["OBNs"], writes=["obn_s"], key="k_obn")
            for ti, (tok0, rows) in enumerate(tiles):
                oa, oan_names = oan_fn(ti)
                transposes(oa, rows, 4, aT[:, 0:4, tok0:tok0 + rows], oan_names, ["aT%d" % ti])
                transposes(obn_s[0:rows, ti, :], rows, 4, aT[:, 4:8, tok0:tok0 + rows], ["obn_s"], ["aT%d" % ti])
            wo = [wload(wout_b.ap()[j * 512:(j + 1) * 512, :].rearrange("(c p) n -> p c n", p=128),
                        lambda v: v.rearrange("p (c n) -> p c n", n=1024)) for j in range(2)]
            for ti, (tok0, rows) in enumerate(tiles):
                for hf in range(2):
                    ps_, nm = proj_tm(aT, tok0, rows, lambda c, hf=hf: wo[c // 4][0][:, c % 4, hf * 512:(hf + 1) * 512],
                                      ["aT%d" % ti, wo[0][1], wo[1][1]])
                    P.op("dve", (lambda e, ti=ti, rows=rows, hf=hf, ps_=ps_: e.tensor_tensor(
                        out=hres[0:rows, ti, hf * 512:(hf + 1) * 512], in0=ps_, in1=hres[0:rows, ti, hf * 512:(hf + 1) * 512], op=ALU.add)),
                        reads=[nm, "hres%d" % ti], writes=["hres%d" % ti], self_sync=False)
            for ti, (tok0, rows) in enumerate(tiles):
                rc, rn = rms_rstd(hres[0:rows, ti, :], rows, 1024, eps6, 3 * ti, ["hres%d" % ti], "c")
                P.op("dve", (lambda e, ti=ti, rows=rows, rc=rc: e.scalar_tensor_tensor(out=cs[0:rows, ti, :], in0=hres[0:rows, ti, :],
                                                                                       scalar=rc, in1=gmlpbc[0:rows, :], op0=ALU.mult,
                                                                                       op1=ALU.mult)),
                     reads=["hres%d" % ti, rn, "gmlpbc"], writes=["cs%d" % ti])
                transposes(cs[0:rows, ti, :], rows, 8, aT[:, :, tok0:tok0 + rows], ["cs%d" % ti], ["aT%d" % ti])
            for j in range(8):
                wu, wn = wload(wup_b.ap()[:, j * 512:(j + 1) * 512].rearrange("(c p) n -> p c n", p=128),
                               lambda v: v.rearrange("p (c n) -> p c n", n=512))
                for hl in range(4):
                    hc = j * 4 + hl
                    ps_, nm = proj_fm(lambda c, hl=hl, wu=wu: wu[:, c, hl * 128:(hl + 1) * 128], aT, NT, ATN + [wn])
                    rb, rbn = (tmpf, "tmpf") if hc % 2 else (gs, "gs")
                    P.op("act", (lambda e, ps_=ps_, rb=rb: e.activation(out=rb[:, 0:NT], in_=ps_, func=AF.Relu)),
                         reads=[nm], writes=[rbn])
                    P.op("pool", (lambda e, hc=hc, rb=rb: e.tensor_tensor(out=ACT_T[:, hc, 0:NT], in0=rb[:, 0:NT], in1=rb[:, 0:NT],
                                                                          op=ALU.mult)), reads=[rbn], writes=["ACT_T"])
            for hf in range(2):
                accs = []
                for ti in range(nt):
                    accs.append(ti)
                for j in range(4):
                    wd, wn = wload(wdown_b.ap()[j * 1024:(j + 1) * 1024, hf * 512:(hf + 1) * 512].rearrange("(c p) n -> p c n", p=128),
                                   lambda v: v.rearrange("p (c n) -> p c n", n=512))
                    for hl in range(8):
                        hc = j * 8 + hl
                        for ti, (tok0, rows) in enumerate(tiles):
                            pe_op(128, rows, (lambda e, ti=ti, tok0=tok0, rows=rows, hc=hc, hl=hl, wd=wd: e.matmul(
                                PS[0:rows, ti, :], lhsT=ACT_T[:, hc, tok0:tok0 + rows], rhs=wd[:, hl, :], start=(hc == 0), stop=(hc == 31))),
                                reads=["ACT_T", wn], writes=["PS%d" % ti])
                for ti, (tok0, rows) in enumerate(tiles):
                    P.op("dve", (lambda e, ti=ti, rows=rows, hf=hf: e.tensor_tensor(
                        out=hres[0:rows, ti, hf * 512:(hf + 1) * 512], in0=PS[0:rows, ti, :], in1=hres[0:rows, ti, hf * 512:(hf + 1) * 512],
                        op=ALU.add)), reads=["PS%d" % ti, "hres%d" % ti], writes=["hres%d" % ti], self_sync=False)
            for ti, (tok0, rows) in enumerate(tiles):
                P.op("act", (lambda e, ti=ti, rows=rows: e.activation(out=cs[0:rows, ti, :], in_=hres[0:rows, ti, :], func=AF.Copy, scale=1.0)),
                     reads=["hres%d" % ti], writes=["cs%d" % ti])
                transposes(cs[0:rows, ti, :], rows, 8, aT[:, :, tok0:tok0 + rows], ["cs%d" % ti], ["aT%d" % ti])
                P.op("dve", (lambda e, ti=ti, rows=rows: e.tensor_copy(out=pb[0:rows, ti, :], in_=p_s[0:rows, ti, :])),
                     reads=["p_s%d" % ti], writes=["pb%d" % ti])
                transposes(pb[0:rows, ti, :], rows, 2, pT[:, :, tok0:tok0 + rows], ["pb%d" % ti], ["pT%d" % ti])
            wg = [wload(wgate_b.ap()[j * 512:(j + 1) * 512, :].rearrange("(c p) n -> p c n", p=128),
                        lambda v: v.rearrange("p (c n) -> p c n", n=1024)) for j in range(2)]
            wp, wpn = wload(wple_b.ap().rearrange("(c p) n -> p c n", p=128),
                            lambda v: v[:, 0:2048].rearrange("p (c n) -> p c n", n=1024))
            for ti, (tok0, rows) in enumerate(tiles):
                for hf in range(2):
                    ps_, nm = proj_tm(aT, tok0, rows, lambda c, hf=hf: wg[c // 4][0][:, c % 4, hf * 512:(hf + 1) * 512],
                                      ["aT%d" % ti, wg[0][1], wg[1][1]])
                    P.op("act", (lambda e, rows=rows, ps_=ps_: e.activation(out=gs[0:rows, :], in_=ps_, func=AF.Sigmoid)),
                         reads=[nm], writes=["gs"])
                    ps2, nm2 = proj_tm(pT, tok0, rows, lambda c, hf=hf: wp[:, c, hf * 512:(hf + 1) * 512], ["pT%d" % ti, wpn], nk=2)
                    P.op("dve", (lambda e, rows=rows, ps2=ps2: e.tensor_tensor(out=tmpf[0:rows, :], in0=gs[0:rows, :], in1=ps2, op=ALU.mult)),
                         reads=["gs", nm2], writes=["tmpf"])
                    P.op("dve", (lambda e, ti=ti, rows=rows, hf=hf: e.tensor_tensor(
                        out=hres[0:rows, ti, hf * 512:(hf + 1) * 512], in0=tmpf[0:rows, :], in1=hres[0:rows, ti, hf * 512:(hf + 1) * 512],
                        op=ALU.add)), reads=["tmpf", "hres%d" % ti], writes=["hres%d" % ti])
            for ti, (tok0, rows) in enumerate(tiles):
                rc, rn = rms_rstd(hres[0:rows, ti, :], rows, 1024, eps6, 3 * ti, ["hres%d" % ti], "f")
                P.op("dve", (lambda e, ti=ti, rows=rows, rc=rc: e.scalar_tensor_tensor(out=hres[0:rows, ti, :], in0=hres[0:rows, ti, :],
                                                                                       scalar=rc, in1=gfinbc[0:rows, :], op0=ALU.mult,
                                                                                       op1=ALU.mult)),
                     reads=["hres%d" % ti, rn, "gfinbc"], writes=["hres%d" % ti])
            if rows0 == 128:
                dmaO(y_ap.rearrange("(t p) f -> p t f", p=128), hres[:, 0:nt, :], reads=HR, key="k_y")
            else:
                dmaO(y_ap, hres[0:rows0, 0, :], reads=HR, key="k_y")

        if "C" in phases or "D" in phases:
            dmaL(gmlpbc, bass.AP(g_mlp, 0, [[0, 128], [1, 1024]]), writes=["gmlpbc"], key="k_c12")
            dmaL(gfinbc, bass.AP(g_final, 0, [[0, 128], [1, 1024]]), writes=["gfinbc"], key="k_c13")
        if "C" in phases:
            for I in range(NOWN):
                phaseC_group([(i * 128, 128) for i in range(4)], xv.ap()[(4 * I + 3) * 512:(4 * I + 4) * 512, :],
                             pv.ap()[I * 512:(I + 1) * 512, :],
                             lambda ti, I=I: (OAN[:, I * 4 + ti, :], ["OAN"]),
                             OBNs.ap()[I * 512:(I + 1) * 512, :], y.ap()[I * 512:(I + 1) * 512, :])
        P.barrier(bar[:])

        def phaseD2():
            oanl = R2b[:, 0:512]
            dmaL(oanl[0:64, :], OANss.ap(), reads=["OANss"], writes=["oanl"], key="k_oanl")
            phaseC_group([(0, 64)], xsm.ap(), psm.ap(), lambda ti: (oanl[0:64, :], ["oanl"]), OBNss.ap(), ys.ap())

        if "D" in phases:
            phaseD2()

        P.emit(final_dma_keys=[] if "5" in phases else sorted(outkeys))
    return nc


_NC_CACHE = {}


def _run(x_prompt, x_sample, cache_a_k, cache_a_v, cache_b_k, cache_b_v, p_prompt, p_sample,
         t5_table, g_attn, w_in, lambda_q1, lambda_k1, lambda_q2, lambda_k2, subln_g,
         band_table, out_norm_b, w_out, g_mlp, w_up, w_down, w_ple_gate, w_ple_proj, g_final,
         phases="ABCD", cores=None, trace=False):
    f = lambda a: np.ascontiguousarray(np.asarray(a, dtype=np.float32))
    x_prompt = f(x_prompt); x_sample = f(x_sample); p_prompt = f(p_prompt); p_sample = f(p_sample)
    cache_a_k = f(cache_a_k); cache_a_v = f(cache_a_v); cache_b_k = f(cache_b_k); cache_b_v = f(cache_b_v)
    S = x_prompt.shape[1]
    nblk = S // 512
    nown = nblk // 4
    seqv = nblk * 512
    ident, J, CA, CB = _consts()
    ck = (phases, nblk)
    if ck not in _NC_CACHE:
        _NC_CACHE[ck] = build_program(phases, nblk)
    nc = _NC_CACHE[ck]
    common = {
        "t5": f(t5_table), "bt": f(band_table)[0], "g_attn": f(g_attn), "w_in": f(w_in)[0],
        "lq1": f(lambda_q1), "lk1": f(lambda_k1), "lq2": f(lambda_q2), "lk2": f(lambda_k2),
        "subln": f(subln_g), "onb": f(out_norm_b), "w_out": f(w_out)[0], "g_mlp": f(g_mlp),
        "w_up": f(w_up)[0], "w_down": f(w_down)[0], "w_gate": f(w_ple_gate)[0], "w_ple": f(w_ple_proj)[0],
        "g_final": f(g_final).reshape(1, 1024), "ident": ident, "J": J, "CA": CA, "CB": CB,
    }
    cores = list(range(8)) if cores is None else list(cores)
    in_maps = []
    for c in cores:
        b, j = divmod(c, 4)
        npad = 3 - j
        xvv = np.zeros((seqv, 1024), np.float32)
        nreal = (nblk - npad) * 512
        xvv[npad * 512:] = x_prompt[b, :nreal]
        pvv = np.concatenate([p_prompt[0, b, (4 * I + j) * 512:(4 * I + j + 1) * 512] for I in range(nown)], 0)
        pm = np.zeros((128, 4), np.float32)
        pm[:, :npad] = NEGM
        m = dict(common)
        m.update({
            "xv": xvv, "pv": np.ascontiguousarray(pvv),
            "xsm": x_sample[2 * c:2 * c + 2].reshape(64, 1024), "psm": p_sample[0, 2 * c:2 * c + 2].reshape(64, 256),
            "cak": cache_a_k[0, 2 * c:2 * c + 2].reshape(2048, 512), "cav": cache_a_v[0, 2 * c:2 * c + 2].reshape(2048, 512),
            "cbk": cache_b_k[0, 2 * c:2 * c + 2].reshape(1024, 512), "cbv": cache_b_v[0, 2 * c:2 * c + 2].reshape(1024, 512),
            "padmask": pm,
        })
        in_maps.append({k: np.ascontiguousarray(v) for k, v in m.items()})
    if trace:
        res = run_bass_kernel_spmd(nc, in_maps, core_ids=list(range(len(cores))), trace=True)
        print("EXEC_TIME_NS", res.exec_time_ns)
    else:
        res = run_bass_kernel_spmd(nc, in_maps, core_ids=list(range(len(cores))))
    R = res.results
    y_prompt = np.zeros((2, S, 1024), np.float32)
    nakp = np.zeros((1, 2, S, 512), np.float32)
    navp = np.zeros((1, 2, S, 512), np.float32)
    nbkp = np.zeros((1, 2, 512, 512), np.float32)
    nbvp = np.zeros((1, 2, 512, 512), np.float32)
    y_sample = np.zeros((16, 32, 1024), np.float32)
    saks = np.zeros((1, 16, 32, 512), np.float32); savs = np.zeros((1, 16, 32, 512), np.float32)
    sbks = np.zeros((1, 16, 32, 512), np.float32); sbvs = np.zeros((1, 16, 32, 512), np.float32)
    for ci, c in enumerate(cores):
        b, j = divmod(c, 4)
        r = R[ci]
        for I in range(nown):
            g0 = (4 * I + j) * 512
            y_prompt[b, g0:g0 + 512] = r["y"][I * 512:(I + 1) * 512]
            nakp[0, b, g0:g0 + 512] = r["nak"][I * 512:(I + 1) * 512]
            navp[0, b, g0:g0 + 512] = r["nav"][I * 512:(I + 1) * 512]
        if j == 3:
            nbkp[0, b] = r["nbk"]
            nbvp[0, b] = r["nbv"]
        y_sample[2 * c:2 * c + 2] = r["ys"].reshape(2, 32, 1024)
        saks[0, 2 * c:2 * c + 2] = r["sak"].reshape(2, 32, 512)
        savs[0, 2 * c:2 * c + 2] = r["sav"].reshape(2, 32, 512)
        sbks[0, 2 * c:2 * c + 2] = r["sbk"].reshape(2, 32, 512)
        sbvs[0, 2 * c:2 * c + 2] = r["sbv"].reshape(2, 32, 512)
    return (y_prompt, y_sample,
            nakp.reshape(1, 2, S, 4, 2, 64), navp.reshape(1, 2, S, 4, 128),
            nbkp.reshape(1, 2, 512, 8, 64), nbvp.reshape(1, 2, 512, 8, 64),
            saks.reshape(1, 16, 32, 4, 2, 64), savs.reshape(1, 16, 32, 4, 128),
            sbks.reshape(1, 16, 32, 8, 64), sbvs.reshape(1, 16, 32, 8, 64))


def kernel(x_prompt, x_sample, cache_a_k, cache_a_v, cache_b_k, cache_b_v, p_prompt, p_sample,
           t5_table, g_attn, w_in, lambda_q1, lambda_k1, lambda_q2, lambda_k2, subln_g,
           band_table, out_norm_b, w_out, g_mlp, w_up, w_down, w_ple_gate, w_ple_proj, g_final):
    return _run(x_prompt, x_sample, cache_a_k, cache_a_v, cache_b_k, cache_b_v, p_prompt, p_sample,
                t5_table, g_attn, w_in, lambda_q1, lambda_k1, lambda_q2, lambda_k2, subln_g,
                band_table, out_norm_b, w_out, g_mlp, w_up, w_down, w_ple_gate, w_ple_proj, g_final)
```

```python
import math
import contextlib
import numpy as np
import concourse.bass as bass
import concourse.mybir as mybir
from concourse.bass_utils import run_bass_kernel_spmd

ENGS = ("pe", "act", "dve", "pool", "sp")


class Op:
    __slots__ = ("idx", "eng", "fn", "dma_key", "dma_cnt", "waits", "signal", "sigcnt", "eidx")

    def __init__(self, idx, eng, fn, dma_key):
        self.idx = idx
        self.eng = eng
        self.fn = fn
        self.dma_key = dma_key
        self.dma_cnt = 0
        self.waits = []
        self.signal = False
        self.sigcnt = 0
        self.eidx = 0


class Prog:
    def __init__(self, nc):
        self.nc = nc
        self.ops = []
        self.by_eng = {e: [] for e in ENGS}
        self.last_w = {}
        self.readers = {}
        self.seen = {e: {f: -1 for f in ENGS} for e in ENGS}
        self.seen_dma = {e: {} for e in ENGS}
        self.dma_counts = {}
        self.final_dma = []
        self.force = {}

    def op(self, eng, fn, reads=(), writes=(), dma_key=None, self_sync=True, extra=()):
        o = Op(len(self.ops), eng, fn, dma_key)
        o.eidx = len(self.by_eng[eng])
        deps = []
        for r in reads:
            w = self.last_w.get(r)
            if w is not None:
                deps.append(w)
        for w_ in writes:
            w = self.last_w.get(w_)
            if w is not None:
                deps.append(w)
            deps.extend(self.readers.get(w_, ()))
        for r in reads:
            self.readers.setdefault(r, []).append(o)
        for w_ in writes:
            self.last_w[w_] = o
            self.readers[w_] = [x for x in self.readers.get(w_, ()) if x is o]
        f = self.force.pop(eng, None)
        if f is not None:
            deps.append(f)
        deps.extend(extra)
        if dma_key is not None:
            self.dma_counts[dma_key] = self.dma_counts.get(dma_key, 0) + 16
            o.dma_cnt = self.dma_counts[dma_key]
        for d in deps:
            if d is o:
                continue
            if d.dma_key is not None:
                cur = self.seen_dma[eng].get(d.dma_key, 0)
                if cur >= d.dma_cnt:
                    continue
                self.seen_dma[eng][d.dma_key] = d.dma_cnt
                o.waits.append(("dma", d.dma_key, d.dma_cnt))
            else:
                if d.eng == eng and (eng == "pe" or not self_sync):
                    continue
                if self.seen[eng][d.eng] >= d.eidx:
                    continue
                self.seen[eng][d.eng] = d.eidx
                d.signal = True
                o.waits.append(("eng", d.eng, d))
        self.ops.append(o)
        self.by_eng[eng].append(o)
        return o

    def barrier(self, tile, skip=()):
        extra = [self.by_eng[e][-1] for e in ENGS if self.by_eng[e]]
        last_dma = {}
        for o in self.ops:
            if o.dma_key is not None and o.dma_key not in skip:
                last_dma[o.dma_key] = o
        extra.extend(last_dma.values())
        b = self.op("dve", lambda e: e.memset(tile, 0.0), extra=extra)
        for e in ENGS:
            if e != "dve":
                self.force[e] = b
        return b

    def emit(self, final_dma_keys=()):
        nc = self.nc
        import contextlib
        for o in self.ops:
            best = {}
            for w in o.waits:
                if w[0] == "dma":
                    k = ("dma", w[1])
                    if k not in best or best[k][2] < w[2]:
                        best[k] = w
                else:
                    k = ("eng", w[1])
                    if k not in best or best[k][2].eidx < w[2].eidx:
                        best[k] = w
            o.waits = list(best.values())
        for e in ENGS:
            c = 0
            for o in self.by_eng[e]:
                if o.signal:
                    c += 1
                    o.sigcnt = c
        with contextlib.ExitStack() as st:
            esem = {e: st.enter_context(nc.semaphore("s_" + e)) for e in ENGS}
            dsem = {k: st.enter_context(nc.semaphore("d_%d" % i))
                    for i, k in enumerate(sorted(self.dma_counts))}
            block = st.enter_context(nc.Block())

            def run(e, eng):
                for o in self.by_eng[e]:
                    for w in o.waits:
                        if w[0] == "dma":
                            eng.wait_ge(dsem[w[1]], w[2])
                        else:
                            eng.wait_ge(esem[w[1]], w[2].sigcnt)
                    inst = o.fn(eng)
                    if o.dma_key is not None:
                        inst.then_inc(dsem[o.dma_key], 16)
                    elif o.signal:
                        inst.then_inc(esem[e], 1)
                if e == "sp":
                    for k in final_dma_keys:
                        eng.wait_ge(dsem[k], self.dma_counts[k])

            @block.tensor
            def _(eng):
                run("pe", eng)

            @block.scalar
            def _(eng):
                run("act", eng)

            @block.vector
            def _(eng):
                run("dve", eng)

            @block.gpsimd
            def _(eng):
                run("pool", eng)

            @block.sync
            def _(eng):
                run("sp", eng)

F32 = mybir.dt.float32
BF16 = mybir.dt.bfloat16
AF = mybir.ActivationFunctionType
ALU = mybir.AluOpType
NEGM = -30000.0
NBLK = 32
NOWN = 8
SEQV = NBLK * 512


def _t5_bucket_np(rel):
    half = 16
    n = -rel
    ret = np.where(n < 0, half, 0)
    n = np.abs(n)
    max_exact = 8
    nf = np.maximum(n, 1).astype(np.float32)
    large = max_exact + (np.log(nf / np.float32(max_exact)) / np.float32(math.log(128 / max_exact))
                         * np.float32(half - max_exact)).astype(np.int32)
    large = np.minimum(large, half - 1)
    return ret + np.where(n < max_exact, n, large)


def _consts():
    ident = np.eye(128, dtype=np.float32)
    J = ident[::-1].copy()
    i = np.arange(384)
    delta = i - 255
    CA = np.zeros((32, 384), np.float32)
    bk = _t5_bucket_np(delta.astype(np.int32))
    CA[bk, i] += 1.0
    CA[15, :] -= 1.0
    CA[:, 383] = 0.0
    CB = np.zeros((384, 384), np.float32)
    idx = np.clip(delta, -128, 128) + 128
    CB[idx, i] += 1.0
    CB[0, :] -= 1.0
    CB[:, 383] = 0.0
    return ident, J, CA, CB


def build_program(phases="ABCD", nblk=32):
    global NBLK, NOWN, SEQV
    NBLK = nblk
    NOWN = nblk // 4
    SEQV = nblk * 512
    NTOK = NOWN * 512
    nc = bass.Bass("TRN2", target_bir_lowering=False)
    T = {}

    def din(name, shape, dt=F32):
        T[name] = nc.dram_tensor(name, shape, dt, kind="ExternalInput")
        return T[name]

    def dout(name, shape, dt=F32):
        T[name] = nc.dram_tensor(name, shape, dt, kind="ExternalOutput")
        return T[name]

    def dscr(name, shape, dt=BF16):
        T[name] = nc.dram_tensor(name, shape, dt, kind="Internal")
        return T[name]

    xv = din("xv", [SEQV, 1024]); pv = din("pv", [NTOK, 256])
    xsm = din("xsm", [64, 1024]); psm = din("psm", [64, 256])
    cak = din("cak", [2048, 512]); cav = din("cav", [2048, 512])
    cbk = din("cbk", [1024, 512]); cbv = din("cbv", [1024, 512])
    padmask = din("padmask", [128, 4])
    t5 = din("t5", [32, 4]); bt = din("bt", [257, 8])
    g_attn = din("g_attn", [1, 1024]); w_in = din("w_in", [1024, 3072])
    lq1 = din("lq1", [1, 64]); lk1 = din("lk1", [1, 64]); lq2 = din("lq2", [1, 64]); lk2 = din("lk2", [1, 64])
    subln = din("subln", [1, 128]); onb = din("onb", [1, 512])
    w_out = din("w_out", [1024, 1024]); g_mlp = din("g_mlp", [1, 1024])
    w_up = din("w_up", [1024, 4096]); w_down = din("w_down", [4096, 1024])
    w_gate = din("w_gate", [1024, 1024]); w_ple = din("w_ple", [256, 1024]); g_final = din("g_final", [1, 1024])
    identd = din("ident", [128, 128]); Jd = din("J", [128, 128]); CAd = din("CA", [32, 384]); CBd = din("CB", [384, 384])

    y = dout("y", [NTOK, 1024]); ys = dout("ys", [64, 1024])
    nak = dout("nak", [NTOK, 512]); nav = dout("nav", [NTOK, 512])
    nbk = dout("nbk", [512, 512]); nbv = dout("nbv", [512, 512])
    sak = dout("sak", [64, 512]); sav = dout("sav", [64, 512]); sbk = dout("sbk", [64, 512]); sbv = dout("sbv", [64, 512])

    KTs = dscr("KTs", [4, 128, SEQV]); VAs = dscr("VAs", [SEQV, 512]); QTs = dscr("QTs", [4, 128, NTOK])
    OBNs = dscr("OBNs", [NTOK, 512]); OANss = dscr("OANss", [64, 512]); OBNss = dscr("OBNss", [64, 512])
    vecA = dscr("vecA", [4, 384], F32); vecB = dscr("vecB", [8, 384], F32)
    wout_b = dscr("wout_b", [1024, 1024]); wup_b = dscr("wup_b", [1024, 4096]); wdown_b = dscr("wdown_b", [4096, 1024])
    wgate_b = dscr("wgate_b", [1024, 1024]); wple_b = dscr("wple_b", [256, 1024])

    P = Prog(nc)
    outkeys = set()
    with contextlib.ExitStack() as st:
        def sb(name, shape, dt):
            return st.enter_context(nc.sbuf_tensor(name, shape, dt))

        def psm_(name, shape, dt):
            return st.enter_context(nc.psum_tensor(name, shape, dt))

        R1 = sb("R1", [128, 33024], BF16)
        R2a = sb("R2a", [128, 8192], F32)
        R2b = sb("R2b", [128, 28800], BF16)
        idb = sb("idb", [128, 128], BF16)
        Js = sb("Js", [128, 128], F32)
        Hs = sb("Hs", [128, 128], F32)
        TA = sb("TA", [128, 4, 2, 128], F32)
        TB = sb("TB", [128, 8, 2, 128], F32)
        Tm4 = sb("Tm4", [128, 128], F32)
        chA = sb("chA", [128, 4], F32); chB = sb("chB", [128, 8], F32)
        cmA = sb("cmA", [128, 4, 3], F32); cmB = sb("cmB", [128, 8], F32)
        pmk = sb("pmk", [128, 4], F32)
        lam4 = sb("lam4", [128, 4, 64], F32)
        lcol = sb("lcol", [128, 8], F32)
        sublnbc = sb("sublnbc", [128, 128], F32)
        onbbc = sb("onbbc", [128, 512], F32)
        junk = sb("junk", [128, 1024], BF16)
        Eb = sb("Eb", [128, 3, 2, 512], BF16)
        cols = sb("cols", [128, 64], F32)
        eps6 = sb("eps6", [128, 1], F32); eps5 = sb("eps5", [128, 1], F32)
        bar = sb("bar", [128, 1], F32)
        osm = sb("osm", [128, 2, 2, 128], F32)
        t5s = sb("t5s", [32, 4], F32); bts = sb("bts", [128, 3, 8], F32)
        CAs = sb("CAs", [32, 384], F32); CBs = sb("CBs", [128, 3, 384], F32)
        vst = sb("vst", [8, 384], F32)

        TP = psm_("TP", [128, 2, 1024], BF16)
        PS = psm_("PS", [128, 6, 512], F32)
        TPf = TP.bitcast(F32)
        obanks = [(PS[:, 4, :], "PS4"), (PS[:, 5, :], "PS5"), (TPf[:, 0, :], "TP0")]

        cnt = {"ev": 0, "tp": 0, "ps": 0, "sl": 0, "ost": 0, "el": 0}

        KM = {"k_idb": "q1", "k_w": "q10", "k_v0": "q14", "k_v1": "q15", "k_h": "q16",
              "k_xb": "q0", "k_ktst": "q1", "k_vast": "q2", "k_qast": "q3", "k_ost0": "q4", "k_ost1": "q5", "k_obst": "q6",
              "k_win": "q7", "k_c11": "q8",
              "k_kth0": "q0", "k_kth1": "q1", "k_kth2": "q2", "k_kth3": "q3", "k_vh0": "q0", "k_vh1": "q1", "k_vh2": "q2",
              "k_vh3": "q3", "k_qth": "q4",
              "k_wr0": "q0", "k_wr1": "q1", "k_wr2": "q2", "k_wr3": "q3", "k_hres": "q4", "k_ps": "q5", "k_obn": "q6",
              "k_y": "q7", "k_c12": "q8", "k_c13": "q9",
              "k_cst": "q1", "k_vc": "q2", "k_vbc": "q3", "k_oans": "q6", "k_obns": "q10", "k_oanl": "q9"}
        for i_ in range(11):
            KM["k_c%d" % i_] = "q0"
        KM.update({"k_xb0": "q0", "k_xb1": "q11", "k_xb2": "q12", "k_xb3": "q13"})

        def dmaL(out, in_, reads=(), writes=(), key=None, eng="sp"):
            key = KM[key]
            return P.op(eng, lambda e: e.dma_start(out=out, in_=in_), reads=reads, writes=writes, dma_key=key)

        def dmaO(out, in_, reads, key):
            key = KM[key]
            outkeys.add(key)
            return P.op("sp" if "4" in phases else "pool", lambda e: e.dma_start(out=out, in_=in_), reads=reads, dma_key=key)

        def evac(out, in_, reads, writes, scale=None, eng=None):
            if eng is None:
                cnt["ev"] += 1
                eng = "act" if cnt["ev"] % 2 else "dve"
            if eng == "act":
                s = 1.0 if scale is None else scale
                return P.op("act", lambda e: e.activation(out=out, in_=in_, func=AF.Copy, scale=s), reads=reads, writes=writes)
            if scale is None:
                return P.op("dve", lambda e: e.tensor_copy(out=out, in_=in_), reads=reads, writes=writes)
            return P.op("dve", lambda e: e.tensor_scalar(out=out, in0=in_, scalar1=scale, scalar2=None, op0=ALU.mult),
                        reads=reads, writes=writes)

        pe_state = {"mode": None}

        def pe_op(K, M, fn, reads=(), writes=()):
            r = lambda x: 32 if x <= 32 else (64 if x <= 64 else 128)
            mode = (r(K), r(M))
            if pe_state["mode"] is not None and pe_state["mode"] != mode:
                P.op("pe", lambda e: e.drain())
            pe_state["mode"] = mode
            return P.op("pe", fn, reads=reads, writes=writes)

        def nextps():
            cnt["ps"] = (cnt["ps"] + 1) % 4
            return cnt["ps"]

        def rms_rstd(src, rows, F, eps_t, ci, reads, tag):
            P.op("act", lambda e: e.activation(out=junk[0:rows, 0:F], in_=src, func=AF.Square,
                                               accum_out=cols[0:rows, ci:ci + 1]),
                 reads=reads, writes=["junk", "c%d" % ci])
            P.op("act", lambda e: e.activation(out=cols[0:rows, ci + 1:ci + 2], in_=cols[0:rows, ci:ci + 1], func=AF.Sqrt,
                                               bias=eps_t[0:rows, 0:1], scale=1.0 / F),
                 reads=["c%d" % ci, "eps"], writes=["c%d" % (ci + 1)])
            P.op("dve", lambda e: e.reciprocal(out=cols[0:rows, ci + 2:ci + 3], in_=cols[0:rows, ci + 1:ci + 2]),
                 reads=["c%d" % (ci + 1)], writes=["c%d" % (ci + 2)])
            return cols[0:rows, ci + 2:ci + 3], "c%d" % (ci + 2)

        def transposes(src, rows, nch, dst, reads, writes):
            b = cnt["tp"] % 2
            cnt["tp"] += 1
            for c in range(nch):
                pe_op(rows, 128, (lambda e, c=c: e.transpose(TP[:, b, c * 128:c * 128 + rows], src[:, c * 128:(c + 1) * 128],
                                                       idb[0:rows, 0:rows])),
                     reads=list(reads) + ["idb"], writes=["TP%d" % b])
            tv = TP[:, b, :].rearrange("p (a r) -> p a r", r=128)[:, 0:nch, 0:rows]
            evac(dst, tv, reads=["TP%d" % b], writes=writes)

        def proj_fm(wfn, rhsT, n, reads):
            b = nextps()
            for c in range(8):
                pe_op(128, 128, (lambda e, c=c: e.matmul(PS[:, b, 0:n], lhsT=wfn(c), rhs=rhsT[:, c, 0:n], start=(c == 0), stop=(c == 7))),
                     reads=reads, writes=["PS%d" % b])
            return PS[:, b, 0:n], "PS%d" % b

        def proj_tm(xT, tok0, rows, wfn, reads, nk=8):
            b = nextps()
            for c in range(nk):
                pe_op(128, rows, (lambda e, c=c: e.matmul(PS[0:rows, b, :], lhsT=xT[:, c, tok0:tok0 + rows], rhs=wfn(c),
                                                    start=(c == 0), stop=(c == nk - 1))),
                     reads=reads, writes=["PS%d" % b])
            return PS[0:rows, b, :], "PS%d" % b

        osb_state = {"i": 0}

        def pair_attn(keytiles, QT, qtiles, ed, finalize, qreads, osb=None):
            nqt = len(qtiles)
            per_bank = 512 // (ed + 1)
            started = set()

            def oloc(u, qi):
                g = u * nqt + qi
                bank, slot = divmod(g, per_bank)
                ap, nm = obanks[bank]
                rows = qtiles[qi][1]
                return ap[0:rows, slot * (ed + 1):(slot + 1) * (ed + 1)], nm, bank

            def qk(kt):
                sl = cnt["sl"] % 2
                cnt["sl"] += 1
                kt["sl"] = sl
                kt["el"] = cnt["el"] % 3
                cnt["el"] += 1
                nk = kt["nk"]
                c0 = qtiles[kt["qlo"]][0]
                c1 = qtiles[kt["qhi"]][0] + qtiles[kt["qhi"]][1]
                kt["c"] = (c0, c1)
                for u in range(2):
                    pe_op(128, nk, (lambda e, u=u: e.matmul(PS[0:nk, sl * 2 + u, c0:c1], lhsT=kt["KT"],
                                                            rhs=QT[u][:, c0:c1], start=True, stop=True)),
                         reads=list(kt["reads"]) + list(qreads), writes=["PS%d" % (sl * 2 + u)])
            def qk2(kt):
                sl = kt["sl"]
                el = kt["el"]
                nk = kt["nk"]
                c0, c1 = kt["c"]
                for (qi, Ts) in ([] if "k" in phases else kt["adds"]):
                    q0, qr = qtiles[qi]
                    for u in range(2):
                        P.op("dve", (lambda e, u=u, q0=q0, qr=qr, Ts=Ts: e.tensor_tensor(
                            out=PS[0:nk, sl * 2 + u, q0:q0 + qr], in0=PS[0:nk, sl * 2 + u, q0:q0 + qr], in1=Ts[u], op=ALU.add)),
                            reads=["PS%d" % (sl * 2 + u), "Tt"], writes=["PS%d" % (sl * 2 + u)], self_sync=False)
                if kt["bias"][0] is kt["bias"][1]:
                    P.op("act", lambda e: e.activation(out=Eb[0:nk, el, :, c0:c1], in_=PS[0:nk, sl * 2:sl * 2 + 2, c0:c1],
                                                       func=AF.Exp, bias=kt["bias"][0], scale=1.0),
                         reads=["PS%d" % (sl * 2), "PS%d" % (sl * 2 + 1), "bias"], writes=["E%d" % el])
                else:
                    for u in range(2):
                        P.op("act", (lambda e, u=u: e.activation(out=Eb[0:nk, el, u, c0:c1], in_=PS[0:nk, sl * 2 + u, c0:c1],
                                                                 func=AF.Exp, bias=kt["bias"][u], scale=1.0)),
                             reads=["PS%d" % (sl * 2 + u), "bias"], writes=["E%d" % el])

            def pvm(kt):
                if "l" in phases:
                    return
                sl = kt["el"]
                nk = kt["nk"]
                for u in range(2):
                    for qi in range(kt["qlo"], kt["qhi"] + 1):
                        oap, nm, bank = oloc(u, qi)
                        q0, qr = qtiles[qi]
                        first = bank not in started
                        started.add(bank)
                        pe_op(nk, qr, (lambda e, u=u, oap=oap, q0=q0, qr=qr, first=first: e.matmul(
                            oap, lhsT=Eb[0:nk, sl, u, q0:q0 + qr], rhs=kt["V"][u], start=first, stop=False,
                            skip_group_check=True)),
                            reads=["E%d" % sl] + list(kt["reads"]), writes=[nm])

            pend = []
            for kt in keytiles:
                qk(kt)
                pend.append(kt)
                if len(pend) > 2:
                    pvm(pend.pop(0))
                qk2(kt)
            for kt in pend:
                pvm(kt)
            O = [[oloc(u, qi)[0] for qi in range(nqt)] for u in range(2)]
            names = sorted({oloc(u, qi)[1] for u in range(2) for qi in range(nqt)})
            if osb is not None:
                sset = osb[osb_state["i"] % len(osb)]
                osb_state["i"] += 1
                used = sorted({oloc(u, qi)[2] for u in range(2) for qi in range(nqt)})
                for bk in used:
                    bap, bnm = obanks[bk]
                    P.op("dve", (lambda e, bk=bk, bap=bap: e.tensor_copy(out=sset[0][:, bk, :], in_=bap)), reads=[bnm],
                         writes=[sset[1] + str(bk)])

                def oloc2(u, qi):
                    g = u * nqt + qi
                    bank, slot = divmod(g, per_bank)
                    rows = qtiles[qi][1]
                    return sset[0][0:rows, bank, slot * (ed + 1):(slot + 1) * (ed + 1)], sset[1] + str(bank)
                O = [[oloc2(u, qi)[0] for qi in range(nqt)] for u in range(2)]
                names = sorted({oloc2(u, qi)[1] for u in range(2) for qi in range(nqt)})
            if "m" not in phases:
                finalize(O, names)

        def load_win():
            Wv = R1[:, 0:24576].rearrange("p (c n) -> p c n", n=3072)
            for c in range(8):
                for hh in range(2):
                    dmaL(Wv[:, c, hh * 1536:(hh + 1) * 1536], w_in.ap()[c * 128:(c + 1) * 128, hh * 1536:(hh + 1) * 1536],
                         writes=["W"], key="k_win", eng="pool")
            return Wv

        Wv = load_win()
        P.op("dve", lambda e: e.memset(eps6[:], 1e-6), writes=["eps"])
        P.op("dve", lambda e: e.memset(eps5[:], 1e-5), writes=["eps"])
        dmaL(idb[:], identd.ap(), writes=["idb"], key="k_idb", eng="pool")
        dmaL(Js[:], Jd.ap(), writes=["Js"], key="k_c0")
        dmaL(t5s[:], t5.ap(), writes=["t5s"], key="k_c1")
        P.op("dve", lambda e: e.memset(bts[:], 0.0), writes=["bts"])
        dmaL(bts[:, 0:2, :], bt.ap()[0:256, :].rearrange("(a p) h -> p a h", p=128), writes=["bts"], key="k_c2")
        dmaL(bts[0:1, 2, :], bt.ap()[256:257, :], writes=["bts"], key="k_c2")
        dmaL(CAs[:], CAd.ap(), writes=["CAs"], key="k_c3")
        dmaL(CBs[:], CBd.ap().rearrange("(a p) n -> p a n", p=128), writes=["CBs"], key="k_c4")
        dmaL(chA[:], bass.AP(t5, 15 * 4, [[0, 128], [1, 4]]), writes=["chA"], key="k_c5")
        dmaL(chB[:], bass.AP(bt, 0, [[0, 128], [1, 8]]), writes=["chB"], key="k_c6")
        dmaL(pmk[:], padmask.ap(), writes=["pmk"], key="k_c7")
        for i, lt in enumerate([lq1, lk1, lq2, lk2]):
            dmaL(lam4[:, i, :], bass.AP(lt, 0, [[0, 128], [1, 64]]), writes=["lam4"], key="k_c8")
        dmaL(sublnbc[:], bass.AP(subln, 0, [[0, 128], [1, 128]]), writes=["sublnbc"], key="k_c9")
        dmaL(onbbc[:], bass.AP(onb, 0, [[0, 128], [1, 512]]), writes=["onbbc"], key="k_c10")
        P.barrier(bar[:], skip=("q7",))
        P.op("dve", lambda e: e.tensor_scalar(out=sublnbc[:], in0=sublnbc[:], scalar1=0.8, scalar2=None, op0=ALU.mult),
             reads=["sublnbc"], writes=["sublnbc"])
        P.op("dve", lambda e: e.tensor_tensor(out=lam4[:, 0, :], in0=lam4[:, 0, :], in1=lam4[:, 1, :], op=ALU.mult),
             reads=["lam4"], writes=["lam4"])
        P.op("dve", lambda e: e.tensor_tensor(out=lam4[:, 2, :], in0=lam4[:, 2, :], in1=lam4[:, 3, :], op=ALU.mult),
             reads=["lam4"], writes=["lam4"])
        P.op("dve", lambda e: e.tensor_reduce(out=lcol[:, 0:1], in_=lam4[:, 0, :], axis=mybir.AxisListType.X, op=ALU.add),
             reads=["lam4"], writes=["lcol"])
        P.op("dve", lambda e: e.tensor_reduce(out=lcol[:, 1:2], in_=lam4[:, 2, :], axis=mybir.AxisListType.X, op=ALU.add),
             reads=["lam4"], writes=["lcol"])
        P.op("act", lambda e: e.activation(out=lcol[:, 2:4], in_=lcol[:, 0:2], func=AF.Exp), reads=["lcol"], writes=["lcol"])
        P.op("dve", lambda e: e.scalar_tensor_tensor(out=lcol[:, 4:5], in0=lcol[:, 3:4], scalar=-0.2, in1=lcol[:, 2:3],
                                                     op0=ALU.add, op1=ALU.subtract), reads=["lcol"], writes=["neglam"])
        neglam = lcol[:, 4:5]
        for h in range(4):
            P.op("dve", (lambda e, h=h: e.tensor_scalar(out=cmA[:, h, :], in0=pmk[:, 0:3], scalar1=chA[:, h:h + 1], scalar2=None,
                                                        op0=ALU.add)), reads=["pmk", "chA"], writes=["bias"])
        P.op("dve", lambda e: e.tensor_scalar(out=cmB[:], in0=chB[:], scalar1=pmk[:, 2:3], scalar2=None, op0=ALU.add),
             reads=["pmk", "chB"], writes=["bias"])
        pe_op(32, 4, lambda e: e.matmul(PS[0:4, 0, 0:384], lhsT=t5s[:], rhs=CAs[:], start=True, stop=True),
             reads=["t5s", "CAs"], writes=["PS0"])
        P.op("dve", lambda e: e.tensor_copy(out=vst[0:4, :], in_=PS[0:4, 0, 0:384]), reads=["PS0"], writes=["vst"])
        dmaL(vecA.ap(), vst[0:4, :], reads=["vst"], writes=["vecA"], key="k_v0")
        for a in range(3):
            pe_op(128, 8, (lambda e, a=a: e.matmul(PS[0:8, 1, 0:384], lhsT=bts[:, a, :], rhs=CBs[:, a, :], start=(a == 0), stop=(a == 2))),
                 reads=["bts", "CBs"], writes=["PS1"])
        P.op("dve", lambda e: e.tensor_copy(out=vst[0:8, :], in_=PS[0:8, 1, 0:384]), reads=["PS1", "vecA"], writes=["vst"])
        dmaL(vecB.ap(), vst[0:8, :], reads=["vst"], writes=["vecB"], key="k_v1")
        Hall = R2a[:, 4096:7168].rearrange("p (i n) -> p i n", n=128)
        hi = 0
        hlist = []
        for (vec, Tt, nh) in ((vecA, TA, 4), (vecB, TB, 8)):
            for h in range(nh):
                for kind, base in ((0, 128), (1, 0)):
                    hank = bass.AP(vec, h * 384 + base, [[1, 128], [1, 128]])
                    dmaL(Hall[:, hi, :], hank, reads=["vecA", "vecB"], writes=["Hall"], key="k_h")
                    hlist.append((hi, Tt, h, kind))
                    hi += 1
        for (hi, Tt, h, kind) in hlist:
            bk = 2 + hi % 2
            pe_op(128, 128, (lambda e, hi=hi, bk=bk: e.matmul(PS[:, bk, 0:128], lhsT=Hall[:, hi, :], rhs=Js[:], start=True, stop=True)),
                  reads=["Hall", "Js"], writes=["PS%d" % bk])
            P.op("dve", (lambda e, Tt=Tt, h=h, kind=kind, bk=bk: e.tensor_copy(out=Tt[:, h, kind, :], in_=PS[:, bk, 0:128])),
                 reads=["PS%d" % bk], writes=["Tt"], self_sync=False)
            if kind == 0:
                P.op("dve", (lambda e, Tt=Tt, h=h: e.memset(Tt[64:128, h, 0, 0:64], NEGM)), reads=["Tt"], writes=["Tt"])
        P.op("dve", lambda e: e.memset(Tm4[:], 0.0), writes=["Tt"])
        P.op("dve", lambda e: e.memset(Tm4[0:64, 64:128], NEGM), reads=["Tt"], writes=["Tt"])
        def weight_casts():
            for (src, dst, rows, colsn) in ((w_out, wout_b, 1024, 1024), (w_up, wup_b, 1024, 4096), (w_down, wdown_b, 4096, 1024),
                                           (w_gate, wgate_b, 1024, 1024), (w_ple, wple_b, 256, 1024)):
                sv = src.ap().rearrange("r (a n) -> (r a) n", n=1024)
                dv = dst.ap().rearrange("r (a n) -> (r a) n", n=1024)
                tot = rows * colsn // 1024
                for r0 in range(0, tot, 512):
                    n_ = min(512, tot - r0)
                    P.op("pool", (lambda e, r0=r0, n_=n_, dv=dv, sv=sv: e.dma_start(out=dv[r0:r0 + n_, :], in_=sv[r0:r0 + n_, :],
                                                                                    max_dma_last_dim=2048)),
                         writes=["wscr"], dma_key=KM["k_w"])


        P.barrier(bar[:], skip=("q7",))
        xb = R2a[:, 0:4096].rearrange("p (t f) -> p t f", f=1024)
        ostg = R2a[:, 4096:5120].rearrange("p (s f) -> p s f", f=512)
        obraw = R2a[:, 5120:7168].rearrange("p (t f) -> p t f", f=512)
        gattnbc = R2a[:, 7168:8192]
        xs = R2b[:, 0:4096].rearrange("p (t f) -> p t f", f=1024)
        xsT = R2b[:, 4096:8192].rearrange("p (c n) -> p c n", n=512)
        KTst = R2b[:, 8192:10240].rearrange("p (h n) -> p h n", n=512)
        VAst = R2b[:, 10240:12288].rearrange("p (t n) -> p t n", n=512)
        KBT = R2b[:, 12288:16384].rearrange("p (s c n) -> p s c n", s=2, n=512)
        VBa = R2b[:, 16384:20608].rearrange("p (s t h e) -> p s t h e", s=2, t=4, e=66)
        QBz = R2b[:, 20608:24704].rearrange("p (u c n) -> p u c n", u=2, n=512)
        QAst = R2b[:, 24704:26752].rearrange("p (h n) -> p h n", n=512)
        OBst = R2b[:, 26752:28800].rearrange("p (t n) -> p t n", n=512)
        P.op("pool", lambda e: e.memset(QBz, 0.0), writes=["QBT"])
        dmaL(gattnbc, bass.AP(g_attn, 0, [[0, 128], [1, 1024]]), writes=["gattnbc"], key="k_c11")
        P.op("dve", lambda e: e.memset(VBa[:, :, :, :, 64:66], 1.0), writes=["VBones"])

        def out_store(dst_ap, psum_ap, psname, rows=128):
            s = cnt["ost"] % 2
            cnt["ost"] += 1
            evac(ostg[0:rows, s, :], psum_ap, reads=[psname], writes=["ostg%d" % s])
            dmaO(dst_ap, ostg[0:rows, s, :], reads=["ostg%d" % s], key="k_ost%d" % s)
            return ostg[0:rows, s, :], "ostg%d" % s

        def band_attention(p, I):
            sp_, so_ = (p - 1) % 2, p % 2
            qtl = [(i * 128, 128) for i in range(4)]
            for cb in range(4):
                kts = []
                for r in range(-4, 4):
                    slot, tk = (sp_, r + 4) if r < 0 else (so_, r)
                    qlo, qhi = max(0, r), min(3, r + 4)
                    adds = []
                    for qi in range(qlo, qhi + 1):
                        rel = r - qi
                        if rel == 0:
                            adds.append((qi, [TB[:, 2 * cb + u, 0, :] for u in range(2)]))
                        elif rel == -1:
                            adds.append((qi, [TB[:, 2 * cb + u, 1, :] for u in range(2)]))
                        elif rel == -4:
                            adds.append((qi, [Tm4[:], Tm4[:]]))
                    bsrc = cmB if (p == 3 and r < 0) else chB
                    kts.append(dict(KT=KBT[:, slot, cb, tk * 128:(tk + 1) * 128], nk=128,
                                    V=[VBa[:, slot, tk, 2 * cb + u, 0:65] for u in range(2)],
                                    bias=[bsrc[:, 2 * cb + u:2 * cb + u + 1] for u in range(2)],
                                    qlo=qlo, qhi=qhi, adds=adds, reads=["KBT%d" % slot, "VB%d" % slot, "VBones"]))

                def fin(O, names, cb=cb):
                    for u in range(2):
                        hb = 2 * cb + u
                        for qi in range(4):
                            ci = 8 + (qi * 2 + u)
                            P.op("dve", (lambda e, u=u, qi=qi, ci=ci: e.reciprocal(out=cols[:, ci:ci + 1], in_=O[u][qi][:, 64:65])),
                                 reads=names, writes=["c%d" % ci])
                            P.op("dve", (lambda e, u=u, qi=qi, ci=ci, hb=hb: e.tensor_scalar(
                                out=obraw[:, qi, hb * 64:(hb + 1) * 64], in0=O[u][qi][:, 0:64], scalar1=cols[:, ci:ci + 1],
                                scalar2=None, op0=ALU.mult)), reads=names + ["c%d" % ci], writes=["obraw"])
                pair_attn(kts, [QBz[:, 0, cb, :], QBz[:, 1, cb, :]], qtl, 64, fin, ["QBT"])
            for qi in range(4):
                rc, rn = rms_rstd(obraw[:, qi, :], 128, 512, eps6, 16 + 3 * qi, ["obraw"], "ob")
                P.op("dve", (lambda e, qi=qi, rc=rc: e.scalar_tensor_tensor(out=OBst[:, qi, :], in0=obraw[:, qi, :], scalar=rc,
                                                                              in1=onbbc[:], op0=ALU.mult, op1=ALU.mult)),
                     reads=["obraw", rn, "onbbc"], writes=["OBst"])
            dmaL(OBNs.ap()[I * 512:(I + 1) * 512, :].rearrange("(t p) n -> p t n", p=128), OBst, reads=["OBst"], writes=["OBNs"],
                 key="k_obst", eng="pool")

        def phaseA_block(p):
            own = (p % 4 == 3)
            I = p // 4
            last = (p == NBLK - 1)
            so_ = p % 2
            for t in range(4):
                dmaL(xb[:, t, :], xv.ap()[p * 512 + t * 128:p * 512 + (t + 1) * 128, :], writes=["xb%d" % t], key="k_xb%d" % t)
            for t in range(4):
                rc, rn = rms_rstd(xb[:, t, :], 128, 1024, eps6, 3 * t, ["xb%d" % t], "x")
                P.op("dve", (lambda e, t=t, rc=rc: e.scalar_tensor_tensor(out=xs[:, t, :], in0=xb[:, t, :], scalar=rc, in1=gattnbc,
                                                                            op0=ALU.mult, op1=ALU.mult)),
                     reads=["xb%d" % t, rn, "gattnbc"], writes=["xs%d" % t])
            for t in range(4):
                transposes(xs[:, t, :], 128, 8, xsT[:, :, t * 128:(t + 1) * 128], ["xs%d" % t], ["xsT%d" % t])
            for h in range(4):
                ps_, nm = proj_fm(lambda c, h=h: Wv[:, c, 512 + h * 128:512 + (h + 1) * 128], xsT, 512, ["W", "xsT0", "xsT1", "xsT2", "xsT3"])
                evac(KTst[:, h, :], ps_, reads=[nm], writes=["KTst"])
            dmaL(KTs.ap().rearrange("h p n -> p h n")[:, :, p * 512:(p + 1) * 512], KTst, reads=["KTst"], writes=["KTs"],
                 key="k_ktst", eng="pool")
            needb = (p % 4 >= 2)
            for cb in (range(4) if needb else ()):
                ps_, nm = proj_fm(lambda c, cb=cb: Wv[:, c, 2048 + cb * 128:2048 + (cb + 1) * 128], xsT, 512, ["W", "xsT0", "xsT1", "xsT2", "xsT3"])
                evac(KBT[:, so_, cb, :], ps_, reads=[nm], writes=["KBT%d" % so_])
            if own and "3" not in phases:
                for h in range(4):
                    ps_, nm = proj_fm(lambda c, h=h: Wv[:, c, h * 128:(h + 1) * 128], xsT, 512, ["W", "xsT0", "xsT1", "xsT2", "xsT3"])
                    evac(QAst[:, h, :], ps_, reads=[nm], writes=["QAst"], scale=0.125)
                dmaL(QTs.ap().rearrange("h p n -> p h n")[:, :, I * 512:(I + 1) * 512], QAst, reads=["QAst"], writes=["QTs"],
                     key="k_qast", eng="pool")
                for cb in range(4):
                    ps_, nm = proj_fm(lambda c, cb=cb: Wv[:, c, 1536 + cb * 128:1536 + (cb + 1) * 128], xsT, 512, ["W", "xsT0", "xsT1", "xsT2", "xsT3"])
                    evac(QBz[:, 0, cb, :], ps_, reads=[nm], writes=["QBT"], scale=0.125)
                    P.op("pool", (lambda e, cb=cb: e.tensor_copy(out=QBz[64:128, 1, cb, :], in_=QBz[64:128, 0, cb, :])),
                         reads=["QBT"], writes=["QBT"])
                    P.op("pool", (lambda e, cb=cb: e.memset(QBz[64:128, 0, cb, :], 0.0)), reads=["QBT"], writes=["QBT"])
            for t in range(4):
                ps_, nm = proj_tm(xsT, t * 128, 128, lambda c: Wv[:, c, 1024:1536], ["W", "xsT%d" % t])
                if own:
                    sa, sn = out_store(nav.ap()[I * 512 + t * 128:I * 512 + (t + 1) * 128, :], ps_, nm)
                    P.op("pool", (lambda e, t=t, sa=sa: e.tensor_copy(out=VAst[:, t, :], in_=sa)), reads=[sn], writes=["VAst"])
                else:
                    evac(VAst[:, t, :], ps_, reads=[nm], writes=["VAst"])
                if not needb:
                    continue
                ps_, nm = proj_tm(xsT, t * 128, 128, lambda c: Wv[:, c, 2560:3072], ["W", "xsT%d" % t])
                if last:
                    sa, sn = out_store(nbv.ap()[t * 128:(t + 1) * 128, :], ps_, nm)
                    P.op("pool", (lambda e, t=t, sa=sa: e.tensor_copy(out=VBa[:, so_, t, :, 0:64],
                                                                      in_=sa.rearrange("p (h e) -> p h e", e=64))),
                         reads=[sn], writes=["VB%d" % so_])
                else:
                    evac(VBa[:, so_, t, :, 0:64], ps_.rearrange("p (h e) -> p h e", e=64), reads=[nm], writes=["VB%d" % so_])
                if own:
                    ps_, nm = proj_tm(xsT, t * 128, 128, lambda c: Wv[:, c, 512:1024], ["W", "xsT%d" % t])
                    out_store(nak.ap()[I * 512 + t * 128:I * 512 + (t + 1) * 128, :], ps_, nm)
                if last:
                    ps_, nm = proj_tm(xsT, t * 128, 128, lambda c: Wv[:, c, 2048:2560], ["W", "xsT%d" % t])
                    out_store(nbk.ap()[t * 128:(t + 1) * 128, :], ps_, nm)
            dmaL(VAs.ap()[p * 512:(p + 1) * 512, :].rearrange("(t p) n -> p t n", p=128), VAst, reads=["VAst"], writes=["VAs"],
                 key="k_vast", eng="pool")
            if own and "1" not in phases:
                band_attention(p, I)

        if "A" in phases:
            for p in range(NBLK):
                phaseA_block(p)
        if "a" in phases:
            for p in range(4):
                phaseA_block(p)
        if "e" in phases:
            for p in range(3):
                phaseA_block(p)
        P.barrier(bar[:])

        KTh = R1[:, 0:16384]
        Vaug = R1[:, 16384:33024].rearrange("p (t e) -> p t e", e=130)
        OAN = R2b[:, 0:16384].rearrange("p (t n) -> p t n", n=512)
        QTz = R2b[:, 16384:24576].rearrange("p (u n) -> p u n", u=2)

        def finA_factory(h, dst_fn, rows):
            def fin(O, names):
                nqt = len(O[0])
                for qi in range(nqt):
                    pr = qi % 2
                    cb_ = 28 + 8 * pr
                    P.op("dve", (lambda e, qi=qi, cb_=cb_: e.reciprocal(out=cols[0:rows, cb_:cb_ + 1], in_=O[0][qi][:, 128:129])),
                         reads=names, writes=["fa%d" % pr])
                    P.op("dve", (lambda e, qi=qi, cb_=cb_: e.reciprocal(out=cols[0:rows, cb_ + 1:cb_ + 2], in_=O[1][qi][:, 128:129])),
                         reads=names + ["fa%d" % pr], writes=["fa%d" % pr])
                    P.op("dve", (lambda e, cb_=cb_: e.tensor_scalar(out=cols[0:rows, cb_ + 2:cb_ + 3], in0=cols[0:rows, cb_ + 1:cb_ + 2],
                                                                   scalar1=neglam[0:rows, :], scalar2=None, op0=ALU.mult)),
                         reads=["fa%d" % pr, "neglam"], writes=["fa%d" % pr])
                    P.op("dve", (lambda e, qi=qi, cb_=cb_, pr=pr: e.tensor_scalar(out=osm[0:rows, pr, 0, :], in0=O[1][qi][:, 0:128],
                                                                                  scalar1=cols[0:rows, cb_ + 2:cb_ + 3], scalar2=None,
                                                                                  op0=ALU.mult)),
                         reads=names + ["fa%d" % pr], writes=["osm%d" % pr])
                    P.op("dve", (lambda e, qi=qi, cb_=cb_, pr=pr: e.scalar_tensor_tensor(
                        out=osm[0:rows, pr, 1, :], in0=O[0][qi][:, 0:128], scalar=cols[0:rows, cb_:cb_ + 1], in1=osm[0:rows, pr, 0, :],
                        op0=ALU.mult, op1=ALU.add)), reads=names + ["fa%d" % pr, "osm%d" % pr], writes=["osm%d" % pr])
                    ci = cb_ + 3
                    P.op("dve", (lambda e, pr=pr, ci=ci: e.scalar_tensor_tensor(
                        out=osm[0:rows, pr, 0, :], in0=osm[0:rows, pr, 1, :], scalar=1.0, in1=osm[0:rows, pr, 1, :],
                        op0=ALU.mult, op1=ALU.mult, accum_out=cols[0:rows, ci:ci + 1])),
                        reads=["osm%d" % pr], writes=["osm%d" % pr, "fb%d" % pr])
                    P.op("act", (lambda e, ci=ci: e.activation(out=cols[0:rows, ci + 1:ci + 2], in_=cols[0:rows, ci:ci + 1], func=AF.Ln,
                                                              bias=eps5[0:rows, 0:1], scale=1.0 / 128)),
                         reads=["fb%d" % pr, "eps"], writes=["fb%d" % pr])
                    P.op("act", (lambda e, ci=ci: e.activation(out=cols[0:rows, ci + 2:ci + 3], in_=cols[0:rows, ci + 1:ci + 2],
                                                              func=AF.Exp, scale=-0.5)),
                         reads=["fb%d" % pr], writes=["fb%d" % pr])
                    dst, dnm = dst_fn(qi)
                    P.op("dve", (lambda e, pr=pr, ci=ci, dst=dst: e.scalar_tensor_tensor(
                        out=dst, in0=osm[0:rows, pr, 1, :], scalar=cols[0:rows, ci + 2:ci + 3], in1=sublnbc[0:rows, :],
                        op0=ALU.mult, op1=ALU.mult)), reads=["osm%d" % pr, "fb%d" % pr, "sublnbc"], writes=[dnm])
            return fin

        def phaseD1():
            xb = R2a[:, 0:1024]
            ostg = R2a[:, 4096:5120].rearrange("p (s f) -> p s f", f=512)
            obraw = R2a[:, 1024:1536]
            o = 0

            def carve(n):
                nonlocal o
                a = R2b[:, o:o + n]
                o += n
                return a
            oanl = carve(512)
            xs = carve(1024)
            xsT = carve(8 * 64).rearrange("p (c n) -> p c n", n=64)
            cst = carve(8 * 512).rearrange("p (t n) -> p t n", n=512)
            KTc = carve(4 * 1056).rearrange("p (h n) -> p h n", n=1056)
            Vc = carve(9 * 4 * 130).rearrange("p (t h e) -> p t h e", h=4, e=130)
            KBc = carve(4 * 544).rearrange("p (c n) -> p c n", n=544)
            VBc = carve(5 * 8 * 66).rearrange("p (t h e) -> p t h e", h=8, e=66)
            QAz = carve(2 * 4 * 64).rearrange("p (u h n) -> p u h n", u=2, n=64)
            QBzs = carve(2 * 4 * 64).rearrange("p (u c n) -> p u c n", u=2, n=64)
            P.op("pool", lambda e: e.memset(QAz, 0.0), writes=["QAs"])
            P.op("pool", lambda e: e.memset(QBzs, 0.0), writes=["QBs"])
            oans = carve(512)
            obns = carve(512)
            P.op("dve", lambda e: e.memset(Vc[:, :, :, 128:130], 1.0), writes=["Vc1"])
            P.op("dve", lambda e: e.memset(VBc[:, :, :, 64:66], 1.0), writes=["VBc1"])
            dmaL(xb[0:64, :], xsm.ap(), writes=["xb"], key="k_xb")
            rc, rn = rms_rstd(xb[0:64, :], 64, 1024, eps6, 0, ["xb"], "x")
            P.op("dve", (lambda e, rc=rc: e.scalar_tensor_tensor(out=xs[0:64, :], in0=xb[0:64, :], scalar=rc, in1=gattnbc[0:64, :],
                                                                  op0=ALU.mult, op1=ALU.mult)), reads=["xb", rn, "gattnbc"], writes=["xs"])
            transposes(xs[0:64, :], 64, 8, xsT[:, :, 0:64], ["xs"], ["xsT"])
            for (c0, dst) in ((512, sak), (1024, sav), (2048, sbk), (2560, sbv)):
                ps_, nm = proj_tm(xsT, 0, 64, lambda c, c0=c0: Wv[:, c, c0:c0 + 512], ["W", "xsT"])
                out_store(dst.ap(), ps_, nm, rows=64)
            for h in range(4):
                ps_, nm = proj_fm(lambda c, h=h: Wv[:, c, h * 128:(h + 1) * 128], xsT, 64, ["W", "xsT"])
                evac(QAz[:, 0, h, :], ps_, reads=[nm], writes=["QAs"], scale=0.125)
                P.op("pool", (lambda e, h=h: e.tensor_copy(out=QAz[64:128, 1, h, :], in_=QAz[64:128, 0, h, :])), reads=["QAs"], writes=["QAs"])
                P.op("pool", (lambda e, h=h: e.memset(QAz[64:128, 0, h, :], 0.0)), reads=["QAs"], writes=["QAs"])
                ps_, nm = proj_fm(lambda c, h=h: Wv[:, c, 1536 + h * 128:1536 + (h + 1) * 128], xsT, 64, ["W", "xsT"])
                evac(QBzs[:, 0, h, :], ps_, reads=[nm], writes=["QBs"], scale=0.125)
                P.op("pool", (lambda e, h=h: e.tensor_copy(out=QBzs[64:128, 1, h, :], in_=QBzs[64:128, 0, h, :])), reads=["QBs"], writes=["QBs"])
                P.op("pool", (lambda e, h=h: e.memset(QBzs[64:128, 0, h, :], 0.0)), reads=["QBs"], writes=["QBs"])
            for s in range(2):
                for h in range(4):
                    ps_, nm = proj_fm(lambda c, h=h: Wv[:, c, 512 + h * 128:512 + (h + 1) * 128], xsT[:, :, s * 32:(s + 1) * 32], 32,
                                      ["W", "xsT"])
                    evac(KTc[:, h, 1024:1056], ps_, reads=[nm], writes=["KTc"])
                    ps_, nm = proj_fm(lambda c, h=h: Wv[:, c, 2048 + h * 128:2048 + (h + 1) * 128], xsT[:, :, s * 32:(s + 1) * 32], 32,
                                      ["W", "xsT"])
                    evac(KBc[:, h, 512:544], ps_, reads=[nm], writes=["KBc"])
                ps_, nm = proj_tm(xsT, s * 32, 32, lambda c: Wv[:, c, 1024:1536], ["W", "xsT"])
                evac(Vc[0:32, 8, :, 0:128], ps_.rearrange("p (h e) -> p h e", e=128), reads=[nm], writes=["Vc"])
                ps_, nm = proj_tm(xsT, s * 32, 32, lambda c: Wv[:, c, 2560:3072], ["W", "xsT"])
                evac(VBc[0:32, 4, :, 0:64], ps_.rearrange("p (h e) -> p h e", e=64), reads=[nm], writes=["VBc"])
                dmaL(cst, cak.ap()[s * 1024:(s + 1) * 1024, :].rearrange("(t p) n -> p t n", p=128), writes=["cst"], key="k_cst",
                     eng="pool")
                for t in range(8):
                    transposes(cst[:, t, :], 128, 4, KTc[:, :, t * 128:(t + 1) * 128], ["cst"], ["KTc"])
                dmaL(cst[:, 0:4, :], cbk.ap()[s * 512:(s + 1) * 512, :].rearrange("(t p) n -> p t n", p=128), writes=["cst"],
                     key="k_cst", eng="pool")
                for t in range(4):
                    transposes(cst[:, t, :], 128, 4, KBc[:, :, t * 128:(t + 1) * 128], ["cst"], ["KBc"])
                for h_ in range(4):
                    dmaL(Vc[:, 0:8, h_, 0:128],
                         cav.ap()[s * 1024:(s + 1) * 1024, h_ * 128:(h_ + 1) * 128].rearrange("(t p) e -> p t e", p=128),
                         reads=["Vc1"], writes=["Vc"], key="k_vc", eng="pool")
                for h_ in range(8):
                    dmaL(VBc[:, 0:4, h_, 0:64],
                         cbv.ap()[s * 512:(s + 1) * 512, h_ * 64:(h_ + 1) * 64].rearrange("(t p) e -> p t e", p=128),
                         reads=["VBc1"], writes=["VBc"], key="k_vbc", eng="pool")
                for h in range(4):
                    kts = []
                    for t in range(9):
                        nk = 128 if t < 8 else 32
                        adds = []
                        if t == 7:
                            adds.append((0, [TA[:, h, 1, 0:32]] * 2))
                        if t == 8:
                            adds.append((0, [TA[0:32, h, 0, 0:32]] * 2))
                        bcol = chA[0:nk, h:h + 1]
                        kts.append(dict(KT=KTc[:, h, t * 128:t * 128 + nk], nk=nk, V=[Vc[0:nk, t, h, 0:129]] * 2, bias=[bcol, bcol],
                                        qlo=0, qhi=0, adds=adds, reads=["KTc", "Vc", "Vc1"]))
                    fin = finA_factory(h, lambda qi, h=h: (oans[0:32, h * 128:(h + 1) * 128], "oans"), 32)
                    pair_attn(kts, [QAz[:, u_, h, s * 32:(s + 1) * 32] for u_ in range(2)], [(0, 32)], 128, fin, ["QAs"])
                dmaL(OANss.ap()[s * 32:(s + 1) * 32, :], oans[0:32, :], reads=["oans"], writes=["OANss"], key="k_oans", eng="pool")
                for cb in range(4):
                    kts = []
                    for t in range(5):
                        nk = 128 if t < 4 else 32
                        adds = []
                        if t == 3:
                            adds.append((0, [TB[:, 2 * cb + u, 1, 0:32] for u in range(2)]))
                        if t == 4:
                            adds.append((0, [TB[0:32, 2 * cb + u, 0, 0:32] for u in range(2)]))
                        kts.append(dict(KT=KBc[:, cb, t * 128:t * 128 + nk], nk=nk,
                                        V=[VBc[0:nk, t, 2 * cb + u, 0:65] for u in range(2)],
                                        bias=[chB[0:nk, 2 * cb + u:2 * cb + u + 1] for u in range(2)],
                                        qlo=0, qhi=0, adds=adds, reads=["KBc", "VBc", "VBc1"]))

                    def finb(O, names, cb=cb):
                        for u in range(2):
                            hb = 2 * cb + u
                            ci = 8 + u
                            P.op("dve", (lambda e, u=u, ci=ci: e.reciprocal(out=cols[0:32, ci:ci + 1], in_=O[u][0][:, 64:65])),
                                 reads=names, writes=["c%d" % ci])
                            P.op("dve", (lambda e, u=u, ci=ci, hb=hb: e.tensor_scalar(
                                out=obraw[0:32, hb * 64:(hb + 1) * 64], in0=O[u][0][:, 0:64], scalar1=cols[0:32, ci:ci + 1],
                                scalar2=None, op0=ALU.mult)), reads=names + ["c%d" % ci], writes=["obraw"])
                    pair_attn(kts, [QBzs[:, u_, cb, s * 32:(s + 1) * 32] for u_ in range(2)], [(0, 32)], 64, finb, ["QBs"])
                rc, rn = rms_rstd(obraw[0:32, :], 32, 512, eps6, 16, ["obraw"], "ob")
                P.op("dve", (lambda e, rc=rc: e.scalar_tensor_tensor(out=obns[0:32, :], in0=obraw[0:32, :], scalar=rc, in1=onbbc[0:32, :],
                                                                      op0=ALU.mult, op1=ALU.mult)),
                     reads=["obraw", rn, "onbbc"], writes=["obns"])
                dmaL(OBNss.ap()[s * 32:(s + 1) * 32, :], obns[0:32, :], reads=["obns"], writes=["OBNs"], key="k_obns", eng="pool")
        if "D" in phases:
            phaseD1()
            P.barrier(bar[:])

        weight_casts()
        osbB = [(R2a[:, 0:1536].rearrange("p (b n) -> p b n", n=512), "osbA"),
                (R2a[:, 1536:3072].rearrange("p (b n) -> p b n", n=512), "osbB")]
        if "B" in phases:
            P.op("dve", lambda e: e.memset(Vaug[:, :, 128:130], 1.0), writes=["Vones"])
            P.op("dve", lambda e: e.memset(QTz, 0.0), writes=["QTh", "QTh0"])
            NCH = 4
            for h in range(4):
                for ch in range(NCH):
                    k0 = ch * (SEQV // NCH)
                    k1 = (ch + 1) * (SEQV // NCH)
                    dmaL(KTh[:, k0:k1], KTs.ap()[h, :, k0:k1], reads=["Vones"], writes=["KTh%d" % ch], key="k_kth%d" % ch)
                    dmaL(Vaug[:, k0 // 128:k1 // 128, 0:128],
                         VAs.ap()[k0:k1, h * 128:(h + 1) * 128].rearrange("(t p) e -> p t e", p=128),
                         reads=["Vones"], writes=["Vh%d" % ch], key="k_vh%d" % ch)
                for u_ in range(2):
                    dmaL(QTz[64 * u_:64 * u_ + 64, u_, 0:NTOK], QTs.ap()[h, 64 * u_:64 * u_ + 64, :], reads=["QTh0"], writes=["QTh"],
                         key="k_qth")
                for I in range(NOWN):
                    p = 4 * I + 3
                    kts = []
                    for kt in range(4 * p + 4):
                        r = kt - 4 * p
                        qlo = max(0, r)
                        adds = []
                        if r >= 0:
                            adds.append((r, [TA[:, h, 0, :]] * 2))
                            if r + 1 <= 3:
                                adds.append((r + 1, [TA[:, h, 1, :]] * 2))
                        elif r == -1:
                            adds.append((0, [TA[:, h, 1, :]] * 2))
                        bcol = cmA[:, h, kt // 4:kt // 4 + 1] if kt < 12 else chA[:, h:h + 1]
                        ch = kt * 128 // (SEQV // NCH)
                        kts.append(dict(KT=KTh[:, kt * 128:(kt + 1) * 128], nk=128, V=[Vaug[:, kt, 0:129]] * 2, bias=[bcol, bcol],
                                        qlo=qlo, qhi=3, adds=adds, reads=["KTh%d" % ch, "Vh%d" % ch, "Vones"]))
                    fin = finA_factory(h, lambda qi, I=I, h=h: (OAN[:, I * 4 + qi, h * 128:(h + 1) * 128], "OAN"), 128)
                    pair_attn(kts, [QTz[:, u_, I * 512:(I + 1) * 512] for u_ in range(2)], [(i * 128, 128) for i in range(4)], 128, fin,
                              ["QTh"], osb=osbB)
        P.barrier(bar[:])

        ACT_T = R1[:, 0:16384].rearrange("p (h n) -> p h n", n=512)
        WR = R1[:, 16384:32768].rearrange("p (s n) -> p s n", n=4096)
        hres = R2a[:, 0:4096].rearrange("p (t f) -> p t f", f=1024)
        p_s = R2a[:, 4096:5120].rearrange("p (t f) -> p t f", f=256)
        gs = R2a[:, 5120:5632]
        tmpf = R2a[:, 5632:6144]
        gmlpbc = R2a[:, 6144:7168]
        gfinbc = R2a[:, 7168:8192]
        cs = R2b[:, 16384:20480].rearrange("p (t f) -> p t f", f=1024)
        aT = R2b[:, 20480:24576].rearrange("p (c n) -> p c n", n=512)
        obn_s = R2b[:, 24576:26624].rearrange("p (t n) -> p t n", n=512)
        pb = R2b[:, 26624:27648].rearrange("p (t f) -> p t f", f=256)
        pT = R2b[:, 27648:28672].rearrange("p (c n) -> p c n", n=512)
        wcnt = {"i": 0}

        def wload(src_ap, shape_view):
            s = wcnt["i"] % 4
            wcnt["i"] += 1
            dst = shape_view(WR[:, s, :])
            dmaL(dst, src_ap, reads=["wscr"], writes=["WR%d" % s], key="k_wr%d" % s)
            return dst, "WR%d" % s

        def phaseC_group(tiles, x_ap, p_ap, oan_fn, obn_src, y_ap):
            NT = tiles[-1][0] + tiles[-1][1]
            nt = len(tiles)
            HR = ["hres%d" % t_ for t_ in range(nt)]
            PSN = ["p_s%d" % t_ for t_ in range(nt)]
            ATN = ["aT%d" % t_ for t_ in range(nt)]
            rows0 = tiles[0][1]
            if rows0 == 128:
                dmaL(hres[:, 0:nt, :], x_ap.rearrange("(t p) f -> p t f", p=128), writes=HR, key="k_hres")
                dmaL(p_s[:, 0:nt, :], p_ap.rearrange("(t p) f -> p t f", p=128), writes=PSN, key="k_ps")
                dmaL(obn_s[:, 0:nt, :], obn_src.rearrange("(t p) f -> p t f", p=128), reads=["OBNs"], writes=["obn_s"], key="k_obn")
            else:
                dmaL(hres[0:rows0, 0, :], x_ap, writes=HR, key="k_hres")
                dmaL(p_s[0:rows0, 0, :], p_ap, writes=PSN, key="k_ps")
                dmaL(obn_s[0:rows0, 0, :], obn_src, reads=["OBNs"], writes=["obn_s"], key="k_obn")
            for ti, (tok0, rows) in enumerate(tiles):
                oa, oan_names = oan_fn(ti)
                transposes(oa, rows, 4, aT[:, 0:4, tok0:tok0 + rows], oan_names, ["aT%d" % ti])
                transposes(obn_s[0:rows, ti, :], rows, 4, aT[:, 4:8, tok0:tok0 + rows], ["obn_s"], ["aT%d" % ti])
            wo = [wload(wout_b.ap()[j * 512:(j + 1) * 512, :].rearrange("(c p) n -> p c n", p=128),
                        lambda v: v.rearrange("p (c n) -> p c n", n=1024)) for j in range(2)]
            for ti, (tok0, rows) in enumerate(tiles):
                for hf in range(2):
                    ps_, nm = proj_tm(aT, tok0, rows, lambda c, hf=hf: wo[c // 4][0][:, c % 4, hf * 512:(hf + 1) * 512],
                                      ["aT%d" % ti, wo[0][1], wo[1][1]])
                    P.op("dve", (lambda e, ti=ti, rows=rows, hf=hf, ps_=ps_: e.tensor_tensor(
                        out=hres[0:rows, ti, hf * 512:(hf + 1) * 512], in0=ps_, in1=hres[0:rows, ti, hf * 512:(hf + 1) * 512], op=ALU.add)),
                        reads=[nm, "hres%d" % ti], writes=["hres%d" % ti], self_sync=False)
            for ti, (tok0, rows) in enumerate(tiles):
                rc, rn = rms_rstd(hres[0:rows, ti, :], rows, 1024, eps6, 3 * ti, ["hres%d" % ti], "c")
                P.op("dve", (lambda e, ti=ti, rows=rows, rc=rc: e.scalar_tensor_tensor(out=cs[0:rows, ti, :], in0=hres[0:rows, ti, :],
                                                                                       scalar=rc, in1=gmlpbc[0:rows, :], op0=ALU.mult,
                                                                                       op1=ALU.mult)),
                     reads=["hres%d" % ti, rn, "gmlpbc"], writes=["cs%d" % ti])
                transposes(cs[0:rows, ti, :], rows, 8, aT[:, :, tok0:tok0 + rows], ["cs%d" % ti], ["aT%d" % ti])
            for j in range(8):
                wu, wn = wload(wup_b.ap()[:, j * 512:(j + 1) * 512].rearrange("(c p) n -> p c n", p=128),
                               lambda v: v.rearrange("p (c n) -> p c n", n=512))
                for hl in range(4):
                    hc = j * 4 + hl
                    ps_, nm = proj_fm(lambda c, hl=hl, wu=wu: wu[:, c, hl * 128:(hl + 1) * 128], aT, NT, ATN + [wn])
                    rb, rbn = (tmpf, "tmpf") if hc % 2 else (gs, "gs")
                    P.op("act", (lambda e, ps_=ps_, rb=rb: e.activation(out=rb[:, 0:NT], in_=ps_, func=AF.Relu)),
                         reads=[nm], writes=[rbn])
                    P.op("pool", (lambda e, hc=hc, rb=rb: e.tensor_tensor(out=ACT_T[:, hc, 0:NT], in0=rb[:, 0:NT], in1=rb[:, 0:NT],
                                                                          op=ALU.mult)), reads=[rbn], writes=["ACT_T"])
            for hf in range(2):
                accs = []
                for ti in range(nt):
                    accs.append(ti)
                for j in range(4):
                    wd, wn = wload(wdown_b.ap()[j * 1024:(j + 1) * 1024, hf * 512:(hf + 1) * 512].rearrange("(c p) n -> p c n", p=128),
                                   lambda v: v.rearrange("p (c n) -> p c n", n=512))
                    for hl in range(8):
                        hc = j * 8 + hl
                        for ti, (tok0, rows) in enumerate(tiles):
                            pe_op(128, rows, (lambda e, ti=ti, tok0=tok0, rows=rows, hc=hc, hl=hl, wd=wd: e.matmul(
                                PS[0:rows, ti, :], lhsT=ACT_T[:, hc, tok0:tok0 + rows], rhs=wd[:, hl, :], start=(hc == 0), stop=(hc == 31))),
                                reads=["ACT_T", wn], writes=["PS%d" % ti])
                for ti, (tok0, rows) in enumerate(tiles):
                    P.op("dve", (lambda e, ti=ti, rows=rows, hf=hf: e.tensor_tensor(
                        out=hres[0:rows, ti, hf * 512:(hf + 1) * 512], in0=PS[0:rows, ti, :], in1=hres[0:rows, ti, hf * 512:(hf + 1) * 512],
                        op=ALU.add)), reads=["PS%d" % ti, "hres%d" % ti], writes=["hres%d" % ti], self_sync=False)
            for ti, (tok0, rows) in enumerate(tiles):
                P.op("act", (lambda e, ti=ti, rows=rows: e.activation(out=cs[0:rows, ti, :], in_=hres[0:rows, ti, :], func=AF.Copy, scale=1.0)),
                     reads=["hres%d" % ti], writes=["cs%d" % ti])
                transposes(cs[0:rows, ti, :], rows, 8, aT[:, :, tok0:tok0 + rows], ["cs%d" % ti], ["aT%d" % ti])
                P.op("dve", (lambda e, ti=ti, rows=rows: e.tensor_copy(out=pb[0:rows, ti, :], in_=p_s[0:rows, ti, :])),
                     reads=["p_s%d" % ti], writes=["pb%d" % ti])
                transposes(pb[0:rows, ti, :], rows, 2, pT[:, :, tok0:tok0 + rows], ["pb%d" % ti], ["pT%d" % ti])
            wg = [wload(wgate_b.ap()[j * 512:(j + 1) * 512, :].rearrange("(c p) n -> p c n", p=128),
                        lambda v: v.rearrange("p (c n) -> p c n", n=1024)) for j in range(2)]
            wp, wpn = wload(wple_b.ap().rearrange("(c p) n -> p c n", p=128),
                            lambda v: v[:, 0:2048].rearrange("p (c n) -> p c n", n=1024))
            for ti, (tok0, rows) in enumerate(tiles):
                for hf in range(2):
                    ps_, nm = proj_tm(aT, tok0, rows, lambda c, hf=hf: wg[c // 4][0][:, c % 4, hf * 512:(hf + 1) * 512],
                                      ["aT%d" % ti, wg[0][1], wg[1][1]])
                    P.op("act", (lambda e, rows=rows, ps_=ps_: e.activation(out=gs[0:rows, :], in_=ps_, func=AF.Sigmoid)),
                         reads=[nm], writes=["gs"])
                    ps2, nm2 = proj_tm(pT, tok0, rows, lambda c, hf=hf: wp[:, c, hf * 512:(hf + 1) * 512], ["pT%d" % ti, wpn], nk=2)
                    P.op("dve", (lambda e, rows=rows, ps2=ps2: e.tensor_tensor(out=tmpf[0:rows, :], in0=gs[0:rows, :], in1=ps2, op=ALU.mult)),
                         reads=["gs", nm2], writes=["tmpf"])
                    P.op("dve", (lambda e, ti=ti, rows=rows, hf=hf: e.tensor_tensor(
                        out=hres[0:rows, ti, hf * 512:(hf + 1) * 512], in0=tmpf[0:rows, :], in1=hres[0:rows, ti, hf * 512:(hf + 1) * 512],
                        op=ALU.add)), reads=["tmpf", "hres%d" % ti], writes=["hres%d" % ti])
            for ti, (tok0, rows) in enumerate(tiles):
                rc, rn = rms_rstd(hres[0:rows, ti, :], rows, 1024, eps6, 3 * ti, ["hres%d" % ti], "f")
                P.op("dve", (lambda e, ti=ti, rows=rows, rc=rc: e.scalar_tensor_tensor(out=hres[0:rows, ti, :], in0=hres[0:rows, ti, :],
                                                                                       scalar=rc, in1=gfinbc[0:rows, :], op0=ALU.mult,
                                                                                       op1=ALU.mult)),
                     reads=["hres%d" % ti, rn, "gfinbc"], writes=["hres%d" % ti])
            if rows0 == 128:
                dmaO(y_ap.rearrange("(t p) f -> p t f", p=128), hres[:, 0:nt, :], reads=HR, key="k_y")
            else:
                dmaO(y_ap, hres[0:rows0, 0, :], reads=HR, key="k_y")

        if "C" in phases or "D" in phases:
            dmaL(gmlpbc, bass.AP(g_mlp, 0, [[0, 128], [1, 1024]]), writes=["gmlpbc"], key="k_c12")
            dmaL(gfinbc, bass.AP(g_final, 0, [[0, 128], [1, 1024]]), writes=["gfinbc"], key="k_c13")
        if "C" in phases:
            for I in range(NOWN):
                phaseC_group([(i * 128, 128) for i in range(4)], xv.ap()[(4 * I + 3) * 512:(4 * I + 4) * 512, :],
                             pv.ap()[I * 512:(I + 1) * 512, :],
                             lambda ti, I=I: (OAN[:, I * 4 + ti, :], ["OAN"]),
                             OBNs.ap()[I * 512:(I + 1) * 512, :], y.ap()[I * 512:(I + 1) * 512, :])
        P.barrier(bar[:])

        def phaseD2():
            oanl = R2b[:, 0:512]
            dmaL(oanl[0:64, :], OANss.ap(), reads=["OANss"], writes=["oanl"], key="k_oanl")
            phaseC_group([(0, 64)], xsm.ap(), psm.ap(), lambda ti: (oanl[0:64, :], ["oanl"]), OBNss.ap(), ys.ap())

        if "D" in phases:
            phaseD2()

        P.emit(final_dma_keys=[] if "5" in phases else sorted(outkeys))
    return nc


_NC_CACHE = {}


def _run(x_prompt, x_sample, cache_a_k, cache_a_v, cache_b_k, cache_b_v, p_prompt, p_sample,
         t5_table, g_attn, w_in, lambda_q1, lambda_k1, lambda_q2, lambda_k2, subln_g,
         band_table, out_norm_b, w_out, g_mlp, w_up, w_down, w_ple_gate, w_ple_proj, g_final,
         phases="ABCD", cores=None, trace=False):
    f = lambda a: np.ascontiguousarray(np.asarray(a, dtype=np.float32))
    x_prompt = f(x_prompt); x_sample = f(x_sample); p_prompt = f(p_prompt); p_sample = f(p_sample)
    cache_a_k = f(cache_a_k); cache_a_v = f(cache_a_v); cache_b_k = f(cache_b_k); cache_b_v = f(cache_b_v)
    S = x_prompt.shape[1]
    nblk = S // 512
    nown = nblk // 4
    seqv = nblk * 512
    ident, J, CA, CB = _consts()
    ck = (phases, nblk)
    if ck not in _NC_CACHE:
        _NC_CACHE[ck] = build_program(phases, nblk)
    nc = _NC_CACHE[ck]
    common = {
        "t5": f(t5_table), "bt": f(band_table)[0], "g_attn": f(g_attn), "w_in": f(w_in)[0],
        "lq1": f(lambda_q1), "lk1": f(lambda_k1), "lq2": f(lambda_q2), "lk2": f(lambda_k2),
        "subln": f(subln_g), "onb": f(out_norm_b), "w_out": f(w_out)[0], "g_mlp": f(g_mlp),
        "w_up": f(w_up)[0], "w_down": f(w_down)[0], "w_gate": f(w_ple_gate)[0], "w_ple": f(w_ple_proj)[0],
        "g_final": f(g_final).reshape(1, 1024), "ident": ident, "J": J, "CA": CA, "CB": CB,
    }
    cores = list(range(8)) if cores is None else list(cores)
    in_maps = []
    for c in cores:
        b, j = divmod(c, 4)
        npad = 3 - j
        xvv = np.zeros((seqv, 1024), np.float32)
        nreal = (nblk - npad) * 512
        xvv[npad * 512:] = x_prompt[b, :nreal]
        pvv = np.concatenate([p_prompt[0, b, (4 * I + j) * 512:(4 * I + j + 1) * 512] for I in range(nown)], 0)
        pm = np.zeros((128, 4), np.float32)
        pm[:, :npad] = NEGM
        m = dict(common)
        m.update({
            "xv": xvv, "pv": np.ascontiguousarray(pvv),
            "xsm": x_sample[2 * c:2 * c + 2].reshape(64, 1024), "psm": p_sample[0, 2 * c:2 * c + 2].reshape(64, 256),
            "cak": cache_a_k[0, 2 * c:2 * c + 2].reshape(2048, 512), "cav": cache_a_v[0, 2 * c:2 * c + 2].reshape(2048, 512),
            "cbk": cache_b_k[0, 2 * c:2 * c + 2].reshape(1024, 512), "cbv": cache_b_v[0, 2 * c:2 * c + 2].reshape(1024, 512),
            "padmask": pm,
        })
        in_maps.append({k: np.ascontiguousarray(v) for k, v in m.items()})
    if trace:
        res = run_bass_kernel_spmd(nc, in_maps, core_ids=list(range(len(cores))), trace=True)
        print("EXEC_TIME_NS", res.exec_time_ns)
    else:
        res = run_bass_kernel_spmd(nc, in_maps, core_ids=list(range(len(cores))))
    R = res.results
    y_prompt = np.zeros((2, S, 1024), np.float32)
    nakp = np.zeros((1, 2, S, 512), np.float32)
    navp = np.zeros((1, 2, S, 512), np.float32)
    nbkp = np.zeros((1, 2, 512, 512), np.float32)
    nbvp = np.zeros((1, 2, 512, 512), np.float32)
    y_sample = np.zeros((16, 32, 1024), np.float32)
    saks = np.zeros((1, 16, 32, 512), np.float32); savs = np.zeros((1, 16, 32, 512), np.float32)
    sbks = np.zeros((1, 16, 32, 512), np.float32); sbvs = np.zeros((1, 16, 32, 512), np.float32)
    for ci, c in enumerate(cores):
        b, j = divmod(c, 4)
        r = R[ci]
        for I in range(nown):
            g0 = (4 * I + j) * 512
            y_prompt[b, g0:g0 + 512] = r["y"][I * 512:(I + 1) * 512]
            nakp[0, b, g0:g0 + 512] = r["nak"][I * 512:(I + 1) * 512]
            navp[0, b, g0:g0 + 512] = r["nav"][I * 512:(I + 1) * 512]
        if j == 3:
            nbkp[0, b] = r["nbk"]
            nbvp[0, b] = r["nbv"]
        y_sample[2 * c:2 * c + 2] = r["ys"].reshape(2, 32, 1024)
        saks[0, 2 * c:2 * c + 2] = r["sak"].reshape(2, 32, 512)
        savs[0, 2 * c:2 * c + 2] = r["sav"].reshape(2, 32, 512)
        sbks[0, 2 * c:2 * c + 2] = r["sbk"].reshape(2, 32, 512)
        sbvs[0, 2 * c:2 * c + 2] = r["sbv"].reshape(2, 32, 512)
    return (y_prompt, y_sample,
            nakp.reshape(1, 2, S, 4, 2, 64), navp.reshape(1, 2, S, 4, 128),
            nbkp.reshape(1, 2, 512, 8, 64), nbvp.reshape(1, 2, 512, 8, 64),
            saks.reshape(1, 16, 32, 4, 2, 64), savs.reshape(1, 16, 32, 4, 128),
            sbks.reshape(1, 16, 32, 8, 64), sbvs.reshape(1, 16, 32, 8, 64))


def kernel(x_prompt, x_sample, cache_a_k, cache_a_v, cache_b_k, cache_b_v, p_prompt, p_sample,
           t5_table, g_attn, w_in, lambda_q1, lambda_k1, lambda_q2, lambda_k2, subln_g,
           band_table, out_norm_b, w_out, g_mlp, w_up, w_down, w_ple_gate, w_ple_proj, g_final):
    return _run(x_prompt, x_sample, cache_a_k, cache_a_v, cache_b_k, cache_b_v, p_prompt, p_sample,
                t5_table, g_attn, w_in, lambda_q1, lambda_k1, lambda_q2, lambda_k2, subln_g,
                band_table, out_norm_b, w_out, g_mlp, w_up, w_down, w_ple_gate, w_ple_proj, g_final)
```
